# Optimizing a Trainium2 kernel written in Bass

```python
import math
import jax, jax.numpy as jnp
from jax import lax
import numpy as np

D_MODEL = 1024
BATCH = 8
SEQ = 4096
DEPTH = 2

N_MIXERS = 2
N_RWKV_LAYERS = (DEPTH + 1) // 2
N_DSA_LAYERS = DEPTH // 2
N_META = 16
D_FF = 2816
NORM_EPS = 1e-6
HEAD_DIM = 64
ROPE_DIM = HEAD_DIM // 4
ROPE_THETA = 500000.0
RWKV_HEADS = D_MODEL // HEAD_DIM
D_DECAY_LORA = 64
D_AAA_LORA = 64
D_GATE_LORA = 160
RWKV_LN_EPS = 64e-5
ATT_HEADS = D_MODEL // HEAD_DIM
ATT_KV_HEADS = 4
ATT_GROUP = ATT_HEADS // ATT_KV_HEADS
IDX_HEADS = 8
IDX_DIM = 64
TOPK_MAX = 256
Q_BLOCK = 128
ATT_Q_W = ATT_HEADS * HEAD_DIM
ATT_KV_W = ATT_KV_HEADS * HEAD_DIM
IDX_Q_W = IDX_HEADS * IDX_DIM
ATT_SPLITS = (ATT_Q_W, ATT_Q_W + ATT_KV_W, ATT_Q_W + 2 * ATT_KV_W,
              ATT_Q_W + 2 * ATT_KV_W + IDX_Q_W, ATT_Q_W + 2 * ATT_KV_W + IDX_Q_W + IDX_DIM)
ATT_IN_W = ATT_Q_W + 2 * ATT_KV_W + IDX_Q_W + IDX_DIM + IDX_HEADS

kernel_name = "hybrid_rwkv7_dsa_macaron"


def rms_norm(x, g, eps=NORM_EPS):
    xf = x.astype(jnp.float32)
    y = xf * lax.rsqrt(jnp.mean(xf * xf, axis=-1, keepdims=True) + eps)
    return y.astype(x.dtype) * g


def swiglu(h, w_in, w_out):
    gate, up = jnp.split(h @ w_in, 2, axis=-1)
    return (jax.nn.silu(gate) * up) @ w_out


def rope_tables(n_pos, dtype):
    inv = ROPE_THETA ** (-jnp.arange(0, ROPE_DIM, 2, dtype=jnp.float32) / ROPE_DIM)
    ang = jnp.arange(n_pos, dtype=jnp.float32)[:, None] * inv[None, :]
    return jnp.cos(ang).astype(dtype), jnp.sin(ang).astype(dtype)


def partial_rope(x, cos, sin):
    half = ROPE_DIM // 2
    x1 = x[..., :half]
    x2 = x[..., half:ROPE_DIM]
    c = cos[None, :, None, :]
    s = sin[None, :, None, :]
    return jnp.concatenate([x1 * c - x2 * s, x2 * c + x1 * s, x[..., ROPE_DIM:]], axis=-1)


def wkv7_scan(r, w, k, v, a, b):
    B, T, H, N = r.shape
    xs = tuple(jnp.moveaxis(t.astype(jnp.float32), 1, 0) for t in (r, w, k, v, a, b))

    def step(S, inp):
        r_t, w_t, k_t, v_t, a_t, b_t = inp
        sa = jnp.einsum("bhij,bhj->bhi", S, a_t)
        S = S * w_t[:, :, None, :] + sa[..., None] * b_t[:, :, None, :] + v_t[..., None] * k_t[:, :, None, :]
        y = jnp.einsum("bhij,bhj->bhi", S, r_t)
        return S, y

    S0 = jnp.zeros((B, H, N, N), jnp.float32)
    _, ys = lax.scan(step, S0, xs)
    return jnp.moveaxis(ys, 0, 1).astype(r.dtype)


def rwkv7_time_mix(h, mix, w_r, w_k, w_v, w_o, w0, w1, w2, a0, a1, a2, g1, g2,
                   k_k, k_a, r_k, lnx_g, lnx_b):
    B, T, D = h.shape
    H, N = RWKV_HEADS, HEAD_DIM
    xx = jnp.pad(h, ((0, 0), (1, 0), (0, 0)))[:, :-1] - h
    xm = h[None] + xx[None] * mix[:, None, None, :]
    xr, xw, xk, xv, xa, xg = xm[0], xm[1], xm[2], xm[3], xm[4], xm[5]
    r = xr @ w_r
    k = xk @ w_k
    v = xv @ w_v
    w_log = -jax.nn.softplus(-(w0 + jnp.tanh(xw @ w1) @ w2)) - 0.5
    a = jax.nn.sigmoid(a0 + (xa @ a1) @ a2)
    g = jax.nn.sigmoid(xg @ g1) @ g2
    kk = (k * k_k).reshape(B, T, H, N).astype(jnp.float32)
    kk = (kk / jnp.maximum(jnp.sqrt(jnp.sum(kk * kk, axis=-1, keepdims=True)), 1e-12)).astype(h.dtype)
    k = k * (1.0 + (a - 1.0) * k_a)
    decay = jnp.exp(-jnp.exp(w_log.astype(jnp.float32)))
    rh = r.reshape(B, T, H, N)
    kh = k.reshape(B, T, H, N)
    vh = v.reshape(B, T, H, N)
    ah = a.reshape(B, T, H, N)
    y = wkv7_scan(rh, decay.reshape(B, T, H, N), kh, vh, -kk, kk * ah)
    yf = y.astype(jnp.float32)
    mu = jnp.mean(yf, axis=-1, keepdims=True)
    var = jnp.mean((yf - mu) ** 2, axis=-1, keepdims=True)
    yn = ((yf - mu) * lax.rsqrt(var + RWKV_LN_EPS)).astype(h.dtype).reshape(B, T, D) * lnx_g + lnx_b
    bonus = (jnp.sum(rh * kh * r_k, axis=-1, keepdims=True) * vh).reshape(B, T, D)
    return ((yn + bonus) * g) @ w_o


def dsa_attention(h, w_in, q_g, k_g, w_o, cos, sin, topk):
    B, T, D = h.shape
    q, k, v, qi, ki, wi = jnp.split(h @ w_in, ATT_SPLITS, axis=-1)
    q = partial_rope(rms_norm(q.reshape(B, T, ATT_HEADS, HEAD_DIM), q_g), cos, sin)
    k = partial_rope(rms_norm(k.reshape(B, T, ATT_KV_HEADS, HEAD_DIM), k_g), cos, sin)
    v = v.reshape(B, T, ATT_KV_HEADS, HEAD_DIM)
    qi = partial_rope(qi.reshape(B, T, IDX_HEADS, IDX_DIM), cos, sin)
    ki = partial_rope(ki[:, :, None, :], cos, sin)[:, :, 0, :]
    wi = wi * (IDX_HEADS ** -0.5 * IDX_DIM ** -0.5)

    n_blk = -(-T // Q_BLOCK)
    pad = n_blk * Q_BLOCK - T

    def to_blocks(t):
        t = jnp.pad(t, [(0, 0), (0, pad)] + [(0, 0)] * (t.ndim - 2))
        return jnp.moveaxis(t.reshape((B, n_blk, Q_BLOCK) + t.shape[2:]), 1, 0)

    starts = jnp.arange(n_blk, dtype=jnp.int32) * Q_BLOCK
    kpos = jnp.arange(T, dtype=jnp.int32)
    scale = HEAD_DIM ** -0.5

    def block(args):
        qb, qib, wib, start = args
        qpos = start + jnp.arange(Q_BLOCK, dtype=jnp.int32)
        rel = jax.nn.relu(jnp.einsum("bqhd,bkd->bqhk", qib, ki))
        score = jnp.einsum("bqh,bqhk->bqk", wib, rel).astype(jnp.float32)
        causal = kpos[None, :] <= qpos[:, None]
        score = jnp.where(causal[None], score, -jnp.inf)
        _, idx = lax.top_k(score, topk)
        valid = idx <= qpos[None, :, None]
        kg = jax.vmap(lambda kb, ib: kb[ib])(k, idx)
        vg = jax.vmap(lambda vb, ib: vb[ib])(v, idx)
        qg = qb.reshape(B, Q_BLOCK, ATT_KV_HEADS, ATT_GROUP, HEAD_DIM)
        logits = jnp.einsum("bqgrd,bqkgd->bqgrk", qg, kg).astype(jnp.float32) * scale
        logits = jnp.where(valid[:, :, None, None, :], logits, -jnp.inf)
        p = jax.nn.softmax(logits, axis=-1).astype(v.dtype)
        o = jnp.einsum("bqgrk,bqkgd->bqgrd", p, vg)
        return o.reshape(B, Q_BLOCK, ATT_Q_W)

    out = lax.map(block, (to_blocks(q), to_blocks(qi), to_blocks(wi), starts))
    out = jnp.moveaxis(out, 0, 1).reshape(B, n_blk * Q_BLOCK, ATT_Q_W)[:, :T]
    return out @ w_o


def setup_inputs(seed: int = 0) -> dict:
    key = jax.random.key(seed)
    ks = jax.random.split(key, 32)
    f32 = jnp.float32

    def nrm(k, shape, scale):
        return jax.random.normal(k, shape, f32) * scale

    D = D_MODEL
    A = N_RWKV_LAYERS
    S = N_DSA_LAYERS
    return {
        "x": nrm(ks[0], (BATCH, SEQ, D), 1.0),
        "meta": nrm(ks[1], (N_META, D), 1.0),
        "norm_g": 1.0 + nrm(ks[2], (DEPTH, 3, D), 0.02),
        "ffn_w_in": nrm(ks[3], (DEPTH, 2, D, 2 * D_FF), D ** -0.5),
        "ffn_w_out": nrm(ks[4], (DEPTH, 2, D_FF, D), D_FF ** -0.5),
        "rk_mix": jax.random.uniform(ks[5], (A, 6, D), f32),
        "rk_w_r": nrm(ks[6], (A, D, D), D ** -0.5),
        "rk_w_k": nrm(ks[7], (A, D, D), D ** -0.5),
        "rk_w_v": nrm(ks[8], (A, D, D), D ** -0.5),
        "rk_w_o": nrm(ks[9], (A, D, D), D ** -0.5),
        "rk_w0": -0.6 + nrm(ks[10], (A, D), 0.3),
        "rk_w1": nrm(ks[11], (A, D, D_DECAY_LORA), D ** -0.5),
        "rk_w2": nrm(ks[12], (A, D_DECAY_LORA, D), 0.1 * D_DECAY_LORA ** -0.5),
        "rk_a0": nrm(ks[13], (A, D), 0.1),
        "rk_a1": nrm(ks[14], (A, D, D_AAA_LORA), D ** -0.5),
        "rk_a2": nrm(ks[15], (A, D_AAA_LORA, D), 0.1 * D_AAA_LORA ** -0.5),
        "rk_g1": nrm(ks[16], (A, D, D_GATE_LORA), D ** -0.5),
        "rk_g2": nrm(ks[17], (A, D_GATE_LORA, D), D_GATE_LORA ** -0.5),
        "rk_k_k": 0.85 + nrm(ks[18], (A, D), 0.05),
        "rk_k_a": 1.0 + nrm(ks[19], (A, D), 0.05),
        "rk_r_k": nrm(ks[20], (A, RWKV_HEADS, HEAD_DIM), 0.1),
        "rk_lnx_g": 1.0 + nrm(ks[21], (A, D), 0.02),
        "rk_lnx_b": nrm(ks[22], (A, D), 0.02),
        "at_w_in": nrm(ks[23], (S, D, ATT_IN_W), D ** -0.5),
        "at_q_g": 1.0 + nrm(ks[24], (S, HEAD_DIM), 0.02),
        "at_k_g": 1.0 + nrm(ks[25], (S, HEAD_DIM), 0.02),
        "at_w_o": nrm(ks[26], (S, ATT_Q_W, D), ATT_Q_W ** -0.5),
    }


def reference(x, meta, norm_g, ffn_w_in, ffn_w_out, rk_mix, rk_w_r, rk_w_k, rk_w_v, rk_w_o,
              rk_w0, rk_w1, rk_w2, rk_a0, rk_a1, rk_a2, rk_g1, rk_g2, rk_k_k, rk_k_a, rk_r_k,
              rk_lnx_g, rk_lnx_b, at_w_in, at_q_g, at_k_g, at_w_o):
    B, L, D = x.shape
    topk = min(TOPK_MAX, L // 4)
    h = jnp.concatenate([jnp.broadcast_to(meta.astype(x.dtype)[None], (B, N_META, D)), x], axis=1)
    T = h.shape[1]
    cos, sin = rope_tables(T, x.dtype)
    for i in range(DEPTH):
        g = norm_g[i]
        h = h + 0.5 * swiglu(rms_norm(h, g[0]), ffn_w_in[i, 0], ffn_w_out[i, 0])
        hn = rms_norm(h, g[1])
        j = i // N_MIXERS
        if i % N_MIXERS == 0:
            h = h + rwkv7_time_mix(hn, rk_mix[j], rk_w_r[j], rk_w_k[j], rk_w_v[j], rk_w_o[j],
                                   rk_w0[j], rk_w1[j], rk_w2[j], rk_a0[j], rk_a1[j], rk_a2[j],
                                   rk_g1[j], rk_g2[j], rk_k_k[j], rk_k_a[j], rk_r_k[j],
                                   rk_lnx_g[j], rk_lnx_b[j])
        else:
            h = h + dsa_attention(hn, at_w_in[j], at_q_g[j], at_k_g[j], at_w_o[j], cos, sin, topk)
        h = h + 0.5 * swiglu(rms_norm(h, g[2]), ffn_w_in[i, 1], ffn_w_out[i, 1])
    return h[:, N_META:]
```

```python
import numpy as np
from contextlib import ExitStack
import concourse.bass as bass
import concourse.mybir as mybir
from concourse.bass_utils import run_bass_kernel_spmd

F32 = mybir.dt.float32
BF16 = mybir.dt.bfloat16
F32R = mybir.dt.float32r
USE_F32R = False
IDT = BF16
ALU = mybir.AluOpType
AF = mybir.ActivationFunctionType
AX = mybir.AxisListType

D = 1024
NMETA = 16
SEQ = 4096
T = SEQ + NMETA
TP = 4160
DFF = 2816
NCORES = 8
DEBUG_NQT = 0


class Sched:
    ENGS = ['pe', 'act', 'dve', 'pool', 'sp']

    def __init__(self, nc, es, nds=16):
        self.nc = nc
        self.sem = {e: es.enter_context(nc.semaphore("s_" + e)) for e in self.ENGS}
        self.cnt = {e: 0 for e in self.ENGS}
        self.NDS = nds
        self.dq = {}
        self.dsem = []
        self.dcnt = []
        self.dtok = []
        for q in ('sp', 'pool', 'act'):
            base = len(self.dsem)
            for i in range(nds):
                self.dsem.append(es.enter_context(nc.semaphore("d_%s%d" % (q, i))))
                self.dcnt.append(0)
                self.dtok.append(None)
            self.dq[q] = [base, 0]
        self.pending = {e: [] for e in self.ENGS}
        self.last_w = {}
        self.readers = {}
        self.seen = {e: {} for e in self.ENGS}
        self.nops = 0

    def _semh(self, sk):
        return self.sem[sk[1]] if sk[0] == 'e' else self.dsem[sk[1]]

    def op(self, eng, fn, r=(), w=(), dma=False):
        deps = []
        for k in r:
            if k in self.last_w:
                deps.append(self.last_w[k])
        for k in w:
            if k in self.last_w:
                deps.append(self.last_w[k])
            rd = self.readers.get(k)
            if rd:
                deps.extend(rd.items())
        if dma:
            qd = self.dq[eng]
            i = qd[0] + qd[1] % self.NDS
            qd[1] += 1
            if self.dtok[i] is not None:
                deps.append(self.dtok[i])
            self.dcnt[i] += 16
            tok = (('d', i), self.dcnt[i])
            self.dtok[i] = tok
        else:
            self.cnt[eng] += 1
            tok = (('e', eng), self.cnt[eng])
        waits = {}
        seen = self.seen[eng]
        for (sk, v) in deps:
            if eng == 'pe' and sk == ('e', 'pe'):
                continue
            if seen.get(sk, 0) >= v:
                continue
            if waits.get(sk, 0) < v:
                waits[sk] = v
        for sk, v in waits.items():
            seen[sk] = v
        self.pending[eng].append((fn, list(waits.items()), tok))
        self.nops += 1
        for k in w:
            self.last_w[k] = tok
            self.readers[k] = {}
        ws = set(w)
        for k in r:
            if k in ws:
                continue
            rd = self.readers.setdefault(k, {})
            if rd.get(tok[0], 0) < tok[1]:
                rd[tok[0]] = tok[1]
        return tok

    def barrier(self):
        allt = [(('e', e), self.cnt[e]) for e in self.ENGS if self.cnt[e] > 0]
        allt += [t for t in self.dtok if t is not None]
        for e in self.ENGS:
            waits = {}
            seen = self.seen[e]
            for sk, v in allt:
                if seen.get(sk, 0) >= v:
                    continue
                waits[sk] = max(waits.get(sk, 0), v)
            for sk, v in waits.items():
                seen[sk] = v
            self.pending[e].append((None, list(waits.items()), None))
        self.last_w = {}
        self.readers = {}

    def emit(self):
        nc = self.nc

        def mk(e):
            def body(engh):
                for fn, waits, tok in self.pending[e]:
                    for sk, v in waits:
                        engh.wait_ge(self._semh(sk), v)
                    if fn is None:
                        continue
                    ins = fn(engh)
                    sk, v = tok
                    if sk[0] == 'e':
                        ins.then_inc(self.sem[e], 1)
                    else:
                        ins.then_inc(self.dsem[sk[1]], 16)
            return body

        with nc.Block() as blk:
            blk.tensor(mk('pe'))
            blk.scalar(mk('act'))
            blk.vector(mk('dve'))
            blk.gpsimd(mk('pool'))
            blk.sync(mk('sp'))
        self.pending = {e: [] for e in self.ENGS}


def keys(name, *idx_ranges):
    out = [(name,)]
    for r in idx_ranges:
        out = [o + (i,) for o in out for i in r]
    return out


class Ctx:
    pass


def stage_ingest(C):
    nc, S = C.nc, C.S
    with ExitStack() as st:
        xt = [st.enter_context(nc.sbuf_tensor("in_xt%d" % i, [128, 4, D], F32)) for i in range(2)]
        hx = [st.enter_context(nc.sbuf_tensor("in_hx%d" % i, [128, 8, 512], F32)) for i in range(2)]
        ps = [st.enter_context(nc.psum_tensor("in_ps%d" % i, [128, 512], F32)) for i in range(4)]
        hTv = C.hT.rearrange("(c p) t -> p c t", p=128)
        ident = C.ident
        S.op('pool', lambda e: e.memset(hx[1][:, :, 0:64], 0.0), w=keys('hx', [1], range(8)))
        S.op('pool', lambda e: e.dma_start(out=hTv[:, :, T:TP], in_=hx[1][:, :, 0:TP - T]),
             r=keys('hx', [1], range(8)), w=[('hTpad',)], dma=True)
        S.op('sp', lambda e: e.dma_start(out=xt[1][0:NMETA, 0, :], in_=C.meta[:, :]), w=[('xt', 1)], dma=True)
        for c in range(8):
            S.op('pe', lambda e, c=c: e.transpose(ps[c % 4][:, 0:NMETA], xt[1][0:NMETA, 0, c * 128:(c + 1) * 128],
                                                  ident[0:NMETA, 0:NMETA]),
                 r=[('xt', 1)], w=[('ps', c % 4)])
            S.op('dve', lambda e, c=c: e.tensor_copy(out=hx[1][:, c, 0:NMETA], in_=ps[c % 4][:, 0:NMETA]),
                 r=[('ps', c % 4)], w=[('hx', 1, c)])
        S.op('pool', lambda e: e.dma_start(out=hTv[:, :, 0:NMETA], in_=hx[1][:, :, 0:NMETA]),
             r=keys('hx', [1], range(8)), w=[('hTmeta',)], dma=True)
        xv = C.x.rearrange("(g a p) d -> g p a d", a=4, p=128)
        for g in range(SEQ // 512):
            s = g % 2
            S.op('sp', lambda e, g=g, s=s: e.dma_start(out=xt[s][:, :, :], in_=xv[g]), w=[('xt', s)], dma=True)
            for c in range(8):
                b = c % 4
                for a in range(4):
                    S.op('pe', lambda e, a=a, c=c, b=b, s=s: e.transpose(
                        ps[b][:, a * 128:(a + 1) * 128], xt[s][:, a, c * 128:(c + 1) * 128], ident[:, :]),
                        r=[('xt', s)], w=[('ps', b)])
                eng = 'dve' if c % 2 == 0 else 'act'
                if eng == 'dve':
                    S.op('dve', lambda e, c=c, b=b, s=s: e.tensor_copy(out=hx[s][:, c, :], in_=ps[b][:, :]),
                         r=[('ps', b)], w=[('hx', s, c)])
                else:
                    S.op('act', lambda e, c=c, b=b, s=s: e.copy(out=hx[s][:, c, :], in_=ps[b][:, :]),
                         r=[('ps', b)], w=[('hx', s, c)])
            t0 = NMETA + g * 512
            S.op('pool', lambda e, s=s, t0=t0: e.dma_start(out=hTv[:, :, t0:t0 + 512], in_=hx[s][:, :, :]),
                 r=keys('hx', [s], range(8)), w=[('hTin', g)], dma=True)
        S.barrier()
        S.emit()


def stage_egress(C):
    nc, S = C.nc, C.S
    with ExitStack() as st:
        xt = [st.enter_context(nc.sbuf_tensor("eg_xt%d" % i, [128, 4, D], F32)) for i in range(2)]
        hx = [st.enter_context(nc.sbuf_tensor("eg_hx%d" % i, [128, 8, 512], F32)) for i in range(2)]
        ps = [st.enter_context(nc.psum_tensor("eg_ps%d" % i, [128, 512], F32)) for i in range(4)]
        hTv = C.hT.rearrange("(c p) t -> p c t", p=128)
        ov = C.out.rearrange("(g a p) d -> g p a d", a=4, p=128)
        ident = C.ident
        for g in range(SEQ // 512):
            s = g % 2
            t0 = NMETA + g * 512
            S.op('sp', lambda e, s=s, t0=t0: e.dma_start(out=hx[s][:, :, :], in_=hTv[:, :, t0:t0 + 512]),
                 w=[('hx', s)], dma=True)
            for a in range(4):
                for hf in range(2):
                    b = (a * 2 + hf) % 4
                    for cc in range(4):
                        c = hf * 4 + cc
                        S.op('pe', lambda e, a=a, c=c, cc=cc, b=b, s=s: e.transpose(
                            ps[b][:, cc * 128:(cc + 1) * 128], hx[s][:, c, a * 128:(a + 1) * 128], ident[:, :]),
                            r=[('hx', s)], w=[('ps', b)])
                    if hf == 0:
                        S.op('dve', lambda e, a=a, b=b, s=s: e.tensor_copy(out=xt[s][:, a, 0:512], in_=ps[b][:, :]),
                             r=[('ps', b)], w=[('xt', s, a, 0)])
                    else:
                        S.op('act', lambda e, a=a, b=b, s=s: e.copy(out=xt[s][:, a, 512:1024], in_=ps[b][:, :]),
                             r=[('ps', b)], w=[('xt', s, a, 1)])
            S.op('pool', lambda e, g=g, s=s: e.dma_start(out=ov[g], in_=xt[s][:, :, :]),
                 r=keys('xt', [s], range(4), range(2)), w=[('out', g)], dma=True)
        S.barrier()
        S.emit()


def stage_ffn(C, w_in_d, w_out_d, gcol, tag):
    nc, S = C.nc, C.S
    TT = 256
    tiles = [(i * TT, TT) for i in range(TP // TT)]
    if TP % TT:
        tiles.append((TP - TP % TT, TP % TT))
    NJ = DFF // 128
    with ExitStack() as st:
        sb = lambda n, sh, dt: st.enter_context(nc.sbuf_tensor(tag + n, sh, dt))
        w_in = sb("w_in", [128, 8, 2 * DFF], BF16)
        w_out = sb("w_out", [128, NJ, D], BF16)
        x = [sb("x%d" % i, [128, 8, TT], F32) for i in range(2)]
        sq = [sb("sq%d" % i, [128, 8, TT], BF16) for i in range(2)]
        xn = [sb("xn%d" % i, [128, 8, TT], BF16) for i in range(2)]
        hm = [sb("hm%d" % i, [128, NJ, TT], BF16) for i in range(2)]
        sg = [sb("sg%d" % i, [128, TT], F32) for i in range(2)]
        rstd = [sb("rstd%d" % i, [128, TT], F32) for i in range(2)]
        psn = lambda n: st.enter_context(nc.psum_tensor(tag + n, [128, 512], F32))
        psA = [psn("pA%d" % i) for i in range(2)]
        psB = [psn("pB%d" % i) for i in range(2)]
        psO = [psn("pO%d" % i) for i in range(2)]
        psS = psn("pS")
        hTv = C.hT.rearrange("(c p) t -> p c t", p=128)
        w_in_v = w_in_d.rearrange("(k p) n -> p k n", p=128)
        w_out_v = w_out_d.rearrange("(j p) n -> p j n", p=128)
        ones = C.ones_bf
        vec = C.vec

        NB = 4
        cw = 2 * DFF // NB
        for k in range(8):
            for b in range(NB):
                S.op('pool', lambda e, k=k, b=b: e.dma_start(out=w_in[:, k, b * cw:(b + 1) * cw],
                                                             in_=w_in_v[:, k, b * cw:(b + 1) * cw]),
                     w=[('w_in', k, b)], dma=True)
        for j in range(NJ):
            S.op('pool', lambda e, j=j: e.dma_start(out=w_out[:, j, :], in_=w_out_v[:, j, :]),
                 w=[('w_out', j)], dma=True)
        win_keys = keys('w_in', range(8), range(NB))

        def load(i):
            t0, tw = tiles[i]
            s = i % 2
            S.op('sp', lambda e: e.dma_start(out=x[s][:, :, :tw], in_=hTv[:, :, t0:t0 + tw]),
                 r=[('hT', i)], w=keys('x', [s], range(8)), dma=True)

        def norm(i):
            t0, tw = tiles[i]
            s = i % 2
            S.op('act', lambda e: e.activation(out=sq[s][:, :, :tw], in_=x[s][:, :, :tw], func=AF.Square),
                 r=keys('x', [s], range(8)), w=[('sq', s)])
            for c in range(8):
                S.op('pe', lambda e, c=c: e.matmul(psS[:, :tw], lhsT=ones[:, :], rhs=sq[s][:, c, :tw],
                                                   start=(c == 0), stop=(c == 7)),
                     r=[('sq', s)], w=[('psS',)])
            S.op('act', lambda e: e.activation(out=rstd[s][:, :tw], in_=psS[:, :tw], func=AF.Sqrt,
                                               scale=1.0 / D, bias=C.eps_col[:, 0:1]),
                 r=[('psS',)], w=[('rstd', s)])
            S.op('dve', lambda e: e.reciprocal(out=rstd[s][:, :tw], in_=rstd[s][:, :tw]),
                 r=[('rstd', s)], w=[('rstd', s)])
            for c in range(8):
                S.op('dve', lambda e, c=c: e.scalar_tensor_tensor(
                    out=xn[s][:, c, :tw], in0=x[s][:, c, :tw], scalar=vec[:, gcol + c:gcol + c + 1],
                    in1=rstd[s][:, :tw], op0=ALU.mult, op1=ALU.mult),
                    r=[('x', s, c), ('rstd', s)], w=[('xn', s, c)])

        def mm_in(i):
            t0, tw = tiles[i]
            s = i % 2
            for j in range(NJ):
                q = j % 2
                for k in range(8):
                    S.op('pe', lambda e, j=j, k=k, q=q: e.matmul(
                        psA[q][:, :tw], lhsT=w_in[:, k, j * 128:(j + 1) * 128], rhs=xn[s][:, k, :tw],
                        start=(k == 0), stop=(k == 7)),
                        r=[('xn', s, k)] + (win_keys if (i == 0 and j == 0) else []), w=[('pA', q)])
                for k in range(8):
                    S.op('pe', lambda e, j=j, k=k, q=q: e.matmul(
                        psB[q][:, :tw], lhsT=w_in[:, k, DFF + j * 128:DFF + (j + 1) * 128], rhs=xn[s][:, k, :tw],
                        start=(k == 0), stop=(k == 7)),
                        r=[('xn', s, k)], w=[('pB', q)])
                S.op('act', lambda e, q=q: e.activation(out=sg[q][:, :tw], in_=psA[q][:, :tw], func=AF.Silu),
                     r=[('pA', q)], w=[('sg', q)])
                S.op('dve', lambda e, q=q, j=j: e.tensor_tensor(out=hm[s][:, j, :tw], in0=psB[q][:, :tw],
                                                                in1=sg[q][:, :tw], op=ALU.mult),
                     r=[('pB', q), ('sg', q)], w=[('hm', s, j)])

        def mm_out(i):
            t0, tw = tiles[i]
            s = i % 2
            for m in range(8):
                q = m % 2
                for j in range(NJ):
                    S.op('pe', lambda e, j=j, m=m, q=q: e.matmul(
                        psO[q][:, :tw], lhsT=w_out[:, j, m * 128:(m + 1) * 128], rhs=hm[s][:, j, :tw],
                        start=(j == 0), stop=(j == NJ - 1)),
                        r=[('hm', s, j), ('w_out', j)], w=[('pO', q)])
                S.op('dve', lambda e, m=m, q=q: e.scalar_tensor_tensor(
                    out=x[s][:, m, :tw], in0=psO[q][:, :tw], scalar=0.5, in1=x[s][:, m, :tw],
                    op0=ALU.mult, op1=ALU.add),
                    r=[('pO', q), ('x', s, m)], w=[('x', s, m)])
            S.op('pool', lambda e: e.dma_start(out=hTv[:, :, t0:t0 + tw], in_=x[s][:, :, :tw]),
                 r=keys('x', [s], range(8)), w=[('hT', i)], dma=True)

        n = len(tiles)
        load(0)
        norm(0)
        for i in range(n):
            if i + 1 < n:
                load(i + 1)
            mm_in(i)
            if i + 1 < n:
                norm(i + 1)
            mm_out(i)
        S.barrier()
        S.emit()


class H:
    def __init__(self, S):
        self.S = S

    def mm(self, out, lhsT, rhs, start, stop, r, w):
        if USE_F32R and lhsT.dtype == F32 and rhs.dtype == F32:
            lhsT = lhsT.bitcast(F32R)
            rhs = rhs.bitcast(F32R)
        self.S.op('pe', lambda e: e.matmul(out, lhsT=lhsT, rhs=rhs, start=start, stop=stop), r=r, w=w)

    def tr(self, out, in_, ident, r, w):
        self.S.op('pe', lambda e: e.transpose(out, in_, ident), r=r, w=w)

    def act(self, out, in_, func, r, w, scale=None, bias=None):
        kw = {}
        if scale is not None:
            kw['scale'] = scale
        if bias is not None:
            kw['bias'] = bias
        self.S.op('act', lambda e: e.activation(out=out, in_=in_, func=func, **kw), r=r, w=w)

    def cp(self, eng, out, in_, r, w):
        if eng == 'act':
            self.S.op('act', lambda e: e.copy(out=out, in_=in_), r=r, w=w)
        else:
            self.S.op(eng, lambda e: e.tensor_copy(out=out, in_=in_), r=r, w=w)

    def tt(self, eng, out, in0, in1, op, r, w):
        self.S.op(eng, lambda e: e.tensor_tensor(out=out, in0=in0, in1=in1, op=op), r=r, w=w)

    def ts(self, eng, out, in0, s1, s2, op0, op1, r, w):
        if s2 is None:
            self.S.op(eng, lambda e: e.tensor_scalar(out=out, in0=in0, scalar1=s1, scalar2=None, op0=op0), r=r, w=w)
        else:
            self.S.op(eng, lambda e: e.tensor_scalar(out=out, in0=in0, scalar1=s1, scalar2=s2, op0=op0, op1=op1),
                      r=r, w=w)

    def stt(self, eng, out, in0, scalar, in1, op0, op1, r, w):
        self.S.op(eng, lambda e: e.scalar_tensor_tensor(out=out, in0=in0, scalar=scalar, in1=in1, op0=op0, op1=op1),
                  r=r, w=w)

    def red(self, eng, out, in_, op, r, w):
        self.S.op(eng, lambda e: e.tensor_reduce(out=out, in_=in_, axis=AX.X, op=op), r=r, w=w)

    def dma(self, eng, out, in_, r, w):
        self.S.op(eng, lambda e: e.dma_start(out=out, in_=in_), r=r, w=w, dma=True)


def bk(*bs):
    return [('ps', b) for b in bs]


def stage_rwkv(C):
    nc, S = C.nc, C.S
    Hh = H(S)
    mm, tr, act, cp, tt, ts, stt, red, dma = Hh.mm, Hh.tr, Hh.act, Hh.cp, Hh.tt, Hh.ts, Hh.stt, Hh.red, Hh.dma
    CH = 64
    NT = TP // CH
    NH = 16
    cols = C.cols
    c64 = C.cols64
    with ExitStack() as st:
        sb = lambda n, sh, dt=F32: st.enter_context(nc.sbuf_tensor("rs_" + n, sh, dt))
        Wr = sb("Wr", [128, 8, D], BF16)
        Wk = sb("Wk", [128, 8, D], BF16)
        Wv = sb("Wv", [128, 8, D], BF16)
        Wo = sb("Wo", [128, 8, D], BF16)
        w1 = sb("w1", [128, 8, 64], BF16)
        a1 = sb("a1", [128, 8, 64], BF16)
        g1 = sb("g1", [128, 8, 160], BF16)
        a2 = sb("a2", [64, D], BF16)
        g2a = sb("g2a", [128, D], BF16)
        g2b = sb("g2b", [32, D], BF16)
        w2aug = sb("w2aug", [65, D], F32)
        lnxg = sb("lnxg", [64, D], F32)
        lnxb = sb("lnxb", [64, D], F32)
        omk = sb("omk", [64, NH], F32)
        PS = st.enter_context(nc.psum_tensor("rk_PS", [128, 4096], F32))
        hbuf = sb("hbuf", [128, 8, CH])
        sq = sb("sq", [128, 8, CH], BF16)
        rstd = sb("rstd", [128, CH])
        hn = sb("hn", [128, 8, CH + 1])
        xx = sb("xx", [128, 8, CH])
        xmf = [sb("xmf%d" % i, [128, 8, CH]) for i in range(2)]
        xm = [sb("xm%d" % i, [128, 8, CH], BF16) for i in range(6)]
        r_ = sb("r", [64, NH, CH])
        k_ = sb("k", [64, NH, CH])
        a_ = sb("a", [64, NH, CH])
        kk = sb("kk", [64, NH, CH])
        b_ = sb("b", [64, NH, CH])
        tmp1 = sb("tmp1", [64, NH, CH])
        tmp2 = sb("tmp2", [64, NH, CH])
        G = sb("G", [64, NH, CH])
        Ghat = sb("Ghat", [64, NH, CH])
        cumC = sb("cumC", [64, NH])
        AR = sb("AR", [64, NH, 2 * CH])
        Bt = sb("Bt", [64, NH, CH])
        Kt = sb("Kt", [64, NH, CH])
        Bh = sb("Bh", [64, NH, CH])
        Kh = sb("Kh", [64, NH, CH])
        v_tm = sb("v_tm", [64, D])
        g_tm = sb("g_tm", [64, D])
        lw_tm = sb("lw_tm", [64, D])
        twT = sb("twT", [65, CH])
        taT = sb("taT", [64, CH], BF16)
        sg0 = sb("sg0", [128, CH], BF16)
        sg1 = sb("sg1", [32, CH], BF16)
        bon = sb("bon", [64, NH])
        MX = sb("MX", [64, NH, 2 * CH])
        RKT = sb("RKT", [64, NH, CH])
        LakT = sb("LakT", [64, NH, CH])
        Hst = sb("Hst", [64, NH, CH])
        st1 = sb("st1", [64, NH])
        st2 = sb("st2", [64, NH])
        zT = sb("zT", [128, 8, CH], BF16)

        Gs, Us, yc, ysq, BhT, KhT = tmp1, tmp2, r_, a_, b_, kk
        Ginv = MX[:, :, 0:CH]
        Gex = MX[:, :, CH:2 * CH]
        Lm = Ghat
        RBT = lw_tm[:, :].rearrange("p (h s) -> p h s", h=NH)

        hTv = C.hT.rearrange("(c p) t -> p c t", p=128)
        ident = C.ident
        vec = C.vec
        v64 = C.vec64

        for nm, dst, src in (("Wr", Wr, C.rk['w_r']), ("Wk", Wk, C.rk['w_k']), ("Wv", Wv, C.rk['w_v']),
                             ("Wo", Wo, C.rk['w_o'])):
            v = src.rearrange("(k p) n -> p k n", p=128)
            for k in range(8):
                dma('pool', dst[:, k, :], v[:, k, :], r=[], w=[(nm,)])
        dma('pool', w1[:, :, :], C.rk['w1'].rearrange("(k p) n -> p k n", p=128), r=[], w=[('w1',)])
        dma('pool', a1[:, :, :], C.rk['a1'].rearrange("(k p) n -> p k n", p=128), r=[], w=[('a1',)])
        dma('pool', g1[:, :, :], C.rk['g1'].rearrange("(k p) n -> p k n", p=128), r=[], w=[('g1',)])
        dma('pool', a2[:, :], C.rk['a2'][:, :], r=[], w=[('a2',)])
        dma('pool', g2a[:, :], C.rk['g2'][0:128, :], r=[], w=[('g2',)])
        dma('pool', g2b[:, :], C.rk['g2'][128:160, :], r=[], w=[('g2',)])
        dma('sp', w2aug[0:64, :], C.rk['w2'][:, :], r=[], w=[('w2aug',)])
        dma('sp', w2aug[64:65, :], C.rk['w0'][0:1, :], r=[], w=[('w2aug',)])
        dma('sp', lnxg[:, :], C.rk['lnx_g'][0:1, :].partition_broadcast(64), r=[], w=[('lnxg',)])
        dma('sp', lnxb[:, :], C.rk['lnx_b'][0:1, :].partition_broadcast(64), r=[], w=[('lnxb',)])
        kka = c64['k_a'][0]
        ts('dve', omk[:, :], v64[0:64, kka:kka + NH], -1.0, 1.0, ALU.mult, ALU.add, r=[('c_vec64',)], w=[('omk',)])
        S.op('pool', lambda e: e.memset(Hst[:, :, :], 0.0), w=[('Hst',)])
        S.op('pool', lambda e: e.memset(hn[:, :, 0:1], 0.0), w=[('hn0',)])
        S.op('pool', lambda e: e.memset(twT[64:65, :], 1.0), w=[('twT1',)])

        def bc(ap2, n=CH):
            return ap2.unsqueeze(2).to_broadcast([64, NH, n])

        def prm(name):
            c0 = c64[name][0]
            return bc(v64[0:64, c0:c0 + NH])

        gcol = cols['norm_g_0_1'][0]
        mixc = cols['rk_mix'][0]
        psv2 = lambda b0: PS[0:64, b0 * 512:b0 * 512 + 2048].rearrange("p (h two s) -> p h two s", h=NH, two=2)
        psv1 = lambda b0: PS[0:64, b0 * 512:b0 * 512 + 1024].rearrange("p (h s) -> p h s", h=NH)
        SU = C.masks[0:64, 0:64].unsqueeze(1).to_broadcast([64, NH, CH])
        IU = C.masks[0:64, 64:128].unsqueeze(1).to_broadcast([64, NH, CH])
        SL = C.masks[0:64, 128:192].unsqueeze(1).to_broadcast([64, NH, CH])
        IDb = ident[0:64, 0:64].unsqueeze(1).to_broadcast([64, NH, CH])
        ones64 = C.ones_f[0:64, 0:64]

        for t in range(NT):
            t0 = t * CH
            dma('sp', hbuf[:, :, :], hTv[:, :, t0:t0 + CH], r=[('hT', t)], w=[('hbuf',)])
            act(sq[:, :, :], hbuf[:, :, :], AF.Square, r=[('hbuf',)], w=[('sq',)])
            for c in range(8):
                mm(PS[:, 0:CH], ones_bf(C)[:, :], sq[:, c, :], c == 0, c == 7, r=[('sq',)], w=bk(0))
            act(rstd[:, :], PS[:, 0:CH], AF.Sqrt, r=bk(0), w=[('rstd',)], scale=1.0 / D, bias=C.eps_col[:, 0:1])
            S.op('dve', lambda e: e.reciprocal(out=rstd[:, :], in_=rstd[:, :]), r=[('rstd',)], w=[('rstd',)])
            for c in range(8):
                stt('dve', hn[:, c, 1:CH + 1], hbuf[:, c, :], vec[:, gcol + c:gcol + c + 1], rstd[:, :],
                    ALU.mult, ALU.mult, r=[('hbuf',), ('rstd',), ('hn0',)], w=[('hn', c)])
            hnk = keys('hn', range(8))
            tt('pool', xx[:, :, :], hn[:, :, 0:CH], hn[:, :, 1:CH + 1], ALU.subtract, r=hnk + [('hn0',)], w=[('xx',)])
            for i in range(6):
                mixb = vec[:, mixc + i * 8:mixc + i * 8 + 8].unsqueeze(2).to_broadcast([128, 8, CH])
                tt('pool', xmf[i % 2][:, :, :], xx[:, :, :], mixb, ALU.mult, r=[('xx',), ('c_vec',)], w=[('xmf', i % 2)])
                tt('dve', xm[i][:, :, :], xmf[i % 2][:, :, :], hn[:, :, 1:CH + 1], ALU.add,
                   r=[('xmf', i % 2)] + hnk, w=keys('xm', [i], range(8)))
            cp('pool', hn[:, :, 0:1], hn[:, :, CH:CH + 1], r=hnk + [('xx',)], w=[('hn0',)])
            xr, xw, xk, xv, xa, xg = xm
            for (Wt, wn, xs, xi, b0, dst, dn) in ((Wr, 'Wr', xr, 0, 0, r_, 'r'), (Wk, 'Wk', xk, 2, 2, k_, 'k')):
                for h in range(NH):
                    for kc in range(8):
                        mm(PS[0:64, b0 * 512 + h * 64:b0 * 512 + (h + 1) * 64], Wt[:, kc, h * 64:(h + 1) * 64],
                           xs[:, kc, :], kc == 0, kc == 7, r=[('xm', xi, kc), (wn,)], w=bk(b0 + h // 8))
                cp('act', dst[:, :, :], psv1(b0), r=bk(b0, b0 + 1), w=[(dn,)])
            for n in range(2):
                for kc in range(8):
                    mm(PS[0:64, (4 + n) * 512:(5 + n) * 512], xv[:, kc, :], Wv[:, kc, n * 512:(n + 1) * 512],
                       kc == 0, kc == 7, r=[('xm', 3, kc), ('Wv',)], w=bk(4 + n))
            cp('dve', v_tm[:, :], PS[0:64, 2048:3072], r=bk(4, 5), w=[('v_tm',)])
            for kc in range(8):
                mm(PS[0:64, 3072:3072 + CH], w1[:, kc, :], xw[:, kc, :], kc == 0, kc == 7,
                   r=[('xm', 1, kc), ('w1',)], w=bk(6))
            act(twT[0:64, :], PS[0:64, 3072:3072 + CH], AF.Tanh, r=bk(6), w=[('twT',)])
            for kc in range(8):
                mm(PS[0:64, 3584:3584 + CH], a1[:, kc, :], xa[:, kc, :], kc == 0, kc == 7,
                   r=[('xm', 4, kc), ('a1',)], w=bk(7))
            cp('dve', taT[:, :], PS[0:64, 3584:3584 + CH], r=bk(7), w=[('taT',)])
            for kc in range(8):
                mm(PS[:, 3072:3072 + CH], g1[:, kc, 0:128], xg[:, kc, :], kc == 0, kc == 7,
                   r=[('xm', 5, kc), ('g1',)], w=bk(6))
            act(sg0[:, :], PS[:, 3072:3072 + CH], AF.Sigmoid, r=bk(6), w=[('sg0',)])
            for kc in range(8):
                mm(PS[0:32, 3584:3584 + CH], g1[:, kc, 128:160], xg[:, kc, :], kc == 0, kc == 7,
                   r=[('xm', 5, kc), ('g1',)], w=bk(7))
            act(sg1[:, :], PS[0:32, 3584:3584 + CH], AF.Sigmoid, r=bk(7), w=[('sg1',)])
            for n in range(2):
                mm(PS[0:64, n * 512:(n + 1) * 512], twT[0:65, :], w2aug[0:65, n * 512:(n + 1) * 512], True, True,
                   r=[('twT',), ('twT1',), ('w2aug',)], w=bk(n))
            act(lw_tm[:, :], PS[0:64, 0:1024], AF.Sigmoid, r=bk(0, 1), w=[('lw_tm',)])
            for h in range(NH):
                mm(PS[0:64, 1024 + h * 128:1024 + (h + 1) * 128], lw_tm[0:64, h * 64:(h + 1) * 64],
                   C.tri[0:64, 0:128], True, True, r=[('lw_tm',), ('c_tri',)], w=bk(2 + h // 4))
            pc = psv2(2)
            cb = bk(2, 3, 4, 5)
            act(G[:, :, :], pc[:, :, 0, :], AF.Exp, r=cb, w=[('G',)])
            act(Ginv[:, :, :], pc[:, :, 0, :], AF.Exp, r=cb, w=[('MX0',)], scale=-1.0)
            act(Gex[:, :, :], pc[:, :, 1, :], AF.Exp, r=cb, w=[('MX1',)])
            cp('dve', cumC[:, :], pc[:, :, 0, CH - 1], r=cb, w=[('cumC',)])
            tt('dve', tmp1[:, :, :], bc(cumC[:, :]), pc[:, :, 0, :], ALU.subtract, r=cb + [('cumC',)], w=[('tmp1',)])
            act(Ghat[:, :, :], tmp1[:, :, :], AF.Exp, r=[('tmp1',)], w=[('Ghat',)])
            for h in range(NH):
                mm(PS[0:64, 3072 + h * 64:3072 + (h + 1) * 64], a2[0:64, h * 64:(h + 1) * 64], taT[0:64, :], True, True,
                   r=[('taT',), ('a2',)], w=bk(6 + h // 8))
            tt('dve', a_[:, :, :], psv1(6), prm('a0'), ALU.add, r=bk(6, 7) + [('c_vec64',)], w=[('a',)])
            act(a_[:, :, :], a_[:, :, :], AF.Sigmoid, r=[('a',)], w=[('a',)])
            for n in range(2):
                mm(PS[0:64, n * 512:(n + 1) * 512], sg0[:, :], g2a[:, n * 512:(n + 1) * 512], True, False,
                   r=[('sg0',), ('g2',)], w=bk(n))
                mm(PS[0:64, n * 512:(n + 1) * 512], sg1[0:32, :], g2b[0:32, n * 512:(n + 1) * 512], False, True,
                   r=[('sg1',), ('g2',)], w=bk(n))
            cp('act', g_tm[:, :], PS[0:64, 0:1024], r=bk(0, 1), w=[('g_tm',)])
            tt('dve', kk[:, :, :], k_[:, :, :], prm('k_k'), ALU.mult, r=[('k',), ('c_vec64',)], w=[('kk',)])
            act(tmp2[:, :, :], kk[:, :, :], AF.Square, r=[('kk',)], w=[('tmp2',)])
            t2f = tmp2[:, :, :].rearrange("p h s -> p (h s)")
            for n in range(2):
                mm(PS[0:64, 1024 + n * 512:1024 + (n + 1) * 512], ones64, t2f[:, n * 512:(n + 1) * 512], True, True,
                   r=[('tmp2',), ('c_onesf',)], w=bk(2 + n))
            act(tmp2[:, :, :], psv1(2), AF.Sqrt, r=bk(2, 3), w=[('tmp2',)])
            ts('dve', tmp2[:, :, :], tmp2[:, :, :], 1e-12, None, ALU.max, None, r=[('tmp2',)], w=[('tmp2',)])
            S.op('dve', lambda e: e.reciprocal(out=tmp2[:, :, :], in_=tmp2[:, :, :]), r=[('tmp2',)], w=[('tmp2',)])
            tt('dve', kk[:, :, :], kk[:, :, :], tmp2[:, :, :], ALU.mult, r=[('kk',), ('tmp2',)], w=[('kk',)])
            tt('pool', tmp1[:, :, :], a_[:, :, :], prm('k_a'), ALU.mult, r=[('a',), ('c_vec64',)], w=[('tmp1',)])
            tt('pool', tmp1[:, :, :], tmp1[:, :, :], bc(omk[:, :]), ALU.add, r=[('tmp1',), ('omk',)], w=[('tmp1',)])
            tt('pool', k_[:, :, :], k_[:, :, :], tmp1[:, :, :], ALU.mult, r=[('k',), ('tmp1',)], w=[('k',)])
            tt('dve', b_[:, :, :], kk[:, :, :], a_[:, :, :], ALU.mult, r=[('kk',), ('a',)], w=[('b',)])
            stt('dve', AR[:, :, 0:CH], kk[:, :, :], -1.0, Gex[:, :, :], ALU.mult, ALU.mult,
                r=[('kk',), ('MX1',)], w=[('AR0',)])
            tt('pool', AR[:, :, CH:2 * CH], r_[:, :, :], G[:, :, :], ALU.mult, r=[('r',), ('G',)], w=[('AR1',)])
            tt('dve', Bt[:, :, :], b_[:, :, :], Ginv[:, :, :], ALU.mult, r=[('b',), ('MX0',)], w=[('Bt',)])
            tt('pool', Kt[:, :, :], k_[:, :, :], Ginv[:, :, :], ALU.mult, r=[('k',), ('MX0',)], w=[('Kt',)])
            tt('dve', Bh[:, :, :], b_[:, :, :], Ghat[:, :, :], ALU.mult, r=[('b',), ('Ghat',)], w=[('Bh',)])
            tt('pool', Kh[:, :, :], k_[:, :, :], Ghat[:, :, :], ALU.mult, r=[('k',), ('Ghat',)], w=[('Kh',)])
            tt('pool', tmp1[:, :, :], r_[:, :, :], prm('r_k'), ALU.mult, r=[('r',), ('c_vec64',)], w=[('tmp1',)])
            tt('pool', tmp1[:, :, :], tmp1[:, :, :], k_[:, :, :], ALU.mult, r=[('tmp1',), ('k',)], w=[('tmp1',)])
            for h in range(NH):
                mm(PS[0:64, 3072 + h:3072 + h + 1], tmp1[:, h, :], C.ones_f[0:64, 0:1], True, True,
                   r=[('tmp1',), ('c_onesf',)], w=bk(6))
            cp('dve', bon[:, :], PS[0:64, 3072:3072 + NH], r=bk(6), w=[('bon',)])
            for h in range(NH):
                mm(PS[0:64, h * 128:(h + 1) * 128], Bt[:, h, :], AR[:, h, :], True, True,
                   r=[('Bt',), ('AR0',), ('AR1',)], w=bk(h // 4))
            for h in range(NH):
                mm(PS[0:64, 2048 + h * 128:2048 + (h + 1) * 128], Kt[:, h, :], AR[:, h, :], True, True,
                   r=[('Kt',), ('AR0',), ('AR1',)], w=bk(4 + h // 4))
            p0 = psv2(0)
            p4 = psv2(4)
            tt('dve', MX[:, :, 0:CH], p0[:, :, 0, :], SU, ALU.mult, r=bk(0, 1, 2, 3) + [('c_masks',)], w=[('MX0',)])
            tt('dve', RBT[:, :, :], p0[:, :, 1, :], IU, ALU.mult, r=bk(0, 1, 2, 3) + [('c_masks',)], w=[('lw_tm',)])
            tt('dve', LakT[:, :, :], p4[:, :, 0, :], SU, ALU.mult, r=bk(4, 5, 6, 7) + [('c_masks',)], w=[('LakT',)])
            tt('dve', RKT[:, :, :], p4[:, :, 1, :], IU, ALU.mult, r=bk(4, 5, 6, 7) + [('c_masks',)], w=[('RKT',)])
            for h in range(NH):
                mm(PS[0:64, h * 64:(h + 1) * 64], AR[:, h, 0:CH], Bt[:, h, :], True, True,
                   r=[('Bt',), ('AR0',)], w=bk(h // 8))
            tt('dve', Lm[:, :, :], psv1(0), SL, ALU.mult, r=bk(0, 1) + [('c_masks',)], w=[('Ghat',)])
            cp('pool', MX[:, :, CH:2 * CH], IDb, r=[('c_ident',)], w=[('MX1',)])
            for lvl in range(6):
                for h in range(NH):
                    mm(PS[0:64, 2048 + h * 128:2048 + (h + 1) * 128], Lm[:, h, :], MX[:, h, :], True, True,
                       r=[('Ghat',), ('MX0',), ('MX1',)], w=bk(4 + h // 4))
                if lvl < 5:
                    for h in range(NH):
                        mm(PS[0:64, h * 64:(h + 1) * 64], MX[:, h, 0:CH], Lm[:, h, :], True, True,
                           r=[('Ghat',), ('MX0',)], w=bk(h // 8))
                tt('dve', MX[:, :, CH:2 * CH], p4[:, :, 1, :], MX[:, :, CH:2 * CH], ALU.add,
                   r=bk(4, 5, 6, 7) + [('MX1',)], w=[('MX1',)])
                if lvl < 5:
                    cp('act', MX[:, :, 0:CH], p4[:, :, 0, :], r=bk(4, 5, 6, 7), w=[('MX0',)])
                    cp('act', Lm[:, :, :], psv1(0), r=bk(0, 1), w=[('Ghat',)])
            for h in range(NH):
                tr(PS[0:64, h * 64:(h + 1) * 64], Bh[:, h, :], ident[0:64, 0:64], r=[('Bh',), ('c_ident',)], w=bk(h // 8))
            for h in range(NH):
                tr(PS[0:64, 1024 + h * 64:1024 + (h + 1) * 64], Kh[:, h, :], ident[0:64, 0:64],
                   r=[('Kh',), ('c_ident',)], w=bk(2 + h // 8))
            cp('act', BhT[:, :, :], psv1(0), r=bk(0, 1), w=[('b',)])
            cp('dve', KhT[:, :, :], psv1(2), r=bk(2, 3), w=[('kk',)])
            for h in range(NH):
                o = PS[0:64, 2048 + h * 64:2048 + (h + 1) * 64]
                mm(o, AR[:, h, 0:CH], Hst[:, h, :], True, False, r=[('AR0',), ('Hst',)], w=bk(4 + h // 8))
                mm(o, LakT[:, h, :], v_tm[:, h * 64:(h + 1) * 64], False, True, r=[('LakT',), ('v_tm',)], w=bk(4 + h // 8))
            cp('act', Gs[:, :, :], psv1(4), r=bk(4, 5), w=[('tmp1',)])
            for h in range(NH):
                mm(PS[0:64, 3072 + h * 64:3072 + (h + 1) * 64], MX[:, h, CH:2 * CH], Gs[:, h, :], True, True,
                   r=[('MX1',), ('tmp1',)], w=bk(6 + h // 8))
            cp('dve', Us[:, :, :], psv1(6), r=bk(6, 7), w=[('tmp2',)])
            for h in range(NH):
                o = PS[0:64, 2048 + h * 64:2048 + (h + 1) * 64]
                mm(o, AR[:, h, CH:2 * CH], Hst[:, h, :], True, False, r=[('AR1',), ('Hst',)], w=bk(4 + h // 8))
                mm(o, RBT[:, h, :], Us[:, h, :], False, False, r=[('lw_tm',), ('tmp2',)], w=bk(4 + h // 8))
                mm(o, RKT[:, h, :], v_tm[:, h * 64:(h + 1) * 64], False, True, r=[('RKT',), ('v_tm',)], w=bk(4 + h // 8))
            for h in range(NH):
                o = PS[0:64, h * 64:(h + 1) * 64]
                mm(o, BhT[:, h, :], Us[:, h, :], True, False, r=[('b',), ('tmp2',)], w=bk(h // 8))
                mm(o, KhT[:, h, :], v_tm[:, h * 64:(h + 1) * 64], False, True, r=[('kk',), ('v_tm',)], w=bk(h // 8))
            tt('dve', Hst[:, :, :], Hst[:, :, :], G[:, :, CH - 1:CH].to_broadcast([64, NH, CH]), ALU.mult,
               r=[('Hst',), ('G',)], w=[('Hst',)])
            tt('dve', Hst[:, :, :], psv1(0), Hst[:, :, :], ALU.add, r=bk(0, 1) + [('Hst',)], w=[('Hst',)])
            py = psv1(4)
            yb = bk(4, 5)
            red('dve', st1[:, :], py, ALU.add, r=yb, w=[('st1',)])
            ts('dve', st1[:, :], st1[:, :], -1.0 / 64, None, ALU.mult, None, r=[('st1',)], w=[('st1',)])
            tt('dve', yc[:, :, :], py, bc(st1[:, :]), ALU.add, r=yb + [('st1',)], w=[('r',)])
            act(ysq[:, :, :], yc[:, :, :], AF.Square, r=[('r',)], w=[('a',)])
            red('dve', st2[:, :], ysq[:, :, :], ALU.add, r=[('a',)], w=[('st2',)])
            act(st2[:, :], st2[:, :], AF.Sqrt, r=[('st2',)], w=[('st2',)], scale=1.0 / 64, bias=C.eps_col[0:64, 1:2])
            S.op('dve', lambda e: e.reciprocal(out=st2[:, :], in_=st2[:, :]), r=[('st2',)], w=[('st2',)])
            tt('dve', yc[:, :, :], yc[:, :, :], bc(st2[:, :]), ALU.mult, r=[('r',), ('st2',)], w=[('r',)])
            ycf = yc[:, :, :].rearrange("p h s -> p (h s)")
            tt('pool', ycf, ycf, lnxg[:, :], ALU.mult, r=[('r',), ('lnxg',)], w=[('r',)])
            tt('pool', ycf, ycf, lnxb[:, :], ALU.add, r=[('r',), ('lnxb',)], w=[('r',)])
            vv = v_tm[:, :].rearrange("p (h s) -> p h s", h=NH)
            tt('dve', ysq[:, :, :], vv, bc(bon[:, :]), ALU.mult, r=[('v_tm',), ('bon',)], w=[('a',)])
            tt('pool', yc[:, :, :], yc[:, :, :], ysq[:, :, :], ALU.add, r=[('r',), ('a',)], w=[('r',)])
            tt('dve', ycf, ycf, g_tm[:, :], ALU.mult, r=[('r',), ('g_tm',)], w=[('r',)])
            for c in range(8):
                tr(PS[:, 1024 + c * 64:1024 + (c + 1) * 64], yc[:, 2 * c:2 * c + 2, :].rearrange("p h s -> p (h s)"),
                   ident[0:64, 0:64], r=[('r',), ('c_ident',)], w=bk(2))
            cp('act', zT[:, :, :], PS[:, 1024:1536].rearrange("p (c s) -> p c s", c=8), r=bk(2), w=[('zT',)])
            for co in range(8):
                q = 6 + (co % 2)
                for kc in range(8):
                    mm(PS[:, q * 512:q * 512 + CH], Wo[:, kc, co * 128:(co + 1) * 128], zT[:, kc, :], kc == 0, kc == 7,
                       r=[('zT',), ('Wo',)], w=bk(q))
                tt('dve', hbuf[:, co, :], PS[:, q * 512:q * 512 + CH], hbuf[:, co, :], ALU.add,
                   r=bk(q) + [('hbuf',)], w=[('hbuf',)])
            dma('pool', hTv[:, :, t0:t0 + CH], hbuf[:, :, :], r=[('hbuf',)], w=[('hT', t)])
        S.barrier()
        S.emit()


def stage_dsa(C):
    nc, S = C.nc, C.S
    Hh = H(S)
    mm, tr, act, cp, tt, ts, stt, red, dma = Hh.mm, Hh.tr, Hh.act, Hh.cp, Hh.tt, Hh.ts, Hh.stt, Hh.red, Hh.dma
    TT = 128
    NQT = (TP + TT - 1) // TT
    NQT_RUN = min(NQT, DEBUG_NQT) if DEBUG_NQT else NQT
    c64 = C.cols64
    cols = C.cols
    NBIS = 15
    MB = 240000.0
    with ExitStack() as st:
        sb = lambda n, sh, dt=F32: st.enter_context(nc.sbuf_tensor("ds_" + n, sh, dt))
        Wq = sb("Wq", [128, 8, 1024], BF16)
        Wk = sb("Wk", [128, 8, 256], BF16)
        Wv = sb("Wv", [128, 8, 256], BF16)
        Wqi = sb("Wqi", [128, 8, 512], BF16)
        Wki = sb("Wki", [128, 8, 64], BF16)
        Wwi = sb("Wwi", [128, 8, 8], BF16)
        Wo = sb("Wo", [128, 8, 1024], BF16)
        kT = sb("kT", [64, 4, TP], BF16)
        Vaug = sb("Vaug", [128, NQT, 4, 65], BF16)
        kiT = sb("kiT", [64, TP], BF16)
        score = sb("score", [128, TP])
        work = sb("work", [128, TP])
        mask01 = sb("mask01", [128, TP], BF16)
        maskT = sb("maskT", [128, NQT, TT], BF16)
        hbuf = [sb("hbuf%d" % i, [128, 8, TT]) for i in range(2)]
        hnb = sb("hnb", [128, 8, TT], BF16)
        sq = sb("sq", [128, 8, TT], BF16)
        rstd = sb("rstd", [128, TT])
        qT = [sb("qT%d" % i, [64, 16 * TT], BF16) for i in range(2)]
        qiT = sb("qiT", [64, 8 * TT], BF16)
        tA = sb("tA", [64, 512])
        tB = sb("tB", [64, 512])
        tC = sb("tC", [64, 512])
        rl = [sb("rl%d" % i, [128, 512]) for i in range(2)]
        PT = [sb("PT%d" % i, [128, 512], BF16) for i in range(3)]
        o_tm = sb("o_tm", [128, 1024], BF16)
        oT = sb("oT", [128, 8, TT], BF16)
        cs = sb("cs", [64, 2, TT])
        wi = sb("wi", [128, 8])
        m8 = sb("m8", [128, 8])
        eq8 = sb("eq8", [128, 8])
        iota8 = sb("iota8", [128, 8])
        lo = sb("lo", [128, 1])
        HC = sb("HC", [128, 2])
        MC = sb("MC", [128, 2])
        sel = sb("sel", [128, 1])
        d1 = sb("d1", [128, 1])
        d2 = sb("d2", [128, 2])
        thr = sb("thr", [128, 1])
        nm1 = sb("nm1", [128, 1])
        halfc = sb("halfc", [128, 1])
        negb = sb("negb", [128, 1])
        rden = sb("rden", [128, 16])
        ident_bf = sb("ident_bf", [128, 128], BF16)
        zeros_bf = sb("zeros_bf", [128, 512], BF16)
        rot = sb("rot", [64, 64])
        negmask = sb("negmask", [128, 128])
        PS = st.enter_context(nc.psum_tensor("ds_PS", [128, 3584], F32))
        PSb = st.enter_context(nc.psum_tensor("ds_PSb", [128, 1024], BF16))
        bank = lambda b: PS[:, b * 512:(b + 1) * 512]

        hTv = C.hT.rearrange("(c p) t -> p c t", p=128)
        ident = C.ident
        vec = C.vec
        v64 = C.vec64
        win = C.at['w_in'].rearrange("(k p) n -> p k n", p=128)
        for k in range(8):
            dma('pool', Wq[:, k, :], win[:, k, 0:1024], r=[], w=[('Wq',)])
        dma('pool', Wk[:, :, :], win[:, :, 1024:1280], r=[], w=[('Wk',)])
        dma('pool', Wv[:, :, :], win[:, :, 1280:1536], r=[], w=[('Wv',)])
        for k in range(8):
            dma('pool', Wqi[:, k, :], win[:, k, 1536:2048], r=[], w=[('Wqi',)])
        dma('pool', Wki[:, :, :], win[:, :, 2048:2112], r=[], w=[('Wki',)])
        dma('pool', Wwi[:, :, :], win[:, :, 2112:2120], r=[], w=[('Wwi',)])
        wov = C.at['w_o'].rearrange("(k p) n -> p k n", p=128)
        for k in range(8):
            dma('pool', Wo[:, k, :], wov[:, k, :], r=[], w=[('Wo',)])
        dma('sp', rot[:, :], C.rot_d[:, :], r=[], w=[('rot',)])
        dma('sp', negmask[:, :], C.negmask_d[:, :], r=[], w=[('negmask',)])
        cp('dve', ident_bf[:, :], ident[:, :], r=[('c_ident',)], w=[('ident_bf',)])
        S.op('pool', lambda e: e.memset(zeros_bf[:, :], 0.0), w=[('zeros_bf',)])
        S.op('pool', lambda e: e.memset(Vaug[:, :, :, 64:65], 1.0), w=[('Vones',)])
        S.op('pool', lambda e: e.memset(halfc[:, :], 0.5), w=[('halfc',)])
        S.op('pool', lambda e: e.memset(negb[:, :], -MB), w=[('negb',)])
        for j in range(8):
            S.op('pool', lambda e, j=j: e.memset(iota8[:, j:j + 1], float(j)), w=[('iota8',)])
        gcol = cols['norm_g_1_1'][0]
        qg = v64[0:64, c64['q_g'][0]:c64['q_g'][0] + 1]
        kg = v64[0:64, c64['k_g'][0]:c64['k_g'][0] + 1]
        kwid = lambda kb: min(128, TP - kb * 128)

        def norm_rope(pb, nh, tw, gcolap, out3, okeys_w):
            n = nh * tw
            pin = bank(pb)[0:64, 0:n]
            v3 = lambda ap: ap.rearrange("p (h s) -> p h s", h=nh)
            if gcolap is not None:
                act(tA[:, 0:n], pin, AF.Square, r=bk(pb), w=[('tA',)])
                mm(bank(6)[0:64, 0:n], C.ones_f[0:64, 0:64], tA[:, 0:n], True, True, r=[('tA',), ('c_onesf',)], w=bk(6))
                act(tB[:, 0:n], bank(6)[0:64, 0:n], AF.Sqrt, r=bk(6), w=[('tB',)], scale=1.0 / 64,
                    bias=C.eps_col[0:64, 0:1])
                S.op('dve', lambda e: e.reciprocal(out=tB[:, 0:n], in_=tB[:, 0:n]), r=[('tB',)], w=[('tB',)])
                stt('dve', tC[:, 0:n], pin, gcolap, tB[:, 0:n], ALU.mult, ALU.mult, r=bk(pb) + [('tB',), ('c_vec64',)],
                    w=[('tC',)])
            else:
                cp('act', tC[:, 0:n], pin, r=bk(pb), w=[('tC',)])
            mm(bank(6)[0:64, 0:n], rot[:, :], tC[:, 0:n], True, True, r=[('tC',), ('rot',)], w=bk(6))
            cosb = cs[:, 0, 0:tw].unsqueeze(1).to_broadcast([64, nh, tw])
            sinb = cs[:, 1, 0:tw].unsqueeze(1).to_broadcast([64, nh, tw])
            tt('pool', v3(tA[:, 0:n]), v3(tC[:, 0:n]), cosb, ALU.mult, r=[('tC',), ('cs',)], w=[('tA',)])
            tt('dve', v3(tB[:, 0:n]), v3(bank(6)[0:64, 0:n]), sinb, ALU.mult, r=bk(6) + [('cs',)], w=[('tB',)])
            tt('pool', out3, v3(tA[:, 0:n]), v3(tB[:, 0:n]), ALU.add, r=[('tA',), ('tB',)], w=okeys_w)

        def X(qt):
            s = qt % 2
            t0 = qt * TT
            tw = min(TT, TP - t0)
            n = t0 + tw
            nkb = qt + 1
            hb = hbuf[s]
            dma('sp', hb[:, :, 0:tw], hTv[:, :, t0:t0 + tw], r=[('hT', qt)], w=[('hbuf', s)])
            dma('sp', cs[:, :, 0:tw], C.rope_d[:, :, t0:t0 + tw], r=[], w=[('cs',)])
            act(sq[:, :, 0:tw], hb[:, :, 0:tw], AF.Square, r=[('hbuf', s)], w=[('sq',)])
            for c in range(8):
                mm(bank(6)[:, 0:tw], C.ones_bf[:, :], sq[:, c, 0:tw], c == 0, c == 7, r=[('sq',)], w=bk(6))
            act(rstd[:, 0:tw], bank(6)[:, 0:tw], AF.Sqrt, r=bk(6), w=[('rstd',)], scale=1.0 / D, bias=C.eps_col[:, 0:1])
            S.op('dve', lambda e: e.reciprocal(out=rstd[:, 0:tw], in_=rstd[:, 0:tw]), r=[('rstd',)], w=[('rstd',)])
            for c in range(8):
                stt('dve', hnb[:, c, 0:tw], hb[:, c, 0:tw], vec[:, gcol + c:gcol + c + 1], rstd[:, 0:tw],
                    ALU.mult, ALU.mult, r=[('hbuf', s), ('rstd',)], w=[('hnb',)])
            for g in range(4):
                for kc in range(8):
                    mm(bank(4)[0:64, g * tw:(g + 1) * tw], Wk[:, kc, g * 64:(g + 1) * 64], hnb[:, kc, 0:tw], kc == 0, kc == 7,
                       r=[('hnb',), ('Wk',)], w=bk(4))
            norm_rope(4, 4, tw, kg, kT[:, :, t0:t0 + tw], [('kT',)])
            for kc in range(8):
                mm(bank(5)[0:tw, 0:256], hnb[:, kc, 0:tw], Wv[:, kc, :], kc == 0, kc == 7, r=[('hnb',), ('Wv',)], w=bk(5))
            cp('act', Vaug[0:tw, qt, :, 0:64], bank(5)[0:tw, 0:256].rearrange("p (g d) -> p g d", g=4), r=bk(5),
               w=[('Vaug',)])
            for kc in range(8):
                mm(bank(5)[0:64, 0:tw], Wki[:, kc, :], hnb[:, kc, 0:tw], kc == 0, kc == 7, r=[('hnb',), ('Wki',)], w=bk(5))
            norm_rope(5, 1, tw, None, kiT[:, t0:t0 + tw].unsqueeze(1), [('kiT',)])
            for grp in range(4):
                pb = 4 + grp % 2
                for hh in range(4):
                    h = grp * 4 + hh
                    for kc in range(8):
                        mm(bank(pb)[0:64, hh * tw:(hh + 1) * tw], Wq[:, kc, h * 64:(h + 1) * 64], hnb[:, kc, 0:tw],
                           kc == 0, kc == 7, r=[('hnb',), ('Wq',)], w=bk(pb))
                norm_rope(pb, 4, tw, qg, qT[s][:, grp * 4 * tw:(grp + 1) * 4 * tw].rearrange("p (h s) -> p h s", h=4),
                          [('qT', s)])
            for grp in range(2):
                pb = 4 + grp % 2
                for hh in range(4):
                    h = grp * 4 + hh
                    for kc in range(8):
                        mm(bank(pb)[0:64, hh * tw:(hh + 1) * tw], Wqi[:, kc, h * 64:(h + 1) * 64], hnb[:, kc, 0:tw],
                           kc == 0, kc == 7, r=[('hnb',), ('Wqi',)], w=bk(pb))
                norm_rope(pb, 4, tw, None, qiT[:, grp * 4 * tw:(grp + 1) * 4 * tw].rearrange("p (h s) -> p h s", h=4),
                          [('qiT',)])
            for kc in range(8):
                mm(bank(6)[0:tw, 0:8], hnb[:, kc, 0:tw], Wwi[:, kc, :], kc == 0, kc == 7, r=[('hnb',), ('Wwi',)], w=bk(6))
            ts('dve', wi[0:tw, :], bank(6)[0:tw, 0:8], float(512.0 ** -0.5), None, ALU.mult, None, r=bk(6), w=[('wi',)])
            idx = 0
            for k0 in range(0, n, 512):
                nk = min(512, n - k0)
                for h in range(8):
                    pb = 4 + idx % 2
                    rb = rl[idx % 2]
                    rk = ('rl', idx % 2)
                    idx += 1
                    mm(bank(pb)[0:tw, 0:nk], qiT[:, h * tw:(h + 1) * tw], kiT[:, k0:k0 + nk], True, True,
                       r=[('qiT',), ('kiT',)], w=bk(pb))
                    act(rb[0:tw, 0:nk], bank(pb)[0:tw, 0:nk], AF.Relu, r=bk(pb), w=[rk])
                    if h == 0:
                        ts('dve', score[0:tw, k0:k0 + nk], rb[0:tw, 0:nk], wi[0:tw, 0:1], None, ALU.mult, None,
                           r=[rk, ('wi',)], w=[('score',)])
                    else:
                        stt('dve', score[0:tw, k0:k0 + nk], rb[0:tw, 0:nk], wi[0:tw, h:h + 1], score[0:tw, k0:k0 + nk],
                            ALU.mult, ALU.add, r=[rk, ('wi',), ('score',)], w=[('score',)])
            sc = score[0:tw, 0:n]
            if n > 256:
                red('dve', HC[0:tw, 0:1], sc, ALU.max, r=[('score',)], w=[('HC',)])
                red('dve', lo[0:tw, :], sc, ALU.min, r=[('score',)], w=[('lo',)])
            tt('dve', score[0:tw, t0:t0 + tw], score[0:tw, t0:t0 + tw], negmask[0:tw, 0:tw], ALU.add,
               r=[('score',), ('negmask',)], w=[('score',)])
            if n > 256:
                tt('dve', d1[0:tw, :], HC[0:tw, 0:1], lo[0:tw, :], ALU.subtract, r=[('HC',), ('lo',)], w=[('d1',)])
                stt('dve', HC[0:tw, 0:1], d1[0:tw, :], 1.0e-6, HC[0:tw, 0:1], ALU.mult, ALU.add, r=[('d1',), ('HC',)],
                    w=[('HC',)])
                ts('dve', HC[0:tw, 1:2], d1[0:tw, :], 0.0, None, ALU.mult, None, r=[('d1',), ('HC',)], w=[('HC',)])
                for it in range(NBIS):
                    stt('dve', MC[0:tw, 0:1], lo[0:tw, :], HC[0:tw, 0:1], halfc[0:tw, :], ALU.add, ALU.mult,
                        r=[('lo',), ('HC',), ('halfc',)], w=[('MC',)])
                    S.op('dve', lambda e, tw=tw, n=n: e.tensor_scalar(
                        out=mask01[0:tw, 0:n], in0=score[0:tw, 0:n], scalar1=MC[0:tw, 0:1], scalar2=0.0,
                        op0=ALU.is_ge, op1=ALU.add, accum_out=MC[0:tw, 1:2]),
                        r=[('score',), ('MC',)], w=[('mask01',), ('MC',)])
                    ts('dve', sel[0:tw, :], MC[0:tw, 1:2], 256.0, None, ALU.is_ge, None, r=[('MC',)], w=[('sel',)])
                    tt('dve', d1[0:tw, :], MC[0:tw, 0:1], lo[0:tw, :], ALU.subtract, r=[('MC',), ('lo',)], w=[('d1',)])
                    stt('dve', lo[0:tw, :], d1[0:tw, :], sel[0:tw, 0:1], lo[0:tw, :], ALU.mult, ALU.add,
                        r=[('d1',), ('sel',), ('lo',)], w=[('lo',)])
                    tt('dve', d2[0:tw, :], HC[0:tw, :], MC[0:tw, :], ALU.subtract, r=[('MC',), ('HC',)], w=[('d2',)])
                    stt('dve', HC[0:tw, :], d2[0:tw, :], sel[0:tw, 0:1], MC[0:tw, :], ALU.mult, ALU.add,
                        r=[('d2',), ('sel',), ('MC',)], w=[('HC',)])
                wk = work[0:tw, 0:n]
                ts('dve', wk, sc, HC[0:tw, 0:1], 1.0e20, ALU.is_ge, ALU.mult, r=[('score',), ('HC',)], w=[('work',)])
                tt('dve', wk, sc, wk, ALU.subtract, r=[('score',), ('work',)], w=[('work',)])
                S.op('dve', lambda e, tw=tw, n=n: e.max(out=m8[0:tw, :], in_=work[0:tw, 0:n]), r=[('work',)], w=[('m8',)])
                ts('dve', nm1[0:tw, :], HC[0:tw, 1:2], -1.0, 255.0, ALU.mult, ALU.add, r=[('HC',)], w=[('nm1',)])
                ts('dve', nm1[0:tw, :], nm1[0:tw, :], 7.0, 0.0, ALU.min, ALU.max, r=[('nm1',)], w=[('nm1',)])
                ts('dve', eq8[0:tw, :], iota8[0:tw, :], nm1[0:tw, 0:1], None, ALU.is_equal, None, r=[('nm1',), ('iota8',)],
                   w=[('eq8',)])
                tt('dve', eq8[0:tw, :], eq8[0:tw, :], m8[0:tw, :], ALU.mult, r=[('eq8',), ('m8',)], w=[('eq8',)])
                red('dve', thr[0:tw, :], eq8[0:tw, :], ALU.add, r=[('eq8',)], w=[('thr',)])
                ts('dve', mask01[0:tw, 0:n], sc, thr[0:tw, 0:1], None, ALU.is_ge, None, r=[('score',), ('thr',)],
                   w=[('mask01',)])
            else:
                ts('dve', mask01[0:tw, 0:n], sc, -1.0e29, None, ALU.is_ge, None, r=[('score',)], w=[('mask01',)])
        def X2(qt):
            t0 = qt * TT
            tw = min(TT, TP - t0)
            nkb = qt + 1
            for kb0 in range(0, nkb, 8):
                nb = min(8, nkb - kb0)
                for j in range(nb):
                    kb = kb0 + j
                    kw = kwid(kb)
                    tr(PSb[0:kw, j * 128:j * 128 + tw], mask01[0:tw, kb * 128:kb * 128 + kw], ident_bf[0:tw, 0:tw],
                       r=[('mask01',), ('ident_bf',)], w=[('psb',)])
                kwl = kwid(kb0 + nb - 1)
                nfull = nb if kwl == 128 else nb - 1
                if nfull > 0:
                    act(maskT[:, kb0:kb0 + nfull, 0:tw],
                        PSb[:, 0:nfull * 128].rearrange("p (j s) -> p j s", j=nfull)[:, :, 0:tw], AF.Identity,
                        r=[('psb',), ('negb',)], w=[('maskT', kb) for kb in range(kb0, kb0 + nfull)],
                        scale=MB, bias=negb[:, 0:1])
                if nfull < nb:
                    act(maskT[0:kwl, kb0 + nb - 1, 0:tw], PSb[0:kwl, (nb - 1) * 128:(nb - 1) * 128 + tw], AF.Identity,
                        r=[('psb',), ('negb',)], w=[('maskT', kb0 + nb - 1)], scale=MB, bias=negb[0:kwl, 0:1])

        def Y(qt):
            s = qt % 2
            t0 = qt * TT
            tw = min(TT, TP - t0)
            nkb = qt + 1
            hb = hbuf[s]
            q_ = qT[s]
            for g in range(4):
                S.op('pe', lambda e, g=g: e.matmul(bank(g)[0:tw, 0:260], lhsT=zeros_bf[:, 0:tw],
                                                  rhs=zeros_bf[:, 0:260], start=True, stop=False,
                                                  skip_group_check=True),
                     r=[('zeros_bf',)], w=bk(g))
            it = 0
            for kb in range(nkb):
                kw = kwid(kb)
                mbias = maskT[0:kw, kb, 0:tw].unsqueeze(1).to_broadcast([kw, 4, tw])
                for g in range(4):
                    pb = 4 + it % 2
                    pt = PT[it % 3]
                    pk = ('PT', it % 3)
                    it += 1
                    mm(bank(pb)[0:kw, 0:4 * tw], kT[:, g, kb * 128:kb * 128 + kw], q_[:, g * 4 * tw:(g + 1) * 4 * tw],
                       True, False, r=[('kT',), ('qT', s)], w=bk(pb))
                    mm(bank(pb)[0:kw, 0:4 * tw].rearrange("p (h s) -> p h s", h=4), ident_bf[0:kw, 0:kw], mbias,
                       False, True, r=[('maskT', kb), ('ident_bf',)], w=bk(pb))
                    act(pt[0:kw, 0:4 * tw], bank(pb)[0:kw, 0:4 * tw], AF.Exp, r=bk(pb), w=[pk], scale=0.125)
                    for rr in range(4):
                        S.op('pe', lambda e, g=g, rr=rr, kw=kw, pt=pt, kb=kb, last=(kb == nkb - 1): e.matmul(
                            bank(g)[0:tw, rr * 65:(rr + 1) * 65], lhsT=pt[0:kw, rr * tw:(rr + 1) * tw],
                            rhs=Vaug[0:kw, kb, g, :], start=False, stop=last, skip_group_check=True),
                            r=[pk, ('Vaug',), ('Vones',)], w=bk(g))
            for g in range(4):
                o3 = bank(g)[0:tw, 0:260].rearrange("p (h d) -> p h d", h=4)
                S.op('dve', lambda e, g=g, o3=o3: e.reciprocal(out=rden[0:tw, 4 * g:4 * g + 4], in_=o3[:, :, 64]),
                     r=bk(g), w=[('rden', g)])
                tt('dve', o_tm[0:tw, g * 256:(g + 1) * 256].rearrange("p (h d) -> p h d", h=4), o3[:, :, 0:64],
                   rden[0:tw, 4 * g:4 * g + 4].unsqueeze(2).to_broadcast([tw, 4, 64]), ALU.mult,
                   r=bk(g) + [('rden', g)], w=[('o_tm',)])
            for c in range(8):
                tr(PSb[:, c * 128:c * 128 + tw], o_tm[0:tw, c * 128:(c + 1) * 128], ident_bf[0:tw, 0:tw],
                   r=[('o_tm',), ('ident_bf',)], w=[('psb',)])
            cp('act', oT[:, :, 0:tw], PSb[:, :].rearrange("p (c s) -> p c s", c=8)[:, :, 0:tw],
               r=[('psb',)], w=[('oT',)])
            for co in range(8):
                pb = 4 + co % 2
                for kc in range(8):
                    mm(bank(pb)[:, 0:tw], Wo[:, kc, co * 128:(co + 1) * 128], oT[:, kc, 0:tw], kc == 0, kc == 7,
                       r=[('oT',), ('Wo',)], w=bk(pb))
                tt('dve', hb[:, co, 0:tw], bank(pb)[:, 0:tw], hb[:, co, 0:tw], ALU.add, r=bk(pb) + [('hbuf', s)],
                   w=[('hbuf', s)])
            dma('pool', hTv[:, :, t0:t0 + tw], hb[:, :, 0:tw], r=[('hbuf', s)], w=[('hT', qt)])

        X(0)
        X2(0)
        for qt in range(NQT_RUN):
            if qt + 1 < NQT_RUN:
                X(qt + 1)
            Y(qt)
            if qt + 1 < NQT_RUN:
                X2(qt + 1)
        S.barrier()
        S.emit()


def ones_bf(C):
    return C.ones_bf

VEC_COLS = {}


def _vec_layout():
    cols = {}
    c = 0
    for l in range(2):
        for j in range(3):
            cols['norm_g_%d_%d' % (l, j)] = (c, 8)
            c += 8
    cols['rk_mix'] = (c, 48)
    c += 48
    return cols, c


def _vec64_layout():
    cols = {}
    c = 0
    for n in ('k_k', 'k_a', 'a0', 'r_k'):
        cols[n] = (c, 16)
        c += 16
    for n in ('q_g', 'k_g'):
        cols[n] = (c, 1)
        c += 1
    return cols, c


RK_SHAPES = {'w_r': [D, D], 'w_k': [D, D], 'w_v': [D, D], 'w_o': [D, D], 'w0': [1, D], 'w1': [D, 64], 'w2': [64, D],
             'a1': [D, 64], 'a2': [64, D], 'g1': [D, 160], 'g2': [160, D], 'lnx_g': [1, D], 'lnx_b': [1, D]}


def build_program(stages):
    nc = bass.Bass("TRN2", target_bir_lowering=False)
    C = Ctx()
    C.nc = nc
    din = lambda n, sh, dt=F32: nc.dram_tensor(n, list(sh), dt, kind="ExternalInput").ap()
    C.x = din("x", [SEQ, D])
    C.meta = din("meta", [NMETA, D])
    C.ffn_w_in = din("ffn_w_in", [2, 2, D, 2 * DFF])
    C.ffn_w_out = din("ffn_w_out", [2, 2, DFF, D])
    C.rk = {k: din("rk_" + k, sh) for k, sh in RK_SHAPES.items()}
    cols, nv = _vec_layout()
    cols64, nv64 = _vec64_layout()
    C.cols, C.cols64 = cols, cols64
    C.vec_d = din("vecs", [128, nv])
    C.vec64_d = din("vecs64", [64, nv64])
    C.ident_d = din("ident", [128, 128])
    C.masks_d = din("masks", [64, 192])
    C.tri_d = din("tri", [64, 128])
    C.at = {'w_in': din("at_w_in", [D, 2120]), 'w_o': din("at_w_o", [D, D])}
    C.rope_d = din("rope", [64, 2, TP])
    C.rot_d = din("rot", [64, 64])
    C.negmask_d = din("negmask", [128, 128])
    C.out = nc.dram_tensor("out", [SEQ, D], F32, kind="ExternalOutput").ap()
    C.hT = nc.dram_tensor("hT_scratch", [D, TP], F32, kind="Internal").ap()
    with ExitStack() as es:
        S = Sched(nc, es)
        C.S = S
        C.vec = es.enter_context(nc.sbuf_tensor("c_vec", [128, nv], F32))
        C.vec64 = es.enter_context(nc.sbuf_tensor("c_vec64", [64, nv64], F32))
        C.ident = es.enter_context(nc.sbuf_tensor("c_ident", [128, 128], F32))
        C.masks = es.enter_context(nc.sbuf_tensor("c_masks", [64, 192], F32))
        C.tri = es.enter_context(nc.sbuf_tensor("c_tri", [64, 128], F32))
        C.ones_bf = es.enter_context(nc.sbuf_tensor("c_ones_bf", [128, 128], BF16))
        C.ones_f = es.enter_context(nc.sbuf_tensor("c_ones_f", [128, 128], F32))
        C.eps_col = es.enter_context(nc.sbuf_tensor("c_eps", [128, 2], F32))
        S.op('sp', lambda e: e.dma_start(out=C.vec[:, :], in_=C.vec_d[:, :]), w=[('c_vec',)], dma=True)
        S.op('sp', lambda e: e.dma_start(out=C.vec64[:, :], in_=C.vec64_d[:, :]), w=[('c_vec64',)], dma=True)
        S.op('sp', lambda e: e.dma_start(out=C.ident[:, :], in_=C.ident_d[:, :]), w=[('c_ident',)], dma=True)
        S.op('sp', lambda e: e.dma_start(out=C.masks[:, :], in_=C.masks_d[:, :]), w=[('c_masks',)], dma=True)
        S.op('sp', lambda e: e.dma_start(out=C.tri[:, :], in_=C.tri_d[:, :]), w=[('c_tri',)], dma=True)
        S.op('pool', lambda e: e.memset(C.ones_bf[:, :], 1.0), w=[('c_ones',)])
        S.op('pool', lambda e: e.memset(C.ones_f[:, :], 1.0), w=[('c_onesf',)])
        S.op('pool', lambda e: e.memset(C.eps_col[:, 0:1], 1e-6), w=[('c_eps',)])
        S.op('pool', lambda e: e.memset(C.eps_col[:, 1:2], 64e-5), w=[('c_eps2',)])
        S.barrier()
        stage_ingest(C)
        for sname in stages:
            if sname.startswith('ffn'):
                l, j = int(sname[3]), int(sname[4])
                stage_ffn(C, C.ffn_w_in[l, j], C.ffn_w_out[l, j], cols['norm_g_%d_%d' % (l, 0 if j == 0 else 2)][0],
                          "f%d%d_" % (l, j))
            elif sname == 'rwkv':
                stage_rwkv(C)
            elif sname == 'dsa':
                stage_dsa(C)
        stage_egress(C)
    return nc


def host_consts(inputs):
    cols, nv = _vec_layout()
    cols64, nv64 = _vec64_layout()
    vec = np.zeros((128, nv), np.float32)
    vec64 = np.zeros((64, nv64), np.float32)

    def put(name, v):
        c0, n = cols[name]
        vec[:, c0:c0 + n] = np.asarray(v, np.float32).reshape(n, 128).T

    def put64(name, v):
        c0, n = cols64[name]
        vec64[:, c0:c0 + n] = np.asarray(v, np.float32).reshape(n, 64).T

    ng = np.asarray(inputs['norm_g'])
    for l in range(2):
        for j in range(3):
            put('norm_g_%d_%d' % (l, j), ng[l, j])
    put('rk_mix', np.asarray(inputs['rk_mix'])[0].reshape(-1))
    put64('k_k', inputs['rk_k_k'][0])
    put64('k_a', inputs['rk_k_a'][0])
    put64('a0', inputs['rk_a0'][0])
    put64('r_k', np.asarray(inputs['rk_r_k'])[0].reshape(-1))
    vec64[:, cols64['q_g'][0]] = np.asarray(inputs['at_q_g'], np.float32)[0]
    vec64[:, cols64['k_g'][0]] = np.asarray(inputs['at_k_g'], np.float32)[0]
    inv = (np.float32(500000.0) ** (-np.arange(0, 16, 2, dtype=np.float32) / np.float32(16))).astype(np.float32)
    ang = (np.arange(TP, dtype=np.float32)[:, None] * inv[None, :]).astype(np.float32)
    rope = np.zeros((64, 2, TP), np.float32)
    rope[:, 0, :] = 1.0
    rope[0:8, 0, :] = np.cos(ang).T
    rope[8:16, 0, :] = np.cos(ang).T
    rope[0:8, 1, :] = np.sin(ang).T
    rope[8:16, 1, :] = np.sin(ang).T
    rot = np.zeros((64, 64), np.float32)
    for d in range(8):
        rot[d + 8, d] = -1.0
        rot[d, d + 8] = 1.0
    i128 = np.arange(128)
    negmask = np.where(i128[None, :] <= i128[:, None], 0.0, -1.0e30).astype(np.float32)
    ii = np.arange(64)
    su = (ii[:, None] < ii[None, :]).astype(np.float32)
    iu = (ii[:, None] <= ii[None, :]).astype(np.float32)
    sl = (ii[:, None] > ii[None, :]).astype(np.float32)
    masks = np.concatenate([su, iu, sl], axis=1)
    cdec = np.float32(-np.exp(-0.5))
    tri = np.concatenate([iu, su], axis=1) * cdec
    out = {"vecs": vec, "vecs64": vec64, "ident": np.eye(128, dtype=np.float32), "masks": masks,
           "tri": tri.astype(np.float32), "rope": rope, "rot": rot, "negmask": negmask,
           "at_w_in": np.ascontiguousarray(np.asarray(inputs['at_w_in'], np.float32)[0]),
           "at_w_o": np.ascontiguousarray(np.asarray(inputs['at_w_o'], np.float32)[0])}
    for k in RK_SHAPES:
        out["rk_" + k] = np.ascontiguousarray(np.asarray(inputs["rk_" + k], np.float32)[0].reshape(RK_SHAPES[k]))
    return out


ALL_STAGES = ['ffn00', 'rwkv', 'ffn01', 'ffn10', 'dsa', 'ffn11']
_cache = {}


def run(inputs, stages, cores=NCORES, trace=False):
    key = tuple(stages)
    if key not in _cache:
        _cache[key] = build_program(stages)
    nc = _cache[key]
    consts = host_consts(inputs)
    x = np.asarray(inputs['x'], np.float32)
    shared = {
        "meta": np.ascontiguousarray(np.asarray(inputs['meta'], np.float32)),
        "ffn_w_in": np.ascontiguousarray(np.asarray(inputs['ffn_w_in'], np.float32)),
        "ffn_w_out": np.ascontiguousarray(np.asarray(inputs['ffn_w_out'], np.float32)),
    }
    shared.update(consts)
    in_maps = []
    for b in range(cores):
        m = dict(shared)
        m["x"] = np.ascontiguousarray(x[b])
        in_maps.append(m)
    res = run_bass_kernel_spmd(nc, in_maps, core_ids=list(range(cores)), trace=trace)
    out = np.stack([np.asarray(r["out"], np.float32) for r in res.results], axis=0)
    return out, res


def kernel(**inputs):
    out, _ = run(inputs, ALL_STAGES)
    return out
```

```python
import numpy as np
from contextlib import ExitStack
import concourse.bass as bass
import concourse.mybir as mybir
from concourse.bass_utils import run_bass_kernel_spmd

F32 = mybir.dt.float32
BF16 = mybir.dt.bfloat16
F32R = mybir.dt.float32r
USE_F32R = False
IDT = BF16
TRDT = F32
ALU = mybir.AluOpType
AF = mybir.ActivationFunctionType
AX = mybir.AxisListType

D = 1024
NMETA = 16
SEQ = 4096
T = SEQ + NMETA
TP = 4160
DFF = 2816
NCORES = 8
DEBUG_NQT = 0
NOVBF = False
DEBUG_RK = None


class Sched:
    ENGS = ['pe', 'act', 'dve', 'pool', 'sp']

    def __init__(self, nc, es, nds=16):
        self.nc = nc
        self.sem = {e: es.enter_context(nc.semaphore("s_" + e)) for e in self.ENGS}
        self.cnt = {e: 0 for e in self.ENGS}
        self.NDS = nds
        self.dq = {}
        self.dsem = []
        self.dcnt = []
        self.dtok = []
        for q in ('sp', 'pool', 'act'):
            base = len(self.dsem)
            for i in range(nds):
                self.dsem.append(es.enter_context(nc.semaphore("d_%s%d" % (q, i))))
                self.dcnt.append(0)
                self.dtok.append(None)
            self.dq[q] = [base, 0]
        self.pending = {e: [] for e in self.ENGS}
        self.last_w = {}
        self.readers = {}
        self.seen = {e: {} for e in self.ENGS}
        self.nops = 0

    def _semh(self, sk):
        return self.sem[sk[1]] if sk[0] == 'e' else self.dsem[sk[1]]

    def capture(self):
        self._cap = []
        return self._cap

    def end_capture(self):
        c = self._cap
        self._cap = None
        return c

    def replay_merged(self, a, b):
        na, nb = len(a), len(b)
        ia = ib = 0
        while ia < na or ib < nb:
            if ib >= nb or (ia < na and ia * nb <= ib * na):
                self.op(*a[ia])
                ia += 1
            else:
                self.op(*b[ib])
                ib += 1

    def op(self, eng, fn, r=(), w=(), dma=False):
        if getattr(self, '_cap', None) is not None:
            self._cap.append((eng, fn, tuple(r), tuple(w), dma))
            return None
        pr = [k for k in r if k[0] in PSKEYS and k not in w]
        if pr:
            w = list(w) + pr
        deps = []
        for k in r:
            if k in self.last_w:
                deps.append(self.last_w[k])
        for k in w:
            if k in self.last_w:
                deps.append(self.last_w[k])
            rd = self.readers.get(k)
            if rd:
                deps.extend(rd.items())
        if dma:
            qd = self.dq[eng]
            i = qd[0] + qd[1] % self.NDS
            qd[1] += 1
            if self.dtok[i] is not None:
                deps.append(self.dtok[i])
            self.dcnt[i] += 16
            tok = (('d', i), self.dcnt[i])
            self.dtok[i] = tok
        else:
            self.cnt[eng] += 1
            tok = (('e', eng), self.cnt[eng])
        waits = {}
        seen = self.seen[eng]
        for (sk, v) in deps:
            if eng == 'pe' and sk == ('e', 'pe'):
                continue
            if seen.get(sk, 0) >= v:
                continue
            if waits.get(sk, 0) < v:
                waits[sk] = v
        for sk, v in waits.items():
            seen[sk] = v
        self.pending[eng].append((fn, list(waits.items()), tok))
        self.nops += 1
        for k in w:
            self.last_w[k] = tok
            self.readers[k] = {}
        ws = set(w)
        for k in r:
            if k in ws:
                continue
            rd = self.readers.setdefault(k, {})
            if rd.get(tok[0], 0) < tok[1]:
                rd[tok[0]] = tok[1]
        return tok

    def barrier(self):
        allt = [(('e', e), self.cnt[e]) for e in self.ENGS if self.cnt[e] > 0]
        allt += [t for t in self.dtok if t is not None]
        for e in self.ENGS:
            waits = {}
            seen = self.seen[e]
            for sk, v in allt:
                if seen.get(sk, 0) >= v:
                    continue
                waits[sk] = max(waits.get(sk, 0), v)
            for sk, v in waits.items():
                seen[sk] = v
            self.pending[e].append((None, list(waits.items()), None))
        self.last_w = {}
        self.readers = {}

    def emit(self):
        nc = self.nc

        def mk(e):
            def body(engh):
                for fn, waits, tok in self.pending[e]:
                    for sk, v in waits:
                        engh.wait_ge(self._semh(sk), v)
                    if fn is None:
                        continue
                    ins = fn(engh)
                    sk, v = tok
                    if sk[0] == 'e':
                        ins.then_inc(self.sem[e], 1)
                    else:
                        ins.then_inc(self.dsem[sk[1]], 16)
            return body

        with nc.Block() as blk:
            blk.tensor(mk('pe'))
            blk.scalar(mk('act'))
            blk.vector(mk('dve'))
            blk.gpsimd(mk('pool'))
            blk.sync(mk('sp'))
        self.pending = {e: [] for e in self.ENGS}


PSKEYS = {'ps', 'psb', 'pA', 'pB', 'pO', 'psS'}


def keys(name, *idx_ranges):
    out = [(name,)]
    for r in idx_ranges:
        out = [o + (i,) for o in out for i in r]
    return out


class Ctx:
    pass


def stage_ingest(C):
    nc, S = C.nc, C.S
    with ExitStack() as st:
        xt = [st.enter_context(nc.sbuf_tensor("in_xt%d" % i, [128, 4, D], F32)) for i in range(2)]
        hx = [st.enter_context(nc.sbuf_tensor("in_hx%d" % i, [128, 8, 512], F32)) for i in range(2)]
        ps = [st.enter_context(nc.psum_tensor("in_ps%d" % i, [128, 512], F32)) for i in range(4)]
        hTv = C.hT.rearrange("(c p) t -> p c t", p=128)
        ident = C.ident
        S.op('pool', lambda e: e.memset(hx[1][:, :, 0:64], 0.0), w=keys('hx', [1], range(8)))
        S.op('pool', lambda e: e.dma_start(out=hTv[:, :, T:TP], in_=hx[1][:, :, 0:TP - T]),
             r=keys('hx', [1], range(8)), w=[('hTpad',)], dma=True)
        S.op('sp', lambda e: e.dma_start(out=xt[1][0:NMETA, 0, :], in_=C.meta[:, :]), w=[('xt', 1)], dma=True)
        for c in range(8):
            S.op('pe', lambda e, c=c: e.transpose(ps[c % 4][:, 0:NMETA], xt[1][0:NMETA, 0, c * 128:(c + 1) * 128],
                                                  ident[0:NMETA, 0:NMETA]),
                 r=[('xt', 1)], w=[('ps', c % 4)])
            S.op('dve', lambda e, c=c: e.tensor_copy(out=hx[1][:, c, 0:NMETA], in_=ps[c % 4][:, 0:NMETA]),
                 r=[('ps', c % 4)], w=[('hx', 1, c)])
        S.op('pool', lambda e: e.dma_start(out=hTv[:, :, 0:NMETA], in_=hx[1][:, :, 0:NMETA]),
             r=keys('hx', [1], range(8)), w=[('hTmeta',)], dma=True)
        xv = C.x.rearrange("(g a p) d -> g p a d", a=4, p=128)
        for g in range(SEQ // 512):
            s = g % 2
            S.op('sp', lambda e, g=g, s=s: e.dma_start(out=xt[s][:, :, :], in_=xv[g]), w=[('xt', s)], dma=True)
            for c in range(8):
                b = c % 4
                for a in range(4):
                    S.op('pe', lambda e, a=a, c=c, b=b, s=s: e.transpose(
                        ps[b][:, a * 128:(a + 1) * 128], xt[s][:, a, c * 128:(c + 1) * 128], ident[:, :]),
                        r=[('xt', s)], w=[('ps', b)])
                eng = 'dve' if c % 2 == 0 else 'act'
                if eng == 'dve':
                    S.op('dve', lambda e, c=c, b=b, s=s: e.tensor_copy(out=hx[s][:, c, :], in_=ps[b][:, :]),
                         r=[('ps', b)], w=[('hx', s, c)])
                else:
                    S.op('act', lambda e, c=c, b=b, s=s: e.copy(out=hx[s][:, c, :], in_=ps[b][:, :]),
                         r=[('ps', b)], w=[('hx', s, c)])
            t0 = NMETA + g * 512
            S.op('pool', lambda e, s=s, t0=t0: e.dma_start(out=hTv[:, :, t0:t0 + 512], in_=hx[s][:, :, :]),
                 r=keys('hx', [s], range(8)), w=[('hTin', g)], dma=True)
        S.barrier()
        S.emit()


def stage_egress(C):
    nc, S = C.nc, C.S
    with ExitStack() as st:
        xt = [st.enter_context(nc.sbuf_tensor("eg_xt%d" % i, [128, 4, D], F32)) for i in range(2)]
        hx = [st.enter_context(nc.sbuf_tensor("eg_hx%d" % i, [128, 8, 512], F32)) for i in range(2)]
        ps = [st.enter_context(nc.psum_tensor("eg_ps%d" % i, [128, 512], F32)) for i in range(4)]
        hTv = C.hT.rearrange("(c p) t -> p c t", p=128)
        ov = C.out.rearrange("(g a p) d -> g p a d", a=4, p=128)
        ident = C.ident
        for g in range(SEQ // 512):
            s = g % 2
            t0 = NMETA + g * 512
            S.op('sp', lambda e, s=s, t0=t0: e.dma_start(out=hx[s][:, :, :], in_=hTv[:, :, t0:t0 + 512]),
                 w=[('hx', s)], dma=True)
            for a in range(4):
                for hf in range(2):
                    b = (a * 2 + hf) % 4
                    for cc in range(4):
                        c = hf * 4 + cc
                        S.op('pe', lambda e, a=a, c=c, cc=cc, b=b, s=s: e.transpose(
                            ps[b][:, cc * 128:(cc + 1) * 128], hx[s][:, c, a * 128:(a + 1) * 128], ident[:, :]),
                            r=[('hx', s)], w=[('ps', b)])
                    if hf == 0:
                        S.op('dve', lambda e, a=a, b=b, s=s: e.tensor_copy(out=xt[s][:, a, 0:512], in_=ps[b][:, :]),
                             r=[('ps', b)], w=[('xt', s, a, 0)])
                    else:
                        S.op('act', lambda e, a=a, b=b, s=s: e.copy(out=xt[s][:, a, 512:1024], in_=ps[b][:, :]),
                             r=[('ps', b)], w=[('xt', s, a, 1)])
            S.op('pool', lambda e, g=g, s=s: e.dma_start(out=ov[g], in_=xt[s][:, :, :]),
                 r=keys('xt', [s], range(4), range(2)), w=[('out', g)], dma=True)
        S.barrier()
        S.emit()


def stage_ffn(C, w_in_d, w_out_d, gcol, tag):
    nc, S = C.nc, C.S
    TT = 256
    tiles = [(i * TT, TT) for i in range(TP // TT)]
    if TP % TT:
        tiles.append((TP - TP % TT, TP % TT))
    NJ = DFF // 128
    with ExitStack() as st:
        sb = lambda n, sh, dt: st.enter_context(nc.sbuf_tensor(tag + n, sh, dt))
        w_in = sb("w_in", [128, 8, 2 * DFF], BF16)
        w_out = sb("w_out", [128, NJ, D], BF16)
        x = [sb("x%d" % i, [128, 8, TT], F32) for i in range(2)]
        sq = [sb("sq%d" % i, [128, 8, TT], BF16) for i in range(2)]
        xn = [sb("xn%d" % i, [128, 8, TT], BF16) for i in range(2)]
        hm = [sb("hm%d" % i, [128, NJ, TT], BF16) for i in range(2)]
        sg = [sb("sg%d" % i, [128, TT], F32) for i in range(2)]
        rstd = [sb("rstd%d" % i, [128, TT], F32) for i in range(2)]
        psn = lambda n: st.enter_context(nc.psum_tensor(tag + n, [128, 512], F32))
        psA = [psn("pA%d" % i) for i in range(2)]
        psB = [psn("pB%d" % i) for i in range(2)]
        psO = [psn("pO%d" % i) for i in range(2)]
        psS = psn("pS")
        hTv = C.hT.rearrange("(c p) t -> p c t", p=128)
        w_in_v = w_in_d.rearrange("(k p) n -> p k n", p=128)
        w_out_v = w_out_d.rearrange("(j p) n -> p j n", p=128)
        ones = C.ones_bf
        vec = C.vec

        NB = 4
        cw = 2 * DFF // NB
        for k in range(8):
            for b in range(NB):
                S.op('pool', lambda e, k=k, b=b: e.dma_start(out=w_in[:, k, b * cw:(b + 1) * cw],
                                                             in_=w_in_v[:, k, b * cw:(b + 1) * cw]),
                     w=[('w_in', k, b)], dma=True)
        for j in range(NJ):
            S.op('pool', lambda e, j=j: e.dma_start(out=w_out[:, j, :], in_=w_out_v[:, j, :]),
                 w=[('w_out', j)], dma=True)
        win_keys = keys('w_in', range(8), range(NB))

        def load(i):
            t0, tw = tiles[i]
            s = i % 2
            S.op('sp', lambda e: e.dma_start(out=x[s][:, :, :tw], in_=hTv[:, :, t0:t0 + tw]),
                 r=[('hT', i)], w=keys('x', [s], range(8)), dma=True)

        def norm(i):
            t0, tw = tiles[i]
            s = i % 2
            S.op('act', lambda e: e.activation(out=sq[s][:, :, :tw], in_=x[s][:, :, :tw], func=AF.Square),
                 r=keys('x', [s], range(8)), w=[('sq', s)])
            for c in range(8):
                S.op('pe', lambda e, c=c: e.matmul(psS[:, :tw], lhsT=ones[:, :], rhs=sq[s][:, c, :tw],
                                                   start=(c == 0), stop=(c == 7)),
                     r=[('sq', s)], w=[('psS',)])
            S.op('act', lambda e: e.activation(out=rstd[s][:, :tw], in_=psS[:, :tw], func=AF.Sqrt,
                                               scale=1.0 / D, bias=C.eps_col[:, 0:1]),
                 r=[('psS',)], w=[('rstd', s)])
            S.op('dve', lambda e: e.reciprocal(out=rstd[s][:, :tw], in_=rstd[s][:, :tw]),
                 r=[('rstd', s)], w=[('rstd', s)])
            for c in range(8):
                S.op('dve', lambda e, c=c: e.scalar_tensor_tensor(
                    out=xn[s][:, c, :tw], in0=x[s][:, c, :tw], scalar=vec[:, gcol + c:gcol + c + 1],
                    in1=rstd[s][:, :tw], op0=ALU.mult, op1=ALU.mult),
                    r=[('x', s, c), ('rstd', s)], w=[('xn', s, c)])

        def mm_in(i):
            t0, tw = tiles[i]
            s = i % 2
            for j in range(NJ):
                q = j % 2
                for k in range(8):
                    S.op('pe', lambda e, j=j, k=k, q=q: e.matmul(
                        psA[q][:, :tw], lhsT=w_in[:, k, j * 128:(j + 1) * 128], rhs=xn[s][:, k, :tw],
                        start=(k == 0), stop=(k == 7)),
                        r=[('xn', s, k)] + (win_keys if (i == 0 and j == 0) else []), w=[('pA', q)])
                for k in range(8):
                    S.op('pe', lambda e, j=j, k=k, q=q: e.matmul(
                        psB[q][:, :tw], lhsT=w_in[:, k, DFF + j * 128:DFF + (j + 1) * 128], rhs=xn[s][:, k, :tw],
                        start=(k == 0), stop=(k == 7)),
                        r=[('xn', s, k)], w=[('pB', q)])
                S.op('act', lambda e, q=q: e.activation(out=sg[q][:, :tw], in_=psA[q][:, :tw], func=AF.Silu),
                     r=[('pA', q)], w=[('sg', q)])
                S.op('dve', lambda e, q=q, j=j: e.tensor_tensor(out=hm[s][:, j, :tw], in0=psB[q][:, :tw],
                                                                in1=sg[q][:, :tw], op=ALU.mult),
                     r=[('pB', q), ('sg', q)], w=[('hm', s, j)])

        def mm_out(i):
            t0, tw = tiles[i]
            s = i % 2
            for m in range(8):
                q = m % 2
                for j in range(NJ):
                    S.op('pe', lambda e, j=j, m=m, q=q: e.matmul(
                        psO[q][:, :tw], lhsT=w_out[:, j, m * 128:(m + 1) * 128], rhs=hm[s][:, j, :tw],
                        start=(j == 0), stop=(j == NJ - 1)),
                        r=[('hm', s, j), ('w_out', j)], w=[('pO', q)])
                S.op('dve', lambda e, m=m, q=q: e.scalar_tensor_tensor(
                    out=x[s][:, m, :tw], in0=psO[q][:, :tw], scalar=0.5, in1=x[s][:, m, :tw],
                    op0=ALU.mult, op1=ALU.add),
                    r=[('pO', q), ('x', s, m)], w=[('x', s, m)])
            S.op('pool', lambda e: e.dma_start(out=hTv[:, :, t0:t0 + tw], in_=x[s][:, :, :tw]),
                 r=keys('x', [s], range(8)), w=[('hT', i)], dma=True)

        n = len(tiles)
        load(0)
        norm(0)
        for i in range(n):
            if i + 1 < n:
                load(i + 1)
            mm_in(i)
            if i + 1 < n:
                norm(i + 1)
            mm_out(i)
        S.barrier()
        S.emit()


class H:
    def __init__(self, S):
        self.S = S

    def mm(self, out, lhsT, rhs, start, stop, r, w):
        if USE_F32R and lhsT.dtype == F32 and rhs.dtype == F32:
            lhsT = lhsT.bitcast(F32R)
            rhs = rhs.bitcast(F32R)
        self.S.op('pe', lambda e: e.matmul(out, lhsT=lhsT, rhs=rhs, start=start, stop=stop), r=r, w=w)

    def tr(self, out, in_, ident, r, w):
        self.S.op('pe', lambda e: e.transpose(out, in_, ident), r=r, w=w)

    def act(self, out, in_, func, r, w, scale=None, bias=None):
        kw = {}
        if scale is not None:
            kw['scale'] = scale
        if bias is not None:
            kw['bias'] = bias
        self.S.op('act', lambda e: e.activation(out=out, in_=in_, func=func, **kw), r=r, w=w)

    def cp(self, eng, out, in_, r, w):
        if eng == 'act':
            self.S.op('act', lambda e: e.copy(out=out, in_=in_), r=r, w=w)
        else:
            self.S.op(eng, lambda e: e.tensor_copy(out=out, in_=in_), r=r, w=w)

    def tt(self, eng, out, in0, in1, op, r, w):
        self.S.op(eng, lambda e: e.tensor_tensor(out=out, in0=in0, in1=in1, op=op), r=r, w=w)

    def ts(self, eng, out, in0, s1, s2, op0, op1, r, w):
        if s2 is None:
            self.S.op(eng, lambda e: e.tensor_scalar(out=out, in0=in0, scalar1=s1, scalar2=None, op0=op0), r=r, w=w)
        else:
            self.S.op(eng, lambda e: e.tensor_scalar(out=out, in0=in0, scalar1=s1, scalar2=s2, op0=op0, op1=op1),
                      r=r, w=w)

    def stt(self, eng, out, in0, scalar, in1, op0, op1, r, w):
        self.S.op(eng, lambda e: e.scalar_tensor_tensor(out=out, in0=in0, scalar=scalar, in1=in1, op0=op0, op1=op1),
                  r=r, w=w)

    def red(self, eng, out, in_, op, r, w):
        self.S.op(eng, lambda e: e.tensor_reduce(out=out, in_=in_, axis=AX.X, op=op), r=r, w=w)

    def dma(self, eng, out, in_, r, w):
        self.S.op(eng, lambda e: e.dma_start(out=out, in_=in_), r=r, w=w, dma=True)


def bk(*bs):
    return [('ps', b) for b in bs]


def stage_rwkv(C):
    nc, S = C.nc, C.S
    Hh = H(S)
    mm, tr, act, cp, tt, ts, stt, red, dma = Hh.mm, Hh.tr, Hh.act, Hh.cp, Hh.tt, Hh.ts, Hh.stt, Hh.red, Hh.dma
    CH = 64
    NT = TP // CH
    NH = 16
    cols = C.cols
    c64 = C.cols64
    with ExitStack() as st:
        sb = lambda n, sh, dt=F32: st.enter_context(nc.sbuf_tensor("rs_" + n, sh, dt))
        Wr = sb("Wr", [128, 8, D], BF16)
        Wk = sb("Wk", [128, 8, D], BF16)
        Wv = sb("Wv", [128, 8, D], BF16)
        Wo = sb("Wo", [128, 8, D], BF16)
        w1 = sb("w1", [128, 8, 64], BF16)
        a1 = sb("a1", [128, 8, 64], BF16)
        g1 = sb("g1", [128, 8, 160], BF16)
        a2 = sb("a2", [64, D], BF16)
        g2a = sb("g2a", [128, D], BF16)
        g2b = sb("g2b", [32, D], BF16)
        w2aug = sb("w2aug", [65, D], F32)
        lnxg = sb("lnxg", [64, D], F32)
        lnxb = sb("lnxb", [64, D], F32)
        omk = sb("omk", [64, NH], F32)
        PS = st.enter_context(nc.psum_tensor("rk_PS", [128, 3584], F32))
        PSb = st.enter_context(nc.psum_tensor("rk_PSb", [128, 1024], BF16))
        hbuf = sb("hbuf", [128, 8, CH])
        sq = sb("sq", [128, 8, CH], BF16)
        rstd = sb("rstd", [128, CH])
        hn = sb("hn", [128, 8, CH + 1])
        xx = sb("xx", [128, 8, CH])
        xmf = [sb("xmf%d" % i, [128, 8, CH]) for i in range(1)]
        xm = [sb("xm%d" % i, [128, 8, CH], BF16) for i in range(6)]
        r_ = sb("r", [64, NH, CH])
        k_ = sb("k", [64, NH, CH])
        a_ = sb("a", [64, NH, CH])
        kk = sb("kk", [64, NH, CH])
        b_ = sb("b", [64, NH, CH])
        tmp1 = sb("tmp1", [64, NH, CH])
        tmp2 = sb("tmp2", [64, NH, CH])
        G = sb("G", [64, NH, CH])
        Ghat = sb("Ghat", [64, NH, CH])
        cumC = sb("cumC", [64, NH])
        AR = sb("AR", [64, NH, 2 * CH], BF16)
        Bt = sb("Bt", [64, NH, CH], BF16)
        Kt = sb("Kt", [64, NH, CH], BF16)
        Bh = sb("Bh", [64, NH, CH], TRDT)
        Kh = sb("Kh", [64, NH, CH], TRDT)
        v_tm = sb("v_tm", [64, D])
        g_tm = sb("g_tm", [64, D])
        lw_tm = sb("lw_tm", [64, D])
        twT = sb("twT", [65, CH])
        taT = sb("taT", [64, CH], BF16)
        sg0 = sb("sg0", [128, CH], BF16)
        sg1 = sb("sg1", [32, CH], BF16)
        bon = sb("bon", [64, NH])
        MX = sb("MX", [64, NH, 2 * CH], BF16)
        GG = sb("GG", [64, NH, 2 * CH])
        RKT = sb("RKT", [64, NH, CH], BF16)
        LakT = sb("LakT", [64, NH, CH], BF16)
        Hst = sb("Hst", [64, NH, CH])
        st1 = sb("st1", [64, NH])
        st2 = sb("st2", [64, NH])
        zT = sb("zT", [128, 8, CH], BF16)

        yc, ysq = r_, a_
        Gs = sb("Gs", [64, NH, CH], BF16)
        Us = sb("Us", [64, NH, CH], BF16)
        BhT = sb("BhT", [64, NH, CH], BF16)
        KhT = sb("KhT", [64, NH, CH], BF16)
        Lm = sb("Lm", [64, NH, CH], BF16)
        RBT = sb("RBT", [64, NH, CH], BF16)
        v_bf = sb("v_bf", [64, D], BF16)
        Hb = sb("Hb", [64, NH, CH], BF16)
        ident_bf = sb("ident_bf", [64, 64], BF16)
        Ginv = GG[:, :, 0:CH]
        Gex = GG[:, :, CH:2 * CH]

        hTv = C.hT.rearrange("(c p) t -> p c t", p=128)
        ident = C.ident
        vec = C.vec
        v64 = C.vec64

        for nm, dst, src in (("Wr", Wr, C.rk['w_r']), ("Wk", Wk, C.rk['w_k']), ("Wv", Wv, C.rk['w_v']),
                             ("Wo", Wo, C.rk['w_o'])):
            v = src.rearrange("(k p) n -> p k n", p=128)
            for k in range(8):
                dma('pool', dst[:, k, :], v[:, k, :], r=[], w=[(nm,)])
        dma('pool', w1[:, :, :], C.rk['w1'].rearrange("(k p) n -> p k n", p=128), r=[], w=[('w1',)])
        dma('pool', a1[:, :, :], C.rk['a1'].rearrange("(k p) n -> p k n", p=128), r=[], w=[('a1',)])
        dma('pool', g1[:, :, :], C.rk['g1'].rearrange("(k p) n -> p k n", p=128), r=[], w=[('g1',)])
        dma('pool', a2[:, :], C.rk['a2'][:, :], r=[], w=[('a2',)])
        dma('pool', g2a[:, :], C.rk['g2'][0:128, :], r=[], w=[('g2',)])
        dma('pool', g2b[:, :], C.rk['g2'][128:160, :], r=[], w=[('g2',)])
        dma('sp', w2aug[0:64, :], C.rk['w2'][:, :], r=[], w=[('w2aug',)])
        dma('sp', w2aug[64:65, :], C.rk['w0'][0:1, :], r=[], w=[('w2aug',)])
        dma('sp', lnxg[:, :], C.rk['lnx_g'][0:1, :].partition_broadcast(64), r=[], w=[('lnxg',)])
        dma('sp', lnxb[:, :], C.rk['lnx_b'][0:1, :].partition_broadcast(64), r=[], w=[('lnxb',)])
        kka = c64['k_a'][0]
        ts('dve', omk[:, :], v64[0:64, kka:kka + NH], -1.0, 1.0, ALU.mult, ALU.add, r=[('c_vec64',)], w=[('omk',)])
        S.op('pool', lambda e: e.memset(Hst[:, :, :], 0.0), w=[('Hst',)])
        S.op('pool', lambda e: e.memset(Hb[:, :, :], 0.0), w=[('Hb',)])
        cp('dve', ident_bf[:, :], ident[0:64, 0:64], r=[('c_ident',)], w=[('ident_bf',)])
        S.op('pool', lambda e: e.memset(hn[:, :, 0:1], 0.0), w=[('hn0',)])
        S.op('pool', lambda e: e.memset(twT[64:65, :], 1.0), w=[('twT1',)])

        def bc(ap2, n=CH):
            return ap2.unsqueeze(2).to_broadcast([64, NH, n])

        def prm(name):
            c0 = c64[name][0]
            return bc(v64[0:64, c0:c0 + NH])

        gcol = cols['norm_g_0_1'][0]
        mixc = cols['rk_mix'][0]
        psv2 = lambda b0: PS[0:64, b0 * 512:b0 * 512 + 2048].rearrange("p (h two s) -> p h two s", h=NH, two=2)
        psv1 = lambda b0: PS[0:64, b0 * 512:b0 * 512 + 1024].rearrange("p (h s) -> p h s", h=NH)
        SU = C.masks[0:64, 0:64].unsqueeze(1).to_broadcast([64, NH, CH])
        IU = C.masks[0:64, 64:128].unsqueeze(1).to_broadcast([64, NH, CH])
        SL = C.masks[0:64, 128:192].unsqueeze(1).to_broadcast([64, NH, CH])
        IDb = ident[0:64, 0:64].unsqueeze(1).to_broadcast([64, NH, CH])
        ones64 = C.ones_f[0:64, 0:64]

        for t in range(NT if not DEBUG_RK else DEBUG_RK[0]):
            t0 = t * CH
            PH = DEBUG_RK[1] if DEBUG_RK else 99
            dma('sp', hbuf[:, :, :], hTv[:, :, t0:t0 + CH], r=[('hT', t)], w=[('hbuf',)])
            act(sq[:, :, :], hbuf[:, :, :], AF.Square, r=[('hbuf',)], w=[('sq',)])
            for c in range(8):
                mm(PS[:, 0:CH], ones_bf(C)[:, :], sq[:, c, :], c == 0, c == 7, r=[('sq',)], w=bk(0))
            act(rstd[:, :], PS[:, 0:CH], AF.Sqrt, r=bk(0), w=[('rstd',)], scale=1.0 / D, bias=C.eps_col[:, 0:1])
            S.op('dve', lambda e: e.reciprocal(out=rstd[:, :], in_=rstd[:, :]), r=[('rstd',)], w=[('rstd',)])
            for c in range(8):
                stt('dve', hn[:, c, 1:CH + 1], hbuf[:, c, :], vec[:, gcol + c:gcol + c + 1], rstd[:, :],
                    ALU.mult, ALU.mult, r=[('hbuf',), ('rstd',), ('hn0',)], w=[('hn', c)])
            hnk = keys('hn', range(8))
            tt('pool', xx[:, :, :], hn[:, :, 0:CH], hn[:, :, 1:CH + 1], ALU.subtract, r=hnk + [('hn0',)], w=[('xx',)])
            for i in range(6):
                mixb = vec[:, mixc + i * 8:mixc + i * 8 + 8].unsqueeze(2).to_broadcast([128, 8, CH])
                tt('pool', xmf[0][:, :, :], xx[:, :, :], mixb, ALU.mult, r=[('xx',), ('c_vec',)], w=[('xmf', 0)])
                tt('dve', xm[i][:, :, :], xmf[0][:, :, :], hn[:, :, 1:CH + 1], ALU.add,
                   r=[('xmf', 0)] + hnk, w=keys('xm', [i], range(8)))
            cp('pool', hn[:, :, 0:1], hn[:, :, CH:CH + 1], r=hnk + [('xx',)], w=[('hn0',)])
            xr, xw, xk, xv, xa, xg = xm
            if PH < -2:
                continue
            for (Wt, wn, xs, xi, b0, dst, dn) in ((Wr, 'Wr', xr, 0, 0, r_, 'r'), (Wk, 'Wk', xk, 2, 2, k_, 'k')):
                for h in range(NH):
                    for kc in range(8):
                        mm(PS[0:64, b0 * 512 + h * 64:b0 * 512 + (h + 1) * 64], Wt[:, kc, h * 64:(h + 1) * 64],
                           xs[:, kc, :], kc == 0, kc == 7, r=[('xm', xi, kc), (wn,)], w=bk(b0 + h // 8))
                cp('act', dst[:, :, :], psv1(b0), r=bk(b0, b0 + 1), w=[(dn,)])
            for n in range(2):
                for kc in range(8):
                    mm(PS[0:64, (4 + n) * 512:(5 + n) * 512], xv[:, kc, :], Wv[:, kc, n * 512:(n + 1) * 512],
                       kc == 0, kc == 7, r=[('xm', 3, kc), ('Wv',)], w=bk(4 + n))
            cp('dve', v_tm[:, :], PS[0:64, 2048:3072], r=bk(4, 5), w=[('v_tm',)])
            if not NOVBF:
                cp('act', v_bf[:, :], PS[0:64, 2048:3072], r=bk(4, 5), w=[('v_bf',)])
            for kc in range(8):
                mm(PS[0:64, 3072:3072 + CH], w1[:, kc, :], xw[:, kc, :], kc == 0, kc == 7,
                   r=[('xm', 1, kc), ('w1',)], w=bk(6))
            act(twT[0:64, :], PS[0:64, 3072:3072 + CH], AF.Tanh, r=bk(6), w=[('twT',)])
            for kc in range(8):
                mm(PS[0:64, 0:CH], a1[:, kc, :], xa[:, kc, :], kc == 0, kc == 7,
                   r=[('xm', 4, kc), ('a1',)], w=bk(0))
            cp('dve', taT[:, :], PS[0:64, 0:CH], r=bk(0), w=[('taT',)])
            for kc in range(8):
                mm(PS[:, 3072:3072 + CH], g1[:, kc, 0:128], xg[:, kc, :], kc == 0, kc == 7,
                   r=[('xm', 5, kc), ('g1',)], w=bk(6))
            act(sg0[:, :], PS[:, 3072:3072 + CH], AF.Sigmoid, r=bk(6), w=[('sg0',)])
            for kc in range(8):
                mm(PS[0:32, 512:512 + CH], g1[:, kc, 128:160], xg[:, kc, :], kc == 0, kc == 7,
                   r=[('xm', 5, kc), ('g1',)], w=bk(1))
            act(sg1[:, :], PS[0:32, 512:512 + CH], AF.Sigmoid, r=bk(1), w=[('sg1',)])
            for n in range(2):
                mm(PS[0:64, n * 512:(n + 1) * 512], twT[0:65, :], w2aug[0:65, n * 512:(n + 1) * 512], True, True,
                   r=[('twT',), ('twT1',), ('w2aug',)], w=bk(n))
            act(lw_tm[:, :], PS[0:64, 0:1024], AF.Sigmoid, r=bk(0, 1), w=[('lw_tm',)])
            for h in range(NH):
                mm(PS[0:64, 1024 + h * 128:1024 + (h + 1) * 128], lw_tm[0:64, h * 64:(h + 1) * 64],
                   C.tri[0:64, 0:128], True, True, r=[('lw_tm',), ('c_tri',)], w=bk(2 + h // 4))
            pc = psv2(2)
            cb = bk(2, 3, 4, 5)
            act(G[:, :, :], pc[:, :, 0, :], AF.Exp, r=cb, w=[('G',)])
            act(Ginv[:, :, :], pc[:, :, 0, :], AF.Exp, r=cb, w=[('Ginv',)], scale=-1.0)
            act(Gex[:, :, :], pc[:, :, 1, :], AF.Exp, r=cb, w=[('Gex',)])
            cp('dve', cumC[:, :], pc[:, :, 0, CH - 1], r=cb, w=[('cumC',)])
            tt('dve', tmp1[:, :, :], bc(cumC[:, :]), pc[:, :, 0, :], ALU.subtract, r=cb + [('cumC',)], w=[('tmp1',)])
            act(Ghat[:, :, :], tmp1[:, :, :], AF.Exp, r=[('tmp1',)], w=[('Ghat',)])
            for h in range(NH):
                mm(PS[0:64, 2048 + h * 64:2048 + (h + 1) * 64], a2[0:64, h * 64:(h + 1) * 64], taT[0:64, :], True, True,
                   r=[('taT',), ('a2',)], w=bk(4 + h // 8))
            tt('dve', a_[:, :, :], psv1(4), prm('a0'), ALU.add, r=bk(4, 5) + [('c_vec64',)], w=[('a',)])
            act(a_[:, :, :], a_[:, :, :], AF.Sigmoid, r=[('a',)], w=[('a',)])
            for n in range(2):
                mm(PS[0:64, n * 512:(n + 1) * 512], sg0[:, :], g2a[:, n * 512:(n + 1) * 512], True, False,
                   r=[('sg0',), ('g2',)], w=bk(n))
                mm(PS[0:64, n * 512:(n + 1) * 512], sg1[0:32, :], g2b[0:32, n * 512:(n + 1) * 512], False, True,
                   r=[('sg1',), ('g2',)], w=bk(n))
            cp('act', g_tm[:, :], PS[0:64, 0:1024], r=bk(0, 1), w=[('g_tm',)])
            if PH < -1:
                continue
            tt('dve', kk[:, :, :], k_[:, :, :], prm('k_k'), ALU.mult, r=[('k',), ('c_vec64',)], w=[('kk',)])
            act(tmp2[:, :, :], kk[:, :, :], AF.Square, r=[('kk',)], w=[('tmp2',)])
            t2f = tmp2[:, :, :].rearrange("p h s -> p (h s)")
            for n in range(2):
                mm(PS[0:64, 1024 + n * 512:1024 + (n + 1) * 512], ones64, t2f[:, n * 512:(n + 1) * 512], True, True,
                   r=[('tmp2',), ('c_onesf',)], w=bk(2 + n))
            act(tmp2[:, :, :], psv1(2), AF.Sqrt, r=bk(2, 3), w=[('tmp2',)])
            ts('dve', tmp2[:, :, :], tmp2[:, :, :], 1e-12, None, ALU.max, None, r=[('tmp2',)], w=[('tmp2',)])
            S.op('dve', lambda e: e.reciprocal(out=tmp2[:, :, :], in_=tmp2[:, :, :]), r=[('tmp2',)], w=[('tmp2',)])
            tt('dve', kk[:, :, :], kk[:, :, :], tmp2[:, :, :], ALU.mult, r=[('kk',), ('tmp2',)], w=[('kk',)])
            tt('pool', tmp1[:, :, :], a_[:, :, :], prm('k_a'), ALU.mult, r=[('a',), ('c_vec64',)], w=[('tmp1',)])
            tt('pool', tmp1[:, :, :], tmp1[:, :, :], bc(omk[:, :]), ALU.add, r=[('tmp1',), ('omk',)], w=[('tmp1',)])
            tt('pool', k_[:, :, :], k_[:, :, :], tmp1[:, :, :], ALU.mult, r=[('k',), ('tmp1',)], w=[('k',)])
            tt('dve', b_[:, :, :], kk[:, :, :], a_[:, :, :], ALU.mult, r=[('kk',), ('a',)], w=[('b',)])
            stt('dve', AR[:, :, 0:CH], kk[:, :, :], -1.0, Gex[:, :, :], ALU.mult, ALU.mult,
                r=[('kk',), ('Gex',)], w=[('AR0',)])
            tt('pool', AR[:, :, CH:2 * CH], r_[:, :, :], G[:, :, :], ALU.mult, r=[('r',), ('G',)], w=[('AR1',)])
            tt('dve', Bt[:, :, :], b_[:, :, :], Ginv[:, :, :], ALU.mult, r=[('b',), ('Ginv',)], w=[('Bt',)])
            tt('pool', Kt[:, :, :], k_[:, :, :], Ginv[:, :, :], ALU.mult, r=[('k',), ('Ginv',)], w=[('Kt',)])
            tt('dve', Bh[:, :, :], b_[:, :, :], Ghat[:, :, :], ALU.mult, r=[('b',), ('Ghat',)], w=[('Bh',)])
            tt('pool', Kh[:, :, :], k_[:, :, :], Ghat[:, :, :], ALU.mult, r=[('k',), ('Ghat',)], w=[('Kh',)])
            tt('pool', tmp1[:, :, :], r_[:, :, :], prm('r_k'), ALU.mult, r=[('r',), ('c_vec64',)], w=[('tmp1',)])
            tt('pool', tmp1[:, :, :], tmp1[:, :, :], k_[:, :, :], ALU.mult, r=[('tmp1',), ('k',)], w=[('tmp1',)])
            for h in range(NH):
                mm(PS[0:64, 3072 + h:3072 + h + 1], tmp1[:, h, :], C.ones_f[0:64, 0:1], True, True,
                   r=[('tmp1',), ('c_onesf',)], w=bk(6))
            cp('dve', bon[:, :], PS[0:64, 3072:3072 + NH], r=bk(6), w=[('bon',)])
            if PH < 1:
                continue
            GR = [(0, 8), (8, 8)]

            def hv(ap, g):
                return ap[:, GR[g][0]:GR[g][0] + 8, :]

            def pg2(b0):
                return PS[0:64, b0 * 512:b0 * 512 + 1024].rearrange("p (h two s) -> p h two s", h=8, two=2)

            def pg1(b0):
                return PS[0:64, b0 * 512:b0 * 512 + 512].rearrange("p (h s) -> p h s", h=8)

            SU8 = C.masks[0:64, 0:64].unsqueeze(1).to_broadcast([64, 8, CH])
            IU8 = C.masks[0:64, 64:128].unsqueeze(1).to_broadcast([64, 8, CH])
            SL8 = C.masks[0:64, 128:192].unsqueeze(1).to_broadcast([64, 8, CH])
            ID8 = ident[0:64, 0:64].unsqueeze(1).to_broadcast([64, 8, CH])
            for g in range(2):
                h0 = GR[g][0]
                bA = 0 if g == 0 else 3
                for hh in range(8):
                    h = h0 + hh
                    mm(PS[0:64, bA * 512 + hh * 128:bA * 512 + (hh + 1) * 128], Bt[:, h, :], AR[:, h, :], True, True,
                       r=[('Bt',), ('AR0',), ('AR1',)], w=bk(bA + hh // 4))
                tt('dve', hv(MX[:, :, 0:CH], g), pg2(bA)[:, :, 0, :], SU8, ALU.mult, r=bk(bA, bA + 1) + [('c_masks',)],
                   w=[('MX0', g)])
                tt('dve', hv(RBT, g), pg2(bA)[:, :, 1, :], IU8, ALU.mult, r=bk(bA, bA + 1) + [('c_masks',)],
                   w=[('RBT', g)])
                for hh in range(8):
                    h = h0 + hh
                    mm(PS[0:64, bA * 512 + hh * 128:bA * 512 + (hh + 1) * 128], Kt[:, h, :], AR[:, h, :], True, True,
                       r=[('Kt',), ('AR0',), ('AR1',)], w=bk(bA + hh // 4))
                tt('dve', hv(LakT, g), pg2(bA)[:, :, 0, :], SU8, ALU.mult, r=bk(bA, bA + 1) + [('c_masks',)],
                   w=[('LakT', g)])
                tt('dve', hv(RKT, g), pg2(bA)[:, :, 1, :], IU8, ALU.mult, r=bk(bA, bA + 1) + [('c_masks',)],
                   w=[('RKT', g)])
                for hh in range(8):
                    h = h0 + hh
                    mm(PS[0:64, (bA + 2) * 512 + hh * 64:(bA + 2) * 512 + (hh + 1) * 64], AR[:, h, 0:CH], Bt[:, h, :],
                       True, True, r=[('Bt',), ('AR0',)], w=bk(bA + 2))
                tt('dve', hv(Lm, g), pg1(bA + 2), SL8, ALU.mult, r=bk(bA + 2) + [('c_masks',)], w=[('Lm', g)])
                cp('pool', hv(MX[:, :, CH:2 * CH], g), ID8, r=[('c_ident',)], w=[('MX1', g)])
            if PH < 2:
                continue
            for lvl in range(6):
                for g in range(2):
                    h0 = GR[g][0]
                    bA = 0 if g == 0 else 3
                    for hh in range(8):
                        h = h0 + hh
                        mm(PS[0:64, bA * 512 + hh * 128:bA * 512 + (hh + 1) * 128], Lm[:, h, :], MX[:, h, :], True, True,
                           r=[('Lm', g), ('MX0', g), ('MX1', g)], w=bk(bA + hh // 4))
                    if lvl < 5:
                        for hh in range(8):
                            h = h0 + hh
                            mm(PS[0:64, (bA + 2) * 512 + hh * 64:(bA + 2) * 512 + (hh + 1) * 64], MX[:, h, 0:CH], Lm[:, h, :],
                               True, True, r=[('Lm', g), ('MX0', g)], w=bk(bA + 2))
                for g in range(2):
                    bA = 0 if g == 0 else 3
                    tt('dve', hv(MX[:, :, CH:2 * CH], g), pg2(bA)[:, :, 1, :], hv(MX[:, :, CH:2 * CH], g), ALU.add,
                       r=bk(bA, bA + 1) + [('MX1', g)], w=[('MX1', g)])
                    if lvl < 5:
                        cp('act', hv(MX[:, :, 0:CH], g), pg2(bA)[:, :, 0, :], r=bk(bA, bA + 1), w=[('MX0', g)])
                        cp('act', hv(Lm, g), pg1(bA + 2), r=bk(bA + 2), w=[('Lm', g)])
            if PH < 3:
                continue
            if TRDT == BF16:
                psb3 = PSb[0:64, :].rearrange("p (h s) -> p h s", h=NH)
                for h in range(NH):
                    tr(PSb[0:64, h * 64:(h + 1) * 64], Bh[:, h, :], ident_bf[:, :], r=[('Bh',), ('ident_bf',)], w=[('psb',)])
                cp('act', BhT[:, :, :], psb3, r=[('psb',)], w=[('BhT',)])
                for h in range(NH):
                    tr(PSb[0:64, h * 64:(h + 1) * 64], Kh[:, h, :], ident_bf[:, :], r=[('Kh',), ('ident_bf',)], w=[('psb',)])
                cp('dve', KhT[:, :, :], psb3, r=[('psb',)], w=[('KhT',)])
            else:
                for h in range(NH):
                    tr(PS[0:64, h * 64:(h + 1) * 64], Bh[:, h, :], ident[0:64, 0:64], r=[('Bh',), ('c_ident',)], w=bk(h // 8))
                cp('act', BhT[:, :, :], psv1(0), r=bk(0, 1), w=[('BhT',)])
                for h in range(NH):
                    tr(PS[0:64, 1024 + h * 64:1024 + (h + 1) * 64], Kh[:, h, :], ident[0:64, 0:64],
                       r=[('Kh',), ('c_ident',)], w=bk(2 + h // 8))
                cp('dve', KhT[:, :, :], psv1(2), r=bk(2, 3), w=[('KhT',)])
            if PH < 4:
                continue
            for h in range(NH):
                o = PS[0:64, h * 64:(h + 1) * 64]
                mm(o, AR[:, h, 0:CH], Hb[:, h, :], True, False, r=[('AR0',), ('Hb',)], w=bk(h // 8))
                mm(o, LakT[:, h, :], v_bf[:, h * 64:(h + 1) * 64], False, True, r=[('LakT', h // 8), ('v_bf',)], w=bk(h // 8))
            cp('act', Gs[:, :, :], psv1(0), r=bk(0, 1), w=[('Gs',)])
            for h in range(NH):
                mm(PS[0:64, 1024 + h * 64:1024 + (h + 1) * 64], MX[:, h, CH:2 * CH], Gs[:, h, :], True, True,
                   r=[('MX1', h // 8), ('Gs',)], w=bk(2 + h // 8))
            cp('dve', Us[:, :, :], psv1(2), r=bk(2, 3), w=[('Us',)])
            for h in range(NH):
                o = PS[0:64, 2048 + h * 64:2048 + (h + 1) * 64]
                mm(o, AR[:, h, CH:2 * CH], Hb[:, h, :], True, False, r=[('AR1',), ('Hb',)], w=bk(4 + h // 8))
                mm(o, RBT[:, h, :], Us[:, h, :], False, False, r=[('RBT', h // 8), ('Us',)], w=bk(4 + h // 8))
                mm(o, RKT[:, h, :], v_bf[:, h * 64:(h + 1) * 64], False, True, r=[('RKT', h // 8), ('v_bf',)], w=bk(4 + h // 8))
            for h in range(NH):
                o = PS[0:64, h * 64:(h + 1) * 64]
                mm(o, BhT[:, h, :], Us[:, h, :], True, False, r=[('BhT',), ('Us',)], w=bk(h // 8))
                mm(o, KhT[:, h, :], v_bf[:, h * 64:(h + 1) * 64], False, True, r=[('KhT',), ('v_bf',)], w=bk(h // 8))
            tt('dve', Hst[:, :, :], Hst[:, :, :], G[:, :, CH - 1:CH].to_broadcast([64, NH, CH]), ALU.mult,
               r=[('Hst',), ('G',)], w=[('Hst',)])
            tt('dve', Hst[:, :, :], psv1(0), Hst[:, :, :], ALU.add, r=bk(0, 1) + [('Hst',)], w=[('Hst',)])
            cp('act', Hb[:, :, :], Hst[:, :, :], r=[('Hst',)], w=[('Hb',)])
            if PH < 5:
                continue
            py = psv1(4)
            yb = bk(4, 5)
            red('dve', st1[:, :], py, ALU.add, r=yb, w=[('st1',)])
            ts('dve', st1[:, :], st1[:, :], -1.0 / 64, None, ALU.mult, None, r=[('st1',)], w=[('st1',)])
            tt('dve', yc[:, :, :], py, bc(st1[:, :]), ALU.add, r=yb + [('st1',)], w=[('r',)])
            act(ysq[:, :, :], yc[:, :, :], AF.Square, r=[('r',)], w=[('a',)])
            red('dve', st2[:, :], ysq[:, :, :], ALU.add, r=[('a',)], w=[('st2',)])
            act(st2[:, :], st2[:, :], AF.Sqrt, r=[('st2',)], w=[('st2',)], scale=1.0 / 64, bias=C.eps_col[0:64, 1:2])
            S.op('dve', lambda e: e.reciprocal(out=st2[:, :], in_=st2[:, :]), r=[('st2',)], w=[('st2',)])
            tt('dve', yc[:, :, :], yc[:, :, :], bc(st2[:, :]), ALU.mult, r=[('r',), ('st2',)], w=[('r',)])
            ycf = yc[:, :, :].rearrange("p h s -> p (h s)")
            tt('pool', ycf, ycf, lnxg[:, :], ALU.mult, r=[('r',), ('lnxg',)], w=[('r',)])
            tt('pool', ycf, ycf, lnxb[:, :], ALU.add, r=[('r',), ('lnxb',)], w=[('r',)])
            vv = v_tm[:, :].rearrange("p (h s) -> p h s", h=NH)
            tt('dve', ysq[:, :, :], vv, bc(bon[:, :]), ALU.mult, r=[('v_tm',), ('bon',)], w=[('a',)])
            tt('pool', yc[:, :, :], yc[:, :, :], ysq[:, :, :], ALU.add, r=[('r',), ('a',)], w=[('r',)])
            tt('dve', ycf, ycf, g_tm[:, :], ALU.mult, r=[('r',), ('g_tm',)], w=[('r',)])
            for c in range(8):
                tr(PS[:, 3072 + c * 64:3072 + (c + 1) * 64], yc[:, 2 * c:2 * c + 2, :].rearrange("p h s -> p (h s)"),
                   ident[0:64, 0:64], r=[('r',), ('c_ident',)], w=bk(6))
            cp('act', zT[:, :, :], PS[:, 3072:3584].rearrange("p (c s) -> p c s", c=8), r=bk(6), w=[('zT',)])
            for co in range(8):
                q = 2 + (co % 2)
                for kc in range(8):
                    mm(PS[:, q * 512:q * 512 + CH], Wo[:, kc, co * 128:(co + 1) * 128], zT[:, kc, :], kc == 0, kc == 7,
                       r=[('zT',), ('Wo',)], w=bk(q))
                tt('dve', hbuf[:, co, :], PS[:, q * 512:q * 512 + CH], hbuf[:, co, :], ALU.add,
                   r=bk(q) + [('hbuf',)], w=[('hbuf',)])
            dma('pool', hTv[:, :, t0:t0 + CH], hbuf[:, :, :], r=[('hbuf',)], w=[('hT', t)])
        S.barrier()
        S.emit()


def stage_dsa(C):
    nc, S = C.nc, C.S
    Hh = H(S)
    mm, tr, act, cp, tt, ts, stt, red, dma = Hh.mm, Hh.tr, Hh.act, Hh.cp, Hh.tt, Hh.ts, Hh.stt, Hh.red, Hh.dma
    TT = 128
    NQT = (TP + TT - 1) // TT
    NQT_RUN = min(NQT, DEBUG_NQT) if DEBUG_NQT else NQT
    c64 = C.cols64
    cols = C.cols
    NBIS = 15
    MB = 240000.0
    with ExitStack() as st:
        sb = lambda n, sh, dt=F32: st.enter_context(nc.sbuf_tensor("ds_" + n, sh, dt))
        Wq = sb("Wq", [128, 8, 1024], BF16)
        Wk = sb("Wk", [128, 8, 256], BF16)
        Wv = sb("Wv", [128, 8, 256], BF16)
        Wqi = sb("Wqi", [128, 8, 512], BF16)
        Wki = sb("Wki", [128, 8, 64], BF16)
        Wwi = sb("Wwi", [128, 8, 8], BF16)
        Wo = sb("Wo", [128, 8, 1024], BF16)
        kT = sb("kT", [64, 4, TP], BF16)
        Vaug = sb("Vaug", [128, NQT, 4, 65], BF16)
        kiT = sb("kiT", [64, TP], BF16)
        score = sb("score", [128, TP])
        work = sb("work", [128, TP])
        mask01 = sb("mask01", [128, TP], BF16)
        maskT = sb("maskT", [128, NQT, TT], BF16)
        hbuf = [sb("hbuf%d" % i, [128, 8, TT]) for i in range(2)]
        hnb = sb("hnb", [128, 8, TT], BF16)
        sq = sb("sq", [128, 8, TT], BF16)
        rstd = sb("rstd", [128, TT])
        qT = [sb("qT%d" % i, [64, 16 * TT], BF16) for i in range(2)]
        qiT = sb("qiT", [64, 8 * TT], BF16)
        tA = sb("tA", [64, 512])
        tB = sb("tB", [64, 512])
        tC = sb("tC", [64, 512])
        rl = [sb("rl%d" % i, [128, 512]) for i in range(2)]
        PT = [sb("PT%d" % i, [128, 512], BF16) for i in range(3)]
        o_tm = sb("o_tm", [128, 1024], BF16)
        oT = sb("oT", [128, 8, TT], BF16)
        cs = sb("cs", [64, 2, TT])
        wi = sb("wi", [128, 8])
        m8 = sb("m8", [128, 8])
        eq8 = sb("eq8", [128, 8])
        iota8 = sb("iota8", [128, 8])
        lo = sb("lo", [128, 1])
        HC = sb("HC", [128, 2])
        MC = sb("MC", [128, 2])
        sel = sb("sel", [128, 1])
        d1 = sb("d1", [128, 1])
        d2 = sb("d2", [128, 2])
        thr = sb("thr", [128, 1])
        nm1 = sb("nm1", [128, 1])
        halfc = sb("halfc", [128, 1])
        negb = sb("negb", [128, 1])
        rden = sb("rden", [128, 16])
        ident_bf = sb("ident_bf", [128, 128], BF16)
        zeros_bf = sb("zeros_bf", [128, 512], BF16)
        rot = sb("rot", [64, 64])
        negmask = sb("negmask", [128, 128])
        PS = st.enter_context(nc.psum_tensor("ds_PS", [128, 3584], F32))
        PSb = st.enter_context(nc.psum_tensor("ds_PSb", [128, 1024], BF16))
        bank = lambda b: PS[:, b * 512:(b + 1) * 512]

        hTv = C.hT.rearrange("(c p) t -> p c t", p=128)
        ident = C.ident
        vec = C.vec
        v64 = C.vec64
        win = C.at['w_in'].rearrange("(k p) n -> p k n", p=128)
        for k in range(8):
            dma('pool', Wq[:, k, :], win[:, k, 0:1024], r=[], w=[('Wq',)])
        dma('pool', Wk[:, :, :], win[:, :, 1024:1280], r=[], w=[('Wk',)])
        dma('pool', Wv[:, :, :], win[:, :, 1280:1536], r=[], w=[('Wv',)])
        for k in range(8):
            dma('pool', Wqi[:, k, :], win[:, k, 1536:2048], r=[], w=[('Wqi',)])
        dma('pool', Wki[:, :, :], win[:, :, 2048:2112], r=[], w=[('Wki',)])
        dma('pool', Wwi[:, :, :], win[:, :, 2112:2120], r=[], w=[('Wwi',)])
        wov = C.at['w_o'].rearrange("(k p) n -> p k n", p=128)
        for k in range(8):
            dma('pool', Wo[:, k, :], wov[:, k, :], r=[], w=[('Wo',)])
        dma('sp', rot[:, :], C.rot_d[:, :], r=[], w=[('rot',)])
        dma('sp', negmask[:, :], C.negmask_d[:, :], r=[], w=[('negmask',)])
        cp('dve', ident_bf[:, :], ident[:, :], r=[('c_ident',)], w=[('ident_bf',)])
        S.op('pool', lambda e: e.memset(zeros_bf[:, :], 0.0), w=[('zeros_bf',)])
        S.op('pool', lambda e: e.memset(Vaug[:, :, :, 64:65], 1.0), w=[('Vones',)])
        S.op('pool', lambda e: e.memset(halfc[:, :], 0.5), w=[('halfc',)])
        S.op('pool', lambda e: e.memset(negb[:, :], -MB), w=[('negb',)])
        for j in range(8):
            S.op('pool', lambda e, j=j: e.memset(iota8[:, j:j + 1], float(j)), w=[('iota8',)])
        gcol = cols['norm_g_1_1'][0]
        qg = v64[0:64, c64['q_g'][0]:c64['q_g'][0] + 1]
        kg = v64[0:64, c64['k_g'][0]:c64['k_g'][0] + 1]
        kwid = lambda kb: min(128, TP - kb * 128)

        def norm_rope(pb, nh, tw, gcolap, out3, okeys_w):
            n = nh * tw
            pin = bank(pb)[0:64, 0:n]
            v3 = lambda ap: ap.rearrange("p (h s) -> p h s", h=nh)
            if gcolap is not None:
                act(tA[:, 0:n], pin, AF.Square, r=bk(pb), w=[('tA',)])
                mm(bank(6)[0:64, 0:n], C.ones_f[0:64, 0:64], tA[:, 0:n], True, True, r=[('tA',), ('c_onesf',)], w=bk(6))
                act(tB[:, 0:n], bank(6)[0:64, 0:n], AF.Sqrt, r=bk(6), w=[('tB',)], scale=1.0 / 64,
                    bias=C.eps_col[0:64, 0:1])
                S.op('dve', lambda e: e.reciprocal(out=tB[:, 0:n], in_=tB[:, 0:n]), r=[('tB',)], w=[('tB',)])
                stt('dve', tC[:, 0:n], pin, gcolap, tB[:, 0:n], ALU.mult, ALU.mult, r=bk(pb) + [('tB',), ('c_vec64',)],
                    w=[('tC',)])
            else:
                cp('act', tC[:, 0:n], pin, r=bk(pb), w=[('tC',)])
            mm(bank(6)[0:64, 0:n], rot[:, :], tC[:, 0:n], True, True, r=[('tC',), ('rot',)], w=bk(6))
            cosb = cs[:, 0, 0:tw].unsqueeze(1).to_broadcast([64, nh, tw])
            sinb = cs[:, 1, 0:tw].unsqueeze(1).to_broadcast([64, nh, tw])
            tt('pool', v3(tA[:, 0:n]), v3(tC[:, 0:n]), cosb, ALU.mult, r=[('tC',), ('cs',)], w=[('tA',)])
            tt('dve', v3(tB[:, 0:n]), v3(bank(6)[0:64, 0:n]), sinb, ALU.mult, r=bk(6) + [('cs',)], w=[('tB',)])
            tt('pool', out3, v3(tA[:, 0:n]), v3(tB[:, 0:n]), ALU.add, r=[('tA',), ('tB',)], w=okeys_w)

        def X(qt):
            s = qt % 2
            t0 = qt * TT
            tw = min(TT, TP - t0)
            n = t0 + tw
            nkb = qt + 1
            hb = hbuf[s]
            dma('sp', hb[:, :, 0:tw], hTv[:, :, t0:t0 + tw], r=[('hT', qt)], w=[('hbuf', s)])
            dma('sp', cs[:, :, 0:tw], C.rope_d[:, :, t0:t0 + tw], r=[], w=[('cs',)])
            act(sq[:, :, 0:tw], hb[:, :, 0:tw], AF.Square, r=[('hbuf', s)], w=[('sq',)])
            for c in range(8):
                mm(bank(6)[:, 0:tw], C.ones_bf[:, :], sq[:, c, 0:tw], c == 0, c == 7, r=[('sq',)], w=bk(6))
            act(rstd[:, 0:tw], bank(6)[:, 0:tw], AF.Sqrt, r=bk(6), w=[('rstd',)], scale=1.0 / D, bias=C.eps_col[:, 0:1])
            S.op('dve', lambda e: e.reciprocal(out=rstd[:, 0:tw], in_=rstd[:, 0:tw]), r=[('rstd',)], w=[('rstd',)])
            for c in range(8):
                stt('dve', hnb[:, c, 0:tw], hb[:, c, 0:tw], vec[:, gcol + c:gcol + c + 1], rstd[:, 0:tw],
                    ALU.mult, ALU.mult, r=[('hbuf', s), ('rstd',)], w=[('hnb',)])
            for g in range(4):
                for kc in range(8):
                    mm(bank(5)[0:64, g * tw:(g + 1) * tw], Wk[:, kc, g * 64:(g + 1) * 64], hnb[:, kc, 0:tw], kc == 0, kc == 7,
                       r=[('hnb',), ('Wk',)], w=bk(5))
            norm_rope(5, 4, tw, kg, kT[:, :, t0:t0 + tw], [('kT',)])
            for kc in range(8):
                mm(bank(5)[0:tw, 0:256], hnb[:, kc, 0:tw], Wv[:, kc, :], kc == 0, kc == 7, r=[('hnb',), ('Wv',)], w=bk(5))
            cp('act', Vaug[0:tw, qt, :, 0:64], bank(5)[0:tw, 0:256].rearrange("p (g d) -> p g d", g=4), r=bk(5),
               w=[('Vaug',)])
            for kc in range(8):
                mm(bank(5)[0:64, 0:tw], Wki[:, kc, :], hnb[:, kc, 0:tw], kc == 0, kc == 7, r=[('hnb',), ('Wki',)], w=bk(5))
            norm_rope(5, 1, tw, None, kiT[:, t0:t0 + tw].unsqueeze(1), [('kiT',)])
            for grp in range(4):
                pb = 5
                for hh in range(4):
                    h = grp * 4 + hh
                    for kc in range(8):
                        mm(bank(pb)[0:64, hh * tw:(hh + 1) * tw], Wq[:, kc, h * 64:(h + 1) * 64], hnb[:, kc, 0:tw],
                           kc == 0, kc == 7, r=[('hnb',), ('Wq',)], w=bk(pb))
                norm_rope(pb, 4, tw, qg, qT[s][:, grp * 4 * tw:(grp + 1) * 4 * tw].rearrange("p (h s) -> p h s", h=4),
                          [('qT', s)])
            for grp in range(2):
                pb = 5
                for hh in range(4):
                    h = grp * 4 + hh
                    for kc in range(8):
                        mm(bank(pb)[0:64, hh * tw:(hh + 1) * tw], Wqi[:, kc, h * 64:(h + 1) * 64], hnb[:, kc, 0:tw],
                           kc == 0, kc == 7, r=[('hnb',), ('Wqi',)], w=bk(pb))
                norm_rope(pb, 4, tw, None, qiT[:, grp * 4 * tw:(grp + 1) * 4 * tw].rearrange("p (h s) -> p h s", h=4),
                          [('qiT',)])
            for kc in range(8):
                mm(bank(6)[0:tw, 0:8], hnb[:, kc, 0:tw], Wwi[:, kc, :], kc == 0, kc == 7, r=[('hnb',), ('Wwi',)], w=bk(6))
            ts('dve', wi[0:tw, :], bank(6)[0:tw, 0:8], float(512.0 ** -0.5), None, ALU.mult, None, r=bk(6), w=[('wi',)])
            idx = 0
            for k0 in range(0, n, 512):
                nk = min(512, n - k0)
                for h in range(8):
                    pb = 5 + idx % 2
                    rb = rl[idx % 2]
                    rk = ('rl', idx % 2)
                    idx += 1
                    mm(bank(pb)[0:tw, 0:nk], qiT[:, h * tw:(h + 1) * tw], kiT[:, k0:k0 + nk], True, True,
                       r=[('qiT',), ('kiT',)], w=bk(pb))
                    act(rb[0:tw, 0:nk], bank(pb)[0:tw, 0:nk], AF.Relu, r=bk(pb), w=[rk])
                    if h == 0:
                        ts('dve', score[0:tw, k0:k0 + nk], rb[0:tw, 0:nk], wi[0:tw, 0:1], None, ALU.mult, None,
                           r=[rk, ('wi',)], w=[('score',)])
                    else:
                        stt('dve', score[0:tw, k0:k0 + nk], rb[0:tw, 0:nk], wi[0:tw, h:h + 1], score[0:tw, k0:k0 + nk],
                            ALU.mult, ALU.add, r=[rk, ('wi',), ('score',)], w=[('score',)])
            sc = score[0:tw, 0:n]
            if n > 256:
                red('dve', HC[0:tw, 0:1], sc, ALU.max, r=[('score',)], w=[('HC',)])
                red('dve', lo[0:tw, :], sc, ALU.min, r=[('score',)], w=[('lo',)])
            tt('dve', score[0:tw, t0:t0 + tw], score[0:tw, t0:t0 + tw], negmask[0:tw, 0:tw], ALU.add,
               r=[('score',), ('negmask',)], w=[('score',)])
            if n > 256:
                tt('dve', d1[0:tw, :], HC[0:tw, 0:1], lo[0:tw, :], ALU.subtract, r=[('HC',), ('lo',)], w=[('d1',)])
                stt('dve', HC[0:tw, 0:1], d1[0:tw, :], 1.0e-6, HC[0:tw, 0:1], ALU.mult, ALU.add, r=[('d1',), ('HC',)],
                    w=[('HC',)])
                ts('dve', HC[0:tw, 1:2], d1[0:tw, :], 0.0, None, ALU.mult, None, r=[('d1',), ('HC',)], w=[('HC',)])
                for it in range(NBIS):
                    stt('dve', MC[0:tw, 0:1], lo[0:tw, :], HC[0:tw, 0:1], halfc[0:tw, :], ALU.add, ALU.mult,
                        r=[('lo',), ('HC',), ('halfc',)], w=[('MC',)])
                    S.op('dve', lambda e, tw=tw, n=n: e.tensor_scalar(
                        out=mask01[0:tw, 0:n], in0=score[0:tw, 0:n], scalar1=MC[0:tw, 0:1], scalar2=0.0,
                        op0=ALU.is_ge, op1=ALU.add, accum_out=MC[0:tw, 1:2]),
                        r=[('score',), ('MC',)], w=[('mask01',), ('MC',)])
                    ts('dve', sel[0:tw, :], MC[0:tw, 1:2], 256.0, None, ALU.is_ge, None, r=[('MC',)], w=[('sel',)])
                    tt('dve', d1[0:tw, :], MC[0:tw, 0:1], lo[0:tw, :], ALU.subtract, r=[('MC',), ('lo',)], w=[('d1',)])
                    stt('dve', lo[0:tw, :], d1[0:tw, :], sel[0:tw, 0:1], lo[0:tw, :], ALU.mult, ALU.add,
                        r=[('d1',), ('sel',), ('lo',)], w=[('lo',)])
                    tt('dve', d2[0:tw, :], HC[0:tw, :], MC[0:tw, :], ALU.subtract, r=[('MC',), ('HC',)], w=[('d2',)])
                    stt('dve', HC[0:tw, :], d2[0:tw, :], sel[0:tw, 0:1], MC[0:tw, :], ALU.mult, ALU.add,
                        r=[('d2',), ('sel',), ('MC',)], w=[('HC',)])
                wk = work[0:tw, 0:n]
                ts('dve', wk, sc, HC[0:tw, 0:1], 1.0e20, ALU.is_ge, ALU.mult, r=[('score',), ('HC',)], w=[('work',)])
                tt('dve', wk, sc, wk, ALU.subtract, r=[('score',), ('work',)], w=[('work',)])
                S.op('dve', lambda e, tw=tw, n=n: e.max(out=m8[0:tw, :], in_=work[0:tw, 0:n]), r=[('work',)], w=[('m8',)])
                ts('dve', nm1[0:tw, :], HC[0:tw, 1:2], -1.0, 255.0, ALU.mult, ALU.add, r=[('HC',)], w=[('nm1',)])
                ts('dve', nm1[0:tw, :], nm1[0:tw, :], 7.0, 0.0, ALU.min, ALU.max, r=[('nm1',)], w=[('nm1',)])
                ts('dve', eq8[0:tw, :], iota8[0:tw, :], nm1[0:tw, 0:1], None, ALU.is_equal, None, r=[('nm1',), ('iota8',)],
                   w=[('eq8',)])
                tt('dve', eq8[0:tw, :], eq8[0:tw, :], m8[0:tw, :], ALU.mult, r=[('eq8',), ('m8',)], w=[('eq8',)])
                red('dve', thr[0:tw, :], eq8[0:tw, :], ALU.add, r=[('eq8',)], w=[('thr',)])
                ts('dve', mask01[0:tw, 0:n], sc, thr[0:tw, 0:1], None, ALU.is_ge, None, r=[('score',), ('thr',)],
                   w=[('mask01',)])
            else:
                ts('dve', mask01[0:tw, 0:n], sc, -1.0e29, None, ALU.is_ge, None, r=[('score',)], w=[('mask01',)])
        def X2(qt):
            t0 = qt * TT
            tw = min(TT, TP - t0)
            nkb = qt + 1
            for kb0 in range(0, nkb, 8):
                nb = min(8, nkb - kb0)
                for j in range(nb):
                    kb = kb0 + j
                    kw = kwid(kb)
                    tr(PSb[0:kw, j * 128:j * 128 + tw], mask01[0:tw, kb * 128:kb * 128 + kw], ident_bf[0:tw, 0:tw],
                       r=[('mask01',), ('ident_bf',)], w=[('psb',)])
                kwl = kwid(kb0 + nb - 1)
                nfull = nb if kwl == 128 else nb - 1
                if nfull > 0:
                    act(maskT[:, kb0:kb0 + nfull, 0:tw],
                        PSb[:, 0:nfull * 128].rearrange("p (j s) -> p j s", j=nfull)[:, :, 0:tw], AF.Identity,
                        r=[('psb',), ('negb',)], w=[('maskT', kb) for kb in range(kb0, kb0 + nfull)],
                        scale=MB, bias=negb[:, 0:1])
                if nfull < nb:
                    act(maskT[0:kwl, kb0 + nb - 1, 0:tw], PSb[0:kwl, (nb - 1) * 128:(nb - 1) * 128 + tw], AF.Identity,
                        r=[('psb',), ('negb',)], w=[('maskT', kb0 + nb - 1)], scale=MB, bias=negb[0:kwl, 0:1])

        def Y(qt):
            s = qt % 2
            t0 = qt * TT
            tw = min(TT, TP - t0)
            nkb = qt + 1
            hb = hbuf[s]
            q_ = qT[s]
            OB = [(0, 0, 7), (1, 7, 7), (2, 14, 2)]
            for (ob, h0, nh) in OB:
                S.op('pe', lambda e, ob=ob, nh=nh: e.matmul(bank(ob)[0:tw, 0:nh * 65], lhsT=zeros_bf[:, 0:tw],
                                                          rhs=zeros_bf[:, 0:nh * 65], start=True, stop=False,
                                                          skip_group_check=True),
                     r=[('zeros_bf',)], w=bk(ob))
            jobs = [(kb, g) for kb in range(nkb) for g in range(4)]

            def s_part(i):
                kb, g = jobs[i]
                kw = kwid(kb)
                pb = 3 + i % 2
                pt = PT[i % 3]
                pk = ('PT', i % 3)
                mbias = maskT[0:kw, kb, 0:tw].unsqueeze(1).to_broadcast([kw, 4, tw])
                mm(bank(pb)[0:kw, 0:4 * tw], kT[:, g, kb * 128:kb * 128 + kw], q_[:, g * 4 * tw:(g + 1) * 4 * tw],
                   True, False, r=[('kT',), ('qT', s)], w=bk(pb))
                mm(bank(pb)[0:kw, 0:4 * tw].rearrange("p (h s) -> p h s", h=4), ident_bf[0:kw, 0:kw], mbias,
                   False, True, r=[('maskT', kb), ('ident_bf',)], w=bk(pb))
                act(pt[0:kw, 0:4 * tw], bank(pb)[0:kw, 0:4 * tw], AF.Exp, r=bk(pb), w=[pk], scale=0.125)

            def v_part(i):
                kb, g = jobs[i]
                kw = kwid(kb)
                pt = PT[i % 3]
                pk = ('PT', i % 3)
                for rr in range(4):
                    h = 4 * g + rr
                    S.op('pe', lambda e, h=h, rr=rr, kw=kw, pt=pt, kb=kb, g=g, last=(kb == nkb - 1): e.matmul(
                        bank(h // 7)[0:tw, (h % 7) * 65:(h % 7 + 1) * 65], lhsT=pt[0:kw, rr * tw:(rr + 1) * tw],
                        rhs=Vaug[0:kw, kb, g, :], start=False, stop=last, skip_group_check=True),
                        r=[pk, ('Vaug',), ('Vones',)], w=bk(h // 7))

            s_part(0)
            for i in range(len(jobs)):
                if i + 1 < len(jobs):
                    s_part(i + 1)
                v_part(i)
            for (ob, h0, nh) in OB:
                o3 = bank(ob)[0:tw, 0:nh * 65].rearrange("p (h d) -> p h d", h=nh)
                S.op('dve', lambda e, o3=o3, h0=h0, nh=nh: e.reciprocal(out=rden[0:tw, h0:h0 + nh], in_=o3[:, :, 64]),
                     r=bk(ob), w=[('rden', ob)])
                tt('dve', o_tm[0:tw, h0 * 64:(h0 + nh) * 64].rearrange("p (h d) -> p h d", h=nh), o3[:, :, 0:64],
                   rden[0:tw, h0:h0 + nh].unsqueeze(2).to_broadcast([tw, nh, 64]), ALU.mult,
                   r=bk(ob) + [('rden', ob)], w=[('o_tm',)])
            for c in range(8):
                tr(PSb[:, c * 128:c * 128 + tw], o_tm[0:tw, c * 128:(c + 1) * 128], ident_bf[0:tw, 0:tw],
                   r=[('o_tm',), ('ident_bf',)], w=[('psb',)])
            cp('act', oT[:, :, 0:tw], PSb[:, :].rearrange("p (c s) -> p c s", c=8)[:, :, 0:tw],
               r=[('psb',)], w=[('oT',)])
            for co in range(8):
                pb = 3 + co % 2
                for kc in range(8):
                    mm(bank(pb)[:, 0:tw], Wo[:, kc, co * 128:(co + 1) * 128], oT[:, kc, 0:tw], kc == 0, kc == 7,
                       r=[('oT',), ('Wo',)], w=bk(pb))
                tt('dve', hb[:, co, 0:tw], bank(pb)[:, 0:tw], hb[:, co, 0:tw], ALU.add, r=bk(pb) + [('hbuf', s)],
                   w=[('hbuf', s)])
            dma('pool', hTv[:, :, t0:t0 + tw], hb[:, :, 0:tw], r=[('hbuf', s)], w=[('hT', qt)])

        X(0)
        X2(0)
        for qt in range(NQT_RUN):
            if qt + 1 < NQT_RUN:
                S.capture()
                X(qt + 1)
                la = S.end_capture()
                S.capture()
                Y(qt)
                lb = S.end_capture()
                S.replay_merged(la, lb)
                X2(qt + 1)
            else:
                Y(qt)
        S.barrier()
        S.emit()


def ones_bf(C):
    return C.ones_bf

VEC_COLS = {}


def _vec_layout():
    cols = {}
    c = 0
    for l in range(2):
        for j in range(3):
            cols['norm_g_%d_%d' % (l, j)] = (c, 8)
            c += 8
    cols['rk_mix'] = (c, 48)
    c += 48
    return cols, c


def _vec64_layout():
    cols = {}
    c = 0
    for n in ('k_k', 'k_a', 'a0', 'r_k'):
        cols[n] = (c, 16)
        c += 16
    for n in ('q_g', 'k_g'):
        cols[n] = (c, 1)
        c += 1
    return cols, c


RK_SHAPES = {'w_r': [D, D], 'w_k': [D, D], 'w_v': [D, D], 'w_o': [D, D], 'w0': [1, D], 'w1': [D, 64], 'w2': [64, D],
             'a1': [D, 64], 'a2': [64, D], 'g1': [D, 160], 'g2': [160, D], 'lnx_g': [1, D], 'lnx_b': [1, D]}


def build_program(stages):
    nc = bass.Bass("TRN2", target_bir_lowering=False)
    C = Ctx()
    C.nc = nc
    din = lambda n, sh, dt=F32: nc.dram_tensor(n, list(sh), dt, kind="ExternalInput").ap()
    C.x = din("x", [SEQ, D])
    C.meta = din("meta", [NMETA, D])
    C.ffn_w_in = din("ffn_w_in", [2, 2, D, 2 * DFF])
    C.ffn_w_out = din("ffn_w_out", [2, 2, DFF, D])
    C.rk = {k: din("rk_" + k, sh) for k, sh in RK_SHAPES.items()}
    cols, nv = _vec_layout()
    cols64, nv64 = _vec64_layout()
    C.cols, C.cols64 = cols, cols64
    C.vec_d = din("vecs", [128, nv])
    C.vec64_d = din("vecs64", [64, nv64])
    C.ident_d = din("ident", [128, 128])
    C.masks_d = din("masks", [64, 192])
    C.tri_d = din("tri", [64, 128])
    C.at = {'w_in': din("at_w_in", [D, 2120]), 'w_o': din("at_w_o", [D, D])}
    C.rope_d = din("rope", [64, 2, TP])
    C.rot_d = din("rot", [64, 64])
    C.negmask_d = din("negmask", [128, 128])
    C.out = nc.dram_tensor("out", [SEQ, D], F32, kind="ExternalOutput").ap()
    C.hT = nc.dram_tensor("hT_scratch", [D, TP], F32, kind="Internal").ap()
    with ExitStack() as es:
        S = Sched(nc, es)
        C.S = S
        C.vec = es.enter_context(nc.sbuf_tensor("c_vec", [128, nv], F32))
        C.vec64 = es.enter_context(nc.sbuf_tensor("c_vec64", [64, nv64], F32))
        C.ident = es.enter_context(nc.sbuf_tensor("c_ident", [128, 128], F32))
        C.masks = es.enter_context(nc.sbuf_tensor("c_masks", [64, 192], F32))
        C.tri = es.enter_context(nc.sbuf_tensor("c_tri", [64, 128], F32))
        C.ones_bf = es.enter_context(nc.sbuf_tensor("c_ones_bf", [128, 128], BF16))
        C.ones_f = es.enter_context(nc.sbuf_tensor("c_ones_f", [128, 128], F32))
        C.eps_col = es.enter_context(nc.sbuf_tensor("c_eps", [128, 2], F32))
        S.op('sp', lambda e: e.dma_start(out=C.vec[:, :], in_=C.vec_d[:, :]), w=[('c_vec',)], dma=True)
        S.op('sp', lambda e: e.dma_start(out=C.vec64[:, :], in_=C.vec64_d[:, :]), w=[('c_vec64',)], dma=True)
        S.op('sp', lambda e: e.dma_start(out=C.ident[:, :], in_=C.ident_d[:, :]), w=[('c_ident',)], dma=True)
        S.op('sp', lambda e: e.dma_start(out=C.masks[:, :], in_=C.masks_d[:, :]), w=[('c_masks',)], dma=True)
        S.op('sp', lambda e: e.dma_start(out=C.tri[:, :], in_=C.tri_d[:, :]), w=[('c_tri',)], dma=True)
        S.op('pool', lambda e: e.memset(C.ones_bf[:, :], 1.0), w=[('c_ones',)])
        S.op('pool', lambda e: e.memset(C.ones_f[:, :], 1.0), w=[('c_onesf',)])
        S.op('pool', lambda e: e.memset(C.eps_col[:, 0:1], 1e-6), w=[('c_eps',)])
        S.op('pool', lambda e: e.memset(C.eps_col[:, 1:2], 64e-5), w=[('c_eps2',)])
        S.barrier()
        stage_ingest(C)
        for sname in stages:
            if sname.startswith('ffn'):
                l, j = int(sname[3]), int(sname[4])
                stage_ffn(C, C.ffn_w_in[l, j], C.ffn_w_out[l, j], cols['norm_g_%d_%d' % (l, 0 if j == 0 else 2)][0],
                          "f%d%d_" % (l, j))
            elif sname == 'rwkv':
                stage_rwkv(C)
            elif sname == 'dsa':
                stage_dsa(C)
        stage_egress(C)
    return nc


def host_consts(inputs):
    cols, nv = _vec_layout()
    cols64, nv64 = _vec64_layout()
    vec = np.zeros((128, nv), np.float32)
    vec64 = np.zeros((64, nv64), np.float32)

    def put(name, v):
        c0, n = cols[name]
        vec[:, c0:c0 + n] = np.asarray(v, np.float32).reshape(n, 128).T

    def put64(name, v):
        c0, n = cols64[name]
        vec64[:, c0:c0 + n] = np.asarray(v, np.float32).reshape(n, 64).T

    ng = np.asarray(inputs['norm_g'])
    for l in range(2):
        for j in range(3):
            put('norm_g_%d_%d' % (l, j), ng[l, j])
    put('rk_mix', np.asarray(inputs['rk_mix'])[0].reshape(-1))
    put64('k_k', inputs['rk_k_k'][0])
    put64('k_a', inputs['rk_k_a'][0])
    put64('a0', inputs['rk_a0'][0])
    put64('r_k', np.asarray(inputs['rk_r_k'])[0].reshape(-1))
    vec64[:, cols64['q_g'][0]] = np.asarray(inputs['at_q_g'], np.float32)[0]
    vec64[:, cols64['k_g'][0]] = np.asarray(inputs['at_k_g'], np.float32)[0]
    inv = (np.float32(500000.0) ** (-np.arange(0, 16, 2, dtype=np.float32) / np.float32(16))).astype(np.float32)
    ang = (np.arange(TP, dtype=np.float32)[:, None] * inv[None, :]).astype(np.float32)
    rope = np.zeros((64, 2, TP), np.float32)
    rope[:, 0, :] = 1.0
    rope[0:8, 0, :] = np.cos(ang).T
    rope[8:16, 0, :] = np.cos(ang).T
    rope[0:8, 1, :] = np.sin(ang).T
    rope[8:16, 1, :] = np.sin(ang).T
    rot = np.zeros((64, 64), np.float32)
    for d in range(8):
        rot[d + 8, d] = -1.0
        rot[d, d + 8] = 1.0
    i128 = np.arange(128)
    negmask = np.where(i128[None, :] <= i128[:, None], 0.0, -1.0e30).astype(np.float32)
    ii = np.arange(64)
    su = (ii[:, None] < ii[None, :]).astype(np.float32)
    iu = (ii[:, None] <= ii[None, :]).astype(np.float32)
    sl = (ii[:, None] > ii[None, :]).astype(np.float32)
    masks = np.concatenate([su, iu, sl], axis=1)
    cdec = np.float32(-np.exp(-0.5))
    tri = np.concatenate([iu, su], axis=1) * cdec
    out = {"vecs": vec, "vecs64": vec64, "ident": np.eye(128, dtype=np.float32), "masks": masks,
           "tri": tri.astype(np.float32), "rope": rope, "rot": rot, "negmask": negmask,
           "at_w_in": np.ascontiguousarray(np.asarray(inputs['at_w_in'], np.float32)[0]),
           "at_w_o": np.ascontiguousarray(np.asarray(inputs['at_w_o'], np.float32)[0])}
    for k in RK_SHAPES:
        out["rk_" + k] = np.ascontiguousarray(np.asarray(inputs["rk_" + k], np.float32)[0].reshape(RK_SHAPES[k]))
    return out


ALL_STAGES = ['ffn00', 'rwkv', 'ffn01', 'ffn10', 'dsa', 'ffn11']
_cache = {}


def run(inputs, stages, cores=NCORES, trace=False):
    key = tuple(stages)
    if key not in _cache:
        _cache[key] = build_program(stages)
    nc = _cache[key]
    consts = host_consts(inputs)
    x = np.asarray(inputs['x'], np.float32)
    shared = {
        "meta": np.ascontiguousarray(np.asarray(inputs['meta'], np.float32)),
        "ffn_w_in": np.ascontiguousarray(np.asarray(inputs['ffn_w_in'], np.float32)),
        "ffn_w_out": np.ascontiguousarray(np.asarray(inputs['ffn_w_out'], np.float32)),
    }
    shared.update(consts)
    in_maps = []
    for b in range(cores):
        m = dict(shared)
        m["x"] = np.ascontiguousarray(x[b])
        in_maps.append(m)
    res = run_bass_kernel_spmd(nc, in_maps, core_ids=list(range(cores)), trace=trace)
    out = np.stack([np.asarray(r["out"], np.float32) for r in res.results], axis=0)
    return out, res


def kernel(**inputs):
    out, _ = run(inputs, ALL_STAGES)
    return out
```

```python
import numpy as np
from contextlib import ExitStack
import concourse.bass as bass
import concourse.mybir as mybir
from concourse.bass_utils import run_bass_kernel_spmd

F32 = mybir.dt.float32
BF16 = mybir.dt.bfloat16
F32R = mybir.dt.float32r
USE_F32R = False
IDT = BF16
TRDT = F32
ALU = mybir.AluOpType
AF = mybir.ActivationFunctionType
AX = mybir.AxisListType

D = 1024
NMETA = 16
SEQ = 4096
T = SEQ + NMETA
TP = 4160
DFF = 2816
NCORES = 8
DEBUG_NQT = 0
MERGE_FRAC = 0.03
NOVBF = False
DEBUG_RK = None


class Sched:
    ENGS = ['pe', 'act', 'dve', 'pool', 'sp']

    def __init__(self, nc, es, nds=16):
        self.nc = nc
        self.sem = {e: es.enter_context(nc.semaphore("s_" + e)) for e in self.ENGS}
        self.cnt = {e: 0 for e in self.ENGS}
        self.NDS = nds
        self.dq = {}
        self.dsem = []
        self.dcnt = []
        self.dtok = []
        for q in ('sp', 'pool', 'act'):
            base = len(self.dsem)
            for i in range(nds):
                self.dsem.append(es.enter_context(nc.semaphore("d_%s%d" % (q, i))))
                self.dcnt.append(0)
                self.dtok.append(None)
            self.dq[q] = [base, 0]
        self.pending = {e: [] for e in self.ENGS}
        self.last_w = {}
        self.readers = {}
        self.seen = {e: {} for e in self.ENGS}
        self.nops = 0

    def _semh(self, sk):
        return self.sem[sk[1]] if sk[0] == 'e' else self.dsem[sk[1]]

    def capture(self):
        self._cap = []
        return self._cap

    def end_capture(self):
        c = self._cap
        self._cap = None
        return c

    def replay_merged(self, a, b, frac=1.0):
        na, nb = len(a), len(b)
        nbe = max(1, int(nb * frac))
        ia = ib = 0
        while ia < na or ib < nb:
            if ib >= nb or (ia < na and ia * nbe <= ib * na):
                self.op(*a[ia])
                ia += 1
            else:
                self.op(*b[ib])
                ib += 1

    def op(self, eng, fn, r=(), w=(), dma=False):
        if getattr(self, '_cap', None) is not None:
            self._cap.append((eng, fn, tuple(r), tuple(w), dma))
            return None
        pr = [k for k in r if k[0] in PSKEYS and k not in w]
        if pr:
            w = list(w) + pr
        deps = []
        for k in r:
            if k in self.last_w:
                deps.append(self.last_w[k])
        for k in w:
            if k in self.last_w:
                deps.append(self.last_w[k])
            rd = self.readers.get(k)
            if rd:
                deps.extend(rd.items())
        if dma:
            qd = self.dq[eng]
            i = qd[0] + qd[1] % self.NDS
            qd[1] += 1
            if self.dtok[i] is not None:
                deps.append(self.dtok[i])
            self.dcnt[i] += 16
            tok = (('d', i), self.dcnt[i])
            self.dtok[i] = tok
        else:
            self.cnt[eng] += 1
            tok = (('e', eng), self.cnt[eng])
        waits = {}
        seen = self.seen[eng]
        for (sk, v) in deps:
            if eng == 'pe' and sk == ('e', 'pe'):
                continue
            if seen.get(sk, 0) >= v:
                continue
            if waits.get(sk, 0) < v:
                waits[sk] = v
        for sk, v in waits.items():
            seen[sk] = v
        self.pending[eng].append((fn, list(waits.items()), tok))
        self.nops += 1
        for k in w:
            self.last_w[k] = tok
            self.readers[k] = {}
        ws = set(w)
        for k in r:
            if k in ws:
                continue
            rd = self.readers.setdefault(k, {})
            if rd.get(tok[0], 0) < tok[1]:
                rd[tok[0]] = tok[1]
        return tok

    def barrier(self):
        allt = [(('e', e), self.cnt[e]) for e in self.ENGS if self.cnt[e] > 0]
        allt += [t for t in self.dtok if t is not None]
        for e in self.ENGS:
            waits = {}
            seen = self.seen[e]
            for sk, v in allt:
                if seen.get(sk, 0) >= v:
                    continue
                waits[sk] = max(waits.get(sk, 0), v)
            for sk, v in waits.items():
                seen[sk] = v
            self.pending[e].append((None, list(waits.items()), None))
        self.last_w = {}
        self.readers = {}

    def emit(self):
        nc = self.nc

        def mk(e):
            def body(engh):
                for fn, waits, tok in self.pending[e]:
                    for sk, v in waits:
                        engh.wait_ge(self._semh(sk), v)
                    if fn is None:
                        continue
                    ins = fn(engh)
                    sk, v = tok
                    if sk[0] == 'e':
                        ins.then_inc(self.sem[e], 1)
                    else:
                        ins.then_inc(self.dsem[sk[1]], 16)
            return body

        with nc.Block() as blk:
            blk.tensor(mk('pe'))
            blk.scalar(mk('act'))
            blk.vector(mk('dve'))
            blk.gpsimd(mk('pool'))
            blk.sync(mk('sp'))
        self.pending = {e: [] for e in self.ENGS}


PSKEYS = {'ps', 'psb', 'pA', 'pB', 'pO', 'psS'}


def keys(name, *idx_ranges):
    out = [(name,)]
    for r in idx_ranges:
        out = [o + (i,) for o in out for i in r]
    return out


class Ctx:
    pass


def stage_ingest(C):
    nc, S = C.nc, C.S
    with ExitStack() as st:
        xt = [st.enter_context(nc.sbuf_tensor("in_xt%d" % i, [128, 4, D], F32)) for i in range(2)]
        hx = [st.enter_context(nc.sbuf_tensor("in_hx%d" % i, [128, 8, 512], F32)) for i in range(2)]
        ps = [st.enter_context(nc.psum_tensor("in_ps%d" % i, [128, 512], F32)) for i in range(4)]
        hTv = C.hT.rearrange("(c p) t -> p c t", p=128)
        ident = C.ident
        S.op('pool', lambda e: e.memset(hx[1][:, :, 0:64], 0.0), w=keys('hx', [1], range(8)))
        S.op('pool', lambda e: e.dma_start(out=hTv[:, :, T:TP], in_=hx[1][:, :, 0:TP - T]),
             r=keys('hx', [1], range(8)), w=[('hTpad',)], dma=True)
        S.op('sp', lambda e: e.dma_start(out=xt[1][0:NMETA, 0, :], in_=C.meta[:, :]), w=[('xt', 1)], dma=True)
        for c in range(8):
            S.op('pe', lambda e, c=c: e.transpose(ps[c % 4][:, 0:NMETA], xt[1][0:NMETA, 0, c * 128:(c + 1) * 128],
                                                  ident[0:NMETA, 0:NMETA]),
                 r=[('xt', 1)], w=[('ps', c % 4)])
            S.op('dve', lambda e, c=c: e.tensor_copy(out=hx[1][:, c, 0:NMETA], in_=ps[c % 4][:, 0:NMETA]),
                 r=[('ps', c % 4)], w=[('hx', 1, c)])
        S.op('pool', lambda e: e.dma_start(out=hTv[:, :, 0:NMETA], in_=hx[1][:, :, 0:NMETA]),
             r=keys('hx', [1], range(8)), w=[('hTmeta',)], dma=True)
        xv = C.x.rearrange("(g a p) d -> g p a d", a=4, p=128)
        for g in range(SEQ // 512):
            s = g % 2
            S.op('sp', lambda e, g=g, s=s: e.dma_start(out=xt[s][:, :, :], in_=xv[g]), w=[('xt', s)], dma=True)
            for c in range(8):
                b = c % 4
                for a in range(4):
                    S.op('pe', lambda e, a=a, c=c, b=b, s=s: e.transpose(
                        ps[b][:, a * 128:(a + 1) * 128], xt[s][:, a, c * 128:(c + 1) * 128], ident[:, :]),
                        r=[('xt', s)], w=[('ps', b)])
                eng = 'dve' if c % 2 == 0 else 'act'
                if eng == 'dve':
                    S.op('dve', lambda e, c=c, b=b, s=s: e.tensor_copy(out=hx[s][:, c, :], in_=ps[b][:, :]),
                         r=[('ps', b)], w=[('hx', s, c)])
                else:
                    S.op('act', lambda e, c=c, b=b, s=s: e.copy(out=hx[s][:, c, :], in_=ps[b][:, :]),
                         r=[('ps', b)], w=[('hx', s, c)])
            t0 = NMETA + g * 512
            S.op('pool', lambda e, s=s, t0=t0: e.dma_start(out=hTv[:, :, t0:t0 + 512], in_=hx[s][:, :, :]),
                 r=keys('hx', [s], range(8)), w=[('hTin', g)], dma=True)
        S.barrier()
        S.emit()


def stage_egress(C):
    nc, S = C.nc, C.S
    with ExitStack() as st:
        xt = [st.enter_context(nc.sbuf_tensor("eg_xt%d" % i, [128, 4, D], F32)) for i in range(2)]
        hx = [st.enter_context(nc.sbuf_tensor("eg_hx%d" % i, [128, 8, 512], F32)) for i in range(2)]
        ps = [st.enter_context(nc.psum_tensor("eg_ps%d" % i, [128, 512], F32)) for i in range(4)]
        hTv = C.hT.rearrange("(c p) t -> p c t", p=128)
        ov = C.out.rearrange("(g a p) d -> g p a d", a=4, p=128)
        ident = C.ident
        for g in range(SEQ // 512):
            s = g % 2
            t0 = NMETA + g * 512
            S.op('sp', lambda e, s=s, t0=t0: e.dma_start(out=hx[s][:, :, :], in_=hTv[:, :, t0:t0 + 512]),
                 w=[('hx', s)], dma=True)
            for a in range(4):
                for hf in range(2):
                    b = (a * 2 + hf) % 4
                    for cc in range(4):
                        c = hf * 4 + cc
                        S.op('pe', lambda e, a=a, c=c, cc=cc, b=b, s=s: e.transpose(
                            ps[b][:, cc * 128:(cc + 1) * 128], hx[s][:, c, a * 128:(a + 1) * 128], ident[:, :]),
                            r=[('hx', s)], w=[('ps', b)])
                    if hf == 0:
                        S.op('dve', lambda e, a=a, b=b, s=s: e.tensor_copy(out=xt[s][:, a, 0:512], in_=ps[b][:, :]),
                             r=[('ps', b)], w=[('xt', s, a, 0)])
                    else:
                        S.op('act', lambda e, a=a, b=b, s=s: e.copy(out=xt[s][:, a, 512:1024], in_=ps[b][:, :]),
                             r=[('ps', b)], w=[('xt', s, a, 1)])
            S.op('pool', lambda e, g=g, s=s: e.dma_start(out=ov[g], in_=xt[s][:, :, :]),
                 r=keys('xt', [s], range(4), range(2)), w=[('out', g)], dma=True)
        S.barrier()
        S.emit()


def stage_ffn(C, w_in_d, w_out_d, gcol, tag):
    nc, S = C.nc, C.S
    TT = 256
    tiles = [(i * TT, TT) for i in range(TP // TT)]
    if TP % TT:
        tiles.append((TP - TP % TT, TP % TT))
    NJ = DFF // 128
    with ExitStack() as st:
        sb = lambda n, sh, dt: st.enter_context(nc.sbuf_tensor(tag + n, sh, dt))
        w_in = sb("w_in", [128, 8, 2 * DFF], BF16)
        w_out = sb("w_out", [128, NJ, D], BF16)
        x = [sb("x%d" % i, [128, 8, TT], F32) for i in range(2)]
        sq = [sb("sq%d" % i, [128, 8, TT], BF16) for i in range(2)]
        xn = [sb("xn%d" % i, [128, 8, TT], BF16) for i in range(2)]
        hm = [sb("hm%d" % i, [128, NJ, TT], BF16) for i in range(2)]
        sg = [sb("sg%d" % i, [128, TT], F32) for i in range(2)]
        rstd = [sb("rstd%d" % i, [128, TT], F32) for i in range(2)]
        psn = lambda n: st.enter_context(nc.psum_tensor(tag + n, [128, 512], F32))
        psA = [psn("pA%d" % i) for i in range(2)]
        psB = [psn("pB%d" % i) for i in range(2)]
        psO = [psn("pO%d" % i) for i in range(2)]
        psS = psn("pS")
        hTv = C.hT.rearrange("(c p) t -> p c t", p=128)
        w_in_v = w_in_d.rearrange("(k p) n -> p k n", p=128)
        w_out_v = w_out_d.rearrange("(j p) n -> p j n", p=128)
        ones = C.ones_bf
        vec = C.vec

        NB = 4
        cw = 2 * DFF // NB
        for k in range(8):
            for b in range(NB):
                S.op('pool', lambda e, k=k, b=b: e.dma_start(out=w_in[:, k, b * cw:(b + 1) * cw],
                                                             in_=w_in_v[:, k, b * cw:(b + 1) * cw]),
                     w=[('w_in', k, b)], dma=True)
        for j in range(NJ):
            S.op('pool', lambda e, j=j: e.dma_start(out=w_out[:, j, :], in_=w_out_v[:, j, :]),
                 w=[('w_out', j)], dma=True)
        win_keys = keys('w_in', range(8), range(NB))

        def load(i):
            t0, tw = tiles[i]
            s = i % 2
            S.op('sp', lambda e: e.dma_start(out=x[s][:, :, :tw], in_=hTv[:, :, t0:t0 + tw]),
                 r=[('hT', i)], w=keys('x', [s], range(8)), dma=True)

        def norm(i):
            t0, tw = tiles[i]
            s = i % 2
            S.op('act', lambda e: e.activation(out=sq[s][:, :, :tw], in_=x[s][:, :, :tw], func=AF.Square),
                 r=keys('x', [s], range(8)), w=[('sq', s)])
            for c in range(8):
                S.op('pe', lambda e, c=c: e.matmul(psS[:, :tw], lhsT=ones[:, :], rhs=sq[s][:, c, :tw],
                                                   start=(c == 0), stop=(c == 7)),
                     r=[('sq', s)], w=[('psS',)])
            S.op('act', lambda e: e.activation(out=rstd[s][:, :tw], in_=psS[:, :tw], func=AF.Sqrt,
                                               scale=1.0 / D, bias=C.eps_col[:, 0:1]),
                 r=[('psS',)], w=[('rstd', s)])
            S.op('dve', lambda e: e.reciprocal(out=rstd[s][:, :tw], in_=rstd[s][:, :tw]),
                 r=[('rstd', s)], w=[('rstd', s)])
            for c in range(8):
                S.op('dve', lambda e, c=c: e.scalar_tensor_tensor(
                    out=xn[s][:, c, :tw], in0=x[s][:, c, :tw], scalar=vec[:, gcol + c:gcol + c + 1],
                    in1=rstd[s][:, :tw], op0=ALU.mult, op1=ALU.mult),
                    r=[('x', s, c), ('rstd', s)], w=[('xn', s, c)])

        def mm_in(i):
            t0, tw = tiles[i]
            s = i % 2
            for j in range(NJ):
                q = j % 2
                for k in range(8):
                    S.op('pe', lambda e, j=j, k=k, q=q: e.matmul(
                        psA[q][:, :tw], lhsT=w_in[:, k, j * 128:(j + 1) * 128], rhs=xn[s][:, k, :tw],
                        start=(k == 0), stop=(k == 7)),
                        r=[('xn', s, k)] + (win_keys if (i == 0 and j == 0) else []), w=[('pA', q)])
                for k in range(8):
                    S.op('pe', lambda e, j=j, k=k, q=q: e.matmul(
                        psB[q][:, :tw], lhsT=w_in[:, k, DFF + j * 128:DFF + (j + 1) * 128], rhs=xn[s][:, k, :tw],
                        start=(k == 0), stop=(k == 7)),
                        r=[('xn', s, k)], w=[('pB', q)])
                S.op('act', lambda e, q=q: e.activation(out=sg[q][:, :tw], in_=psA[q][:, :tw], func=AF.Silu),
                     r=[('pA', q)], w=[('sg', q)])
                S.op('dve', lambda e, q=q, j=j: e.tensor_tensor(out=hm[s][:, j, :tw], in0=psB[q][:, :tw],
                                                                in1=sg[q][:, :tw], op=ALU.mult),
                     r=[('pB', q), ('sg', q)], w=[('hm', s, j)])

        def mm_out(i):
            t0, tw = tiles[i]
            s = i % 2
            for m in range(8):
                q = m % 2
                for j in range(NJ):
                    S.op('pe', lambda e, j=j, m=m, q=q: e.matmul(
                        psO[q][:, :tw], lhsT=w_out[:, j, m * 128:(m + 1) * 128], rhs=hm[s][:, j, :tw],
                        start=(j == 0), stop=(j == NJ - 1)),
                        r=[('hm', s, j), ('w_out', j)], w=[('pO', q)])
                S.op('dve', lambda e, m=m, q=q: e.scalar_tensor_tensor(
                    out=x[s][:, m, :tw], in0=psO[q][:, :tw], scalar=0.5, in1=x[s][:, m, :tw],
                    op0=ALU.mult, op1=ALU.add),
                    r=[('pO', q), ('x', s, m)], w=[('x', s, m)])
            S.op('pool', lambda e: e.dma_start(out=hTv[:, :, t0:t0 + tw], in_=x[s][:, :, :tw]),
                 r=keys('x', [s], range(8)), w=[('hT', i)], dma=True)

        n = len(tiles)
        load(0)
        norm(0)
        for i in range(n):
            if i + 1 < n:
                load(i + 1)
            mm_in(i)
            if i + 1 < n:
                norm(i + 1)
            mm_out(i)
        S.barrier()
        S.emit()


class H:
    def __init__(self, S):
        self.S = S

    def mm(self, out, lhsT, rhs, start, stop, r, w):
        if USE_F32R and lhsT.dtype == F32 and rhs.dtype == F32:
            lhsT = lhsT.bitcast(F32R)
            rhs = rhs.bitcast(F32R)
        self.S.op('pe', lambda e: e.matmul(out, lhsT=lhsT, rhs=rhs, start=start, stop=stop), r=r, w=w)

    def tr(self, out, in_, ident, r, w):
        self.S.op('pe', lambda e: e.transpose(out, in_, ident), r=r, w=w)

    def act(self, out, in_, func, r, w, scale=None, bias=None):
        kw = {}
        if scale is not None:
            kw['scale'] = scale
        if bias is not None:
            kw['bias'] = bias
        self.S.op('act', lambda e: e.activation(out=out, in_=in_, func=func, **kw), r=r, w=w)

    def cp(self, eng, out, in_, r, w):
        if eng == 'act':
            self.S.op('act', lambda e: e.copy(out=out, in_=in_), r=r, w=w)
        else:
            self.S.op(eng, lambda e: e.tensor_copy(out=out, in_=in_), r=r, w=w)

    def tt(self, eng, out, in0, in1, op, r, w):
        self.S.op(eng, lambda e: e.tensor_tensor(out=out, in0=in0, in1=in1, op=op), r=r, w=w)

    def ts(self, eng, out, in0, s1, s2, op0, op1, r, w):
        if s2 is None:
            self.S.op(eng, lambda e: e.tensor_scalar(out=out, in0=in0, scalar1=s1, scalar2=None, op0=op0), r=r, w=w)
        else:
            self.S.op(eng, lambda e: e.tensor_scalar(out=out, in0=in0, scalar1=s1, scalar2=s2, op0=op0, op1=op1),
                      r=r, w=w)

    def stt(self, eng, out, in0, scalar, in1, op0, op1, r, w):
        self.S.op(eng, lambda e: e.scalar_tensor_tensor(out=out, in0=in0, scalar=scalar, in1=in1, op0=op0, op1=op1),
                  r=r, w=w)

    def red(self, eng, out, in_, op, r, w):
        self.S.op(eng, lambda e: e.tensor_reduce(out=out, in_=in_, axis=AX.X, op=op), r=r, w=w)

    def dma(self, eng, out, in_, r, w):
        self.S.op(eng, lambda e: e.dma_start(out=out, in_=in_), r=r, w=w, dma=True)


def bk(*bs):
    return [('ps', b) for b in bs]


def stage_rwkv(C):
    nc, S = C.nc, C.S
    Hh = H(S)
    mm, tr, act, cp, tt, ts, stt, red, dma = Hh.mm, Hh.tr, Hh.act, Hh.cp, Hh.tt, Hh.ts, Hh.stt, Hh.red, Hh.dma
    CH = 64
    NT = TP // CH
    NH = 16
    cols = C.cols
    c64 = C.cols64
    with ExitStack() as st:
        sb = lambda n, sh, dt=F32: st.enter_context(nc.sbuf_tensor("rs_" + n, sh, dt))
        Wr = sb("Wr", [128, 8, D], BF16)
        Wk = sb("Wk", [128, 8, D], BF16)
        Wv = sb("Wv", [128, 8, D], BF16)
        Wo = sb("Wo", [128, 8, D], BF16)
        w1 = sb("w1", [128, 8, 64], BF16)
        a1 = sb("a1", [128, 8, 64], BF16)
        g1 = sb("g1", [128, 8, 160], BF16)
        a2 = sb("a2", [64, D], BF16)
        g2a = sb("g2a", [128, D], BF16)
        g2b = sb("g2b", [32, D], BF16)
        w2aug = sb("w2aug", [65, D], F32)
        lnxg = sb("lnxg", [64, D], F32)
        lnxb = sb("lnxb", [64, D], F32)
        omk = sb("omk", [64, NH], F32)
        PS = st.enter_context(nc.psum_tensor("rk_PS", [128, 3584], F32))
        PSb = st.enter_context(nc.psum_tensor("rk_PSb", [128, 1024], BF16))
        hbuf = sb("hbuf", [128, 8, CH])
        sq = sb("sq", [128, 8, CH], BF16)
        rstd = sb("rstd", [128, CH])
        hn = sb("hn", [128, 8, CH + 1])
        xx = sb("xx", [128, 8, CH])
        xmf = [sb("xmf%d" % i, [128, 8, CH]) for i in range(1)]
        xm = [sb("xm%d" % i, [128, 8, CH], BF16) for i in range(6)]
        r_ = sb("r", [64, NH, CH])
        k_ = sb("k", [64, NH, CH])
        a_ = sb("a", [64, NH, CH])
        kk = sb("kk", [64, NH, CH])
        b_ = sb("b", [64, NH, CH])
        tmp1 = sb("tmp1", [64, NH, CH])
        tmp2 = sb("tmp2", [64, NH, CH])
        G = sb("G", [64, NH, CH])
        Ghat = sb("Ghat", [64, NH, CH])
        cumC = sb("cumC", [64, NH])
        AR = sb("AR", [64, NH, 2 * CH], BF16)
        Bt = sb("Bt", [64, NH, CH], BF16)
        Kt = sb("Kt", [64, NH, CH], BF16)
        Bh = sb("Bh", [64, NH, CH], TRDT)
        Kh = sb("Kh", [64, NH, CH], TRDT)
        v_tm = sb("v_tm", [64, D])
        g_tm = sb("g_tm", [64, D])
        lw_tm = sb("lw_tm", [64, D])
        twT = sb("twT", [65, CH])
        taT = sb("taT", [64, CH], BF16)
        sg0 = sb("sg0", [128, CH], BF16)
        sg1 = sb("sg1", [32, CH], BF16)
        bon = sb("bon", [64, NH])
        MX = sb("MX", [64, NH, 2 * CH], BF16)
        GG = sb("GG", [64, NH, 2 * CH])
        RKT = sb("RKT", [64, NH, CH], BF16)
        LakT = sb("LakT", [64, NH, CH], BF16)
        Hst = sb("Hst", [64, NH, CH])
        st1 = sb("st1", [64, NH])
        st2 = sb("st2", [64, NH])
        zT = sb("zT", [128, 8, CH], BF16)

        yc, ysq = r_, a_
        Gs = sb("Gs", [64, NH, CH], BF16)
        Us = sb("Us", [64, NH, CH], BF16)
        BhT = sb("BhT", [64, NH, CH], BF16)
        KhT = sb("KhT", [64, NH, CH], BF16)
        Lm = sb("Lm", [64, NH, CH], BF16)
        RBT = sb("RBT", [64, NH, CH], BF16)
        v_bf = sb("v_bf", [64, D], BF16)
        Hb = sb("Hb", [64, NH, CH], BF16)
        ident_bf = sb("ident_bf", [64, 64], BF16)
        Ginv = GG[:, :, 0:CH]
        Gex = GG[:, :, CH:2 * CH]

        hTv = C.hT.rearrange("(c p) t -> p c t", p=128)
        ident = C.ident
        vec = C.vec
        v64 = C.vec64

        for nm, dst, src in (("Wr", Wr, C.rk['w_r']), ("Wk", Wk, C.rk['w_k']), ("Wv", Wv, C.rk['w_v']),
                             ("Wo", Wo, C.rk['w_o'])):
            v = src.rearrange("(k p) n -> p k n", p=128)
            for k in range(8):
                dma('pool', dst[:, k, :], v[:, k, :], r=[], w=[(nm,)])
        dma('pool', w1[:, :, :], C.rk['w1'].rearrange("(k p) n -> p k n", p=128), r=[], w=[('w1',)])
        dma('pool', a1[:, :, :], C.rk['a1'].rearrange("(k p) n -> p k n", p=128), r=[], w=[('a1',)])
        dma('pool', g1[:, :, :], C.rk['g1'].rearrange("(k p) n -> p k n", p=128), r=[], w=[('g1',)])
        dma('pool', a2[:, :], C.rk['a2'][:, :], r=[], w=[('a2',)])
        dma('pool', g2a[:, :], C.rk['g2'][0:128, :], r=[], w=[('g2',)])
        dma('pool', g2b[:, :], C.rk['g2'][128:160, :], r=[], w=[('g2',)])
        dma('sp', w2aug[0:64, :], C.rk['w2'][:, :], r=[], w=[('w2aug',)])
        dma('sp', w2aug[64:65, :], C.rk['w0'][0:1, :], r=[], w=[('w2aug',)])
        dma('sp', lnxg[:, :], C.rk['lnx_g'][0:1, :].partition_broadcast(64), r=[], w=[('lnxg',)])
        dma('sp', lnxb[:, :], C.rk['lnx_b'][0:1, :].partition_broadcast(64), r=[], w=[('lnxb',)])
        kka = c64['k_a'][0]
        ts('dve', omk[:, :], v64[0:64, kka:kka + NH], -1.0, 1.0, ALU.mult, ALU.add, r=[('c_vec64',)], w=[('omk',)])
        S.op('pool', lambda e: e.memset(Hst[:, :, :], 0.0), w=[('Hst',)])
        S.op('pool', lambda e: e.memset(Hb[:, :, :], 0.0), w=[('Hb',)])
        cp('dve', ident_bf[:, :], ident[0:64, 0:64], r=[('c_ident',)], w=[('ident_bf',)])
        S.op('pool', lambda e: e.memset(hn[:, :, 0:1], 0.0), w=[('hn0',)])
        S.op('pool', lambda e: e.memset(twT[64:65, :], 1.0), w=[('twT1',)])

        def bc(ap2, n=CH):
            return ap2.unsqueeze(2).to_broadcast([64, NH, n])

        def prm(name):
            c0 = c64[name][0]
            return bc(v64[0:64, c0:c0 + NH])

        gcol = cols['norm_g_0_1'][0]
        mixc = cols['rk_mix'][0]
        psv2 = lambda b0: PS[0:64, b0 * 512:b0 * 512 + 2048].rearrange("p (h two s) -> p h two s", h=NH, two=2)
        psv1 = lambda b0: PS[0:64, b0 * 512:b0 * 512 + 1024].rearrange("p (h s) -> p h s", h=NH)
        SU = C.masks[0:64, 0:64].unsqueeze(1).to_broadcast([64, NH, CH])
        IU = C.masks[0:64, 64:128].unsqueeze(1).to_broadcast([64, NH, CH])
        SL = C.masks[0:64, 128:192].unsqueeze(1).to_broadcast([64, NH, CH])
        IDb = ident[0:64, 0:64].unsqueeze(1).to_broadcast([64, NH, CH])
        ones64 = C.ones_f[0:64, 0:64]

        for t in range(NT if not DEBUG_RK else DEBUG_RK[0]):
            t0 = t * CH
            PH = DEBUG_RK[1] if DEBUG_RK else 99
            dma('sp', hbuf[:, :, :], hTv[:, :, t0:t0 + CH], r=[('hT', t)], w=[('hbuf',)])
            act(sq[:, :, :], hbuf[:, :, :], AF.Square, r=[('hbuf',)], w=[('sq',)])
            for c in range(8):
                mm(PS[:, 0:CH], ones_bf(C)[:, :], sq[:, c, :], c == 0, c == 7, r=[('sq',)], w=bk(0))
            act(rstd[:, :], PS[:, 0:CH], AF.Sqrt, r=bk(0), w=[('rstd',)], scale=1.0 / D, bias=C.eps_col[:, 0:1])
            S.op('dve', lambda e: e.reciprocal(out=rstd[:, :], in_=rstd[:, :]), r=[('rstd',)], w=[('rstd',)])
            for c in range(8):
                stt('dve', hn[:, c, 1:CH + 1], hbuf[:, c, :], vec[:, gcol + c:gcol + c + 1], rstd[:, :],
                    ALU.mult, ALU.mult, r=[('hbuf',), ('rstd',), ('hn0',)], w=[('hn', c)])
            hnk = keys('hn', range(8))
            tt('pool', xx[:, :, :], hn[:, :, 0:CH], hn[:, :, 1:CH + 1], ALU.subtract, r=hnk + [('hn0',)], w=[('xx',)])
            for i in range(6):
                mixb = vec[:, mixc + i * 8:mixc + i * 8 + 8].unsqueeze(2).to_broadcast([128, 8, CH])
                tt('pool', xmf[0][:, :, :], xx[:, :, :], mixb, ALU.mult, r=[('xx',), ('c_vec',)], w=[('xmf', 0)])
                tt('dve', xm[i][:, :, :], xmf[0][:, :, :], hn[:, :, 1:CH + 1], ALU.add,
                   r=[('xmf', 0)] + hnk, w=keys('xm', [i], range(8)))
            cp('pool', hn[:, :, 0:1], hn[:, :, CH:CH + 1], r=hnk + [('xx',)], w=[('hn0',)])
            xr, xw, xk, xv, xa, xg = xm
            if PH < -2:
                continue
            for (Wt, wn, xs, xi, b0, dst, dn) in ((Wr, 'Wr', xr, 0, 0, r_, 'r'), (Wk, 'Wk', xk, 2, 2, k_, 'k')):
                for h in range(NH):
                    for kc in range(8):
                        mm(PS[0:64, b0 * 512 + h * 64:b0 * 512 + (h + 1) * 64], Wt[:, kc, h * 64:(h + 1) * 64],
                           xs[:, kc, :], kc == 0, kc == 7, r=[('xm', xi, kc), (wn,)], w=bk(b0 + h // 8))
                cp('act', dst[:, :, :], psv1(b0), r=bk(b0, b0 + 1), w=[(dn,)])
            for n in range(2):
                for kc in range(8):
                    mm(PS[0:64, (4 + n) * 512:(5 + n) * 512], xv[:, kc, :], Wv[:, kc, n * 512:(n + 1) * 512],
                       kc == 0, kc == 7, r=[('xm', 3, kc), ('Wv',)], w=bk(4 + n))
            cp('dve', v_tm[:, :], PS[0:64, 2048:3072], r=bk(4, 5), w=[('v_tm',)])
            if not NOVBF:
                cp('act', v_bf[:, :], PS[0:64, 2048:3072], r=bk(4, 5), w=[('v_bf',)])
            for kc in range(8):
                mm(PS[0:64, 3072:3072 + CH], w1[:, kc, :], xw[:, kc, :], kc == 0, kc == 7,
                   r=[('xm', 1, kc), ('w1',)], w=bk(6))
            act(twT[0:64, :], PS[0:64, 3072:3072 + CH], AF.Tanh, r=bk(6), w=[('twT',)])
            for kc in range(8):
                mm(PS[0:64, 0:CH], a1[:, kc, :], xa[:, kc, :], kc == 0, kc == 7,
                   r=[('xm', 4, kc), ('a1',)], w=bk(0))
            cp('dve', taT[:, :], PS[0:64, 0:CH], r=bk(0), w=[('taT',)])
            for kc in range(8):
                mm(PS[:, 3072:3072 + CH], g1[:, kc, 0:128], xg[:, kc, :], kc == 0, kc == 7,
                   r=[('xm', 5, kc), ('g1',)], w=bk(6))
            act(sg0[:, :], PS[:, 3072:3072 + CH], AF.Sigmoid, r=bk(6), w=[('sg0',)])
            for kc in range(8):
                mm(PS[0:32, 512:512 + CH], g1[:, kc, 128:160], xg[:, kc, :], kc == 0, kc == 7,
                   r=[('xm', 5, kc), ('g1',)], w=bk(1))
            act(sg1[:, :], PS[0:32, 512:512 + CH], AF.Sigmoid, r=bk(1), w=[('sg1',)])
            for n in range(2):
                mm(PS[0:64, n * 512:(n + 1) * 512], twT[0:65, :], w2aug[0:65, n * 512:(n + 1) * 512], True, True,
                   r=[('twT',), ('twT1',), ('w2aug',)], w=bk(n))
            act(lw_tm[:, :], PS[0:64, 0:1024], AF.Sigmoid, r=bk(0, 1), w=[('lw_tm',)])
            for h in range(NH):
                mm(PS[0:64, 1024 + h * 128:1024 + (h + 1) * 128], lw_tm[0:64, h * 64:(h + 1) * 64],
                   C.tri[0:64, 0:128], True, True, r=[('lw_tm',), ('c_tri',)], w=bk(2 + h // 4))
            pc = psv2(2)
            cb = bk(2, 3, 4, 5)
            act(G[:, :, :], pc[:, :, 0, :], AF.Exp, r=cb, w=[('G',)])
            act(Ginv[:, :, :], pc[:, :, 0, :], AF.Exp, r=cb, w=[('Ginv',)], scale=-1.0)
            act(Gex[:, :, :], pc[:, :, 1, :], AF.Exp, r=cb, w=[('Gex',)])
            cp('dve', cumC[:, :], pc[:, :, 0, CH - 1], r=cb, w=[('cumC',)])
            tt('dve', tmp1[:, :, :], bc(cumC[:, :]), pc[:, :, 0, :], ALU.subtract, r=cb + [('cumC',)], w=[('tmp1',)])
            act(Ghat[:, :, :], tmp1[:, :, :], AF.Exp, r=[('tmp1',)], w=[('Ghat',)])
            for h in range(NH):
                mm(PS[0:64, 2048 + h * 64:2048 + (h + 1) * 64], a2[0:64, h * 64:(h + 1) * 64], taT[0:64, :], True, True,
                   r=[('taT',), ('a2',)], w=bk(4 + h // 8))
            tt('dve', a_[:, :, :], psv1(4), prm('a0'), ALU.add, r=bk(4, 5) + [('c_vec64',)], w=[('a',)])
            act(a_[:, :, :], a_[:, :, :], AF.Sigmoid, r=[('a',)], w=[('a',)])
            for n in range(2):
                mm(PS[0:64, n * 512:(n + 1) * 512], sg0[:, :], g2a[:, n * 512:(n + 1) * 512], True, False,
                   r=[('sg0',), ('g2',)], w=bk(n))
                mm(PS[0:64, n * 512:(n + 1) * 512], sg1[0:32, :], g2b[0:32, n * 512:(n + 1) * 512], False, True,
                   r=[('sg1',), ('g2',)], w=bk(n))
            cp('act', g_tm[:, :], PS[0:64, 0:1024], r=bk(0, 1), w=[('g_tm',)])
            if PH < -1:
                continue
            tt('dve', kk[:, :, :], k_[:, :, :], prm('k_k'), ALU.mult, r=[('k',), ('c_vec64',)], w=[('kk',)])
            act(tmp2[:, :, :], kk[:, :, :], AF.Square, r=[('kk',)], w=[('tmp2',)])
            t2f = tmp2[:, :, :].rearrange("p h s -> p (h s)")
            for n in range(2):
                mm(PS[0:64, 1024 + n * 512:1024 + (n + 1) * 512], ones64, t2f[:, n * 512:(n + 1) * 512], True, True,
                   r=[('tmp2',), ('c_onesf',)], w=bk(2 + n))
            act(tmp2[:, :, :], psv1(2), AF.Sqrt, r=bk(2, 3), w=[('tmp2',)])
            ts('dve', tmp2[:, :, :], tmp2[:, :, :], 1e-12, None, ALU.max, None, r=[('tmp2',)], w=[('tmp2',)])
            S.op('dve', lambda e: e.reciprocal(out=tmp2[:, :, :], in_=tmp2[:, :, :]), r=[('tmp2',)], w=[('tmp2',)])
            tt('dve', kk[:, :, :], kk[:, :, :], tmp2[:, :, :], ALU.mult, r=[('kk',), ('tmp2',)], w=[('kk',)])
            tt('pool', tmp1[:, :, :], a_[:, :, :], prm('k_a'), ALU.mult, r=[('a',), ('c_vec64',)], w=[('tmp1',)])
            tt('pool', tmp1[:, :, :], tmp1[:, :, :], bc(omk[:, :]), ALU.add, r=[('tmp1',), ('omk',)], w=[('tmp1',)])
            tt('pool', k_[:, :, :], k_[:, :, :], tmp1[:, :, :], ALU.mult, r=[('k',), ('tmp1',)], w=[('k',)])
            tt('dve', b_[:, :, :], kk[:, :, :], a_[:, :, :], ALU.mult, r=[('kk',), ('a',)], w=[('b',)])
            stt('dve', AR[:, :, 0:CH], kk[:, :, :], -1.0, Gex[:, :, :], ALU.mult, ALU.mult,
                r=[('kk',), ('Gex',)], w=[('AR0',)])
            tt('pool', AR[:, :, CH:2 * CH], r_[:, :, :], G[:, :, :], ALU.mult, r=[('r',), ('G',)], w=[('AR1',)])
            tt('dve', Bt[:, :, :], b_[:, :, :], Ginv[:, :, :], ALU.mult, r=[('b',), ('Ginv',)], w=[('Bt',)])
            tt('pool', Kt[:, :, :], k_[:, :, :], Ginv[:, :, :], ALU.mult, r=[('k',), ('Ginv',)], w=[('Kt',)])
            tt('dve', Bh[:, :, :], b_[:, :, :], Ghat[:, :, :], ALU.mult, r=[('b',), ('Ghat',)], w=[('Bh',)])
            tt('pool', Kh[:, :, :], k_[:, :, :], Ghat[:, :, :], ALU.mult, r=[('k',), ('Ghat',)], w=[('Kh',)])
            tt('pool', tmp1[:, :, :], r_[:, :, :], prm('r_k'), ALU.mult, r=[('r',), ('c_vec64',)], w=[('tmp1',)])
            tt('pool', tmp1[:, :, :], tmp1[:, :, :], k_[:, :, :], ALU.mult, r=[('tmp1',), ('k',)], w=[('tmp1',)])
            for h in range(NH):
                mm(PS[0:64, 3072 + h:3072 + h + 1], tmp1[:, h, :], C.ones_f[0:64, 0:1], True, True,
                   r=[('tmp1',), ('c_onesf',)], w=bk(6))
            cp('dve', bon[:, :], PS[0:64, 3072:3072 + NH], r=bk(6), w=[('bon',)])
            if PH < 1:
                continue
            GR = [(0, 8), (8, 8)]

            def hv(ap, g):
                return ap[:, GR[g][0]:GR[g][0] + 8, :]

            def pg2(b0):
                return PS[0:64, b0 * 512:b0 * 512 + 1024].rearrange("p (h two s) -> p h two s", h=8, two=2)

            def pg1(b0):
                return PS[0:64, b0 * 512:b0 * 512 + 512].rearrange("p (h s) -> p h s", h=8)

            SU8 = C.masks[0:64, 0:64].unsqueeze(1).to_broadcast([64, 8, CH])
            IU8 = C.masks[0:64, 64:128].unsqueeze(1).to_broadcast([64, 8, CH])
            SL8 = C.masks[0:64, 128:192].unsqueeze(1).to_broadcast([64, 8, CH])
            ID8 = ident[0:64, 0:64].unsqueeze(1).to_broadcast([64, 8, CH])
            for g in range(2):
                h0 = GR[g][0]
                bA = 0 if g == 0 else 3
                for hh in range(8):
                    h = h0 + hh
                    mm(PS[0:64, bA * 512 + hh * 128:bA * 512 + (hh + 1) * 128], Bt[:, h, :], AR[:, h, :], True, True,
                       r=[('Bt',), ('AR0',), ('AR1',)], w=bk(bA + hh // 4))
                tt('dve', hv(MX[:, :, 0:CH], g), pg2(bA)[:, :, 0, :], SU8, ALU.mult, r=bk(bA, bA + 1) + [('c_masks',)],
                   w=[('MX0', g)])
                tt('dve', hv(RBT, g), pg2(bA)[:, :, 1, :], IU8, ALU.mult, r=bk(bA, bA + 1) + [('c_masks',)],
                   w=[('RBT', g)])
                for hh in range(8):
                    h = h0 + hh
                    mm(PS[0:64, bA * 512 + hh * 128:bA * 512 + (hh + 1) * 128], Kt[:, h, :], AR[:, h, :], True, True,
                       r=[('Kt',), ('AR0',), ('AR1',)], w=bk(bA + hh // 4))
                tt('dve', hv(LakT, g), pg2(bA)[:, :, 0, :], SU8, ALU.mult, r=bk(bA, bA + 1) + [('c_masks',)],
                   w=[('LakT', g)])
                tt('dve', hv(RKT, g), pg2(bA)[:, :, 1, :], IU8, ALU.mult, r=bk(bA, bA + 1) + [('c_masks',)],
                   w=[('RKT', g)])
                for hh in range(8):
                    h = h0 + hh
                    mm(PS[0:64, (bA + 2) * 512 + hh * 64:(bA + 2) * 512 + (hh + 1) * 64], AR[:, h, 0:CH], Bt[:, h, :],
                       True, True, r=[('Bt',), ('AR0',)], w=bk(bA + 2))
                tt('dve', hv(Lm, g), pg1(bA + 2), SL8, ALU.mult, r=bk(bA + 2) + [('c_masks',)], w=[('Lm', g)])
                cp('pool', hv(MX[:, :, CH:2 * CH], g), ID8, r=[('c_ident',)], w=[('MX1', g)])
            if PH < 2:
                continue
            for lvl in range(6):
                for g in range(2):
                    h0 = GR[g][0]
                    bA = 0 if g == 0 else 3
                    for hh in range(8):
                        h = h0 + hh
                        mm(PS[0:64, bA * 512 + hh * 128:bA * 512 + (hh + 1) * 128], Lm[:, h, :], MX[:, h, :], True, True,
                           r=[('Lm', g), ('MX0', g), ('MX1', g)], w=bk(bA + hh // 4))
                    if lvl < 5:
                        for hh in range(8):
                            h = h0 + hh
                            mm(PS[0:64, (bA + 2) * 512 + hh * 64:(bA + 2) * 512 + (hh + 1) * 64], MX[:, h, 0:CH], Lm[:, h, :],
                               True, True, r=[('Lm', g), ('MX0', g)], w=bk(bA + 2))
                for g in range(2):
                    bA = 0 if g == 0 else 3
                    tt('dve', hv(MX[:, :, CH:2 * CH], g), pg2(bA)[:, :, 1, :], hv(MX[:, :, CH:2 * CH], g), ALU.add,
                       r=bk(bA, bA + 1) + [('MX1', g)], w=[('MX1', g)])
                    if lvl < 5:
                        cp('act', hv(MX[:, :, 0:CH], g), pg2(bA)[:, :, 0, :], r=bk(bA, bA + 1), w=[('MX0', g)])
                        cp('act', hv(Lm, g), pg1(bA + 2), r=bk(bA + 2), w=[('Lm', g)])
            if PH < 3:
                continue
            if TRDT == BF16:
                psb3 = PSb[0:64, :].rearrange("p (h s) -> p h s", h=NH)
                for h in range(NH):
                    tr(PSb[0:64, h * 64:(h + 1) * 64], Bh[:, h, :], ident_bf[:, :], r=[('Bh',), ('ident_bf',)], w=[('psb',)])
                cp('act', BhT[:, :, :], psb3, r=[('psb',)], w=[('BhT',)])
                for h in range(NH):
                    tr(PSb[0:64, h * 64:(h + 1) * 64], Kh[:, h, :], ident_bf[:, :], r=[('Kh',), ('ident_bf',)], w=[('psb',)])
                cp('dve', KhT[:, :, :], psb3, r=[('psb',)], w=[('KhT',)])
            else:
                for h in range(NH):
                    tr(PS[0:64, h * 64:(h + 1) * 64], Bh[:, h, :], ident[0:64, 0:64], r=[('Bh',), ('c_ident',)], w=bk(h // 8))
                cp('act', BhT[:, :, :], psv1(0), r=bk(0, 1), w=[('BhT',)])
                for h in range(NH):
                    tr(PS[0:64, 1024 + h * 64:1024 + (h + 1) * 64], Kh[:, h, :], ident[0:64, 0:64],
                       r=[('Kh',), ('c_ident',)], w=bk(2 + h // 8))
                cp('dve', KhT[:, :, :], psv1(2), r=bk(2, 3), w=[('KhT',)])
            if PH < 4:
                continue
            for h in range(NH):
                o = PS[0:64, h * 64:(h + 1) * 64]
                mm(o, AR[:, h, 0:CH], Hb[:, h, :], True, False, r=[('AR0',), ('Hb',)], w=bk(h // 8))
                mm(o, LakT[:, h, :], v_bf[:, h * 64:(h + 1) * 64], False, True, r=[('LakT', h // 8), ('v_bf',)], w=bk(h // 8))
            cp('act', Gs[:, :, :], psv1(0), r=bk(0, 1), w=[('Gs',)])
            for h in range(NH):
                mm(PS[0:64, 1024 + h * 64:1024 + (h + 1) * 64], MX[:, h, CH:2 * CH], Gs[:, h, :], True, True,
                   r=[('MX1', h // 8), ('Gs',)], w=bk(2 + h // 8))
            cp('dve', Us[:, :, :], psv1(2), r=bk(2, 3), w=[('Us',)])
            for h in range(NH):
                o = PS[0:64, 2048 + h * 64:2048 + (h + 1) * 64]
                mm(o, AR[:, h, CH:2 * CH], Hb[:, h, :], True, False, r=[('AR1',), ('Hb',)], w=bk(4 + h // 8))
                mm(o, RBT[:, h, :], Us[:, h, :], False, False, r=[('RBT', h // 8), ('Us',)], w=bk(4 + h // 8))
                mm(o, RKT[:, h, :], v_bf[:, h * 64:(h + 1) * 64], False, True, r=[('RKT', h // 8), ('v_bf',)], w=bk(4 + h // 8))
            for h in range(NH):
                o = PS[0:64, h * 64:(h + 1) * 64]
                mm(o, BhT[:, h, :], Us[:, h, :], True, False, r=[('BhT',), ('Us',)], w=bk(h // 8))
                mm(o, KhT[:, h, :], v_bf[:, h * 64:(h + 1) * 64], False, True, r=[('KhT',), ('v_bf',)], w=bk(h // 8))
            tt('dve', Hst[:, :, :], Hst[:, :, :], G[:, :, CH - 1:CH].to_broadcast([64, NH, CH]), ALU.mult,
               r=[('Hst',), ('G',)], w=[('Hst',)])
            tt('dve', Hst[:, :, :], psv1(0), Hst[:, :, :], ALU.add, r=bk(0, 1) + [('Hst',)], w=[('Hst',)])
            cp('act', Hb[:, :, :], Hst[:, :, :], r=[('Hst',)], w=[('Hb',)])
            if PH < 5:
                continue
            py = psv1(4)
            yb = bk(4, 5)
            red('dve', st1[:, :], py, ALU.add, r=yb, w=[('st1',)])
            ts('dve', st1[:, :], st1[:, :], -1.0 / 64, None, ALU.mult, None, r=[('st1',)], w=[('st1',)])
            tt('dve', yc[:, :, :], py, bc(st1[:, :]), ALU.add, r=yb + [('st1',)], w=[('r',)])
            act(ysq[:, :, :], yc[:, :, :], AF.Square, r=[('r',)], w=[('a',)])
            red('dve', st2[:, :], ysq[:, :, :], ALU.add, r=[('a',)], w=[('st2',)])
            act(st2[:, :], st2[:, :], AF.Sqrt, r=[('st2',)], w=[('st2',)], scale=1.0 / 64, bias=C.eps_col[0:64, 1:2])
            S.op('dve', lambda e: e.reciprocal(out=st2[:, :], in_=st2[:, :]), r=[('st2',)], w=[('st2',)])
            tt('dve', yc[:, :, :], yc[:, :, :], bc(st2[:, :]), ALU.mult, r=[('r',), ('st2',)], w=[('r',)])
            ycf = yc[:, :, :].rearrange("p h s -> p (h s)")
            tt('pool', ycf, ycf, lnxg[:, :], ALU.mult, r=[('r',), ('lnxg',)], w=[('r',)])
            tt('pool', ycf, ycf, lnxb[:, :], ALU.add, r=[('r',), ('lnxb',)], w=[('r',)])
            vv = v_tm[:, :].rearrange("p (h s) -> p h s", h=NH)
            tt('dve', ysq[:, :, :], vv, bc(bon[:, :]), ALU.mult, r=[('v_tm',), ('bon',)], w=[('a',)])
            tt('pool', yc[:, :, :], yc[:, :, :], ysq[:, :, :], ALU.add, r=[('r',), ('a',)], w=[('r',)])
            tt('dve', ycf, ycf, g_tm[:, :], ALU.mult, r=[('r',), ('g_tm',)], w=[('r',)])
            for c in range(8):
                tr(PS[:, 3072 + c * 64:3072 + (c + 1) * 64], yc[:, 2 * c:2 * c + 2, :].rearrange("p h s -> p (h s)"),
                   ident[0:64, 0:64], r=[('r',), ('c_ident',)], w=bk(6))
            cp('act', zT[:, :, :], PS[:, 3072:3584].rearrange("p (c s) -> p c s", c=8), r=bk(6), w=[('zT',)])
            for co in range(8):
                q = 2 + (co % 2)
                for kc in range(8):
                    mm(PS[:, q * 512:q * 512 + CH], Wo[:, kc, co * 128:(co + 1) * 128], zT[:, kc, :], kc == 0, kc == 7,
                       r=[('zT',), ('Wo',)], w=bk(q))
                tt('dve', hbuf[:, co, :], PS[:, q * 512:q * 512 + CH], hbuf[:, co, :], ALU.add,
                   r=bk(q) + [('hbuf',)], w=[('hbuf',)])
            dma('pool', hTv[:, :, t0:t0 + CH], hbuf[:, :, :], r=[('hbuf',)], w=[('hT', t)])
        S.barrier()
        S.emit()


def stage_dsa(C):
    nc, S = C.nc, C.S
    Hh = H(S)
    mm, tr, act, cp, tt, ts, stt, red, dma = Hh.mm, Hh.tr, Hh.act, Hh.cp, Hh.tt, Hh.ts, Hh.stt, Hh.red, Hh.dma
    TT = 128
    NQT = (TP + TT - 1) // TT
    NQT_RUN = min(NQT, DEBUG_NQT) if DEBUG_NQT else NQT
    c64 = C.cols64
    cols = C.cols
    NBIS = 15
    MB = 240000.0
    with ExitStack() as st:
        sb = lambda n, sh, dt=F32: st.enter_context(nc.sbuf_tensor("ds_" + n, sh, dt))
        Wq = sb("Wq", [128, 8, 1024], BF16)
        Wk = sb("Wk", [128, 8, 256], BF16)
        Wv = sb("Wv", [128, 8, 256], BF16)
        Wqi = sb("Wqi", [128, 8, 512], BF16)
        Wki = sb("Wki", [128, 8, 64], BF16)
        Wwi = sb("Wwi", [128, 8, 8], BF16)
        Wo = sb("Wo", [128, 8, 1024], BF16)
        kT = sb("kT", [64, 4, TP], BF16)
        Vaug = sb("Vaug", [128, NQT, 4, 65], BF16)
        kiT = sb("kiT", [64, TP], BF16)
        score = sb("score", [128, TP])
        work = sb("work", [128, TP])
        mask01 = sb("mask01", [128, TP], BF16)
        maskT = sb("maskT", [128, NQT, TT], BF16)
        hbuf = [sb("hbuf%d" % i, [128, 8, TT]) for i in range(2)]
        hnb = sb("hnb", [128, 8, TT], BF16)
        sq = sb("sq", [128, 8, TT], BF16)
        rstd = sb("rstd", [128, TT])
        qT = [sb("qT%d" % i, [64, 16 * TT], BF16) for i in range(2)]
        qiT = sb("qiT", [64, 8 * TT], BF16)
        tA = sb("tA", [64, 512])
        tB = sb("tB", [64, 512])
        tC = sb("tC", [64, 512])
        rl = [sb("rl%d" % i, [128, 512]) for i in range(2)]
        PT = [sb("PT%d" % i, [128, 512], BF16) for i in range(3)]
        o_tm = sb("o_tm", [128, 1024], BF16)
        oT = sb("oT", [128, 8, TT], BF16)
        cs = sb("cs", [64, 2, TT])
        wi = sb("wi", [128, 8])
        m8 = sb("m8", [128, 8])
        eq8 = sb("eq8", [128, 8])
        iota8 = sb("iota8", [128, 8])
        lo = sb("lo", [128, 1])
        HC = sb("HC", [128, 2])
        MC = sb("MC", [128, 2])
        sel = sb("sel", [128, 1])
        d1 = sb("d1", [128, 1])
        d2 = sb("d2", [128, 2])
        thr = sb("thr", [128, 1])
        nm1 = sb("nm1", [128, 1])
        halfc = sb("halfc", [128, 1])
        negb = sb("negb", [128, 1])
        rden = sb("rden", [128, 16])
        ident_bf = sb("ident_bf", [128, 128], BF16)
        zeros_bf = sb("zeros_bf", [128, 512], BF16)
        rot = sb("rot", [64, 64])
        negmask = sb("negmask", [128, 128])
        PS = st.enter_context(nc.psum_tensor("ds_PS", [128, 3584], F32))
        PSb = st.enter_context(nc.psum_tensor("ds_PSb", [128, 1024], BF16))
        bank = lambda b: PS[:, b * 512:(b + 1) * 512]

        hTv = C.hT.rearrange("(c p) t -> p c t", p=128)
        ident = C.ident
        vec = C.vec
        v64 = C.vec64
        win = C.at['w_in'].rearrange("(k p) n -> p k n", p=128)
        for k in range(8):
            dma('pool', Wq[:, k, :], win[:, k, 0:1024], r=[], w=[('Wq',)])
        dma('pool', Wk[:, :, :], win[:, :, 1024:1280], r=[], w=[('Wk',)])
        dma('pool', Wv[:, :, :], win[:, :, 1280:1536], r=[], w=[('Wv',)])
        for k in range(8):
            dma('pool', Wqi[:, k, :], win[:, k, 1536:2048], r=[], w=[('Wqi',)])
        dma('pool', Wki[:, :, :], win[:, :, 2048:2112], r=[], w=[('Wki',)])
        dma('pool', Wwi[:, :, :], win[:, :, 2112:2120], r=[], w=[('Wwi',)])
        wov = C.at['w_o'].rearrange("(k p) n -> p k n", p=128)
        for k in range(8):
            dma('pool', Wo[:, k, :], wov[:, k, :], r=[], w=[('Wo',)])
        dma('sp', rot[:, :], C.rot_d[:, :], r=[], w=[('rot',)])
        dma('sp', negmask[:, :], C.negmask_d[:, :], r=[], w=[('negmask',)])
        cp('dve', ident_bf[:, :], ident[:, :], r=[('c_ident',)], w=[('ident_bf',)])
        S.op('pool', lambda e: e.memset(zeros_bf[:, :], 0.0), w=[('zeros_bf',)])
        S.op('pool', lambda e: e.memset(Vaug[:, :, :, 64:65], 1.0), w=[('Vones',)])
        S.op('pool', lambda e: e.memset(halfc[:, :], 0.5), w=[('halfc',)])
        S.op('pool', lambda e: e.memset(negb[:, :], -MB), w=[('negb',)])
        for j in range(8):
            S.op('pool', lambda e, j=j: e.memset(iota8[:, j:j + 1], float(j)), w=[('iota8',)])
        gcol = cols['norm_g_1_1'][0]
        qg = v64[0:64, c64['q_g'][0]:c64['q_g'][0] + 1]
        kg = v64[0:64, c64['k_g'][0]:c64['k_g'][0] + 1]
        kwid = lambda kb: min(128, TP - kb * 128)

        def norm_rope(pb, nh, tw, gcolap, out3, okeys_w):
            n = nh * tw
            pin = bank(pb)[0:64, 0:n]
            v3 = lambda ap: ap.rearrange("p (h s) -> p h s", h=nh)
            if gcolap is not None:
                act(tA[:, 0:n], pin, AF.Square, r=bk(pb), w=[('tA',)])
                mm(bank(6)[0:64, 0:n], C.ones_f[0:64, 0:64], tA[:, 0:n], True, True, r=[('tA',), ('c_onesf',)], w=bk(6))
                act(tB[:, 0:n], bank(6)[0:64, 0:n], AF.Sqrt, r=bk(6), w=[('tB',)], scale=1.0 / 64,
                    bias=C.eps_col[0:64, 0:1])
                S.op('dve', lambda e: e.reciprocal(out=tB[:, 0:n], in_=tB[:, 0:n]), r=[('tB',)], w=[('tB',)])
                stt('dve', tC[:, 0:n], pin, gcolap, tB[:, 0:n], ALU.mult, ALU.mult, r=bk(pb) + [('tB',), ('c_vec64',)],
                    w=[('tC',)])
            else:
                cp('act', tC[:, 0:n], pin, r=bk(pb), w=[('tC',)])
            mm(bank(6)[0:64, 0:n], rot[:, :], tC[:, 0:n], True, True, r=[('tC',), ('rot',)], w=bk(6))
            cosb = cs[:, 0, 0:tw].unsqueeze(1).to_broadcast([64, nh, tw])
            sinb = cs[:, 1, 0:tw].unsqueeze(1).to_broadcast([64, nh, tw])
            tt('pool', v3(tA[:, 0:n]), v3(tC[:, 0:n]), cosb, ALU.mult, r=[('tC',), ('cs',)], w=[('tA',)])
            tt('dve', v3(tB[:, 0:n]), v3(bank(6)[0:64, 0:n]), sinb, ALU.mult, r=bk(6) + [('cs',)], w=[('tB',)])
            tt('pool', out3, v3(tA[:, 0:n]), v3(tB[:, 0:n]), ALU.add, r=[('tA',), ('tB',)], w=okeys_w)

        def X(qt):
            s = qt % 2
            t0 = qt * TT
            tw = min(TT, TP - t0)
            n = t0 + tw
            nkb = qt + 1
            hb = hbuf[s]
            dma('sp', hb[:, :, 0:tw], hTv[:, :, t0:t0 + tw], r=[('hT', qt)], w=[('hbuf', s)])
            dma('sp', cs[:, :, 0:tw], C.rope_d[:, :, t0:t0 + tw], r=[], w=[('cs',)])
            act(sq[:, :, 0:tw], hb[:, :, 0:tw], AF.Square, r=[('hbuf', s)], w=[('sq',)])
            for c in range(8):
                mm(bank(6)[:, 0:tw], C.ones_bf[:, :], sq[:, c, 0:tw], c == 0, c == 7, r=[('sq',)], w=bk(6))
            act(rstd[:, 0:tw], bank(6)[:, 0:tw], AF.Sqrt, r=bk(6), w=[('rstd',)], scale=1.0 / D, bias=C.eps_col[:, 0:1])
            S.op('dve', lambda e: e.reciprocal(out=rstd[:, 0:tw], in_=rstd[:, 0:tw]), r=[('rstd',)], w=[('rstd',)])
            for c in range(8):
                stt('dve', hnb[:, c, 0:tw], hb[:, c, 0:tw], vec[:, gcol + c:gcol + c + 1], rstd[:, 0:tw],
                    ALU.mult, ALU.mult, r=[('hbuf', s), ('rstd',)], w=[('hnb',)])
            for g in range(4):
                for kc in range(8):
                    mm(bank(5)[0:64, g * tw:(g + 1) * tw], Wk[:, kc, g * 64:(g + 1) * 64], hnb[:, kc, 0:tw], kc == 0, kc == 7,
                       r=[('hnb',), ('Wk',)], w=bk(5))
            norm_rope(5, 4, tw, kg, kT[:, :, t0:t0 + tw], [('kT',)])
            for kc in range(8):
                mm(bank(5)[0:tw, 0:256], hnb[:, kc, 0:tw], Wv[:, kc, :], kc == 0, kc == 7, r=[('hnb',), ('Wv',)], w=bk(5))
            cp('act', Vaug[0:tw, qt, :, 0:64], bank(5)[0:tw, 0:256].rearrange("p (g d) -> p g d", g=4), r=bk(5),
               w=[('Vaug',)])
            for kc in range(8):
                mm(bank(5)[0:64, 0:tw], Wki[:, kc, :], hnb[:, kc, 0:tw], kc == 0, kc == 7, r=[('hnb',), ('Wki',)], w=bk(5))
            norm_rope(5, 1, tw, None, kiT[:, t0:t0 + tw].unsqueeze(1), [('kiT',)])
            for grp in range(4):
                pb = 5
                for hh in range(4):
                    h = grp * 4 + hh
                    for kc in range(8):
                        mm(bank(pb)[0:64, hh * tw:(hh + 1) * tw], Wq[:, kc, h * 64:(h + 1) * 64], hnb[:, kc, 0:tw],
                           kc == 0, kc == 7, r=[('hnb',), ('Wq',)], w=bk(pb))
                norm_rope(pb, 4, tw, qg, qT[s][:, grp * 4 * tw:(grp + 1) * 4 * tw].rearrange("p (h s) -> p h s", h=4),
                          [('qT', s)])
            for grp in range(2):
                pb = 5
                for hh in range(4):
                    h = grp * 4 + hh
                    for kc in range(8):
                        mm(bank(pb)[0:64, hh * tw:(hh + 1) * tw], Wqi[:, kc, h * 64:(h + 1) * 64], hnb[:, kc, 0:tw],
                           kc == 0, kc == 7, r=[('hnb',), ('Wqi',)], w=bk(pb))
                norm_rope(pb, 4, tw, None, qiT[:, grp * 4 * tw:(grp + 1) * 4 * tw].rearrange("p (h s) -> p h s", h=4),
                          [('qiT',)])
            for kc in range(8):
                mm(bank(6)[0:tw, 0:8], hnb[:, kc, 0:tw], Wwi[:, kc, :], kc == 0, kc == 7, r=[('hnb',), ('Wwi',)], w=bk(6))
            ts('dve', wi[0:tw, :], bank(6)[0:tw, 0:8], float(512.0 ** -0.5), None, ALU.mult, None, r=bk(6), w=[('wi',)])
            idx = 0
            for k0 in range(0, n, 512):
                nk = min(512, n - k0)
                for h in range(8):
                    pb = 5 + idx % 2
                    rb = rl[idx % 2]
                    rk = ('rl', idx % 2)
                    idx += 1
                    mm(bank(pb)[0:tw, 0:nk], qiT[:, h * tw:(h + 1) * tw], kiT[:, k0:k0 + nk], True, True,
                       r=[('qiT',), ('kiT',)], w=bk(pb))
                    act(rb[0:tw, 0:nk], bank(pb)[0:tw, 0:nk], AF.Relu, r=bk(pb), w=[rk])
                    if h == 0:
                        ts('dve', score[0:tw, k0:k0 + nk], rb[0:tw, 0:nk], wi[0:tw, 0:1], None, ALU.mult, None,
                           r=[rk, ('wi',)], w=[('score',)])
                    else:
                        stt('dve', score[0:tw, k0:k0 + nk], rb[0:tw, 0:nk], wi[0:tw, h:h + 1], score[0:tw, k0:k0 + nk],
                            ALU.mult, ALU.add, r=[rk, ('wi',), ('score',)], w=[('score',)])
            sc = score[0:tw, 0:n]
            if n > 256:
                red('dve', HC[0:tw, 0:1], sc, ALU.max, r=[('score',)], w=[('HC',)])
                red('dve', lo[0:tw, :], sc, ALU.min, r=[('score',)], w=[('lo',)])
            tt('dve', score[0:tw, t0:t0 + tw], score[0:tw, t0:t0 + tw], negmask[0:tw, 0:tw], ALU.add,
               r=[('score',), ('negmask',)], w=[('score',)])
            if n > 256:
                tt('dve', d1[0:tw, :], HC[0:tw, 0:1], lo[0:tw, :], ALU.subtract, r=[('HC',), ('lo',)], w=[('d1',)])
                stt('dve', HC[0:tw, 0:1], d1[0:tw, :], 1.0e-6, HC[0:tw, 0:1], ALU.mult, ALU.add, r=[('d1',), ('HC',)],
                    w=[('HC',)])
                ts('dve', HC[0:tw, 1:2], d1[0:tw, :], 0.0, None, ALU.mult, None, r=[('d1',), ('HC',)], w=[('HC',)])
                for it in range(NBIS):
                    stt('dve', MC[0:tw, 0:1], lo[0:tw, :], HC[0:tw, 0:1], halfc[0:tw, :], ALU.add, ALU.mult,
                        r=[('lo',), ('HC',), ('halfc',)], w=[('MC',)])
                    S.op('dve', lambda e, tw=tw, n=n: e.tensor_scalar(
                        out=mask01[0:tw, 0:n], in0=score[0:tw, 0:n], scalar1=MC[0:tw, 0:1], scalar2=0.0,
                        op0=ALU.is_ge, op1=ALU.add, accum_out=MC[0:tw, 1:2]),
                        r=[('score',), ('MC',)], w=[('mask01',), ('MC',)])
                    ts('dve', sel[0:tw, :], MC[0:tw, 1:2], 256.0, None, ALU.is_ge, None, r=[('MC',)], w=[('sel',)])
                    tt('dve', d1[0:tw, :], MC[0:tw, 0:1], lo[0:tw, :], ALU.subtract, r=[('MC',), ('lo',)], w=[('d1',)])
                    stt('dve', lo[0:tw, :], d1[0:tw, :], sel[0:tw, 0:1], lo[0:tw, :], ALU.mult, ALU.add,
                        r=[('d1',), ('sel',), ('lo',)], w=[('lo',)])
                    tt('dve', d2[0:tw, :], HC[0:tw, :], MC[0:tw, :], ALU.subtract, r=[('MC',), ('HC',)], w=[('d2',)])
                    stt('dve', HC[0:tw, :], d2[0:tw, :], sel[0:tw, 0:1], MC[0:tw, :], ALU.mult, ALU.add,
                        r=[('d2',), ('sel',), ('MC',)], w=[('HC',)])
                wk = work[0:tw, 0:n]
                ts('dve', wk, sc, HC[0:tw, 0:1], 1.0e20, ALU.is_ge, ALU.mult, r=[('score',), ('HC',)], w=[('work',)])
                tt('dve', wk, sc, wk, ALU.subtract, r=[('score',), ('work',)], w=[('work',)])
                S.op('dve', lambda e, tw=tw, n=n: e.max(out=m8[0:tw, :], in_=work[0:tw, 0:n]), r=[('work',)], w=[('m8',)])
                ts('dve', nm1[0:tw, :], HC[0:tw, 1:2], -1.0, 255.0, ALU.mult, ALU.add, r=[('HC',)], w=[('nm1',)])
                ts('dve', nm1[0:tw, :], nm1[0:tw, :], 7.0, 0.0, ALU.min, ALU.max, r=[('nm1',)], w=[('nm1',)])
                ts('dve', eq8[0:tw, :], iota8[0:tw, :], nm1[0:tw, 0:1], None, ALU.is_equal, None, r=[('nm1',), ('iota8',)],
                   w=[('eq8',)])
                tt('dve', eq8[0:tw, :], eq8[0:tw, :], m8[0:tw, :], ALU.mult, r=[('eq8',), ('m8',)], w=[('eq8',)])
                red('dve', thr[0:tw, :], eq8[0:tw, :], ALU.add, r=[('eq8',)], w=[('thr',)])
                ts('dve', mask01[0:tw, 0:n], sc, thr[0:tw, 0:1], None, ALU.is_ge, None, r=[('score',), ('thr',)],
                   w=[('mask01',)])
            else:
                ts('dve', mask01[0:tw, 0:n], sc, -1.0e29, None, ALU.is_ge, None, r=[('score',)], w=[('mask01',)])
        def X2(qt):
            t0 = qt * TT
            tw = min(TT, TP - t0)
            nkb = qt + 1
            for kb0 in range(0, nkb, 8):
                nb = min(8, nkb - kb0)
                for j in range(nb):
                    kb = kb0 + j
                    kw = kwid(kb)
                    tr(PSb[0:kw, j * 128:j * 128 + tw], mask01[0:tw, kb * 128:kb * 128 + kw], ident_bf[0:tw, 0:tw],
                       r=[('mask01',), ('ident_bf',)], w=[('psb',)])
                kwl = kwid(kb0 + nb - 1)
                nfull = nb if kwl == 128 else nb - 1
                if nfull > 0:
                    act(maskT[:, kb0:kb0 + nfull, 0:tw],
                        PSb[:, 0:nfull * 128].rearrange("p (j s) -> p j s", j=nfull)[:, :, 0:tw], AF.Identity,
                        r=[('psb',), ('negb',)], w=[('maskT', kb) for kb in range(kb0, kb0 + nfull)],
                        scale=MB, bias=negb[:, 0:1])
                if nfull < nb:
                    act(maskT[0:kwl, kb0 + nb - 1, 0:tw], PSb[0:kwl, (nb - 1) * 128:(nb - 1) * 128 + tw], AF.Identity,
                        r=[('psb',), ('negb',)], w=[('maskT', kb0 + nb - 1)], scale=MB, bias=negb[0:kwl, 0:1])

        def Y(qt):
            s = qt % 2
            t0 = qt * TT
            tw = min(TT, TP - t0)
            nkb = qt + 1
            hb = hbuf[s]
            q_ = qT[s]
            OB = [(0, 0, 7), (1, 7, 7), (2, 14, 2)]
            for (ob, h0, nh) in OB:
                S.op('pe', lambda e, ob=ob, nh=nh: e.matmul(bank(ob)[0:tw, 0:nh * 65], lhsT=zeros_bf[:, 0:tw],
                                                          rhs=zeros_bf[:, 0:nh * 65], start=True, stop=False,
                                                          skip_group_check=True),
                     r=[('zeros_bf',)], w=bk(ob))
            jobs = [(kb, g) for kb in range(nkb) for g in range(4)]

            def s_part(i):
                kb, g = jobs[i]
                kw = kwid(kb)
                pb = 3 + i % 2
                pt = PT[i % 3]
                pk = ('PT', i % 3)
                mbias = maskT[0:kw, kb, 0:tw].unsqueeze(1).to_broadcast([kw, 4, tw])
                mm(bank(pb)[0:kw, 0:4 * tw], kT[:, g, kb * 128:kb * 128 + kw], q_[:, g * 4 * tw:(g + 1) * 4 * tw],
                   True, False, r=[('kT',), ('qT', s)], w=bk(pb))
                mm(bank(pb)[0:kw, 0:4 * tw].rearrange("p (h s) -> p h s", h=4), ident_bf[0:kw, 0:kw], mbias,
                   False, True, r=[('maskT', kb), ('ident_bf',)], w=bk(pb))
                act(pt[0:kw, 0:4 * tw], bank(pb)[0:kw, 0:4 * tw], AF.Exp, r=bk(pb), w=[pk], scale=0.125)

            def v_part(i):
                kb, g = jobs[i]
                kw = kwid(kb)
                pt = PT[i % 3]
                pk = ('PT', i % 3)
                for rr in range(4):
                    h = 4 * g + rr
                    S.op('pe', lambda e, h=h, rr=rr, kw=kw, pt=pt, kb=kb, g=g, last=(kb == nkb - 1): e.matmul(
                        bank(h // 7)[0:tw, (h % 7) * 65:(h % 7 + 1) * 65], lhsT=pt[0:kw, rr * tw:(rr + 1) * tw],
                        rhs=Vaug[0:kw, kb, g, :], start=False, stop=last, skip_group_check=True),
                        r=[pk, ('Vaug',), ('Vones',)], w=bk(h // 7))

            s_part(0)
            for i in range(len(jobs)):
                if i + 1 < len(jobs):
                    s_part(i + 1)
                v_part(i)
            for (ob, h0, nh) in OB:
                o3 = bank(ob)[0:tw, 0:nh * 65].rearrange("p (h d) -> p h d", h=nh)
                S.op('dve', lambda e, o3=o3, h0=h0, nh=nh: e.reciprocal(out=rden[0:tw, h0:h0 + nh], in_=o3[:, :, 64]),
                     r=bk(ob), w=[('rden', ob)])
                tt('dve', o_tm[0:tw, h0 * 64:(h0 + nh) * 64].rearrange("p (h d) -> p h d", h=nh), o3[:, :, 0:64],
                   rden[0:tw, h0:h0 + nh].unsqueeze(2).to_broadcast([tw, nh, 64]), ALU.mult,
                   r=bk(ob) + [('rden', ob)], w=[('o_tm',)])
            for c in range(8):
                tr(PSb[:, c * 128:c * 128 + tw], o_tm[0:tw, c * 128:(c + 1) * 128], ident_bf[0:tw, 0:tw],
                   r=[('o_tm',), ('ident_bf',)], w=[('psb',)])
            cp('act', oT[:, :, 0:tw], PSb[:, :].rearrange("p (c s) -> p c s", c=8)[:, :, 0:tw],
               r=[('psb',)], w=[('oT',)])
            for co in range(8):
                pb = 3 + co % 2
                for kc in range(8):
                    mm(bank(pb)[:, 0:tw], Wo[:, kc, co * 128:(co + 1) * 128], oT[:, kc, 0:tw], kc == 0, kc == 7,
                       r=[('oT',), ('Wo',)], w=bk(pb))
                tt('dve', hb[:, co, 0:tw], bank(pb)[:, 0:tw], hb[:, co, 0:tw], ALU.add, r=bk(pb) + [('hbuf', s)],
                   w=[('hbuf', s)])
            dma('pool', hTv[:, :, t0:t0 + tw], hb[:, :, 0:tw], r=[('hbuf', s)], w=[('hT', qt)])

        X(0)
        X2(0)
        for qt in range(NQT_RUN):
            if qt + 1 < NQT_RUN:
                S.capture()
                X(qt + 1)
                la = S.end_capture()
                S.capture()
                Y(qt)
                lb = S.end_capture()
                S.replay_merged(la, lb, frac=MERGE_FRAC)
                X2(qt + 1)
            else:
                Y(qt)
        S.barrier()
        S.emit()


def ones_bf(C):
    return C.ones_bf

VEC_COLS = {}


def _vec_layout():
    cols = {}
    c = 0
    for l in range(2):
        for j in range(3):
            cols['norm_g_%d_%d' % (l, j)] = (c, 8)
            c += 8
    cols['rk_mix'] = (c, 48)
    c += 48
    return cols, c


def _vec64_layout():
    cols = {}
    c = 0
    for n in ('k_k', 'k_a', 'a0', 'r_k'):
        cols[n] = (c, 16)
        c += 16
    for n in ('q_g', 'k_g'):
        cols[n] = (c, 1)
        c += 1
    return cols, c


RK_SHAPES = {'w_r': [D, D], 'w_k': [D, D], 'w_v': [D, D], 'w_o': [D, D], 'w0': [1, D], 'w1': [D, 64], 'w2': [64, D],
             'a1': [D, 64], 'a2': [64, D], 'g1': [D, 160], 'g2': [160, D], 'lnx_g': [1, D], 'lnx_b': [1, D]}


def build_program(stages):
    nc = bass.Bass("TRN2", target_bir_lowering=False)
    C = Ctx()
    C.nc = nc
    din = lambda n, sh, dt=F32: nc.dram_tensor(n, list(sh), dt, kind="ExternalInput").ap()
    C.x = din("x", [SEQ, D])
    C.meta = din("meta", [NMETA, D])
    C.ffn_w_in = din("ffn_w_in", [2, 2, D, 2 * DFF])
    C.ffn_w_out = din("ffn_w_out", [2, 2, DFF, D])
    C.rk = {k: din("rk_" + k, sh) for k, sh in RK_SHAPES.items()}
    cols, nv = _vec_layout()
    cols64, nv64 = _vec64_layout()
    C.cols, C.cols64 = cols, cols64
    C.vec_d = din("vecs", [128, nv])
    C.vec64_d = din("vecs64", [64, nv64])
    C.ident_d = din("ident", [128, 128])
    C.masks_d = din("masks", [64, 192])
    C.tri_d = din("tri", [64, 128])
    C.at = {'w_in': din("at_w_in", [D, 2120]), 'w_o': din("at_w_o", [D, D])}
    C.rope_d = din("rope", [64, 2, TP])
    C.rot_d = din("rot", [64, 64])
    C.negmask_d = din("negmask", [128, 128])
    C.out = nc.dram_tensor("out", [SEQ, D], F32, kind="ExternalOutput").ap()
    C.hT = nc.dram_tensor("hT_scratch", [D, TP], F32, kind="Internal").ap()
    with ExitStack() as es:
        S = Sched(nc, es)
        C.S = S
        C.vec = es.enter_context(nc.sbuf_tensor("c_vec", [128, nv], F32))
        C.vec64 = es.enter_context(nc.sbuf_tensor("c_vec64", [64, nv64], F32))
        C.ident = es.enter_context(nc.sbuf_tensor("c_ident", [128, 128], F32))
        C.masks = es.enter_context(nc.sbuf_tensor("c_masks", [64, 192], F32))
        C.tri = es.enter_context(nc.sbuf_tensor("c_tri", [64, 128], F32))
        C.ones_bf = es.enter_context(nc.sbuf_tensor("c_ones_bf", [128, 128], BF16))
        C.ones_f = es.enter_context(nc.sbuf_tensor("c_ones_f", [128, 128], F32))
        C.eps_col = es.enter_context(nc.sbuf_tensor("c_eps", [128, 2], F32))
        S.op('sp', lambda e: e.dma_start(out=C.vec[:, :], in_=C.vec_d[:, :]), w=[('c_vec',)], dma=True)
        S.op('sp', lambda e: e.dma_start(out=C.vec64[:, :], in_=C.vec64_d[:, :]), w=[('c_vec64',)], dma=True)
        S.op('sp', lambda e: e.dma_start(out=C.ident[:, :], in_=C.ident_d[:, :]), w=[('c_ident',)], dma=True)
        S.op('sp', lambda e: e.dma_start(out=C.masks[:, :], in_=C.masks_d[:, :]), w=[('c_masks',)], dma=True)
        S.op('sp', lambda e: e.dma_start(out=C.tri[:, :], in_=C.tri_d[:, :]), w=[('c_tri',)], dma=True)
        S.op('pool', lambda e: e.memset(C.ones_bf[:, :], 1.0), w=[('c_ones',)])
        S.op('pool', lambda e: e.memset(C.ones_f[:, :], 1.0), w=[('c_onesf',)])
        S.op('pool', lambda e: e.memset(C.eps_col[:, 0:1], 1e-6), w=[('c_eps',)])
        S.op('pool', lambda e: e.memset(C.eps_col[:, 1:2], 64e-5), w=[('c_eps2',)])
        S.barrier()
        stage_ingest(C)
        for sname in stages:
            if sname.startswith('ffn'):
                l, j = int(sname[3]), int(sname[4])
                stage_ffn(C, C.ffn_w_in[l, j], C.ffn_w_out[l, j], cols['norm_g_%d_%d' % (l, 0 if j == 0 else 2)][0],
                          "f%d%d_" % (l, j))
            elif sname == 'rwkv':
                stage_rwkv(C)
            elif sname == 'dsa':
                stage_dsa(C)
        stage_egress(C)
    return nc


def host_consts(inputs):
    cols, nv = _vec_layout()
    cols64, nv64 = _vec64_layout()
    vec = np.zeros((128, nv), np.float32)
    vec64 = np.zeros((64, nv64), np.float32)

    def put(name, v):
        c0, n = cols[name]
        vec[:, c0:c0 + n] = np.asarray(v, np.float32).reshape(n, 128).T

    def put64(name, v):
        c0, n = cols64[name]
        vec64[:, c0:c0 + n] = np.asarray(v, np.float32).reshape(n, 64).T

    ng = np.asarray(inputs['norm_g'])
    for l in range(2):
        for j in range(3):
            put('norm_g_%d_%d' % (l, j), ng[l, j])
    put('rk_mix', np.asarray(inputs['rk_mix'])[0].reshape(-1))
    put64('k_k', inputs['rk_k_k'][0])
    put64('k_a', inputs['rk_k_a'][0])
    put64('a0', inputs['rk_a0'][0])
    put64('r_k', np.asarray(inputs['rk_r_k'])[0].reshape(-1))
    vec64[:, cols64['q_g'][0]] = np.asarray(inputs['at_q_g'], np.float32)[0]
    vec64[:, cols64['k_g'][0]] = np.asarray(inputs['at_k_g'], np.float32)[0]
    inv = (np.float32(500000.0) ** (-np.arange(0, 16, 2, dtype=np.float32) / np.float32(16))).astype(np.float32)
    ang = (np.arange(TP, dtype=np.float32)[:, None] * inv[None, :]).astype(np.float32)
    rope = np.zeros((64, 2, TP), np.float32)
    rope[:, 0, :] = 1.0
    rope[0:8, 0, :] = np.cos(ang).T
    rope[8:16, 0, :] = np.cos(ang).T
    rope[0:8, 1, :] = np.sin(ang).T
    rope[8:16, 1, :] = np.sin(ang).T
    rot = np.zeros((64, 64), np.float32)
    for d in range(8):
        rot[d + 8, d] = -1.0
        rot[d, d + 8] = 1.0
    i128 = np.arange(128)
    negmask = np.where(i128[None, :] <= i128[:, None], 0.0, -1.0e30).astype(np.float32)
    ii = np.arange(64)
    su = (ii[:, None] < ii[None, :]).astype(np.float32)
    iu = (ii[:, None] <= ii[None, :]).astype(np.float32)
    sl = (ii[:, None] > ii[None, :]).astype(np.float32)
    masks = np.concatenate([su, iu, sl], axis=1)
    cdec = np.float32(-np.exp(-0.5))
    tri = np.concatenate([iu, su], axis=1) * cdec
    out = {"vecs": vec, "vecs64": vec64, "ident": np.eye(128, dtype=np.float32), "masks": masks,
           "tri": tri.astype(np.float32), "rope": rope, "rot": rot, "negmask": negmask,
           "at_w_in": np.ascontiguousarray(np.asarray(inputs['at_w_in'], np.float32)[0]),
           "at_w_o": np.ascontiguousarray(np.asarray(inputs['at_w_o'], np.float32)[0])}
    for k in RK_SHAPES:
        out["rk_" + k] = np.ascontiguousarray(np.asarray(inputs["rk_" + k], np.float32)[0].reshape(RK_SHAPES[k]))
    return out


ALL_STAGES = ['ffn00', 'rwkv', 'ffn01', 'ffn10', 'dsa', 'ffn11']
_cache = {}


def run(inputs, stages, cores=NCORES, trace=False):
    key = tuple(stages)
    if key not in _cache:
        _cache[key] = build_program(stages)
    nc = _cache[key]
    consts = host_consts(inputs)
    x = np.asarray(inputs['x'], np.float32)
    shared = {
        "meta": np.ascontiguousarray(np.asarray(inputs['meta'], np.float32)),
        "ffn_w_in": np.ascontiguousarray(np.asarray(inputs['ffn_w_in'], np.float32)),
        "ffn_w_out": np.ascontiguousarray(np.asarray(inputs['ffn_w_out'], np.float32)),
    }
    shared.update(consts)
    in_maps = []
    for b in range(cores):
        m = dict(shared)
        m["x"] = np.ascontiguousarray(x[b])
        in_maps.append(m)
    res = run_bass_kernel_spmd(nc, in_maps, core_ids=list(range(cores)), trace=trace)
    out = np.stack([np.asarray(r["out"], np.float32) for r in res.results], axis=0)
    return out, res


def kernel(**inputs):
    out, _ = run(inputs, ALL_STAGES)
    return out
```

```python
import numpy as np
from contextlib import ExitStack
import concourse.bass as bass
import concourse.mybir as mybir
from concourse.bass_utils import run_bass_kernel_spmd

F32 = mybir.dt.float32
BF16 = mybir.dt.bfloat16
F32R = mybir.dt.float32r
USE_F32R = False
IDT = BF16
TRDT = F32
ALU = mybir.AluOpType
AF = mybir.ActivationFunctionType
AX = mybir.AxisListType

D = 1024
NMETA = 16
SEQ = 4096
T = SEQ + NMETA
TP = 4160
DFF = 2816
NCORES = 8
DEBUG_NQT = 0
MERGE_FRAC = 0.03
NOVBF = False
DEBUG_RK = None


class Sched:
    ENGS = ['pe', 'act', 'dve', 'pool', 'sp']

    def __init__(self, nc, es, nds=16):
        self.nc = nc
        self.sem = {e: es.enter_context(nc.semaphore("s_" + e)) for e in self.ENGS}
        self.cnt = {e: 0 for e in self.ENGS}
        self.NDS = nds
        self.dq = {}
        self.dsem = []
        self.dcnt = []
        self.dtok = []
        for q in ('sp', 'pool', 'act'):
            base = len(self.dsem)
            for i in range(nds):
                self.dsem.append(es.enter_context(nc.semaphore("d_%s%d" % (q, i))))
                self.dcnt.append(0)
                self.dtok.append(None)
            self.dq[q] = [base, 0]
        self.pending = {e: [] for e in self.ENGS}
        self.last_w = {}
        self.readers = {}
        self.seen = {e: {} for e in self.ENGS}
        self.nops = 0

    def _semh(self, sk):
        return self.sem[sk[1]] if sk[0] == 'e' else self.dsem[sk[1]]

    def capture(self):
        self._cap = []
        return self._cap

    def end_capture(self):
        c = self._cap
        self._cap = None
        return c

    def replay_merged(self, a, b, frac=1.0):
        na, nb = len(a), len(b)
        nbe = max(1, int(nb * frac))
        ia = ib = 0
        while ia < na or ib < nb:
            if ib >= nb or (ia < na and ia * nbe <= ib * na):
                self.op(*a[ia])
                ia += 1
            else:
                self.op(*b[ib])
                ib += 1

    def op(self, eng, fn, r=(), w=(), dma=False):
        if getattr(self, '_cap', None) is not None:
            self._cap.append((eng, fn, tuple(r), tuple(w), dma))
            return None
        pr = [k for k in r if k[0] in PSKEYS and k not in w]
        if pr:
            w = list(w) + pr
        deps = []
        for k in r:
            if k in self.last_w:
                deps.append(self.last_w[k])
        for k in w:
            if k in self.last_w:
                deps.append(self.last_w[k])
            rd = self.readers.get(k)
            if rd:
                deps.extend(rd.items())
        if dma:
            qd = self.dq[eng]
            i = qd[0] + qd[1] % self.NDS
            qd[1] += 1
            if self.dtok[i] is not None:
                deps.append(self.dtok[i])
            self.dcnt[i] += 16
            tok = (('d', i), self.dcnt[i])
            self.dtok[i] = tok
        else:
            self.cnt[eng] += 1
            tok = (('e', eng), self.cnt[eng])
        waits = {}
        seen = self.seen[eng]
        for (sk, v) in deps:
            if eng == 'pe' and sk == ('e', 'pe'):
                continue
            if seen.get(sk, 0) >= v:
                continue
            if waits.get(sk, 0) < v:
                waits[sk] = v
        for sk, v in waits.items():
            seen[sk] = v
        self.pending[eng].append((fn, list(waits.items()), tok))
        self.nops += 1
        for k in w:
            self.last_w[k] = tok
            self.readers[k] = {}
        ws = set(w)
        for k in r:
            if k in ws:
                continue
            rd = self.readers.setdefault(k, {})
            if rd.get(tok[0], 0) < tok[1]:
                rd[tok[0]] = tok[1]
        return tok

    def barrier(self):
        allt = [(('e', e), self.cnt[e]) for e in self.ENGS if self.cnt[e] > 0]
        allt += [t for t in self.dtok if t is not None]
        for e in self.ENGS:
            waits = {}
            seen = self.seen[e]
            for sk, v in allt:
                if seen.get(sk, 0) >= v:
                    continue
                waits[sk] = max(waits.get(sk, 0), v)
            for sk, v in waits.items():
                seen[sk] = v
            self.pending[e].append((None, list(waits.items()), None))
        self.last_w = {}
        self.readers = {}

    def emit(self):
        nc = self.nc

        def mk(e):
            def body(engh):
                for fn, waits, tok in self.pending[e]:
                    for sk, v in waits:
                        engh.wait_ge(self._semh(sk), v)
                    if fn is None:
                        continue
                    ins = fn(engh)
                    sk, v = tok
                    if sk[0] == 'e':
                        ins.then_inc(self.sem[e], 1)
                    else:
                        ins.then_inc(self.dsem[sk[1]], 16)
            return body

        with nc.Block() as blk:
            blk.tensor(mk('pe'))
            blk.scalar(mk('act'))
            blk.vector(mk('dve'))
            blk.gpsimd(mk('pool'))
            blk.sync(mk('sp'))
        self.pending = {e: [] for e in self.ENGS}


PSKEYS = {'ps', 'psb', 'pA', 'pB', 'pO', 'psS'}


def keys(name, *idx_ranges):
    out = [(name,)]
    for r in idx_ranges:
        out = [o + (i,) for o in out for i in r]
    return out


class Ctx:
    pass


def stage_ingest(C):
    nc, S = C.nc, C.S
    with ExitStack() as st:
        xt = [st.enter_context(nc.sbuf_tensor("in_xt%d" % i, [128, 4, D], F32)) for i in range(2)]
        hx = [st.enter_context(nc.sbuf_tensor("in_hx%d" % i, [128, 8, 512], F32)) for i in range(2)]
        ps = [st.enter_context(nc.psum_tensor("in_ps%d" % i, [128, 512], F32)) for i in range(4)]
        hTv = C.hT.rearrange("(c p) t -> p c t", p=128)
        ident = C.ident
        S.op('pool', lambda e: e.memset(hx[1][:, :, 0:64], 0.0), w=keys('hx', [1], range(8)))
        S.op('pool', lambda e: e.dma_start(out=hTv[:, :, T:TP], in_=hx[1][:, :, 0:TP - T]),
             r=keys('hx', [1], range(8)), w=[('hTpad',)], dma=True)
        S.op('sp', lambda e: e.dma_start(out=xt[1][0:NMETA, 0, :], in_=C.meta[:, :]), w=[('xt', 1)], dma=True)
        for c in range(8):
            S.op('pe', lambda e, c=c: e.transpose(ps[c % 4][:, 0:NMETA], xt[1][0:NMETA, 0, c * 128:(c + 1) * 128],
                                                  ident[0:NMETA, 0:NMETA]),
                 r=[('xt', 1)], w=[('ps', c % 4)])
            S.op('dve', lambda e, c=c: e.tensor_copy(out=hx[1][:, c, 0:NMETA], in_=ps[c % 4][:, 0:NMETA]),
                 r=[('ps', c % 4)], w=[('hx', 1, c)])
        S.op('pool', lambda e: e.dma_start(out=hTv[:, :, 0:NMETA], in_=hx[1][:, :, 0:NMETA]),
             r=keys('hx', [1], range(8)), w=[('hTmeta',)], dma=True)
        xv = C.x.rearrange("(g a p) d -> g p a d", a=4, p=128)
        for g in range(SEQ // 512):
            s = g % 2
            S.op('sp', lambda e, g=g, s=s: e.dma_start(out=xt[s][:, :, :], in_=xv[g]), w=[('xt', s)], dma=True)
            for c in range(8):
                b = c % 4
                for a in range(4):
                    S.op('pe', lambda e, a=a, c=c, b=b, s=s: e.transpose(
                        ps[b][:, a * 128:(a + 1) * 128], xt[s][:, a, c * 128:(c + 1) * 128], ident[:, :]),
                        r=[('xt', s)], w=[('ps', b)])
                eng = 'dve' if c % 2 == 0 else 'act'
                if eng == 'dve':
                    S.op('dve', lambda e, c=c, b=b, s=s: e.tensor_copy(out=hx[s][:, c, :], in_=ps[b][:, :]),
                         r=[('ps', b)], w=[('hx', s, c)])
                else:
                    S.op('act', lambda e, c=c, b=b, s=s: e.copy(out=hx[s][:, c, :], in_=ps[b][:, :]),
                         r=[('ps', b)], w=[('hx', s, c)])
            t0 = NMETA + g * 512
            S.op('pool', lambda e, s=s, t0=t0: e.dma_start(out=hTv[:, :, t0:t0 + 512], in_=hx[s][:, :, :]),
                 r=keys('hx', [s], range(8)), w=[('hTin', g)], dma=True)
        S.barrier()
        S.emit()


def stage_egress(C):
    nc, S = C.nc, C.S
    with ExitStack() as st:
        xt = [st.enter_context(nc.sbuf_tensor("eg_xt%d" % i, [128, 4, D], F32)) for i in range(2)]
        hx = [st.enter_context(nc.sbuf_tensor("eg_hx%d" % i, [128, 8, 512], F32)) for i in range(2)]
        ps = [st.enter_context(nc.psum_tensor("eg_ps%d" % i, [128, 512], F32)) for i in range(4)]
        hTv = C.hT.rearrange("(c p) t -> p c t", p=128)
        ov = C.out.rearrange("(g a p) d -> g p a d", a=4, p=128)
        ident = C.ident
        for g in range(SEQ // 512):
            s = g % 2
            t0 = NMETA + g * 512
            S.op('sp', lambda e, s=s, t0=t0: e.dma_start(out=hx[s][:, :, :], in_=hTv[:, :, t0:t0 + 512]),
                 w=[('hx', s)], dma=True)
            for a in range(4):
                for hf in range(2):
                    b = (a * 2 + hf) % 4
                    for cc in range(4):
                        c = hf * 4 + cc
                        S.op('pe', lambda e, a=a, c=c, cc=cc, b=b, s=s: e.transpose(
                            ps[b][:, cc * 128:(cc + 1) * 128], hx[s][:, c, a * 128:(a + 1) * 128], ident[:, :]),
                            r=[('hx', s)], w=[('ps', b)])
                    if hf == 0:
                        S.op('dve', lambda e, a=a, b=b, s=s: e.tensor_copy(out=xt[s][:, a, 0:512], in_=ps[b][:, :]),
                             r=[('ps', b)], w=[('xt', s, a, 0)])
                    else:
                        S.op('act', lambda e, a=a, b=b, s=s: e.copy(out=xt[s][:, a, 512:1024], in_=ps[b][:, :]),
                             r=[('ps', b)], w=[('xt', s, a, 1)])
            S.op('pool', lambda e, g=g, s=s: e.dma_start(out=ov[g], in_=xt[s][:, :, :]),
                 r=keys('xt', [s], range(4), range(2)), w=[('out', g)], dma=True)
        S.barrier()
        S.emit()


def stage_ffn(C, w_in_d, w_out_d, gcol, tag):
    nc, S = C.nc, C.S
    TT = 256
    tiles = [(i * TT, TT) for i in range(TP // TT)]
    if TP % TT:
        tiles.append((TP - TP % TT, TP % TT))
    NJ = DFF // 128
    with ExitStack() as st:
        sb = lambda n, sh, dt: st.enter_context(nc.sbuf_tensor(tag + n, sh, dt))
        w_in = sb("w_in", [128, 8, 2 * DFF], BF16)
        w_out = sb("w_out", [128, NJ, D], BF16)
        x = [sb("x%d" % i, [128, 8, TT], F32) for i in range(2)]
        sq = [sb("sq%d" % i, [128, 8, TT], BF16) for i in range(2)]
        xn = [sb("xn%d" % i, [128, 8, TT], BF16) for i in range(2)]
        hm = [sb("hm%d" % i, [128, NJ, TT], BF16) for i in range(2)]
        sg = [sb("sg%d" % i, [128, TT], F32) for i in range(2)]
        rstd = [sb("rstd%d" % i, [128, TT], F32) for i in range(2)]
        psn = lambda n: st.enter_context(nc.psum_tensor(tag + n, [128, 512], F32))
        psA = [psn("pA%d" % i) for i in range(2)]
        psB = [psn("pB%d" % i) for i in range(2)]
        psO = [psn("pO%d" % i) for i in range(2)]
        psS = psn("pS")
        hTv = C.hT.rearrange("(c p) t -> p c t", p=128)
        w_in_v = w_in_d.rearrange("(k p) n -> p k n", p=128)
        w_out_v = w_out_d.rearrange("(j p) n -> p j n", p=128)
        ones = C.ones_bf
        vec = C.vec

        NB = 4
        cw = 2 * DFF // NB
        for k in range(8):
            for b in range(NB):
                S.op('pool', lambda e, k=k, b=b: e.dma_start(out=w_in[:, k, b * cw:(b + 1) * cw],
                                                             in_=w_in_v[:, k, b * cw:(b + 1) * cw]),
                     w=[('w_in', k, b)], dma=True)
        for j in range(NJ):
            S.op('pool', lambda e, j=j: e.dma_start(out=w_out[:, j, :], in_=w_out_v[:, j, :]),
                 w=[('w_out', j)], dma=True)
        win_keys = keys('w_in', range(8), range(NB))

        def load(i):
            t0, tw = tiles[i]
            s = i % 2
            S.op('sp', lambda e: e.dma_start(out=x[s][:, :, :tw], in_=hTv[:, :, t0:t0 + tw]),
                 r=[('hT', i)], w=keys('x', [s], range(8)), dma=True)

        def norm(i):
            t0, tw = tiles[i]
            s = i % 2
            S.op('act', lambda e: e.activation(out=sq[s][:, :, :tw], in_=x[s][:, :, :tw], func=AF.Square),
                 r=keys('x', [s], range(8)), w=[('sq', s)])
            for c in range(8):
                S.op('pe', lambda e, c=c: e.matmul(psS[:, :tw], lhsT=ones[:, :], rhs=sq[s][:, c, :tw],
                                                   start=(c == 0), stop=(c == 7)),
                     r=[('sq', s)], w=[('psS',)])
            S.op('act', lambda e: e.activation(out=rstd[s][:, :tw], in_=psS[:, :tw], func=AF.Sqrt,
                                               scale=1.0 / D, bias=C.eps_col[:, 0:1]),
                 r=[('psS',)], w=[('rstd', s)])
            S.op('dve', lambda e: e.reciprocal(out=rstd[s][:, :tw], in_=rstd[s][:, :tw]),
                 r=[('rstd', s)], w=[('rstd', s)])
            for c in range(8):
                S.op('dve', lambda e, c=c: e.scalar_tensor_tensor(
                    out=xn[s][:, c, :tw], in0=x[s][:, c, :tw], scalar=vec[:, gcol + c:gcol + c + 1],
                    in1=rstd[s][:, :tw], op0=ALU.mult, op1=ALU.mult),
                    r=[('x', s, c), ('rstd', s)], w=[('xn', s, c)])

        def mm_in(i):
            t0, tw = tiles[i]
            s = i % 2
            for j in range(NJ):
                q = j % 2
                for k in range(8):
                    S.op('pe', lambda e, j=j, k=k, q=q: e.matmul(
                        psA[q][:, :tw], lhsT=w_in[:, k, j * 128:(j + 1) * 128], rhs=xn[s][:, k, :tw],
                        start=(k == 0), stop=(k == 7)),
                        r=[('xn', s, k)] + (win_keys if (i == 0 and j == 0) else []), w=[('pA', q)])
                for k in range(8):
                    S.op('pe', lambda e, j=j, k=k, q=q: e.matmul(
                        psB[q][:, :tw], lhsT=w_in[:, k, DFF + j * 128:DFF + (j + 1) * 128], rhs=xn[s][:, k, :tw],
                        start=(k == 0), stop=(k == 7)),
                        r=[('xn', s, k)], w=[('pB', q)])
                S.op('act', lambda e, q=q: e.activation(out=sg[q][:, :tw], in_=psA[q][:, :tw], func=AF.Silu),
                     r=[('pA', q)], w=[('sg', q)])
                S.op('dve', lambda e, q=q, j=j: e.tensor_tensor(out=hm[s][:, j, :tw], in0=psB[q][:, :tw],
                                                                in1=sg[q][:, :tw], op=ALU.mult),
                     r=[('pB', q), ('sg', q)], w=[('hm', s, j)])

        def mm_out(i):
            t0, tw = tiles[i]
            s = i % 2
            for m in range(8):
                q = m % 2
                for j in range(NJ):
                    S.op('pe', lambda e, j=j, m=m, q=q: e.matmul(
                        psO[q][:, :tw], lhsT=w_out[:, j, m * 128:(m + 1) * 128], rhs=hm[s][:, j, :tw],
                        start=(j == 0), stop=(j == NJ - 1)),
                        r=[('hm', s, j), ('w_out', j)], w=[('pO', q)])
                S.op('dve', lambda e, m=m, q=q: e.scalar_tensor_tensor(
                    out=x[s][:, m, :tw], in0=psO[q][:, :tw], scalar=0.5, in1=x[s][:, m, :tw],
                    op0=ALU.mult, op1=ALU.add),
                    r=[('pO', q), ('x', s, m)], w=[('x', s, m)])
            S.op('pool', lambda e: e.dma_start(out=hTv[:, :, t0:t0 + tw], in_=x[s][:, :, :tw]),
                 r=keys('x', [s], range(8)), w=[('hT', i)], dma=True)

        n = len(tiles)
        load(0)
        norm(0)
        for i in range(n):
            if i + 1 < n:
                load(i + 1)
            mm_in(i)
            if i + 1 < n:
                norm(i + 1)
            mm_out(i)
        S.barrier()
        S.emit()


class H:
    def __init__(self, S):
        self.S = S

    def mm(self, out, lhsT, rhs, start, stop, r, w):
        if USE_F32R and lhsT.dtype == F32 and rhs.dtype == F32:
            lhsT = lhsT.bitcast(F32R)
            rhs = rhs.bitcast(F32R)
        self.S.op('pe', lambda e: e.matmul(out, lhsT=lhsT, rhs=rhs, start=start, stop=stop), r=r, w=w)

    def tr(self, out, in_, ident, r, w):
        self.S.op('pe', lambda e: e.transpose(out, in_, ident), r=r, w=w)

    def act(self, out, in_, func, r, w, scale=None, bias=None):
        kw = {}
        if scale is not None:
            kw['scale'] = scale
        if bias is not None:
            kw['bias'] = bias
        self.S.op('act', lambda e: e.activation(out=out, in_=in_, func=func, **kw), r=r, w=w)

    def cp(self, eng, out, in_, r, w):
        if eng == 'act':
            self.S.op('act', lambda e: e.copy(out=out, in_=in_), r=r, w=w)
        else:
            self.S.op(eng, lambda e: e.tensor_copy(out=out, in_=in_), r=r, w=w)

    def tt(self, eng, out, in0, in1, op, r, w):
        self.S.op(eng, lambda e: e.tensor_tensor(out=out, in0=in0, in1=in1, op=op), r=r, w=w)

    def ts(self, eng, out, in0, s1, s2, op0, op1, r, w):
        if s2 is None:
            self.S.op(eng, lambda e: e.tensor_scalar(out=out, in0=in0, scalar1=s1, scalar2=None, op0=op0), r=r, w=w)
        else:
            self.S.op(eng, lambda e: e.tensor_scalar(out=out, in0=in0, scalar1=s1, scalar2=s2, op0=op0, op1=op1),
                      r=r, w=w)

    def stt(self, eng, out, in0, scalar, in1, op0, op1, r, w):
        self.S.op(eng, lambda e: e.scalar_tensor_tensor(out=out, in0=in0, scalar=scalar, in1=in1, op0=op0, op1=op1),
                  r=r, w=w)

    def red(self, eng, out, in_, op, r, w):
        self.S.op(eng, lambda e: e.tensor_reduce(out=out, in_=in_, axis=AX.X, op=op), r=r, w=w)

    def dma(self, eng, out, in_, r, w):
        self.S.op(eng, lambda e: e.dma_start(out=out, in_=in_), r=r, w=w, dma=True)


def bk(*bs):
    return [('ps', b) for b in bs]


def stage_rwkv(C):
    nc, S = C.nc, C.S
    Hh = H(S)
    mm, tr, act, cp, tt, ts, stt, red, dma = Hh.mm, Hh.tr, Hh.act, Hh.cp, Hh.tt, Hh.ts, Hh.stt, Hh.red, Hh.dma
    CH = 64
    NT = TP // CH
    NH = 16
    cols = C.cols
    c64 = C.cols64
    with ExitStack() as st:
        sb = lambda n, sh, dt=F32: st.enter_context(nc.sbuf_tensor("rs_" + n, sh, dt))
        Wr = sb("Wr", [128, 8, D], BF16)
        Wk = sb("Wk", [128, 8, D], BF16)
        Wv = sb("Wv", [128, 8, D], BF16)
        Wo = sb("Wo", [128, 8, D], BF16)
        w1 = sb("w1", [128, 8, 64], BF16)
        a1 = sb("a1", [128, 8, 64], BF16)
        g1 = sb("g1", [128, 8, 160], BF16)
        a2 = sb("a2", [64, D], BF16)
        g2a = sb("g2a", [128, D], BF16)
        g2b = sb("g2b", [32, D], BF16)
        w2aug = sb("w2aug", [65, D], F32)
        lnxg = sb("lnxg", [64, D], F32)
        lnxb = sb("lnxb", [64, D], F32)
        omk = sb("omk", [64, NH], F32)
        PS = st.enter_context(nc.psum_tensor("rk_PS", [128, 3584], F32))
        PSb = st.enter_context(nc.psum_tensor("rk_PSb", [128, 1024], BF16))
        hbuf = [sb("hbuf%d" % i, [128, 8, CH]) for i in range(2)]
        sq = sb("sq", [128, 8, CH], BF16)
        rstd = sb("rstd", [128, CH])
        hn = sb("hn", [128, 8, CH + 1])
        xx = sb("xx", [128, 8, CH])
        xmf = [sb("xmf%d" % i, [128, 8, CH]) for i in range(1)]
        xm = [sb("xm%d" % i, [128, 8, CH], BF16) for i in range(6)]
        r_ = sb("r", [64, NH, CH])
        k_ = sb("k", [64, NH, CH])
        a_ = sb("a", [64, NH, CH])
        kk = sb("kk", [64, NH, CH])
        b_ = sb("b", [64, NH, CH])
        tmp1 = sb("tmp1", [64, NH, CH])
        tmp2 = sb("tmp2", [64, NH, CH])
        G = sb("G", [64, NH, CH])
        Ghat = sb("Ghat", [64, NH, CH])
        cumC = sb("cumC", [64, NH])
        AR = sb("AR", [64, NH, 2 * CH], BF16)
        Bt = sb("Bt", [64, NH, CH], BF16)
        Kt = sb("Kt", [64, NH, CH], BF16)
        Bh = sb("Bh", [64, NH, CH], TRDT)
        Kh = sb("Kh", [64, NH, CH], TRDT)
        g_tm = sb("g_tm", [64, D], BF16)
        lw_tm = sb("lw_tm", [64, D])
        twT = sb("twT", [65, CH])
        taT = sb("taT", [64, CH], BF16)
        sg0 = sb("sg0", [128, CH], BF16)
        sg1 = sb("sg1", [32, CH], BF16)
        bon = sb("bon", [64, NH])
        MX = sb("MX", [64, NH, 2 * CH], BF16)
        GG = sb("GG", [64, NH, 2 * CH])
        RKT = sb("RKT", [64, NH, CH], BF16)
        LakT = sb("LakT", [64, NH, CH], BF16)
        Hst = sb("Hst", [64, NH, CH])
        st1 = sb("st1", [64, NH])
        st2 = sb("st2", [64, NH])
        zT = sb("zT", [128, 8, CH], BF16)

        yc = sb("yc", [64, NH, CH])
        ysq = a_
        Gs = sb("Gs", [64, NH, CH], BF16)
        Us = sb("Us", [64, NH, CH], BF16)
        BhT = sb("BhT", [64, NH, CH], BF16)
        KhT = sb("KhT", [64, NH, CH], BF16)
        Lm = sb("Lm", [64, NH, CH], BF16)
        RBT = sb("RBT", [64, NH, CH], BF16)
        v_bf = sb("v_bf", [64, D], BF16)
        Hb = sb("Hb", [64, NH, CH], BF16)
        ident_bf = sb("ident_bf", [64, 64], BF16)
        Ginv = GG[:, :, 0:CH]
        Gex = GG[:, :, CH:2 * CH]

        hTv = C.hT.rearrange("(c p) t -> p c t", p=128)
        ident = C.ident
        vec = C.vec
        v64 = C.vec64

        for nm, dst, src in (("Wr", Wr, C.rk['w_r']), ("Wk", Wk, C.rk['w_k']), ("Wv", Wv, C.rk['w_v']),
                             ("Wo", Wo, C.rk['w_o'])):
            v = src.rearrange("(k p) n -> p k n", p=128)
            for k in range(8):
                dma('pool', dst[:, k, :], v[:, k, :], r=[], w=[(nm,)])
        dma('pool', w1[:, :, :], C.rk['w1'].rearrange("(k p) n -> p k n", p=128), r=[], w=[('w1',)])
        dma('pool', a1[:, :, :], C.rk['a1'].rearrange("(k p) n -> p k n", p=128), r=[], w=[('a1',)])
        dma('pool', g1[:, :, :], C.rk['g1'].rearrange("(k p) n -> p k n", p=128), r=[], w=[('g1',)])
        dma('pool', a2[:, :], C.rk['a2'][:, :], r=[], w=[('a2',)])
        dma('pool', g2a[:, :], C.rk['g2'][0:128, :], r=[], w=[('g2',)])
        dma('pool', g2b[:, :], C.rk['g2'][128:160, :], r=[], w=[('g2',)])
        dma('sp', w2aug[0:64, :], C.rk['w2'][:, :], r=[], w=[('w2aug',)])
        dma('sp', w2aug[64:65, :], C.rk['w0'][0:1, :], r=[], w=[('w2aug',)])
        dma('sp', lnxg[:, :], C.rk['lnx_g'][0:1, :].partition_broadcast(64), r=[], w=[('lnxg',)])
        dma('sp', lnxb[:, :], C.rk['lnx_b'][0:1, :].partition_broadcast(64), r=[], w=[('lnxb',)])
        kka = c64['k_a'][0]
        ts('dve', omk[:, :], v64[0:64, kka:kka + NH], -1.0, 1.0, ALU.mult, ALU.add, r=[('c_vec64',)], w=[('omk',)])
        S.op('pool', lambda e: e.memset(Hst[:, :, :], 0.0), w=[('Hst',)])
        S.op('pool', lambda e: e.memset(Hb[:, :, :], 0.0), w=[('Hb',)])
        cp('dve', ident_bf[:, :], ident[0:64, 0:64], r=[('c_ident',)], w=[('ident_bf',)])
        S.op('pool', lambda e: e.memset(hn[:, :, 0:1], 0.0), w=[('hn0',)])
        S.op('pool', lambda e: e.memset(twT[64:65, :], 1.0), w=[('twT1',)])

        def bc(ap2, n=CH):
            return ap2.unsqueeze(2).to_broadcast([64, NH, n])

        def prm(name):
            c0 = c64[name][0]
            return bc(v64[0:64, c0:c0 + NH])

        gcol = cols['norm_g_0_1'][0]
        mixc = cols['rk_mix'][0]
        psv2 = lambda b0: PS[0:64, b0 * 512:b0 * 512 + 2048].rearrange("p (h two s) -> p h two s", h=NH, two=2)
        psv1 = lambda b0: PS[0:64, b0 * 512:b0 * 512 + 1024].rearrange("p (h s) -> p h s", h=NH)
        SU = C.masks[0:64, 0:64].unsqueeze(1).to_broadcast([64, NH, CH])
        IU = C.masks[0:64, 64:128].unsqueeze(1).to_broadcast([64, NH, CH])
        SL = C.masks[0:64, 128:192].unsqueeze(1).to_broadcast([64, NH, CH])
        IDb = ident[0:64, 0:64].unsqueeze(1).to_broadcast([64, NH, CH])
        ones64 = C.ones_f[0:64, 0:64]

        def seg_A(t):
            t0 = t * CH
            hb = hbuf[t % 2]
            dma('sp', hb[:, :, :], hTv[:, :, t0:t0 + CH], r=[('hT', t)], w=[('hbuf', t % 2)])
            act(sq[:, :, :], hb[:, :, :], AF.Square, r=[('hbuf', t % 2)], w=[('sq',)])
            for c in range(8):
                mm(PS[:, 0:CH], ones_bf(C)[:, :], sq[:, c, :], c == 0, c == 7, r=[('sq',)], w=bk(0))
            act(rstd[:, :], PS[:, 0:CH], AF.Sqrt, r=bk(0), w=[('rstd',)], scale=1.0 / D, bias=C.eps_col[:, 0:1])
            S.op('dve', lambda e: e.reciprocal(out=rstd[:, :], in_=rstd[:, :]), r=[('rstd',)], w=[('rstd',)])
            for c in range(8):
                stt('dve', hn[:, c, 1:CH + 1], hb[:, c, :], vec[:, gcol + c:gcol + c + 1], rstd[:, :],
                    ALU.mult, ALU.mult, r=[('hbuf', t % 2), ('rstd',), ('hn0',)], w=[('hn', c)])
            hnk = keys('hn', range(8))
            tt('pool', xx[:, :, :], hn[:, :, 0:CH], hn[:, :, 1:CH + 1], ALU.subtract, r=hnk + [('hn0',)], w=[('xx',)])
            for i in range(6):
                mixb = vec[:, mixc + i * 8:mixc + i * 8 + 8].unsqueeze(2).to_broadcast([128, 8, CH])
                tt('pool', xmf[0][:, :, :], xx[:, :, :], mixb, ALU.mult, r=[('xx',), ('c_vec',)], w=[('xmf', 0)])
                tt('dve', xm[i][:, :, :], xmf[0][:, :, :], hn[:, :, 1:CH + 1], ALU.add,
                   r=[('xmf', 0)] + hnk, w=keys('xm', [i], range(8)))
            cp('pool', hn[:, :, 0:1], hn[:, :, CH:CH + 1], r=hnk + [('xx',)], w=[('hn0',)])

        def seg_P(t):
            xr, xw, xk, xv, xa, xg = xm
            for (Wt, wn, xs, xi, b0) in ((Wr, 'Wr', xr, 0, 0), (Wk, 'Wk', xk, 2, 2)):
                for h in range(NH):
                    for kc in range(8):
                        mm(PS[0:64, b0 * 512 + h * 64:b0 * 512 + (h + 1) * 64], Wt[:, kc, h * 64:(h + 1) * 64],
                           xs[:, kc, :], kc == 0, kc == 7, r=[('xm', xi, kc), (wn,)], w=bk(b0 + h // 8))

        def seg_E(t):
            t0 = t * CH
            PH = DEBUG_RK[1] if DEBUG_RK else 99
            hb = hbuf[t % 2]
            xr, xw, xk, xv, xa, xg = xm
            cp('act', r_[:, :, :], psv1(0), r=bk(0, 1), w=[('r',)])
            cp('act', k_[:, :, :], psv1(2), r=bk(2, 3), w=[('k',)])
            for n in range(2):
                for kc in range(8):
                    mm(PS[0:64, (4 + n) * 512:(5 + n) * 512], xv[:, kc, :], Wv[:, kc, n * 512:(n + 1) * 512],
                       kc == 0, kc == 7, r=[('xm', 3, kc), ('Wv',)], w=bk(4 + n))
            cp('dve', v_bf[:, :], PS[0:64, 2048:3072], r=bk(4, 5), w=[('v_bf',)])
            for kc in range(8):
                mm(PS[0:64, 3072:3072 + CH], w1[:, kc, :], xw[:, kc, :], kc == 0, kc == 7,
                   r=[('xm', 1, kc), ('w1',)], w=bk(6))
            act(twT[0:64, :], PS[0:64, 3072:3072 + CH], AF.Tanh, r=bk(6), w=[('twT',)])
            for kc in range(8):
                mm(PS[0:64, 0:CH], a1[:, kc, :], xa[:, kc, :], kc == 0, kc == 7,
                   r=[('xm', 4, kc), ('a1',)], w=bk(0))
            cp('dve', taT[:, :], PS[0:64, 0:CH], r=bk(0), w=[('taT',)])
            for kc in range(8):
                mm(PS[:, 3072:3072 + CH], g1[:, kc, 0:128], xg[:, kc, :], kc == 0, kc == 7,
                   r=[('xm', 5, kc), ('g1',)], w=bk(6))
            act(sg0[:, :], PS[:, 3072:3072 + CH], AF.Sigmoid, r=bk(6), w=[('sg0',)])
            for kc in range(8):
                mm(PS[0:32, 512:512 + CH], g1[:, kc, 128:160], xg[:, kc, :], kc == 0, kc == 7,
                   r=[('xm', 5, kc), ('g1',)], w=bk(1))
            act(sg1[:, :], PS[0:32, 512:512 + CH], AF.Sigmoid, r=bk(1), w=[('sg1',)])
            for n in range(2):
                mm(PS[0:64, n * 512:(n + 1) * 512], twT[0:65, :], w2aug[0:65, n * 512:(n + 1) * 512], True, True,
                   r=[('twT',), ('twT1',), ('w2aug',)], w=bk(n))
            act(lw_tm[:, :], PS[0:64, 0:1024], AF.Sigmoid, r=bk(0, 1), w=[('lw_tm',)])
            for h in range(NH):
                mm(PS[0:64, 1024 + h * 128:1024 + (h + 1) * 128], lw_tm[0:64, h * 64:(h + 1) * 64],
                   C.tri[0:64, 0:128], True, True, r=[('lw_tm',), ('c_tri',)], w=bk(2 + h // 4))
            pc = psv2(2)
            cb = bk(2, 3, 4, 5)
            act(G[:, :, :], pc[:, :, 0, :], AF.Exp, r=cb, w=[('G',)])
            act(Ginv[:, :, :], pc[:, :, 0, :], AF.Exp, r=cb, w=[('Ginv',)], scale=-1.0)
            act(Gex[:, :, :], pc[:, :, 1, :], AF.Exp, r=cb, w=[('Gex',)])
            cp('dve', cumC[:, :], pc[:, :, 0, CH - 1], r=cb, w=[('cumC',)])
            tt('dve', tmp1[:, :, :], bc(cumC[:, :]), pc[:, :, 0, :], ALU.subtract, r=cb + [('cumC',)], w=[('tmp1',)])
            act(Ghat[:, :, :], tmp1[:, :, :], AF.Exp, r=[('tmp1',)], w=[('Ghat',)])
            for h in range(NH):
                mm(PS[0:64, 2048 + h * 64:2048 + (h + 1) * 64], a2[0:64, h * 64:(h + 1) * 64], taT[0:64, :], True, True,
                   r=[('taT',), ('a2',)], w=bk(4 + h // 8))
            tt('dve', a_[:, :, :], psv1(4), prm('a0'), ALU.add, r=bk(4, 5) + [('c_vec64',)], w=[('a',)])
            act(a_[:, :, :], a_[:, :, :], AF.Sigmoid, r=[('a',)], w=[('a',)])
            for n in range(2):
                mm(PS[0:64, n * 512:(n + 1) * 512], sg0[:, :], g2a[:, n * 512:(n + 1) * 512], True, False,
                   r=[('sg0',), ('g2',)], w=bk(n))
                mm(PS[0:64, n * 512:(n + 1) * 512], sg1[0:32, :], g2b[0:32, n * 512:(n + 1) * 512], False, True,
                   r=[('sg1',), ('g2',)], w=bk(n))
            cp('act', g_tm[:, :], PS[0:64, 0:1024], r=bk(0, 1), w=[('g_tm',)])
            if PH < -1:
                return
            tt('dve', kk[:, :, :], k_[:, :, :], prm('k_k'), ALU.mult, r=[('k',), ('c_vec64',)], w=[('kk',)])
            act(tmp2[:, :, :], kk[:, :, :], AF.Square, r=[('kk',)], w=[('tmp2',)])
            t2f = tmp2[:, :, :].rearrange("p h s -> p (h s)")
            for n in range(2):
                mm(PS[0:64, 1024 + n * 512:1024 + (n + 1) * 512], ones64, t2f[:, n * 512:(n + 1) * 512], True, True,
                   r=[('tmp2',), ('c_onesf',)], w=bk(2 + n))
            act(tmp2[:, :, :], psv1(2), AF.Sqrt, r=bk(2, 3), w=[('tmp2',)])
            ts('dve', tmp2[:, :, :], tmp2[:, :, :], 1e-12, None, ALU.max, None, r=[('tmp2',)], w=[('tmp2',)])
            S.op('dve', lambda e: e.reciprocal(out=tmp2[:, :, :], in_=tmp2[:, :, :]), r=[('tmp2',)], w=[('tmp2',)])
            tt('dve', kk[:, :, :], kk[:, :, :], tmp2[:, :, :], ALU.mult, r=[('kk',), ('tmp2',)], w=[('kk',)])
            tt('pool', tmp1[:, :, :], a_[:, :, :], prm('k_a'), ALU.mult, r=[('a',), ('c_vec64',)], w=[('tmp1',)])
            tt('pool', tmp1[:, :, :], tmp1[:, :, :], bc(omk[:, :]), ALU.add, r=[('tmp1',), ('omk',)], w=[('tmp1',)])
            tt('pool', k_[:, :, :], k_[:, :, :], tmp1[:, :, :], ALU.mult, r=[('k',), ('tmp1',)], w=[('k',)])
            tt('dve', b_[:, :, :], kk[:, :, :], a_[:, :, :], ALU.mult, r=[('kk',), ('a',)], w=[('b',)])
            stt('dve', AR[:, :, 0:CH], kk[:, :, :], -1.0, Gex[:, :, :], ALU.mult, ALU.mult,
                r=[('kk',), ('Gex',)], w=[('AR0',)])
            tt('pool', AR[:, :, CH:2 * CH], r_[:, :, :], G[:, :, :], ALU.mult, r=[('r',), ('G',)], w=[('AR1',)])
            tt('dve', Bt[:, :, :], b_[:, :, :], Ginv[:, :, :], ALU.mult, r=[('b',), ('Ginv',)], w=[('Bt',)])
            tt('pool', Kt[:, :, :], k_[:, :, :], Ginv[:, :, :], ALU.mult, r=[('k',), ('Ginv',)], w=[('Kt',)])
            tt('dve', Bh[:, :, :], b_[:, :, :], Ghat[:, :, :], ALU.mult, r=[('b',), ('Ghat',)], w=[('Bh',)])
            tt('pool', Kh[:, :, :], k_[:, :, :], Ghat[:, :, :], ALU.mult, r=[('k',), ('Ghat',)], w=[('Kh',)])
            tt('pool', tmp1[:, :, :], r_[:, :, :], prm('r_k'), ALU.mult, r=[('r',), ('c_vec64',)], w=[('tmp1',)])
            tt('pool', tmp1[:, :, :], tmp1[:, :, :], k_[:, :, :], ALU.mult, r=[('tmp1',), ('k',)], w=[('tmp1',)])
            for h in range(NH):
                mm(PS[0:64, 3072 + h:3072 + h + 1], tmp1[:, h, :], C.ones_f[0:64, 0:1], True, True,
                   r=[('tmp1',), ('c_onesf',)], w=bk(6))
            cp('dve', bon[:, :], PS[0:64, 3072:3072 + NH], r=bk(6), w=[('bon',)])
            if PH < 1:
                return
            GR = [(0, 8), (8, 8)]

            def hv(ap, g):
                return ap[:, GR[g][0]:GR[g][0] + 8, :]

            def pg2(b0):
                return PS[0:64, b0 * 512:b0 * 512 + 1024].rearrange("p (h two s) -> p h two s", h=8, two=2)

            def pg1(b0):
                return PS[0:64, b0 * 512:b0 * 512 + 512].rearrange("p (h s) -> p h s", h=8)

            SU8 = C.masks[0:64, 0:64].unsqueeze(1).to_broadcast([64, 8, CH])
            IU8 = C.masks[0:64, 64:128].unsqueeze(1).to_broadcast([64, 8, CH])
            SL8 = C.masks[0:64, 128:192].unsqueeze(1).to_broadcast([64, 8, CH])
            ID8 = ident[0:64, 0:64].unsqueeze(1).to_broadcast([64, 8, CH])
            for g in range(2):
                h0 = GR[g][0]
                bA = 0 if g == 0 else 3
                for hh in range(8):
                    h = h0 + hh
                    mm(PS[0:64, bA * 512 + hh * 128:bA * 512 + (hh + 1) * 128], Bt[:, h, :], AR[:, h, :], True, True,
                       r=[('Bt',), ('AR0',), ('AR1',)], w=bk(bA + hh // 4))
                tt('dve', hv(MX[:, :, 0:CH], g), pg2(bA)[:, :, 0, :], SU8, ALU.mult, r=bk(bA, bA + 1) + [('c_masks',)],
                   w=[('MX0', g)])
                tt('dve', hv(RBT, g), pg2(bA)[:, :, 1, :], IU8, ALU.mult, r=bk(bA, bA + 1) + [('c_masks',)],
                   w=[('RBT', g)])
                for hh in range(8):
                    h = h0 + hh
                    mm(PS[0:64, bA * 512 + hh * 128:bA * 512 + (hh + 1) * 128], Kt[:, h, :], AR[:, h, :], True, True,
                       r=[('Kt',), ('AR0',), ('AR1',)], w=bk(bA + hh // 4))
                tt('dve', hv(LakT, g), pg2(bA)[:, :, 0, :], SU8, ALU.mult, r=bk(bA, bA + 1) + [('c_masks',)],
                   w=[('LakT', g)])
                tt('dve', hv(RKT, g), pg2(bA)[:, :, 1, :], IU8, ALU.mult, r=bk(bA, bA + 1) + [('c_masks',)],
                   w=[('RKT', g)])
                for hh in range(8):
                    h = h0 + hh
                    mm(PS[0:64, (bA + 2) * 512 + hh * 64:(bA + 2) * 512 + (hh + 1) * 64], AR[:, h, 0:CH], Bt[:, h, :],
                       True, True, r=[('Bt',), ('AR0',)], w=bk(bA + 2))
                tt('dve', hv(Lm, g), pg1(bA + 2), SL8, ALU.mult, r=bk(bA + 2) + [('c_masks',)], w=[('Lm', g)])
                cp('pool', hv(MX[:, :, CH:2 * CH], g), ID8, r=[('c_ident',)], w=[('MX1', g)])
            if PH < 2:
                return
            for lvl in range(6):
                for g in range(2):
                    h0 = GR[g][0]
                    bA = 0 if g == 0 else 3
                    for hh in range(8):
                        h = h0 + hh
                        mm(PS[0:64, bA * 512 + hh * 128:bA * 512 + (hh + 1) * 128], Lm[:, h, :], MX[:, h, :], True, True,
                           r=[('Lm', g), ('MX0', g), ('MX1', g)], w=bk(bA + hh // 4))
                    if lvl < 5:
                        for hh in range(8):
                            h = h0 + hh
                            mm(PS[0:64, (bA + 2) * 512 + hh * 64:(bA + 2) * 512 + (hh + 1) * 64], MX[:, h, 0:CH], Lm[:, h, :],
                               True, True, r=[('Lm', g), ('MX0', g)], w=bk(bA + 2))
                for g in range(2):
                    bA = 0 if g == 0 else 3
                    tt('dve', hv(MX[:, :, CH:2 * CH], g), pg2(bA)[:, :, 1, :], hv(MX[:, :, CH:2 * CH], g), ALU.add,
                       r=bk(bA, bA + 1) + [('MX1', g)], w=[('MX1', g)])
                    if lvl < 5:
                        cp('act', hv(MX[:, :, 0:CH], g), pg2(bA)[:, :, 0, :], r=bk(bA, bA + 1), w=[('MX0', g)])
                        cp('act', hv(Lm, g), pg1(bA + 2), r=bk(bA + 2), w=[('Lm', g)])
            if PH < 3:
                return
            if TRDT == BF16:
                psb3 = PSb[0:64, :].rearrange("p (h s) -> p h s", h=NH)
                for h in range(NH):
                    tr(PSb[0:64, h * 64:(h + 1) * 64], Bh[:, h, :], ident_bf[:, :], r=[('Bh',), ('ident_bf',)], w=[('psb',)])
                cp('act', BhT[:, :, :], psb3, r=[('psb',)], w=[('BhT',)])
                for h in range(NH):
                    tr(PSb[0:64, h * 64:(h + 1) * 64], Kh[:, h, :], ident_bf[:, :], r=[('Kh',), ('ident_bf',)], w=[('psb',)])
                cp('dve', KhT[:, :, :], psb3, r=[('psb',)], w=[('KhT',)])
            else:
                for h in range(NH):
                    tr(PS[0:64, h * 64:(h + 1) * 64], Bh[:, h, :], ident[0:64, 0:64], r=[('Bh',), ('c_ident',)], w=bk(h // 8))
                cp('act', BhT[:, :, :], psv1(0), r=bk(0, 1), w=[('BhT',)])
                for h in range(NH):
                    tr(PS[0:64, 1024 + h * 64:1024 + (h + 1) * 64], Kh[:, h, :], ident[0:64, 0:64],
                       r=[('Kh',), ('c_ident',)], w=bk(2 + h // 8))
                cp('dve', KhT[:, :, :], psv1(2), r=bk(2, 3), w=[('KhT',)])
            if PH < 4:
                return
            for h in range(NH):
                o = PS[0:64, h * 64:(h + 1) * 64]
                mm(o, AR[:, h, 0:CH], Hb[:, h, :], True, False, r=[('AR0',), ('Hb',)], w=bk(h // 8))
                mm(o, LakT[:, h, :], v_bf[:, h * 64:(h + 1) * 64], False, True, r=[('LakT', h // 8), ('v_bf',)], w=bk(h // 8))
            cp('act', Gs[:, :, :], psv1(0), r=bk(0, 1), w=[('Gs',)])
            for h in range(NH):
                mm(PS[0:64, 1024 + h * 64:1024 + (h + 1) * 64], MX[:, h, CH:2 * CH], Gs[:, h, :], True, True,
                   r=[('MX1', h // 8), ('Gs',)], w=bk(2 + h // 8))
            cp('dve', Us[:, :, :], psv1(2), r=bk(2, 3), w=[('Us',)])
            for h in range(NH):
                o = PS[0:64, 2048 + h * 64:2048 + (h + 1) * 64]
                mm(o, AR[:, h, CH:2 * CH], Hb[:, h, :], True, False, r=[('AR1',), ('Hb',)], w=bk(4 + h // 8))
                mm(o, RBT[:, h, :], Us[:, h, :], False, False, r=[('RBT', h // 8), ('Us',)], w=bk(4 + h // 8))
                mm(o, RKT[:, h, :], v_bf[:, h * 64:(h + 1) * 64], False, True, r=[('RKT', h // 8), ('v_bf',)], w=bk(4 + h // 8))
            for h in range(NH):
                o = PS[0:64, h * 64:(h + 1) * 64]
                mm(o, BhT[:, h, :], Us[:, h, :], True, False, r=[('BhT',), ('Us',)], w=bk(h // 8))
                mm(o, KhT[:, h, :], v_bf[:, h * 64:(h + 1) * 64], False, True, r=[('KhT',), ('v_bf',)], w=bk(h // 8))
            tt('dve', Hst[:, :, :], Hst[:, :, :], G[:, :, CH - 1:CH].to_broadcast([64, NH, CH]), ALU.mult,
               r=[('Hst',), ('G',)], w=[('Hst',)])
            tt('dve', Hst[:, :, :], psv1(0), Hst[:, :, :], ALU.add, r=bk(0, 1) + [('Hst',)], w=[('Hst',)])
            cp('act', Hb[:, :, :], Hst[:, :, :], r=[('Hst',)], w=[('Hb',)])
            if PH < 5:
                return

        def seg_H(t):
            t0 = t * CH
            hb = hbuf[t % 2]
            py = psv1(4)
            yb = bk(4, 5)
            red('dve', st1[:, :], py, ALU.add, r=yb, w=[('st1',)])
            ts('dve', st1[:, :], st1[:, :], -1.0 / 64, None, ALU.mult, None, r=[('st1',)], w=[('st1',)])
            tt('dve', yc[:, :, :], py, bc(st1[:, :]), ALU.add, r=yb + [('st1',)], w=[('yc',)])
            act(ysq[:, :, :], yc[:, :, :], AF.Square, r=[('yc',)], w=[('a',)])
            red('dve', st2[:, :], ysq[:, :, :], ALU.add, r=[('a',)], w=[('st2',)])
            act(st2[:, :], st2[:, :], AF.Sqrt, r=[('st2',)], w=[('st2',)], scale=1.0 / 64, bias=C.eps_col[0:64, 1:2])
            S.op('dve', lambda e: e.reciprocal(out=st2[:, :], in_=st2[:, :]), r=[('st2',)], w=[('st2',)])
            tt('dve', yc[:, :, :], yc[:, :, :], bc(st2[:, :]), ALU.mult, r=[('yc',), ('st2',)], w=[('yc',)])
            ycf = yc[:, :, :].rearrange("p h s -> p (h s)")
            tt('pool', ycf, ycf, lnxg[:, :], ALU.mult, r=[('yc',), ('lnxg',)], w=[('yc',)])
            tt('pool', ycf, ycf, lnxb[:, :], ALU.add, r=[('yc',), ('lnxb',)], w=[('yc',)])
            vv = v_bf[:, :].rearrange("p (h s) -> p h s", h=NH)
            tt('dve', ysq[:, :, :], vv, bc(bon[:, :]), ALU.mult, r=[('v_bf',), ('bon',)], w=[('a',)])
            tt('pool', yc[:, :, :], yc[:, :, :], ysq[:, :, :], ALU.add, r=[('yc',), ('a',)], w=[('yc',)])
            tt('dve', ycf, ycf, g_tm[:, :], ALU.mult, r=[('yc',), ('g_tm',)], w=[('yc',)])
            for c in range(8):
                tr(PS[:, 3072 + c * 64:3072 + (c + 1) * 64], yc[:, 2 * c:2 * c + 2, :].rearrange("p h s -> p (h s)"),
                   ident[0:64, 0:64], r=[('yc',), ('c_ident',)], w=bk(6))
            cp('act', zT[:, :, :], PS[:, 3072:3584].rearrange("p (c s) -> p c s", c=8), r=bk(6), w=[('zT',)])
            for co in range(8):
                q = 4 + (co % 2)
                for kc in range(8):
                    mm(PS[:, q * 512:q * 512 + CH], Wo[:, kc, co * 128:(co + 1) * 128], zT[:, kc, :], kc == 0, kc == 7,
                       r=[('zT',), ('Wo',)], w=bk(q))
                tt('dve', hb[:, co, :], PS[:, q * 512:q * 512 + CH], hb[:, co, :], ALU.add,
                   r=bk(q) + [('hbuf', t % 2)], w=[('hbuf', t % 2)])
            dma('pool', hTv[:, :, t0:t0 + CH], hb[:, :, :], r=[('hbuf', t % 2)], w=[('hT', t)])

        NTR = NT if not DEBUG_RK else DEBUG_RK[0]
        if NTR > 0:
            seg_A(0)
            seg_P(0)
        for t in range(NTR):
            seg_E(t)
            if t + 1 < NTR:
                seg_A(t + 1)
                seg_P(t + 1)
            seg_H(t)
        S.barrier()
        S.emit()


def stage_dsa(C):
    nc, S = C.nc, C.S
    Hh = H(S)
    mm, tr, act, cp, tt, ts, stt, red, dma = Hh.mm, Hh.tr, Hh.act, Hh.cp, Hh.tt, Hh.ts, Hh.stt, Hh.red, Hh.dma
    TT = 128
    NQT = (TP + TT - 1) // TT
    NQT_RUN = min(NQT, DEBUG_NQT) if DEBUG_NQT else NQT
    c64 = C.cols64
    cols = C.cols
    NBIS = 15
    MB = 240000.0
    with ExitStack() as st:
        sb = lambda n, sh, dt=F32: st.enter_context(nc.sbuf_tensor("ds_" + n, sh, dt))
        Wq = sb("Wq", [128, 8, 1024], BF16)
        Wk = sb("Wk", [128, 8, 256], BF16)
        Wv = sb("Wv", [128, 8, 256], BF16)
        Wqi = sb("Wqi", [128, 8, 512], BF16)
        Wki = sb("Wki", [128, 8, 64], BF16)
        Wwi = sb("Wwi", [128, 8, 8], BF16)
        Wo = sb("Wo", [128, 8, 1024], BF16)
        kT = sb("kT", [64, 4, TP], BF16)
        Vaug = sb("Vaug", [128, NQT, 4, 65], BF16)
        kiT = sb("kiT", [64, TP], BF16)
        score = sb("score", [128, TP])
        work = sb("work", [128, TP])
        mask01 = sb("mask01", [128, TP], BF16)
        maskT = sb("maskT", [128, NQT, TT], BF16)
        hbuf = [sb("hbuf%d" % i, [128, 8, TT]) for i in range(2)]
        hnb = sb("hnb", [128, 8, TT], BF16)
        sq = sb("sq", [128, 8, TT], BF16)
        rstd = sb("rstd", [128, TT])
        qT = [sb("qT%d" % i, [64, 16 * TT], BF16) for i in range(2)]
        qiT = sb("qiT", [64, 8 * TT], BF16)
        tA = sb("tA", [64, 512])
        tB = sb("tB", [64, 512])
        tC = sb("tC", [64, 512])
        rl = [sb("rl%d" % i, [128, 512]) for i in range(2)]
        PT = [sb("PT%d" % i, [128, 512], BF16) for i in range(3)]
        o_tm = sb("o_tm", [128, 1024], BF16)
        oT = sb("oT", [128, 8, TT], BF16)
        cs = sb("cs", [64, 2, TT])
        wi = sb("wi", [128, 8])
        m8 = sb("m8", [128, 8])
        eq8 = sb("eq8", [128, 8])
        iota8 = sb("iota8", [128, 8])
        lo = sb("lo", [128, 1])
        HC = sb("HC", [128, 2])
        MC = sb("MC", [128, 2])
        sel = sb("sel", [128, 1])
        d1 = sb("d1", [128, 1])
        d2 = sb("d2", [128, 2])
        thr = sb("thr", [128, 1])
        nm1 = sb("nm1", [128, 1])
        halfc = sb("halfc", [128, 1])
        negb = sb("negb", [128, 1])
        rden = sb("rden", [128, 16])
        ident_bf = sb("ident_bf", [128, 128], BF16)
        zeros_bf = sb("zeros_bf", [128, 512], BF16)
        rot = sb("rot", [64, 64])
        negmask = sb("negmask", [128, 128])
        PS = st.enter_context(nc.psum_tensor("ds_PS", [128, 3584], F32))
        PSb = st.enter_context(nc.psum_tensor("ds_PSb", [128, 1024], BF16))
        bank = lambda b: PS[:, b * 512:(b + 1) * 512]

        hTv = C.hT.rearrange("(c p) t -> p c t", p=128)
        ident = C.ident
        vec = C.vec
        v64 = C.vec64
        win = C.at['w_in'].rearrange("(k p) n -> p k n", p=128)
        for k in range(8):
            dma('pool', Wq[:, k, :], win[:, k, 0:1024], r=[], w=[('Wq',)])
        dma('pool', Wk[:, :, :], win[:, :, 1024:1280], r=[], w=[('Wk',)])
        dma('pool', Wv[:, :, :], win[:, :, 1280:1536], r=[], w=[('Wv',)])
        for k in range(8):
            dma('pool', Wqi[:, k, :], win[:, k, 1536:2048], r=[], w=[('Wqi',)])
        dma('pool', Wki[:, :, :], win[:, :, 2048:2112], r=[], w=[('Wki',)])
        dma('pool', Wwi[:, :, :], win[:, :, 2112:2120], r=[], w=[('Wwi',)])
        wov = C.at['w_o'].rearrange("(k p) n -> p k n", p=128)
        for k in range(8):
            dma('pool', Wo[:, k, :], wov[:, k, :], r=[], w=[('Wo',)])
        dma('sp', rot[:, :], C.rot_d[:, :], r=[], w=[('rot',)])
        dma('sp', negmask[:, :], C.negmask_d[:, :], r=[], w=[('negmask',)])
        cp('dve', ident_bf[:, :], ident[:, :], r=[('c_ident',)], w=[('ident_bf',)])
        S.op('pool', lambda e: e.memset(zeros_bf[:, :], 0.0), w=[('zeros_bf',)])
        S.op('pool', lambda e: e.memset(Vaug[:, :, :, 64:65], 1.0), w=[('Vones',)])
        S.op('pool', lambda e: e.memset(halfc[:, :], 0.5), w=[('halfc',)])
        S.op('pool', lambda e: e.memset(negb[:, :], -MB), w=[('negb',)])
        for j in range(8):
            S.op('pool', lambda e, j=j: e.memset(iota8[:, j:j + 1], float(j)), w=[('iota8',)])
        gcol = cols['norm_g_1_1'][0]
        qg = v64[0:64, c64['q_g'][0]:c64['q_g'][0] + 1]
        kg = v64[0:64, c64['k_g'][0]:c64['k_g'][0] + 1]
        kwid = lambda kb: min(128, TP - kb * 128)

        def norm_rope(pb, nh, tw, gcolap, out3, okeys_w):
            n = nh * tw
            pin = bank(pb)[0:64, 0:n]
            v3 = lambda ap: ap.rearrange("p (h s) -> p h s", h=nh)
            if gcolap is not None:
                act(tA[:, 0:n], pin, AF.Square, r=bk(pb), w=[('tA',)])
                mm(bank(6)[0:64, 0:n], C.ones_f[0:64, 0:64], tA[:, 0:n], True, True, r=[('tA',), ('c_onesf',)], w=bk(6))
                act(tB[:, 0:n], bank(6)[0:64, 0:n], AF.Sqrt, r=bk(6), w=[('tB',)], scale=1.0 / 64,
                    bias=C.eps_col[0:64, 0:1])
                S.op('dve', lambda e: e.reciprocal(out=tB[:, 0:n], in_=tB[:, 0:n]), r=[('tB',)], w=[('tB',)])
                stt('dve', tC[:, 0:n], pin, gcolap, tB[:, 0:n], ALU.mult, ALU.mult, r=bk(pb) + [('tB',), ('c_vec64',)],
                    w=[('tC',)])
            else:
                cp('act', tC[:, 0:n], pin, r=bk(pb), w=[('tC',)])
            mm(bank(6)[0:64, 0:n], rot[:, :], tC[:, 0:n], True, True, r=[('tC',), ('rot',)], w=bk(6))
            cosb = cs[:, 0, 0:tw].unsqueeze(1).to_broadcast([64, nh, tw])
            sinb = cs[:, 1, 0:tw].unsqueeze(1).to_broadcast([64, nh, tw])
            tt('pool', v3(tA[:, 0:n]), v3(tC[:, 0:n]), cosb, ALU.mult, r=[('tC',), ('cs',)], w=[('tA',)])
            tt('dve', v3(tB[:, 0:n]), v3(bank(6)[0:64, 0:n]), sinb, ALU.mult, r=bk(6) + [('cs',)], w=[('tB',)])
            tt('pool', out3, v3(tA[:, 0:n]), v3(tB[:, 0:n]), ALU.add, r=[('tA',), ('tB',)], w=okeys_w)

        def X(qt):
            s = qt % 2
            t0 = qt * TT
            tw = min(TT, TP - t0)
            n = t0 + tw
            nkb = qt + 1
            hb = hbuf[s]
            dma('sp', hb[:, :, 0:tw], hTv[:, :, t0:t0 + tw], r=[('hT', qt)], w=[('hbuf', s)])
            dma('sp', cs[:, :, 0:tw], C.rope_d[:, :, t0:t0 + tw], r=[], w=[('cs',)])
            act(sq[:, :, 0:tw], hb[:, :, 0:tw], AF.Square, r=[('hbuf', s)], w=[('sq',)])
            for c in range(8):
                mm(bank(6)[:, 0:tw], C.ones_bf[:, :], sq[:, c, 0:tw], c == 0, c == 7, r=[('sq',)], w=bk(6))
            act(rstd[:, 0:tw], bank(6)[:, 0:tw], AF.Sqrt, r=bk(6), w=[('rstd',)], scale=1.0 / D, bias=C.eps_col[:, 0:1])
            S.op('dve', lambda e: e.reciprocal(out=rstd[:, 0:tw], in_=rstd[:, 0:tw]), r=[('rstd',)], w=[('rstd',)])
            for c in range(8):
                stt('dve', hnb[:, c, 0:tw], hb[:, c, 0:tw], vec[:, gcol + c:gcol + c + 1], rstd[:, 0:tw],
                    ALU.mult, ALU.mult, r=[('hbuf', s), ('rstd',)], w=[('hnb',)])
            for g in range(4):
                for kc in range(8):
                    mm(bank(5)[0:64, g * tw:(g + 1) * tw], Wk[:, kc, g * 64:(g + 1) * 64], hnb[:, kc, 0:tw], kc == 0, kc == 7,
                       r=[('hnb',), ('Wk',)], w=bk(5))
            norm_rope(5, 4, tw, kg, kT[:, :, t0:t0 + tw], [('kT',)])
            for kc in range(8):
                mm(bank(5)[0:tw, 0:256], hnb[:, kc, 0:tw], Wv[:, kc, :], kc == 0, kc == 7, r=[('hnb',), ('Wv',)], w=bk(5))
            cp('act', Vaug[0:tw, qt, :, 0:64], bank(5)[0:tw, 0:256].rearrange("p (g d) -> p g d", g=4), r=bk(5),
               w=[('Vaug',)])
            for kc in range(8):
                mm(bank(5)[0:64, 0:tw], Wki[:, kc, :], hnb[:, kc, 0:tw], kc == 0, kc == 7, r=[('hnb',), ('Wki',)], w=bk(5))
            norm_rope(5, 1, tw, None, kiT[:, t0:t0 + tw].unsqueeze(1), [('kiT',)])
            for grp in range(4):
                pb = 5
                for hh in range(4):
                    h = grp * 4 + hh
                    for kc in range(8):
                        mm(bank(pb)[0:64, hh * tw:(hh + 1) * tw], Wq[:, kc, h * 64:(h + 1) * 64], hnb[:, kc, 0:tw],
                           kc == 0, kc == 7, r=[('hnb',), ('Wq',)], w=bk(pb))
                norm_rope(pb, 4, tw, qg, qT[s][:, grp * 4 * tw:(grp + 1) * 4 * tw].rearrange("p (h s) -> p h s", h=4),
                          [('qT', s)])
            for grp in range(2):
                pb = 5
                for hh in range(4):
                    h = grp * 4 + hh
                    for kc in range(8):
                        mm(bank(pb)[0:64, hh * tw:(hh + 1) * tw], Wqi[:, kc, h * 64:(h + 1) * 64], hnb[:, kc, 0:tw],
                           kc == 0, kc == 7, r=[('hnb',), ('Wqi',)], w=bk(pb))
                norm_rope(pb, 4, tw, None, qiT[:, grp * 4 * tw:(grp + 1) * 4 * tw].rearrange("p (h s) -> p h s", h=4),
                          [('qiT',)])
            for kc in range(8):
                mm(bank(6)[0:tw, 0:8], hnb[:, kc, 0:tw], Wwi[:, kc, :], kc == 0, kc == 7, r=[('hnb',), ('Wwi',)], w=bk(6))
            ts('dve', wi[0:tw, :], bank(6)[0:tw, 0:8], float(512.0 ** -0.5), None, ALU.mult, None, r=bk(6), w=[('wi',)])
            idx = 0
            for k0 in range(0, n, 512):
                nk = min(512, n - k0)
                for h in range(8):
                    pb = 5 + idx % 2
                    rb = rl[idx % 2]
                    rk = ('rl', idx % 2)
                    idx += 1
                    mm(bank(pb)[0:tw, 0:nk], qiT[:, h * tw:(h + 1) * tw], kiT[:, k0:k0 + nk], True, True,
                       r=[('qiT',), ('kiT',)], w=bk(pb))
                    act(rb[0:tw, 0:nk], bank(pb)[0:tw, 0:nk], AF.Relu, r=bk(pb), w=[rk])
                    if h == 0:
                        ts('dve', score[0:tw, k0:k0 + nk], rb[0:tw, 0:nk], wi[0:tw, 0:1], None, ALU.mult, None,
                           r=[rk, ('wi',)], w=[('score',)])
                    else:
                        stt('dve', score[0:tw, k0:k0 + nk], rb[0:tw, 0:nk], wi[0:tw, h:h + 1], score[0:tw, k0:k0 + nk],
                            ALU.mult, ALU.add, r=[rk, ('wi',), ('score',)], w=[('score',)])
            sc = score[0:tw, 0:n]
            if n > 256:
                red('dve', HC[0:tw, 0:1], sc, ALU.max, r=[('score',)], w=[('HC',)])
                red('dve', lo[0:tw, :], sc, ALU.min, r=[('score',)], w=[('lo',)])
            tt('dve', score[0:tw, t0:t0 + tw], score[0:tw, t0:t0 + tw], negmask[0:tw, 0:tw], ALU.add,
               r=[('score',), ('negmask',)], w=[('score',)])
            if n > 256:
                tt('dve', d1[0:tw, :], HC[0:tw, 0:1], lo[0:tw, :], ALU.subtract, r=[('HC',), ('lo',)], w=[('d1',)])
                stt('dve', HC[0:tw, 0:1], d1[0:tw, :], 1.0e-6, HC[0:tw, 0:1], ALU.mult, ALU.add, r=[('d1',), ('HC',)],
                    w=[('HC',)])
                ts('dve', HC[0:tw, 1:2], d1[0:tw, :], 0.0, None, ALU.mult, None, r=[('d1',), ('HC',)], w=[('HC',)])
                for it in range(NBIS):
                    stt('dve', MC[0:tw, 0:1], lo[0:tw, :], HC[0:tw, 0:1], halfc[0:tw, :], ALU.add, ALU.mult,
                        r=[('lo',), ('HC',), ('halfc',)], w=[('MC',)])
                    S.op('dve', lambda e, tw=tw, n=n: e.tensor_scalar(
                        out=mask01[0:tw, 0:n], in0=score[0:tw, 0:n], scalar1=MC[0:tw, 0:1], scalar2=0.0,
                        op0=ALU.is_ge, op1=ALU.add, accum_out=MC[0:tw, 1:2]),
                        r=[('score',), ('MC',)], w=[('mask01',), ('MC',)])
                    ts('dve', sel[0:tw, :], MC[0:tw, 1:2], 256.0, None, ALU.is_ge, None, r=[('MC',)], w=[('sel',)])
                    tt('dve', d1[0:tw, :], MC[0:tw, 0:1], lo[0:tw, :], ALU.subtract, r=[('MC',), ('lo',)], w=[('d1',)])
                    stt('dve', lo[0:tw, :], d1[0:tw, :], sel[0:tw, 0:1], lo[0:tw, :], ALU.mult, ALU.add,
                        r=[('d1',), ('sel',), ('lo',)], w=[('lo',)])
                    tt('dve', d2[0:tw, :], HC[0:tw, :], MC[0:tw, :], ALU.subtract, r=[('MC',), ('HC',)], w=[('d2',)])
                    stt('dve', HC[0:tw, :], d2[0:tw, :], sel[0:tw, 0:1], MC[0:tw, :], ALU.mult, ALU.add,
                        r=[('d2',), ('sel',), ('MC',)], w=[('HC',)])
                wk = work[0:tw, 0:n]
                ts('dve', wk, sc, HC[0:tw, 0:1], 1.0e20, ALU.is_ge, ALU.mult, r=[('score',), ('HC',)], w=[('work',)])
                tt('dve', wk, sc, wk, ALU.subtract, r=[('score',), ('work',)], w=[('work',)])
                S.op('dve', lambda e, tw=tw, n=n: e.max(out=m8[0:tw, :], in_=work[0:tw, 0:n]), r=[('work',)], w=[('m8',)])
                ts('dve', nm1[0:tw, :], HC[0:tw, 1:2], -1.0, 255.0, ALU.mult, ALU.add, r=[('HC',)], w=[('nm1',)])
                ts('dve', nm1[0:tw, :], nm1[0:tw, :], 7.0, 0.0, ALU.min, ALU.max, r=[('nm1',)], w=[('nm1',)])
                ts('dve', eq8[0:tw, :], iota8[0:tw, :], nm1[0:tw, 0:1], None, ALU.is_equal, None, r=[('nm1',), ('iota8',)],
                   w=[('eq8',)])
                tt('dve', eq8[0:tw, :], eq8[0:tw, :], m8[0:tw, :], ALU.mult, r=[('eq8',), ('m8',)], w=[('eq8',)])
                red('dve', thr[0:tw, :], eq8[0:tw, :], ALU.add, r=[('eq8',)], w=[('thr',)])
                ts('dve', mask01[0:tw, 0:n], sc, thr[0:tw, 0:1], None, ALU.is_ge, None, r=[('score',), ('thr',)],
                   w=[('mask01',)])
            else:
                ts('dve', mask01[0:tw, 0:n], sc, -1.0e29, None, ALU.is_ge, None, r=[('score',)], w=[('mask01',)])
        def X2(qt):
            t0 = qt * TT
            tw = min(TT, TP - t0)
            nkb = qt + 1
            for kb0 in range(0, nkb, 8):
                nb = min(8, nkb - kb0)
                for j in range(nb):
                    kb = kb0 + j
                    kw = kwid(kb)
                    tr(PSb[0:kw, j * 128:j * 128 + tw], mask01[0:tw, kb * 128:kb * 128 + kw], ident_bf[0:tw, 0:tw],
                       r=[('mask01',), ('ident_bf',)], w=[('psb',)])
                kwl = kwid(kb0 + nb - 1)
                nfull = nb if kwl == 128 else nb - 1
                if nfull > 0:
                    act(maskT[:, kb0:kb0 + nfull, 0:tw],
                        PSb[:, 0:nfull * 128].rearrange("p (j s) -> p j s", j=nfull)[:, :, 0:tw], AF.Identity,
                        r=[('psb',), ('negb',)], w=[('maskT', kb) for kb in range(kb0, kb0 + nfull)],
                        scale=MB, bias=negb[:, 0:1])
                if nfull < nb:
                    act(maskT[0:kwl, kb0 + nb - 1, 0:tw], PSb[0:kwl, (nb - 1) * 128:(nb - 1) * 128 + tw], AF.Identity,
                        r=[('psb',), ('negb',)], w=[('maskT', kb0 + nb - 1)], scale=MB, bias=negb[0:kwl, 0:1])

        def Y(qt):
            s = qt % 2
            t0 = qt * TT
            tw = min(TT, TP - t0)
            nkb = qt + 1
            hb = hbuf[s]
            q_ = qT[s]
            OB = [(0, 0, 7), (1, 7, 7), (2, 14, 2)]
            for (ob, h0, nh) in OB:
                S.op('pe', lambda e, ob=ob, nh=nh: e.matmul(bank(ob)[0:tw, 0:nh * 65], lhsT=zeros_bf[:, 0:tw],
                                                          rhs=zeros_bf[:, 0:nh * 65], start=True, stop=False,
                                                          skip_group_check=True),
                     r=[('zeros_bf',)], w=bk(ob))
            jobs = [(kb, g) for kb in range(nkb) for g in range(4)]

            def s_part(i):
                kb, g = jobs[i]
                kw = kwid(kb)
                pb = 3 + i % 2
                pt = PT[i % 3]
                pk = ('PT', i % 3)
                mbias = maskT[0:kw, kb, 0:tw].unsqueeze(1).to_broadcast([kw, 4, tw])
                mm(bank(pb)[0:kw, 0:4 * tw], kT[:, g, kb * 128:kb * 128 + kw], q_[:, g * 4 * tw:(g + 1) * 4 * tw],
                   True, False, r=[('kT',), ('qT', s)], w=bk(pb))
                mm(bank(pb)[0:kw, 0:4 * tw].rearrange("p (h s) -> p h s", h=4), ident_bf[0:kw, 0:kw], mbias,
                   False, True, r=[('maskT', kb), ('ident_bf',)], w=bk(pb))
                act(pt[0:kw, 0:4 * tw], bank(pb)[0:kw, 0:4 * tw], AF.Exp, r=bk(pb), w=[pk], scale=0.125)

            def v_part(i):
                kb, g = jobs[i]
                kw = kwid(kb)
                pt = PT[i % 3]
                pk = ('PT', i % 3)
                for rr in range(4):
                    h = 4 * g + rr
                    S.op('pe', lambda e, h=h, rr=rr, kw=kw, pt=pt, kb=kb, g=g, last=(kb == nkb - 1): e.matmul(
                        bank(h // 7)[0:tw, (h % 7) * 65:(h % 7 + 1) * 65], lhsT=pt[0:kw, rr * tw:(rr + 1) * tw],
                        rhs=Vaug[0:kw, kb, g, :], start=False, stop=last, skip_group_check=True),
                        r=[pk, ('Vaug',), ('Vones',)], w=bk(h // 7))

            s_part(0)
            for i in range(len(jobs)):
                if i + 1 < len(jobs):
                    s_part(i + 1)
                v_part(i)
            for (ob, h0, nh) in OB:
                o3 = bank(ob)[0:tw, 0:nh * 65].rearrange("p (h d) -> p h d", h=nh)
                S.op('dve', lambda e, o3=o3, h0=h0, nh=nh: e.reciprocal(out=rden[0:tw, h0:h0 + nh], in_=o3[:, :, 64]),
                     r=bk(ob), w=[('rden', ob)])
                tt('dve', o_tm[0:tw, h0 * 64:(h0 + nh) * 64].rearrange("p (h d) -> p h d", h=nh), o3[:, :, 0:64],
                   rden[0:tw, h0:h0 + nh].unsqueeze(2).to_broadcast([tw, nh, 64]), ALU.mult,
                   r=bk(ob) + [('rden', ob)], w=[('o_tm',)])
            for c in range(8):
                tr(PSb[:, c * 128:c * 128 + tw], o_tm[0:tw, c * 128:(c + 1) * 128], ident_bf[0:tw, 0:tw],
                   r=[('o_tm',), ('ident_bf',)], w=[('psb',)])
            cp('act', oT[:, :, 0:tw], PSb[:, :].rearrange("p (c s) -> p c s", c=8)[:, :, 0:tw],
               r=[('psb',)], w=[('oT',)])
            for co in range(8):
                pb = 3 + co % 2
                for kc in range(8):
                    mm(bank(pb)[:, 0:tw], Wo[:, kc, co * 128:(co + 1) * 128], oT[:, kc, 0:tw], kc == 0, kc == 7,
                       r=[('oT',), ('Wo',)], w=bk(pb))
                tt('dve', hb[:, co, 0:tw], bank(pb)[:, 0:tw], hb[:, co, 0:tw], ALU.add, r=bk(pb) + [('hbuf', s)],
                   w=[('hbuf', s)])
            dma('pool', hTv[:, :, t0:t0 + tw], hb[:, :, 0:tw], r=[('hbuf', s)], w=[('hT', qt)])

        X(0)
        X2(0)
        for qt in range(NQT_RUN):
            if qt + 1 < NQT_RUN:
                S.capture()
                X(qt + 1)
                la = S.end_capture()
                S.capture()
                Y(qt)
                lb = S.end_capture()
                S.replay_merged(la, lb, frac=MERGE_FRAC)
                X2(qt + 1)
            else:
                Y(qt)
        S.barrier()
        S.emit()


def ones_bf(C):
    return C.ones_bf

VEC_COLS = {}


def _vec_layout():
    cols = {}
    c = 0
    for l in range(2):
        for j in range(3):
            cols['norm_g_%d_%d' % (l, j)] = (c, 8)
            c += 8
    cols['rk_mix'] = (c, 48)
    c += 48
    return cols, c


def _vec64_layout():
    cols = {}
    c = 0
    for n in ('k_k', 'k_a', 'a0', 'r_k'):
        cols[n] = (c, 16)
        c += 16
    for n in ('q_g', 'k_g'):
        cols[n] = (c, 1)
        c += 1
    return cols, c


RK_SHAPES = {'w_r': [D, D], 'w_k': [D, D], 'w_v': [D, D], 'w_o': [D, D], 'w0': [1, D], 'w1': [D, 64], 'w2': [64, D],
             'a1': [D, 64], 'a2': [64, D], 'g1': [D, 160], 'g2': [160, D], 'lnx_g': [1, D], 'lnx_b': [1, D]}


def build_program(stages):
    nc = bass.Bass("TRN2", target_bir_lowering=False)
    C = Ctx()
    C.nc = nc
    din = lambda n, sh, dt=F32: nc.dram_tensor(n, list(sh), dt, kind="ExternalInput").ap()
    C.x = din("x", [SEQ, D])
    C.meta = din("meta", [NMETA, D])
    C.ffn_w_in = din("ffn_w_in", [2, 2, D, 2 * DFF])
    C.ffn_w_out = din("ffn_w_out", [2, 2, DFF, D])
    C.rk = {k: din("rk_" + k, sh) for k, sh in RK_SHAPES.items()}
    cols, nv = _vec_layout()
    cols64, nv64 = _vec64_layout()
    C.cols, C.cols64 = cols, cols64
    C.vec_d = din("vecs", [128, nv])
    C.vec64_d = din("vecs64", [64, nv64])
    C.ident_d = din("ident", [128, 128])
    C.masks_d = din("masks", [64, 192])
    C.tri_d = din("tri", [64, 128])
    C.at = {'w_in': din("at_w_in", [D, 2120]), 'w_o': din("at_w_o", [D, D])}
    C.rope_d = din("rope", [64, 2, TP])
    C.rot_d = din("rot", [64, 64])
    C.negmask_d = din("negmask", [128, 128])
    C.out = nc.dram_tensor("out", [SEQ, D], F32, kind="ExternalOutput").ap()
    C.hT = nc.dram_tensor("hT_scratch", [D, TP], F32, kind="Internal").ap()
    with ExitStack() as es:
        S = Sched(nc, es)
        C.S = S
        C.vec = es.enter_context(nc.sbuf_tensor("c_vec", [128, nv], F32))
        C.vec64 = es.enter_context(nc.sbuf_tensor("c_vec64", [64, nv64], F32))
        C.ident = es.enter_context(nc.sbuf_tensor("c_ident", [128, 128], F32))
        C.masks = es.enter_context(nc.sbuf_tensor("c_masks", [64, 192], F32))
        C.tri = es.enter_context(nc.sbuf_tensor("c_tri", [64, 128], F32))
        C.ones_bf = es.enter_context(nc.sbuf_tensor("c_ones_bf", [128, 128], BF16))
        C.ones_f = es.enter_context(nc.sbuf_tensor("c_ones_f", [128, 128], F32))
        C.eps_col = es.enter_context(nc.sbuf_tensor("c_eps", [128, 2], F32))
        S.op('sp', lambda e: e.dma_start(out=C.vec[:, :], in_=C.vec_d[:, :]), w=[('c_vec',)], dma=True)
        S.op('sp', lambda e: e.dma_start(out=C.vec64[:, :], in_=C.vec64_d[:, :]), w=[('c_vec64',)], dma=True)
        S.op('sp', lambda e: e.dma_start(out=C.ident[:, :], in_=C.ident_d[:, :]), w=[('c_ident',)], dma=True)
        S.op('sp', lambda e: e.dma_start(out=C.masks[:, :], in_=C.masks_d[:, :]), w=[('c_masks',)], dma=True)
        S.op('sp', lambda e: e.dma_start(out=C.tri[:, :], in_=C.tri_d[:, :]), w=[('c_tri',)], dma=True)
        S.op('pool', lambda e: e.memset(C.ones_bf[:, :], 1.0), w=[('c_ones',)])
        S.op('pool', lambda e: e.memset(C.ones_f[:, :], 1.0), w=[('c_onesf',)])
        S.op('pool', lambda e: e.memset(C.eps_col[:, 0:1], 1e-6), w=[('c_eps',)])
        S.op('pool', lambda e: e.memset(C.eps_col[:, 1:2], 64e-5), w=[('c_eps2',)])
        S.barrier()
        stage_ingest(C)
        for sname in stages:
            if sname.startswith('ffn'):
                l, j = int(sname[3]), int(sname[4])
                stage_ffn(C, C.ffn_w_in[l, j], C.ffn_w_out[l, j], cols['norm_g_%d_%d' % (l, 0 if j == 0 else 2)][0],
                          "f%d%d_" % (l, j))
            elif sname == 'rwkv':
                stage_rwkv(C)
            elif sname == 'dsa':
                stage_dsa(C)
        stage_egress(C)
    return nc


def host_consts(inputs):
    cols, nv = _vec_layout()
    cols64, nv64 = _vec64_layout()
    vec = np.zeros((128, nv), np.float32)
    vec64 = np.zeros((64, nv64), np.float32)

    def put(name, v):
        c0, n = cols[name]
        vec[:, c0:c0 + n] = np.asarray(v, np.float32).reshape(n, 128).T

    def put64(name, v):
        c0, n = cols64[name]
        vec64[:, c0:c0 + n] = np.asarray(v, np.float32).reshape(n, 64).T

    ng = np.asarray(inputs['norm_g'])
    for l in range(2):
        for j in range(3):
            put('norm_g_%d_%d' % (l, j), ng[l, j])
    put('rk_mix', np.asarray(inputs['rk_mix'])[0].reshape(-1))
    put64('k_k', inputs['rk_k_k'][0])
    put64('k_a', inputs['rk_k_a'][0])
    put64('a0', inputs['rk_a0'][0])
    put64('r_k', np.asarray(inputs['rk_r_k'])[0].reshape(-1))
    vec64[:, cols64['q_g'][0]] = np.asarray(inputs['at_q_g'], np.float32)[0]
    vec64[:, cols64['k_g'][0]] = np.asarray(inputs['at_k_g'], np.float32)[0]
    inv = (np.float32(500000.0) ** (-np.arange(0, 16, 2, dtype=np.float32) / np.float32(16))).astype(np.float32)
    ang = (np.arange(TP, dtype=np.float32)[:, None] * inv[None, :]).astype(np.float32)
    rope = np.zeros((64, 2, TP), np.float32)
    rope[:, 0, :] = 1.0
    rope[0:8, 0, :] = np.cos(ang).T
    rope[8:16, 0, :] = np.cos(ang).T
    rope[0:8, 1, :] = np.sin(ang).T
    rope[8:16, 1, :] = np.sin(ang).T
    rot = np.zeros((64, 64), np.float32)
    for d in range(8):
        rot[d + 8, d] = -1.0
        rot[d, d + 8] = 1.0
    i128 = np.arange(128)
    negmask = np.where(i128[None, :] <= i128[:, None], 0.0, -1.0e30).astype(np.float32)
    ii = np.arange(64)
    su = (ii[:, None] < ii[None, :]).astype(np.float32)
    iu = (ii[:, None] <= ii[None, :]).astype(np.float32)
    sl = (ii[:, None] > ii[None, :]).astype(np.float32)
    masks = np.concatenate([su, iu, sl], axis=1)
    cdec = np.float32(-np.exp(-0.5))
    tri = np.concatenate([iu, su], axis=1) * cdec
    out = {"vecs": vec, "vecs64": vec64, "ident": np.eye(128, dtype=np.float32), "masks": masks,
           "tri": tri.astype(np.float32), "rope": rope, "rot": rot, "negmask": negmask,
           "at_w_in": np.ascontiguousarray(np.asarray(inputs['at_w_in'], np.float32)[0]),
           "at_w_o": np.ascontiguousarray(np.asarray(inputs['at_w_o'], np.float32)[0])}
    for k in RK_SHAPES:
        out["rk_" + k] = np.ascontiguousarray(np.asarray(inputs["rk_" + k], np.float32)[0].reshape(RK_SHAPES[k]))
    return out


ALL_STAGES = ['ffn00', 'rwkv', 'ffn01', 'ffn10', 'dsa', 'ffn11']
_cache = {}


def run(inputs, stages, cores=NCORES, trace=False):
    key = tuple(stages)
    if key not in _cache:
        _cache[key] = build_program(stages)
    nc = _cache[key]
    consts = host_consts(inputs)
    x = np.asarray(inputs['x'], np.float32)
    shared = {
        "meta": np.ascontiguousarray(np.asarray(inputs['meta'], np.float32)),
        "ffn_w_in": np.ascontiguousarray(np.asarray(inputs['ffn_w_in'], np.float32)),
        "ffn_w_out": np.ascontiguousarray(np.asarray(inputs['ffn_w_out'], np.float32)),
    }
    shared.update(consts)
    in_maps = []
    for b in range(cores):
        m = dict(shared)
        m["x"] = np.ascontiguousarray(x[b])
        in_maps.append(m)
    res = run_bass_kernel_spmd(nc, in_maps, core_ids=list(range(cores)), trace=trace)
    out = np.stack([np.asarray(r["out"], np.float32) for r in res.results], axis=0)
    return out, res


def kernel(**inputs):
    out, _ = run(inputs, ALL_STAGES)
    return out
```

```python
import numpy as np
from contextlib import ExitStack
import concourse.bass as bass
import concourse.mybir as mybir
from concourse.bass_utils import run_bass_kernel_spmd

F32 = mybir.dt.float32
BF16 = mybir.dt.bfloat16
F32R = mybir.dt.float32r
USE_F32R = False
IDT = BF16
TRDT = F32
ALU = mybir.AluOpType
AF = mybir.ActivationFunctionType
AX = mybir.AxisListType

D = 1024
NMETA = 16
SEQ = 4096
T = SEQ + NMETA
TP = 4160
DFF = 2816
NCORES = 8
DEBUG_NQT = 0
MERGE_FRAC = 0.03
NOVBF = False
DEBUG_RK = None


class Sched:
    ENGS = ['pe', 'act', 'dve', 'pool', 'sp']

    def __init__(self, nc, es, nds=16):
        self.nc = nc
        self.sem = {e: es.enter_context(nc.semaphore("s_" + e)) for e in self.ENGS}
        self.cnt = {e: 0 for e in self.ENGS}
        self.NDS = nds
        self.dq = {}
        self.dsem = []
        self.dcnt = []
        self.dtok = []
        for q in ('sp', 'pool', 'act'):
            base = len(self.dsem)
            for i in range(nds):
                self.dsem.append(es.enter_context(nc.semaphore("d_%s%d" % (q, i))))
                self.dcnt.append(0)
                self.dtok.append(None)
            self.dq[q] = [base, 0]
        self.pending = {e: [] for e in self.ENGS}
        self.last_w = {}
        self.readers = {}
        self.seen = {e: {} for e in self.ENGS}
        self.nops = 0

    def _semh(self, sk):
        return self.sem[sk[1]] if sk[0] == 'e' else self.dsem[sk[1]]

    def capture(self):
        self._cap = []
        return self._cap

    def end_capture(self):
        c = self._cap
        self._cap = None
        return c

    def replay_merged(self, a, b, frac=1.0):
        na, nb = len(a), len(b)
        nbe = max(1, int(nb * frac))
        ia = ib = 0
        while ia < na or ib < nb:
            if ib >= nb or (ia < na and ia * nbe <= ib * na):
                self.op(*a[ia])
                ia += 1
            else:
                self.op(*b[ib])
                ib += 1

    def op(self, eng, fn, r=(), w=(), dma=False):
        if getattr(self, '_cap', None) is not None:
            self._cap.append((eng, fn, tuple(r), tuple(w), dma))
            return None
        pr = [k for k in r if k[0] in PSKEYS and k not in w]
        if pr:
            w = list(w) + pr
        deps = []
        for k in r:
            if k in self.last_w:
                deps.append(self.last_w[k])
        for k in w:
            if k in self.last_w:
                deps.append(self.last_w[k])
            rd = self.readers.get(k)
            if rd:
                deps.extend(rd.items())
        if dma:
            qd = self.dq[eng]
            i = qd[0] + qd[1] % self.NDS
            qd[1] += 1
            if self.dtok[i] is not None:
                deps.append(self.dtok[i])
            self.dcnt[i] += 16
            tok = (('d', i), self.dcnt[i])
            self.dtok[i] = tok
        else:
            self.cnt[eng] += 1
            tok = (('e', eng), self.cnt[eng])
        waits = {}
        seen = self.seen[eng]
        for (sk, v) in deps:
            if eng == 'pe' and sk == ('e', 'pe'):
                continue
            if seen.get(sk, 0) >= v:
                continue
            if waits.get(sk, 0) < v:
                waits[sk] = v
        for sk, v in waits.items():
            seen[sk] = v
        self.pending[eng].append((fn, list(waits.items()), tok))
        self.nops += 1
        for k in w:
            self.last_w[k] = tok
            self.readers[k] = {}
        ws = set(w)
        for k in r:
            if k in ws:
                continue
            rd = self.readers.setdefault(k, {})
            if rd.get(tok[0], 0) < tok[1]:
                rd[tok[0]] = tok[1]
        return tok

    def barrier(self):
        allt = [(('e', e), self.cnt[e]) for e in self.ENGS if self.cnt[e] > 0]
        allt += [t for t in self.dtok if t is not None]
        for e in self.ENGS:
            waits = {}
            seen = self.seen[e]
            for sk, v in allt:
                if seen.get(sk, 0) >= v:
                    continue
                waits[sk] = max(waits.get(sk, 0), v)
            for sk, v in waits.items():
                seen[sk] = v
            self.pending[e].append((None, list(waits.items()), None))
        self.last_w = {}
        self.readers = {}

    def emit(self):
        nc = self.nc

        def mk(e):
            def body(engh):
                for fn, waits, tok in self.pending[e]:
                    for sk, v in waits:
                        engh.wait_ge(self._semh(sk), v)
                    if fn is None:
                        continue
                    ins = fn(engh)
                    sk, v = tok
                    if sk[0] == 'e':
                        ins.then_inc(self.sem[e], 1)
                    else:
                        ins.then_inc(self.dsem[sk[1]], 16)
            return body

        with nc.Block() as blk:
            blk.tensor(mk('pe'))
            blk.scalar(mk('act'))
            blk.vector(mk('dve'))
            blk.gpsimd(mk('pool'))
            blk.sync(mk('sp'))
        self.pending = {e: [] for e in self.ENGS}


PSKEYS = {'ps', 'psb', 'pA', 'pB', 'pO', 'psS'}


def keys(name, *idx_ranges):
    out = [(name,)]
    for r in idx_ranges:
        out = [o + (i,) for o in out for i in r]
    return out


class Ctx:
    pass


def stage_ingest(C):
    nc, S = C.nc, C.S
    with ExitStack() as st:
        xt = [st.enter_context(nc.sbuf_tensor("in_xt%d" % i, [128, 4, D], F32)) for i in range(2)]
        hx = [st.enter_context(nc.sbuf_tensor("in_hx%d" % i, [128, 8, 512], F32)) for i in range(2)]
        ps = [st.enter_context(nc.psum_tensor("in_ps%d" % i, [128, 512], F32)) for i in range(4)]
        hTv = C.hT.rearrange("(c p) t -> p c t", p=128)
        ident = C.ident
        S.op('pool', lambda e: e.memset(hx[1][:, :, 0:64], 0.0), w=keys('hx', [1], range(8)))
        S.op('pool', lambda e: e.dma_start(out=hTv[:, :, T:TP], in_=hx[1][:, :, 0:TP - T]),
             r=keys('hx', [1], range(8)), w=[('hTpad',)], dma=True)
        S.op('sp', lambda e: e.dma_start(out=xt[1][0:NMETA, 0, :], in_=C.meta[:, :]), w=[('xt', 1)], dma=True)
        for c in range(8):
            S.op('pe', lambda e, c=c: e.transpose(ps[c % 4][:, 0:NMETA], xt[1][0:NMETA, 0, c * 128:(c + 1) * 128],
                                                  ident[0:NMETA, 0:NMETA]),
                 r=[('xt', 1)], w=[('ps', c % 4)])
            S.op('dve', lambda e, c=c: e.tensor_copy(out=hx[1][:, c, 0:NMETA], in_=ps[c % 4][:, 0:NMETA]),
                 r=[('ps', c % 4)], w=[('hx', 1, c)])
        S.op('pool', lambda e: e.dma_start(out=hTv[:, :, 0:NMETA], in_=hx[1][:, :, 0:NMETA]),
             r=keys('hx', [1], range(8)), w=[('hTmeta',)], dma=True)
        xv = C.x.rearrange("(g a p) d -> g p a d", a=4, p=128)
        for g in range(SEQ // 512):
            s = g % 2
            S.op('sp', lambda e, g=g, s=s: e.dma_start(out=xt[s][:, :, :], in_=xv[g]), w=[('xt', s)], dma=True)
            for c in range(8):
                b = c % 4
                for a in range(4):
                    S.op('pe', lambda e, a=a, c=c, b=b, s=s: e.transpose(
                        ps[b][:, a * 128:(a + 1) * 128], xt[s][:, a, c * 128:(c + 1) * 128], ident[:, :]),
                        r=[('xt', s)], w=[('ps', b)])
                eng = 'dve' if c % 2 == 0 else 'act'
                if eng == 'dve':
                    S.op('dve', lambda e, c=c, b=b, s=s: e.tensor_copy(out=hx[s][:, c, :], in_=ps[b][:, :]),
                         r=[('ps', b)], w=[('hx', s, c)])
                else:
                    S.op('act', lambda e, c=c, b=b, s=s: e.copy(out=hx[s][:, c, :], in_=ps[b][:, :]),
                         r=[('ps', b)], w=[('hx', s, c)])
            t0 = NMETA + g * 512
            S.op('pool', lambda e, s=s, t0=t0: e.dma_start(out=hTv[:, :, t0:t0 + 512], in_=hx[s][:, :, :]),
                 r=keys('hx', [s], range(8)), w=[('hTin', g)], dma=True)
        S.barrier()
        S.emit()


def stage_egress(C):
    nc, S = C.nc, C.S
    with ExitStack() as st:
        xt = [st.enter_context(nc.sbuf_tensor("eg_xt%d" % i, [128, 4, D], F32)) for i in range(2)]
        hx = [st.enter_context(nc.sbuf_tensor("eg_hx%d" % i, [128, 8, 512], F32)) for i in range(2)]
        ps = [st.enter_context(nc.psum_tensor("eg_ps%d" % i, [128, 512], F32)) for i in range(4)]
        hTv = C.hT.rearrange("(c p) t -> p c t", p=128)
        ov = C.out.rearrange("(g a p) d -> g p a d", a=4, p=128)
        ident = C.ident
        for g in range(SEQ // 512):
            s = g % 2
            t0 = NMETA + g * 512
            S.op('sp', lambda e, s=s, t0=t0: e.dma_start(out=hx[s][:, :, :], in_=hTv[:, :, t0:t0 + 512]),
                 w=[('hx', s)], dma=True)
            for a in range(4):
                for hf in range(2):
                    b = (a * 2 + hf) % 4
                    for cc in range(4):
                        c = hf * 4 + cc
                        S.op('pe', lambda e, a=a, c=c, cc=cc, b=b, s=s: e.transpose(
                            ps[b][:, cc * 128:(cc + 1) * 128], hx[s][:, c, a * 128:(a + 1) * 128], ident[:, :]),
                            r=[('hx', s)], w=[('ps', b)])
                    if hf == 0:
                        S.op('dve', lambda e, a=a, b=b, s=s: e.tensor_copy(out=xt[s][:, a, 0:512], in_=ps[b][:, :]),
                             r=[('ps', b)], w=[('xt', s, a, 0)])
                    else:
                        S.op('act', lambda e, a=a, b=b, s=s: e.copy(out=xt[s][:, a, 512:1024], in_=ps[b][:, :]),
                             r=[('ps', b)], w=[('xt', s, a, 1)])
            S.op('pool', lambda e, g=g, s=s: e.dma_start(out=ov[g], in_=xt[s][:, :, :]),
                 r=keys('xt', [s], range(4), range(2)), w=[('out', g)], dma=True)
        S.barrier()
        S.emit()


def stage_ffn(C, w_in_d, w_out_d, gcol, tag):
    nc, S = C.nc, C.S
    TT = 256
    tiles = [(i * TT, TT) for i in range(TP // TT)]
    if TP % TT:
        tiles.append((TP - TP % TT, TP % TT))
    NJ = DFF // 128
    with ExitStack() as st:
        sb = lambda n, sh, dt: st.enter_context(nc.sbuf_tensor(tag + n, sh, dt))
        w_in = sb("w_in", [128, 8, 2 * DFF], BF16)
        w_out = sb("w_out", [128, NJ, D], BF16)
        x = [sb("x%d" % i, [128, 8, TT], F32) for i in range(2)]
        sq = [sb("sq%d" % i, [128, 8, TT], BF16) for i in range(2)]
        xn = [sb("xn%d" % i, [128, 8, TT], BF16) for i in range(2)]
        hm = [sb("hm%d" % i, [128, NJ, TT], BF16) for i in range(2)]
        sg = [sb("sg%d" % i, [128, TT], F32) for i in range(2)]
        rstd = [sb("rstd%d" % i, [128, TT], F32) for i in range(2)]
        psn = lambda n: st.enter_context(nc.psum_tensor(tag + n, [128, 512], F32))
        psA = [psn("pA%d" % i) for i in range(2)]
        psB = [psn("pB%d" % i) for i in range(2)]
        psO = [psn("pO%d" % i) for i in range(2)]
        psS = psn("pS")
        hTv = C.hT.rearrange("(c p) t -> p c t", p=128)
        w_in_v = w_in_d.rearrange("(k p) n -> p k n", p=128)
        w_out_v = w_out_d.rearrange("(j p) n -> p j n", p=128)
        ones = C.ones_bf
        vec = C.vec

        NB = 4
        cw = 2 * DFF // NB
        for k in range(8):
            for b in range(NB):
                S.op('pool', lambda e, k=k, b=b: e.dma_start(out=w_in[:, k, b * cw:(b + 1) * cw],
                                                             in_=w_in_v[:, k, b * cw:(b + 1) * cw]),
                     w=[('w_in', k, b)], dma=True)
        for j in range(NJ):
            S.op('pool', lambda e, j=j: e.dma_start(out=w_out[:, j, :], in_=w_out_v[:, j, :]),
                 w=[('w_out', j)], dma=True)
        win_keys = keys('w_in', range(8), range(NB))

        def load(i):
            t0, tw = tiles[i]
            s = i % 2
            S.op('sp', lambda e: e.dma_start(out=x[s][:, :, :tw], in_=hTv[:, :, t0:t0 + tw]),
                 r=[('hT', i)], w=keys('x', [s], range(8)), dma=True)

        def norm(i):
            t0, tw = tiles[i]
            s = i % 2
            S.op('act', lambda e: e.activation(out=sq[s][:, :, :tw], in_=x[s][:, :, :tw], func=AF.Square),
                 r=keys('x', [s], range(8)), w=[('sq', s)])
            for c in range(8):
                S.op('pe', lambda e, c=c: e.matmul(psS[:, :tw], lhsT=ones[:, :], rhs=sq[s][:, c, :tw],
                                                   start=(c == 0), stop=(c == 7)),
                     r=[('sq', s)], w=[('psS',)])
            S.op('act', lambda e: e.activation(out=rstd[s][:, :tw], in_=psS[:, :tw], func=AF.Sqrt,
                                               scale=1.0 / D, bias=C.eps_col[:, 0:1]),
                 r=[('psS',)], w=[('rstd', s)])
            S.op('dve', lambda e: e.reciprocal(out=rstd[s][:, :tw], in_=rstd[s][:, :tw]),
                 r=[('rstd', s)], w=[('rstd', s)])
            for c in range(8):
                S.op('dve', lambda e, c=c: e.scalar_tensor_tensor(
                    out=xn[s][:, c, :tw], in0=x[s][:, c, :tw], scalar=vec[:, gcol + c:gcol + c + 1],
                    in1=rstd[s][:, :tw], op0=ALU.mult, op1=ALU.mult),
                    r=[('x', s, c), ('rstd', s)], w=[('xn', s, c)])

        def mm_in(i):
            t0, tw = tiles[i]
            s = i % 2
            for j in range(NJ):
                q = j % 2
                for k in range(8):
                    S.op('pe', lambda e, j=j, k=k, q=q: e.matmul(
                        psA[q][:, :tw], lhsT=w_in[:, k, j * 128:(j + 1) * 128], rhs=xn[s][:, k, :tw],
                        start=(k == 0), stop=(k == 7)),
                        r=[('xn', s, k)] + (win_keys if (i == 0 and j == 0) else []), w=[('pA', q)])
                for k in range(8):
                    S.op('pe', lambda e, j=j, k=k, q=q: e.matmul(
                        psB[q][:, :tw], lhsT=w_in[:, k, DFF + j * 128:DFF + (j + 1) * 128], rhs=xn[s][:, k, :tw],
                        start=(k == 0), stop=(k == 7)),
                        r=[('xn', s, k)], w=[('pB', q)])
                S.op('act', lambda e, q=q: e.activation(out=sg[q][:, :tw], in_=psA[q][:, :tw], func=AF.Silu),
                     r=[('pA', q)], w=[('sg', q)])
                S.op('dve', lambda e, q=q, j=j: e.tensor_tensor(out=hm[s][:, j, :tw], in0=psB[q][:, :tw],
                                                                in1=sg[q][:, :tw], op=ALU.mult),
                     r=[('pB', q), ('sg', q)], w=[('hm', s, j)])

        def mm_out(i):
            t0, tw = tiles[i]
            s = i % 2
            for m in range(8):
                q = m % 2
                for j in range(NJ):
                    S.op('pe', lambda e, j=j, m=m, q=q: e.matmul(
                        psO[q][:, :tw], lhsT=w_out[:, j, m * 128:(m + 1) * 128], rhs=hm[s][:, j, :tw],
                        start=(j == 0), stop=(j == NJ - 1)),
                        r=[('hm', s, j), ('w_out', j)], w=[('pO', q)])
                S.op('dve', lambda e, m=m, q=q: e.scalar_tensor_tensor(
                    out=x[s][:, m, :tw], in0=psO[q][:, :tw], scalar=0.5, in1=x[s][:, m, :tw],
                    op0=ALU.mult, op1=ALU.add),
                    r=[('pO', q), ('x', s, m)], w=[('x', s, m)])
            S.op('pool', lambda e: e.dma_start(out=hTv[:, :, t0:t0 + tw], in_=x[s][:, :, :tw]),
                 r=keys('x', [s], range(8)), w=[('hT', i)], dma=True)

        n = len(tiles)
        load(0)
        norm(0)
        for i in range(n):
            if i + 1 < n:
                load(i + 1)
            mm_in(i)
            if i + 1 < n:
                norm(i + 1)
            mm_out(i)
        S.barrier()
        S.emit()


class H:
    def __init__(self, S):
        self.S = S

    def mm(self, out, lhsT, rhs, start, stop, r, w):
        if USE_F32R and lhsT.dtype == F32 and rhs.dtype == F32:
            lhsT = lhsT.bitcast(F32R)
            rhs = rhs.bitcast(F32R)
        self.S.op('pe', lambda e: e.matmul(out, lhsT=lhsT, rhs=rhs, start=start, stop=stop), r=r, w=w)

    def tr(self, out, in_, ident, r, w):
        self.S.op('pe', lambda e: e.transpose(out, in_, ident), r=r, w=w)

    def act(self, out, in_, func, r, w, scale=None, bias=None):
        kw = {}
        if scale is not None:
            kw['scale'] = scale
        if bias is not None:
            kw['bias'] = bias
        self.S.op('act', lambda e: e.activation(out=out, in_=in_, func=func, **kw), r=r, w=w)

    def cp(self, eng, out, in_, r, w):
        if eng == 'act':
            self.S.op('act', lambda e: e.copy(out=out, in_=in_), r=r, w=w)
        else:
            self.S.op(eng, lambda e: e.tensor_copy(out=out, in_=in_), r=r, w=w)

    def tt(self, eng, out, in0, in1, op, r, w):
        self.S.op(eng, lambda e: e.tensor_tensor(out=out, in0=in0, in1=in1, op=op), r=r, w=w)

    def ts(self, eng, out, in0, s1, s2, op0, op1, r, w):
        if s2 is None:
            self.S.op(eng, lambda e: e.tensor_scalar(out=out, in0=in0, scalar1=s1, scalar2=None, op0=op0), r=r, w=w)
        else:
            self.S.op(eng, lambda e: e.tensor_scalar(out=out, in0=in0, scalar1=s1, scalar2=s2, op0=op0, op1=op1),
                      r=r, w=w)

    def stt(self, eng, out, in0, scalar, in1, op0, op1, r, w):
        self.S.op(eng, lambda e: e.scalar_tensor_tensor(out=out, in0=in0, scalar=scalar, in1=in1, op0=op0, op1=op1),
                  r=r, w=w)

    def red(self, eng, out, in_, op, r, w):
        self.S.op(eng, lambda e: e.tensor_reduce(out=out, in_=in_, axis=AX.X, op=op), r=r, w=w)

    def dma(self, eng, out, in_, r, w):
        self.S.op(eng, lambda e: e.dma_start(out=out, in_=in_), r=r, w=w, dma=True)


def bk(*bs):
    return [('ps', b) for b in bs]


def stage_rwkv(C):
    nc, S = C.nc, C.S
    Hh = H(S)
    mm, tr, act, cp, tt, ts, stt, red, dma = Hh.mm, Hh.tr, Hh.act, Hh.cp, Hh.tt, Hh.ts, Hh.stt, Hh.red, Hh.dma
    CH = 64
    NT = TP // CH
    NH = 16
    cols = C.cols
    c64 = C.cols64
    with ExitStack() as st:
        sb = lambda n, sh, dt=F32: st.enter_context(nc.sbuf_tensor("rs_" + n, sh, dt))
        Wr = sb("Wr", [128, 8, D], BF16)
        Wk = sb("Wk", [128, 8, D], BF16)
        Wv = sb("Wv", [128, 8, D], BF16)
        Wo = sb("Wo", [128, 8, D], BF16)
        w1 = sb("w1", [128, 8, 64], BF16)
        a1 = sb("a1", [128, 8, 64], BF16)
        g1 = sb("g1", [128, 8, 160], BF16)
        a2 = sb("a2", [64, D], BF16)
        g2a = sb("g2a", [128, D], BF16)
        g2b = sb("g2b", [32, D], BF16)
        w2aug = sb("w2aug", [65, D], F32)
        lnxg = sb("lnxg", [64, D], F32)
        lnxb = sb("lnxb", [64, D], F32)
        omk = sb("omk", [64, NH], F32)
        PS = st.enter_context(nc.psum_tensor("rk_PS", [128, 3584], F32))
        PSb = st.enter_context(nc.psum_tensor("rk_PSb", [128, 1024], BF16))
        hbuf = [sb("hbuf%d" % i, [128, 8, CH]) for i in range(2)]
        sq = sb("sq", [128, 8, CH], BF16)
        rstd = sb("rstd", [128, CH])
        hn = sb("hn", [128, 8, CH + 1])
        xx = sb("xx", [128, 8, CH])
        xmf = [sb("xmf%d" % i, [128, 8, CH]) for i in range(1)]
        xm = [sb("xm%d" % i, [128, 8, CH], BF16) for i in range(6)]
        r_ = sb("r", [64, NH, CH])
        k_ = sb("k", [64, NH, CH])
        a_ = sb("a", [64, NH, CH])
        kk = sb("kk", [64, NH, CH])
        b_ = sb("b", [64, NH, CH])
        tmp1 = sb("tmp1", [64, NH, CH])
        tmp2 = sb("tmp2", [64, NH, CH])
        G = sb("G", [64, NH, CH])
        Ghat = sb("Ghat", [64, NH, CH])
        cumC = sb("cumC", [64, NH])
        AR = sb("AR", [64, NH, 2 * CH], BF16)
        Bt = sb("Bt", [64, NH, CH], BF16)
        Kt = sb("Kt", [64, NH, CH], BF16)
        Bh = sb("Bh", [64, NH, CH], TRDT)
        Kh = sb("Kh", [64, NH, CH], TRDT)
        g_tm = sb("g_tm", [64, D], BF16)
        lw_tm = sb("lw_tm", [64, D])
        twT = sb("twT", [65, CH])
        taT = sb("taT", [64, CH], BF16)
        sg0 = sb("sg0", [128, CH], BF16)
        sg1 = sb("sg1", [32, CH], BF16)
        bon = sb("bon", [64, NH])
        MX = sb("MX", [64, NH, 2 * CH], BF16)
        GG = sb("GG", [64, NH, 2 * CH])
        RKT = sb("RKT", [64, NH, CH], BF16)
        LakT = sb("LakT", [64, NH, CH], BF16)
        Hst = sb("Hst", [64, NH, CH])
        st1 = sb("st1", [64, NH])
        st2 = sb("st2", [64, NH])
        zT = sb("zT", [128, 8, CH], BF16)

        yc = sb("yc", [64, NH, CH])
        ysq = a_
        Gs = sb("Gs", [64, NH, CH], BF16)
        Us = sb("Us", [64, NH, CH], BF16)
        BhT = sb("BhT", [64, NH, CH], BF16)
        KhT = sb("KhT", [64, NH, CH], BF16)
        Lm = sb("Lm", [64, NH, CH], BF16)
        RBT = sb("RBT", [64, NH, CH], BF16)
        v_bf = sb("v_bf", [64, D], BF16)
        Hb = sb("Hb", [64, NH, CH], BF16)
        ident_bf = sb("ident_bf", [64, 64], BF16)
        Ginv = GG[:, :, 0:CH]
        Gex = GG[:, :, CH:2 * CH]

        hTv = C.hT.rearrange("(c p) t -> p c t", p=128)
        ident = C.ident
        vec = C.vec
        v64 = C.vec64

        for nm, dst, src in (("Wr", Wr, C.rk['w_r']), ("Wk", Wk, C.rk['w_k']), ("Wv", Wv, C.rk['w_v']),
                             ("Wo", Wo, C.rk['w_o'])):
            v = src.rearrange("(k p) n -> p k n", p=128)
            for k in range(8):
                dma('pool', dst[:, k, :], v[:, k, :], r=[], w=[(nm,)])
        dma('pool', w1[:, :, :], C.rk['w1'].rearrange("(k p) n -> p k n", p=128), r=[], w=[('w1',)])
        dma('pool', a1[:, :, :], C.rk['a1'].rearrange("(k p) n -> p k n", p=128), r=[], w=[('a1',)])
        dma('pool', g1[:, :, :], C.rk['g1'].rearrange("(k p) n -> p k n", p=128), r=[], w=[('g1',)])
        dma('pool', a2[:, :], C.rk['a2'][:, :], r=[], w=[('a2',)])
        dma('pool', g2a[:, :], C.rk['g2'][0:128, :], r=[], w=[('g2',)])
        dma('pool', g2b[:, :], C.rk['g2'][128:160, :], r=[], w=[('g2',)])
        dma('sp', w2aug[0:64, :], C.rk['w2'][:, :], r=[], w=[('w2aug',)])
        dma('sp', w2aug[64:65, :], C.rk['w0'][0:1, :], r=[], w=[('w2aug',)])
        dma('sp', lnxg[:, :], C.rk['lnx_g'][0:1, :].partition_broadcast(64), r=[], w=[('lnxg',)])
        dma('sp', lnxb[:, :], C.rk['lnx_b'][0:1, :].partition_broadcast(64), r=[], w=[('lnxb',)])
        kka = c64['k_a'][0]
        ts('dve', omk[:, :], v64[0:64, kka:kka + NH], -1.0, 1.0, ALU.mult, ALU.add, r=[('c_vec64',)], w=[('omk',)])
        S.op('pool', lambda e: e.memset(Hst[:, :, :], 0.0), w=[('Hst',)])
        S.op('pool', lambda e: e.memset(Hb[:, :, :], 0.0), w=[('Hb',)])
        cp('dve', ident_bf[:, :], ident[0:64, 0:64], r=[('c_ident',)], w=[('ident_bf',)])
        S.op('pool', lambda e: e.memset(hn[:, :, 0:1], 0.0), w=[('hn0',)])
        S.op('pool', lambda e: e.memset(twT[64:65, :], 1.0), w=[('twT1',)])

        def bc(ap2, n=CH):
            return ap2.unsqueeze(2).to_broadcast([64, NH, n])

        def prm(name):
            c0 = c64[name][0]
            return bc(v64[0:64, c0:c0 + NH])

        gcol = cols['norm_g_0_1'][0]
        mixc = cols['rk_mix'][0]
        psv2 = lambda b0: PS[0:64, b0 * 512:b0 * 512 + 2048].rearrange("p (h two s) -> p h two s", h=NH, two=2)
        psv1 = lambda b0: PS[0:64, b0 * 512:b0 * 512 + 1024].rearrange("p (h s) -> p h s", h=NH)
        SU = C.masks[0:64, 0:64].unsqueeze(1).to_broadcast([64, NH, CH])
        IU = C.masks[0:64, 64:128].unsqueeze(1).to_broadcast([64, NH, CH])
        SL = C.masks[0:64, 128:192].unsqueeze(1).to_broadcast([64, NH, CH])
        IDb = ident[0:64, 0:64].unsqueeze(1).to_broadcast([64, NH, CH])
        ones64 = C.ones_f[0:64, 0:64]

        def seg_A(t):
            t0 = t * CH
            hb = hbuf[t % 2]
            dma('sp', hb[:, :, :], hTv[:, :, t0:t0 + CH], r=[('hT', t)], w=[('hbuf', t % 2)])
            act(sq[:, :, :], hb[:, :, :], AF.Square, r=[('hbuf', t % 2)], w=[('sq',)])
            for c in range(8):
                mm(PS[:, 3072:3072 + CH], ones_bf(C)[:, :], sq[:, c, :], c == 0, c == 7, r=[('sq',)], w=bk(6))
            act(rstd[:, :], PS[:, 3072:3072 + CH], AF.Sqrt, r=bk(6), w=[('rstd',)], scale=1.0 / D, bias=C.eps_col[:, 0:1])
            S.op('dve', lambda e: e.reciprocal(out=rstd[:, :], in_=rstd[:, :]), r=[('rstd',)], w=[('rstd',)])
            for c in range(8):
                stt('dve', hn[:, c, 1:CH + 1], hb[:, c, :], vec[:, gcol + c:gcol + c + 1], rstd[:, :],
                    ALU.mult, ALU.mult, r=[('hbuf', t % 2), ('rstd',), ('hn0',)], w=[('hn', c)])
            hnk = keys('hn', range(8))
            tt('pool', xx[:, :, :], hn[:, :, 0:CH], hn[:, :, 1:CH + 1], ALU.subtract, r=hnk + [('hn0',)], w=[('xx',)])
            for i in range(6):
                mixb = vec[:, mixc + i * 8:mixc + i * 8 + 8].unsqueeze(2).to_broadcast([128, 8, CH])
                tt('pool', xmf[0][:, :, :], xx[:, :, :], mixb, ALU.mult, r=[('xx',), ('c_vec',)], w=[('xmf', 0)])
                tt('pool', xm[i][:, :, :], xmf[0][:, :, :], hn[:, :, 1:CH + 1], ALU.add,
                   r=[('xmf', 0)] + hnk, w=keys('xm', [i], range(8)))
            cp('pool', hn[:, :, 0:1], hn[:, :, CH:CH + 1], r=hnk + [('xx',)], w=[('hn0',)])

        def seg_P(t):
            xr, xw, xk, xv, xa, xg = xm
            for (Wt, wn, xs, xi, b0) in ((Wr, 'Wr', xr, 0, 0), (Wk, 'Wk', xk, 2, 2)):
                for h in range(NH):
                    for kc in range(8):
                        mm(PS[0:64, b0 * 512 + h * 64:b0 * 512 + (h + 1) * 64], Wt[:, kc, h * 64:(h + 1) * 64],
                           xs[:, kc, :], kc == 0, kc == 7, r=[('xm', xi, kc), (wn,)], w=bk(b0 + h // 8))

        def seg_E(t):
            t0 = t * CH
            PH = DEBUG_RK[1] if DEBUG_RK else 99
            hb = hbuf[t % 2]
            xr, xw, xk, xv, xa, xg = xm
            cp('act', r_[:, :, :], psv1(0), r=bk(0, 1), w=[('r',)])
            cp('act', k_[:, :, :], psv1(2), r=bk(2, 3), w=[('k',)])
            for n in range(2):
                for kc in range(8):
                    mm(PS[0:64, (4 + n) * 512:(5 + n) * 512], xv[:, kc, :], Wv[:, kc, n * 512:(n + 1) * 512],
                       kc == 0, kc == 7, r=[('xm', 3, kc), ('Wv',)], w=bk(4 + n))
            cp('dve', v_bf[:, :], PS[0:64, 2048:3072], r=bk(4, 5), w=[('v_bf',)])
            for kc in range(8):
                mm(PS[0:64, 3072:3072 + CH], w1[:, kc, :], xw[:, kc, :], kc == 0, kc == 7,
                   r=[('xm', 1, kc), ('w1',)], w=bk(6))
            act(twT[0:64, :], PS[0:64, 3072:3072 + CH], AF.Tanh, r=bk(6), w=[('twT',)])
            for kc in range(8):
                mm(PS[0:64, 0:CH], a1[:, kc, :], xa[:, kc, :], kc == 0, kc == 7,
                   r=[('xm', 4, kc), ('a1',)], w=bk(0))
            cp('dve', taT[:, :], PS[0:64, 0:CH], r=bk(0), w=[('taT',)])
            for kc in range(8):
                mm(PS[:, 3072:3072 + CH], g1[:, kc, 0:128], xg[:, kc, :], kc == 0, kc == 7,
                   r=[('xm', 5, kc), ('g1',)], w=bk(6))
            act(sg0[:, :], PS[:, 3072:3072 + CH], AF.Sigmoid, r=bk(6), w=[('sg0',)])
            for kc in range(8):
                mm(PS[0:32, 512:512 + CH], g1[:, kc, 128:160], xg[:, kc, :], kc == 0, kc == 7,
                   r=[('xm', 5, kc), ('g1',)], w=bk(1))
            act(sg1[:, :], PS[0:32, 512:512 + CH], AF.Sigmoid, r=bk(1), w=[('sg1',)])
            for n in range(2):
                mm(PS[0:64, n * 512:(n + 1) * 512], twT[0:65, :], w2aug[0:65, n * 512:(n + 1) * 512], True, True,
                   r=[('twT',), ('twT1',), ('w2aug',)], w=bk(n))
            act(lw_tm[:, :], PS[0:64, 0:1024], AF.Sigmoid, r=bk(0, 1), w=[('lw_tm',)])
            for h in range(NH):
                mm(PS[0:64, 1024 + h * 128:1024 + (h + 1) * 128], lw_tm[0:64, h * 64:(h + 1) * 64],
                   C.tri[0:64, 0:128], True, True, r=[('lw_tm',), ('c_tri',)], w=bk(2 + h // 4))
            pc = psv2(2)
            cb = bk(2, 3, 4, 5)
            act(G[:, :, :], pc[:, :, 0, :], AF.Exp, r=cb, w=[('G',)])
            act(Ginv[:, :, :], pc[:, :, 0, :], AF.Exp, r=cb, w=[('Ginv',)], scale=-1.0)
            act(Gex[:, :, :], pc[:, :, 1, :], AF.Exp, r=cb, w=[('Gex',)])
            cp('dve', cumC[:, :], pc[:, :, 0, CH - 1], r=cb, w=[('cumC',)])
            tt('dve', tmp1[:, :, :], bc(cumC[:, :]), pc[:, :, 0, :], ALU.subtract, r=cb + [('cumC',)], w=[('tmp1',)])
            act(Ghat[:, :, :], tmp1[:, :, :], AF.Exp, r=[('tmp1',)], w=[('Ghat',)])
            for h in range(NH):
                mm(PS[0:64, 2048 + h * 64:2048 + (h + 1) * 64], a2[0:64, h * 64:(h + 1) * 64], taT[0:64, :], True, True,
                   r=[('taT',), ('a2',)], w=bk(4 + h // 8))
            tt('dve', a_[:, :, :], psv1(4), prm('a0'), ALU.add, r=bk(4, 5) + [('c_vec64',)], w=[('a',)])
            act(a_[:, :, :], a_[:, :, :], AF.Sigmoid, r=[('a',)], w=[('a',)])
            for n in range(2):
                mm(PS[0:64, n * 512:(n + 1) * 512], sg0[:, :], g2a[:, n * 512:(n + 1) * 512], True, False,
                   r=[('sg0',), ('g2',)], w=bk(n))
                mm(PS[0:64, n * 512:(n + 1) * 512], sg1[0:32, :], g2b[0:32, n * 512:(n + 1) * 512], False, True,
                   r=[('sg1',), ('g2',)], w=bk(n))
            cp('act', g_tm[:, :], PS[0:64, 0:1024], r=bk(0, 1), w=[('g_tm',)])
            if PH < -1:
                return
            tt('dve', kk[:, :, :], k_[:, :, :], prm('k_k'), ALU.mult, r=[('k',), ('c_vec64',)], w=[('kk',)])
            act(tmp2[:, :, :], kk[:, :, :], AF.Square, r=[('kk',)], w=[('tmp2',)])
            t2f = tmp2[:, :, :].rearrange("p h s -> p (h s)")
            for n in range(2):
                mm(PS[0:64, 1024 + n * 512:1024 + (n + 1) * 512], ones64, t2f[:, n * 512:(n + 1) * 512], True, True,
                   r=[('tmp2',), ('c_onesf',)], w=bk(2 + n))
            act(tmp2[:, :, :], psv1(2), AF.Ln, r=bk(2, 3), w=[('tmp2',)], bias=C.eps_col[0:64, 2:3])
            act(tmp2[:, :, :], tmp2[:, :, :], AF.Exp, r=[('tmp2',)], w=[('tmp2',)], scale=-0.5)
            tt('dve', kk[:, :, :], kk[:, :, :], tmp2[:, :, :], ALU.mult, r=[('kk',), ('tmp2',)], w=[('kk',)])
            tt('pool', tmp1[:, :, :], a_[:, :, :], prm('k_a'), ALU.mult, r=[('a',), ('c_vec64',)], w=[('tmp1',)])
            tt('pool', tmp1[:, :, :], tmp1[:, :, :], bc(omk[:, :]), ALU.add, r=[('tmp1',), ('omk',)], w=[('tmp1',)])
            tt('pool', k_[:, :, :], k_[:, :, :], tmp1[:, :, :], ALU.mult, r=[('k',), ('tmp1',)], w=[('k',)])
            tt('dve', b_[:, :, :], kk[:, :, :], a_[:, :, :], ALU.mult, r=[('kk',), ('a',)], w=[('b',)])
            stt('dve', AR[:, :, 0:CH], kk[:, :, :], -1.0, Gex[:, :, :], ALU.mult, ALU.mult,
                r=[('kk',), ('Gex',)], w=[('AR0',)])
            tt('pool', AR[:, :, CH:2 * CH], r_[:, :, :], G[:, :, :], ALU.mult, r=[('r',), ('G',)], w=[('AR1',)])
            tt('dve', Bt[:, :, :], b_[:, :, :], Ginv[:, :, :], ALU.mult, r=[('b',), ('Ginv',)], w=[('Bt',)])
            tt('pool', Kt[:, :, :], k_[:, :, :], Ginv[:, :, :], ALU.mult, r=[('k',), ('Ginv',)], w=[('Kt',)])
            tt('dve', Bh[:, :, :], b_[:, :, :], Ghat[:, :, :], ALU.mult, r=[('b',), ('Ghat',)], w=[('Bh',)])
            tt('pool', Kh[:, :, :], k_[:, :, :], Ghat[:, :, :], ALU.mult, r=[('k',), ('Ghat',)], w=[('Kh',)])
            tt('pool', tmp1[:, :, :], r_[:, :, :], prm('r_k'), ALU.mult, r=[('r',), ('c_vec64',)], w=[('tmp1',)])
            tt('pool', tmp1[:, :, :], tmp1[:, :, :], k_[:, :, :], ALU.mult, r=[('tmp1',), ('k',)], w=[('tmp1',)])
            if PH < 1:
                return
            GR = [(0, 8), (8, 8)]

            def hv(ap, g):
                return ap[:, GR[g][0]:GR[g][0] + 8, :]

            def pg2(b0):
                return PS[0:64, b0 * 512:b0 * 512 + 1024].rearrange("p (h two s) -> p h two s", h=8, two=2)

            def pg1(b0):
                return PS[0:64, b0 * 512:b0 * 512 + 512].rearrange("p (h s) -> p h s", h=8)

            SU8 = C.masks[0:64, 0:64].unsqueeze(1).to_broadcast([64, 8, CH])
            IU8 = C.masks[0:64, 64:128].unsqueeze(1).to_broadcast([64, 8, CH])
            SL8 = C.masks[0:64, 128:192].unsqueeze(1).to_broadcast([64, 8, CH])
            ID8 = ident[0:64, 0:64].unsqueeze(1).to_broadcast([64, 8, CH])
            for g in range(2):
                h0 = GR[g][0]
                bA = 0 if g == 0 else 3
                for hh in range(8):
                    h = h0 + hh
                    mm(PS[0:64, bA * 512 + hh * 128:bA * 512 + (hh + 1) * 128], Bt[:, h, :], AR[:, h, :], True, True,
                       r=[('Bt',), ('AR0',), ('AR1',)], w=bk(bA + hh // 4))
                tt('dve', hv(MX[:, :, 0:CH], g), pg2(bA)[:, :, 0, :], SU8, ALU.mult, r=bk(bA, bA + 1) + [('c_masks',)],
                   w=[('MX0', g)])
                tt('dve', hv(RBT, g), pg2(bA)[:, :, 1, :], IU8, ALU.mult, r=bk(bA, bA + 1) + [('c_masks',)],
                   w=[('RBT', g)])
                for hh in range(8):
                    h = h0 + hh
                    mm(PS[0:64, bA * 512 + hh * 128:bA * 512 + (hh + 1) * 128], Kt[:, h, :], AR[:, h, :], True, True,
                       r=[('Kt',), ('AR0',), ('AR1',)], w=bk(bA + hh // 4))
                tt('dve', hv(LakT, g), pg2(bA)[:, :, 0, :], SU8, ALU.mult, r=bk(bA, bA + 1) + [('c_masks',)],
                   w=[('LakT', g)])
                tt('dve', hv(RKT, g), pg2(bA)[:, :, 1, :], IU8, ALU.mult, r=bk(bA, bA + 1) + [('c_masks',)],
                   w=[('RKT', g)])
                for hh in range(8):
                    h = h0 + hh
                    mm(PS[0:64, (bA + 2) * 512 + hh * 64:(bA + 2) * 512 + (hh + 1) * 64], AR[:, h, 0:CH], Bt[:, h, :],
                       True, True, r=[('Bt',), ('AR0',)], w=bk(bA + 2))
                tt('dve', hv(Lm, g), pg1(bA + 2), SL8, ALU.mult, r=bk(bA + 2) + [('c_masks',)], w=[('Lm', g)])
                cp('pool', hv(MX[:, :, CH:2 * CH], g), ID8, r=[('c_ident',)], w=[('MX1', g)])
            if PH < 2:
                return
            for lvl in range(6):
                for g in range(2):
                    h0 = GR[g][0]
                    bA = 0 if g == 0 else 3
                    for hh in range(8):
                        h = h0 + hh
                        mm(PS[0:64, bA * 512 + hh * 128:bA * 512 + (hh + 1) * 128], Lm[:, h, :], MX[:, h, :], True, True,
                           r=[('Lm', g), ('MX0', g), ('MX1', g)], w=bk(bA + hh // 4))
                    if lvl < 5:
                        for hh in range(8):
                            h = h0 + hh
                            mm(PS[0:64, (bA + 2) * 512 + hh * 64:(bA + 2) * 512 + (hh + 1) * 64], MX[:, h, 0:CH], Lm[:, h, :],
                               True, True, r=[('Lm', g), ('MX0', g)], w=bk(bA + 2))
                for g in range(2):
                    bA = 0 if g == 0 else 3
                    tt('dve', hv(MX[:, :, CH:2 * CH], g), pg2(bA)[:, :, 1, :], hv(MX[:, :, CH:2 * CH], g), ALU.add,
                       r=bk(bA, bA + 1) + [('MX1', g)], w=[('MX1', g)])
                    if lvl < 5:
                        cp('act', hv(MX[:, :, 0:CH], g), pg2(bA)[:, :, 0, :], r=bk(bA, bA + 1), w=[('MX0', g)])
                        cp('act', hv(Lm, g), pg1(bA + 2), r=bk(bA + 2), w=[('Lm', g)])
                if lvl == 0 and t + 1 < NTR:
                    seg_A(t + 1)
            if PH < 3:
                return
            for h in range(NH):
                mm(PS[0:64, 3072 + h:3072 + h + 1], tmp1[:, h, :], C.ones_f[0:64, 0:1], True, True,
                   r=[('tmp1',), ('c_onesf',)], w=bk(6))
            cp('dve', bon[:, :], PS[0:64, 3072:3072 + NH], r=bk(6), w=[('bon',)])
            if TRDT == BF16:
                psb3 = PSb[0:64, :].rearrange("p (h s) -> p h s", h=NH)
                for h in range(NH):
                    tr(PSb[0:64, h * 64:(h + 1) * 64], Bh[:, h, :], ident_bf[:, :], r=[('Bh',), ('ident_bf',)], w=[('psb',)])
                cp('act', BhT[:, :, :], psb3, r=[('psb',)], w=[('BhT',)])
                for h in range(NH):
                    tr(PSb[0:64, h * 64:(h + 1) * 64], Kh[:, h, :], ident_bf[:, :], r=[('Kh',), ('ident_bf',)], w=[('psb',)])
                cp('dve', KhT[:, :, :], psb3, r=[('psb',)], w=[('KhT',)])
            else:
                for h in range(NH):
                    tr(PS[0:64, h * 64:(h + 1) * 64], Bh[:, h, :], ident[0:64, 0:64], r=[('Bh',), ('c_ident',)], w=bk(h // 8))
                cp('act', BhT[:, :, :], psv1(0), r=bk(0, 1), w=[('BhT',)])
                for h in range(NH):
                    tr(PS[0:64, 1024 + h * 64:1024 + (h + 1) * 64], Kh[:, h, :], ident[0:64, 0:64],
                       r=[('Kh',), ('c_ident',)], w=bk(2 + h // 8))
                cp('dve', KhT[:, :, :], psv1(2), r=bk(2, 3), w=[('KhT',)])
            if PH < 4:
                return
            for h in range(NH):
                o = PS[0:64, h * 64:(h + 1) * 64]
                mm(o, AR[:, h, 0:CH], Hb[:, h, :], True, False, r=[('AR0',), ('Hb',)], w=bk(h // 8))
                mm(o, LakT[:, h, :], v_bf[:, h * 64:(h + 1) * 64], False, True, r=[('LakT', h // 8), ('v_bf',)], w=bk(h // 8))
            cp('act', Gs[:, :, :], psv1(0), r=bk(0, 1), w=[('Gs',)])
            for h in range(NH):
                mm(PS[0:64, 1024 + h * 64:1024 + (h + 1) * 64], MX[:, h, CH:2 * CH], Gs[:, h, :], True, True,
                   r=[('MX1', h // 8), ('Gs',)], w=bk(2 + h // 8))
            cp('dve', Us[:, :, :], psv1(2), r=bk(2, 3), w=[('Us',)])
            for h in range(NH):
                o = PS[0:64, 2048 + h * 64:2048 + (h + 1) * 64]
                mm(o, AR[:, h, CH:2 * CH], Hb[:, h, :], True, False, r=[('AR1',), ('Hb',)], w=bk(4 + h // 8))
                mm(o, RBT[:, h, :], Us[:, h, :], False, False, r=[('RBT', h // 8), ('Us',)], w=bk(4 + h // 8))
                mm(o, RKT[:, h, :], v_bf[:, h * 64:(h + 1) * 64], False, True, r=[('RKT', h // 8), ('v_bf',)], w=bk(4 + h // 8))
            for h in range(NH):
                o = PS[0:64, h * 64:(h + 1) * 64]
                mm(o, BhT[:, h, :], Us[:, h, :], True, False, r=[('BhT',), ('Us',)], w=bk(h // 8))
                mm(o, KhT[:, h, :], v_bf[:, h * 64:(h + 1) * 64], False, True, r=[('KhT',), ('v_bf',)], w=bk(h // 8))
            tt('dve', Hst[:, :, :], Hst[:, :, :], G[:, :, CH - 1:CH].to_broadcast([64, NH, CH]), ALU.mult,
               r=[('Hst',), ('G',)], w=[('Hst',)])
            tt('dve', Hst[:, :, :], psv1(0), Hst[:, :, :], ALU.add, r=bk(0, 1) + [('Hst',)], w=[('Hst',)])
            cp('act', Hb[:, :, :], Hst[:, :, :], r=[('Hst',)], w=[('Hb',)])
            if PH < 5:
                return

        def seg_H(t):
            t0 = t * CH
            hb = hbuf[t % 2]
            py = psv1(4)
            yb = bk(4, 5)
            red('dve', st1[:, :], py, ALU.add, r=yb, w=[('st1',)])
            ts('dve', st1[:, :], st1[:, :], -1.0 / 64, None, ALU.mult, None, r=[('st1',)], w=[('st1',)])
            tt('dve', yc[:, :, :], py, bc(st1[:, :]), ALU.add, r=yb + [('st1',)], w=[('yc',)])
            act(ysq[:, :, :], yc[:, :, :], AF.Square, r=[('yc',)], w=[('a',)])
            red('dve', st2[:, :], ysq[:, :, :], ALU.add, r=[('a',)], w=[('st2',)])
            act(st2[:, :], st2[:, :], AF.Sqrt, r=[('st2',)], w=[('st2',)], scale=1.0 / 64, bias=C.eps_col[0:64, 1:2])
            S.op('dve', lambda e: e.reciprocal(out=st2[:, :], in_=st2[:, :]), r=[('st2',)], w=[('st2',)])
            tt('dve', yc[:, :, :], yc[:, :, :], bc(st2[:, :]), ALU.mult, r=[('yc',), ('st2',)], w=[('yc',)])
            ycf = yc[:, :, :].rearrange("p h s -> p (h s)")
            tt('pool', ycf, ycf, lnxg[:, :], ALU.mult, r=[('yc',), ('lnxg',)], w=[('yc',)])
            tt('pool', ycf, ycf, lnxb[:, :], ALU.add, r=[('yc',), ('lnxb',)], w=[('yc',)])
            vv = v_bf[:, :].rearrange("p (h s) -> p h s", h=NH)
            tt('dve', ysq[:, :, :], vv, bc(bon[:, :]), ALU.mult, r=[('v_bf',), ('bon',)], w=[('a',)])
            tt('pool', yc[:, :, :], yc[:, :, :], ysq[:, :, :], ALU.add, r=[('yc',), ('a',)], w=[('yc',)])
            tt('dve', ycf, ycf, g_tm[:, :], ALU.mult, r=[('yc',), ('g_tm',)], w=[('yc',)])
            for c in range(8):
                tr(PS[:, 3072 + c * 64:3072 + (c + 1) * 64], yc[:, 2 * c:2 * c + 2, :].rearrange("p h s -> p (h s)"),
                   ident[0:64, 0:64], r=[('yc',), ('c_ident',)], w=bk(6))
            cp('act', zT[:, :, :], PS[:, 3072:3584].rearrange("p (c s) -> p c s", c=8), r=bk(6), w=[('zT',)])
            for co in range(8):
                q = 4 + (co % 2)
                for kc in range(8):
                    mm(PS[:, q * 512:q * 512 + CH], Wo[:, kc, co * 128:(co + 1) * 128], zT[:, kc, :], kc == 0, kc == 7,
                       r=[('zT',), ('Wo',)], w=bk(q))
                tt('dve', hb[:, co, :], PS[:, q * 512:q * 512 + CH], hb[:, co, :], ALU.add,
                   r=bk(q) + [('hbuf', t % 2)], w=[('hbuf', t % 2)])
            dma('pool', hTv[:, :, t0:t0 + CH], hb[:, :, :], r=[('hbuf', t % 2)], w=[('hT', t)])

        NTR = NT if not DEBUG_RK else DEBUG_RK[0]
        if NTR > 0:
            seg_A(0)
            seg_P(0)
        for t in range(NTR):
            seg_E(t)
            if t + 1 < NTR:
                seg_P(t + 1)
            seg_H(t)
        S.barrier()
        S.emit()


def stage_dsa(C):
    nc, S = C.nc, C.S
    Hh = H(S)
    mm, tr, act, cp, tt, ts, stt, red, dma = Hh.mm, Hh.tr, Hh.act, Hh.cp, Hh.tt, Hh.ts, Hh.stt, Hh.red, Hh.dma
    TT = 128
    NQT = (TP + TT - 1) // TT
    NQT_RUN = min(NQT, DEBUG_NQT) if DEBUG_NQT else NQT
    c64 = C.cols64
    cols = C.cols
    NBIS = 15
    MB = 240000.0
    with ExitStack() as st:
        sb = lambda n, sh, dt=F32: st.enter_context(nc.sbuf_tensor("ds_" + n, sh, dt))
        Wq = sb("Wq", [128, 8, 1024], BF16)
        Wk = sb("Wk", [128, 8, 256], BF16)
        Wv = sb("Wv", [128, 8, 256], BF16)
        Wqi = sb("Wqi", [128, 8, 512], BF16)
        Wki = sb("Wki", [128, 8, 64], BF16)
        Wwi = sb("Wwi", [128, 8, 8], BF16)
        Wo = sb("Wo", [128, 8, 1024], BF16)
        kT = sb("kT", [64, 4, TP], BF16)
        Vaug = sb("Vaug", [128, NQT, 4, 65], BF16)
        kiT = sb("kiT", [64, TP], BF16)
        score = sb("score", [128, TP])
        work = sb("work", [128, TP])
        mask01 = sb("mask01", [128, TP], BF16)
        maskT = sb("maskT", [128, NQT, TT], BF16)
        hbuf = [sb("hbuf%d" % i, [128, 8, TT]) for i in range(3)]
        hnb = sb("hnb", [128, 8, TT], BF16)
        sq = sb("sq", [128, 8, TT], BF16)
        rstd = sb("rstd", [128, TT])
        qT = [sb("qT%d" % i, [64, 16 * TT], BF16) for i in range(2)]
        qiT = sb("qiT", [64, 8 * TT], BF16)
        tA = sb("tA", [64, 512])
        tB = sb("tB", [64, 512])
        tC = sb("tC", [64, 512])
        rl = [sb("rl%d" % i, [128, 512]) for i in range(2)]
        PT = [sb("PT%d" % i, [128, 512], BF16) for i in range(3)]
        o_tm = sb("o_tm", [128, 1024], BF16)
        oT = sb("oT", [128, 8, TT], BF16)
        cs = sb("cs", [64, 2, TT])
        wi = sb("wi", [128, 8])
        m8 = sb("m8", [128, 8])
        eq8 = sb("eq8", [128, 8])
        iota8 = sb("iota8", [128, 8])
        lo = sb("lo", [128, 1])
        HC = sb("HC", [128, 2])
        MC = sb("MC", [128, 2])
        sel = sb("sel", [128, 1])
        d1 = sb("d1", [128, 1])
        d2 = sb("d2", [128, 2])
        thr = sb("thr", [128, 1])
        nm1 = sb("nm1", [128, 1])
        halfc = sb("halfc", [128, 1])
        negb = sb("negb", [128, 1])
        rden = sb("rden", [128, 16])
        ident_bf = sb("ident_bf", [128, 128], BF16)
        zeros_bf = sb("zeros_bf", [128, 512], BF16)
        rot = sb("rot", [64, 64])
        negmask = sb("negmask", [128, 128])
        PS = st.enter_context(nc.psum_tensor("ds_PS", [128, 3584], F32))
        PSb = st.enter_context(nc.psum_tensor("ds_PSb", [128, 1024], BF16))
        bank = lambda b: PS[:, b * 512:(b + 1) * 512]

        hTv = C.hT.rearrange("(c p) t -> p c t", p=128)
        ident = C.ident
        vec = C.vec
        v64 = C.vec64
        win = C.at['w_in'].rearrange("(k p) n -> p k n", p=128)
        for k in range(8):
            dma('pool', Wq[:, k, :], win[:, k, 0:1024], r=[], w=[('Wq',)])
        dma('pool', Wk[:, :, :], win[:, :, 1024:1280], r=[], w=[('Wk',)])
        dma('pool', Wv[:, :, :], win[:, :, 1280:1536], r=[], w=[('Wv',)])
        for k in range(8):
            dma('pool', Wqi[:, k, :], win[:, k, 1536:2048], r=[], w=[('Wqi',)])
        dma('pool', Wki[:, :, :], win[:, :, 2048:2112], r=[], w=[('Wki',)])
        dma('pool', Wwi[:, :, :], win[:, :, 2112:2120], r=[], w=[('Wwi',)])
        wov = C.at['w_o'].rearrange("(k p) n -> p k n", p=128)
        for k in range(8):
            dma('pool', Wo[:, k, :], wov[:, k, :], r=[], w=[('Wo',)])
        dma('sp', rot[:, :], C.rot_d[:, :], r=[], w=[('rot',)])
        dma('sp', negmask[:, :], C.negmask_d[:, :], r=[], w=[('negmask',)])
        cp('dve', ident_bf[:, :], ident[:, :], r=[('c_ident',)], w=[('ident_bf',)])
        S.op('pool', lambda e: e.memset(zeros_bf[:, :], 0.0), w=[('zeros_bf',)])
        S.op('pool', lambda e: e.memset(Vaug[:, :, :, 64:65], 1.0), w=[('Vones',)])
        S.op('pool', lambda e: e.memset(halfc[:, :], 0.5), w=[('halfc',)])
        S.op('pool', lambda e: e.memset(negb[:, :], -MB), w=[('negb',)])
        for j in range(8):
            S.op('pool', lambda e, j=j: e.memset(iota8[:, j:j + 1], float(j)), w=[('iota8',)])
        gcol = cols['norm_g_1_1'][0]
        qg = v64[0:64, c64['q_g'][0]:c64['q_g'][0] + 1]
        kg = v64[0:64, c64['k_g'][0]:c64['k_g'][0] + 1]
        kwid = lambda kb: min(128, TP - kb * 128)

        def norm_rope(pb, nh, tw, gcolap, out3, okeys_w):
            n = nh * tw
            pin = bank(pb)[0:64, 0:n]
            v3 = lambda ap: ap.rearrange("p (h s) -> p h s", h=nh)
            if gcolap is not None:
                act(tC[:, 0:n], pin, AF.Copy, r=bk(pb) + [('c_vec64',)], w=[('tC',)], scale=gcolap)
            else:
                cp('act', tC[:, 0:n], pin, r=bk(pb), w=[('tC',)])
            mm(bank(6)[0:64, 0:n], rot[:, :], tC[:, 0:n], True, True, r=[('tC',), ('rot',)], w=bk(6))
            cosb = cs[:, 0, 0:tw].unsqueeze(1).to_broadcast([64, nh, tw])
            sinb = cs[:, 1, 0:tw].unsqueeze(1).to_broadcast([64, nh, tw])
            if gcolap is not None:
                act(tA[:, 0:n], pin, AF.Square, r=bk(pb), w=[('tA',)])
            tt('pool', v3(tC[:, 0:n]), v3(tC[:, 0:n]), cosb, ALU.mult, r=[('tC',), ('cs',)], w=[('tC',)])
            tt('dve', v3(tB[:, 0:n]), v3(bank(6)[0:64, 0:n]), sinb, ALU.mult, r=bk(6) + [('cs',)], w=[('tB',)])
            if gcolap is None:
                tt('pool', out3, v3(tC[:, 0:n]), v3(tB[:, 0:n]), ALU.add, r=[('tC',), ('tB',)], w=okeys_w)
                return
            mm(bank(6)[0:64, 0:n], C.ones_f[0:64, 0:64], tA[:, 0:n], True, True, r=[('tA',), ('c_onesf',)], w=bk(6))
            act(tA[:, 0:n], bank(6)[0:64, 0:n], AF.Sqrt, r=bk(6), w=[('tA',)], scale=1.0 / 64, bias=C.eps_col[0:64, 0:1])
            S.op('dve', lambda e: e.reciprocal(out=tA[:, 0:n], in_=tA[:, 0:n]), r=[('tA',)], w=[('tA',)])
            tt('pool', tC[:, 0:n], tC[:, 0:n], tB[:, 0:n], ALU.add, r=[('tC',), ('tB',)], w=[('tC',)])
            tt('dve', out3, v3(tC[:, 0:n]), v3(tA[:, 0:n]), ALU.mult, r=[('tC',), ('tA',)], w=okeys_w)

        def XA(qt):
            s = qt % 2
            s3 = qt % 3
            t0 = qt * TT
            tw = min(TT, TP - t0)
            n = t0 + tw
            nkb = qt + 1
            hb = hbuf[s3]
            dma('sp', hb[:, :, 0:tw], hTv[:, :, t0:t0 + tw], r=[('hT', qt)], w=[('hbuf', s3)])
            dma('sp', cs[:, :, 0:tw], C.rope_d[:, :, t0:t0 + tw], r=[], w=[('cs',)])
            act(sq[:, :, 0:tw], hb[:, :, 0:tw], AF.Square, r=[('hbuf', s3)], w=[('sq',)])
            for c in range(8):
                mm(bank(6)[:, 0:tw], C.ones_bf[:, :], sq[:, c, 0:tw], c == 0, c == 7, r=[('sq',)], w=bk(6))
            act(rstd[:, 0:tw], bank(6)[:, 0:tw], AF.Sqrt, r=bk(6), w=[('rstd',)], scale=1.0 / D, bias=C.eps_col[:, 0:1])
            S.op('dve', lambda e: e.reciprocal(out=rstd[:, 0:tw], in_=rstd[:, 0:tw]), r=[('rstd',)], w=[('rstd',)])
            for c in range(8):
                stt('dve', hnb[:, c, 0:tw], hb[:, c, 0:tw], vec[:, gcol + c:gcol + c + 1], rstd[:, 0:tw],
                    ALU.mult, ALU.mult, r=[('hbuf', s3), ('rstd',)], w=[('hnb',)])

        def XR(qt):
            s = qt % 2
            t0 = qt * TT
            tw = min(TT, TP - t0)
            n = t0 + tw
            nkb = qt + 1
            for g in range(4):
                for kc in range(8):
                    mm(bank(5)[0:64, g * tw:(g + 1) * tw], Wk[:, kc, g * 64:(g + 1) * 64], hnb[:, kc, 0:tw], kc == 0, kc == 7,
                       r=[('hnb',), ('Wk',)], w=bk(5))
            norm_rope(5, 4, tw, kg, kT[:, :, t0:t0 + tw], [('kT',)])
            for kc in range(8):
                mm(bank(5)[0:tw, 0:256], hnb[:, kc, 0:tw], Wv[:, kc, :], kc == 0, kc == 7, r=[('hnb',), ('Wv',)], w=bk(5))
            cp('act', Vaug[0:tw, qt, :, 0:64], bank(5)[0:tw, 0:256].rearrange("p (g d) -> p g d", g=4), r=bk(5),
               w=[('Vaug',)])
            for kc in range(8):
                mm(bank(5)[0:64, 0:tw], Wki[:, kc, :], hnb[:, kc, 0:tw], kc == 0, kc == 7, r=[('hnb',), ('Wki',)], w=bk(5))
            norm_rope(5, 1, tw, None, kiT[:, t0:t0 + tw].unsqueeze(1), [('kiT',)])
            for grp in range(4):
                pb = 5
                for hh in range(4):
                    h = grp * 4 + hh
                    for kc in range(8):
                        mm(bank(pb)[0:64, hh * tw:(hh + 1) * tw], Wq[:, kc, h * 64:(h + 1) * 64], hnb[:, kc, 0:tw],
                           kc == 0, kc == 7, r=[('hnb',), ('Wq',)], w=bk(pb))
                norm_rope(pb, 4, tw, qg, qT[s][:, grp * 4 * tw:(grp + 1) * 4 * tw].rearrange("p (h s) -> p h s", h=4),
                          [('qT', s)])
            for grp in range(2):
                pb = 5
                for hh in range(4):
                    h = grp * 4 + hh
                    for kc in range(8):
                        mm(bank(pb)[0:64, hh * tw:(hh + 1) * tw], Wqi[:, kc, h * 64:(h + 1) * 64], hnb[:, kc, 0:tw],
                           kc == 0, kc == 7, r=[('hnb',), ('Wqi',)], w=bk(pb))
                norm_rope(pb, 4, tw, None, qiT[:, grp * 4 * tw:(grp + 1) * 4 * tw].rearrange("p (h s) -> p h s", h=4),
                          [('qiT',)])
            for kc in range(8):
                mm(bank(6)[0:tw, 0:8], hnb[:, kc, 0:tw], Wwi[:, kc, :], kc == 0, kc == 7, r=[('hnb',), ('Wwi',)], w=bk(6))
            ts('dve', wi[0:tw, :], bank(6)[0:tw, 0:8], float(512.0 ** -0.5), None, ALU.mult, None, r=bk(6), w=[('wi',)])
            idx = 0
            for k0 in range(0, n, 512):
                nk = min(512, n - k0)
                for h in range(8):
                    pb = 5 + idx % 2
                    rb = rl[idx % 2]
                    rk = ('rl', idx % 2)
                    idx += 1
                    mm(bank(pb)[0:tw, 0:nk], qiT[:, h * tw:(h + 1) * tw], kiT[:, k0:k0 + nk], True, True,
                       r=[('qiT',), ('kiT',)], w=bk(pb))
                    act(rb[0:tw, 0:nk], bank(pb)[0:tw, 0:nk], AF.Relu, r=bk(pb), w=[rk])
                    if h == 0:
                        ts('dve', score[0:tw, k0:k0 + nk], rb[0:tw, 0:nk], wi[0:tw, 0:1], None, ALU.mult, None,
                           r=[rk, ('wi',)], w=[('score',)])
                    else:
                        stt('dve', score[0:tw, k0:k0 + nk], rb[0:tw, 0:nk], wi[0:tw, h:h + 1], score[0:tw, k0:k0 + nk],
                            ALU.mult, ALU.add, r=[rk, ('wi',), ('score',)], w=[('score',)])
            sc = score[0:tw, 0:n]
            if n > 256:
                red('dve', HC[0:tw, 0:1], sc, ALU.max, r=[('score',)], w=[('HC',)])
                red('dve', lo[0:tw, :], sc, ALU.min, r=[('score',)], w=[('lo',)])
            tt('dve', score[0:tw, t0:t0 + tw], score[0:tw, t0:t0 + tw], negmask[0:tw, 0:tw], ALU.add,
               r=[('score',), ('negmask',)], w=[('score',)])
            if n > 256:
                tt('dve', d1[0:tw, :], HC[0:tw, 0:1], lo[0:tw, :], ALU.subtract, r=[('HC',), ('lo',)], w=[('d1',)])
                stt('dve', HC[0:tw, 0:1], d1[0:tw, :], 1.0e-6, HC[0:tw, 0:1], ALU.mult, ALU.add, r=[('d1',), ('HC',)],
                    w=[('HC',)])
                ts('dve', HC[0:tw, 1:2], d1[0:tw, :], 0.0, None, ALU.mult, None, r=[('d1',), ('HC',)], w=[('HC',)])
                for it in range(NBIS):
                    stt('dve', MC[0:tw, 0:1], lo[0:tw, :], HC[0:tw, 0:1], halfc[0:tw, :], ALU.add, ALU.mult,
                        r=[('lo',), ('HC',), ('halfc',)], w=[('MC',)])
                    S.op('dve', lambda e, tw=tw, n=n: e.tensor_scalar(
                        out=mask01[0:tw, 0:n], in0=score[0:tw, 0:n], scalar1=MC[0:tw, 0:1], scalar2=0.0,
                        op0=ALU.is_ge, op1=ALU.add, accum_out=MC[0:tw, 1:2]),
                        r=[('score',), ('MC',)], w=[('mask01',), ('MC',)])
                    ts('dve', sel[0:tw, :], MC[0:tw, 1:2], 256.0, None, ALU.is_ge, None, r=[('MC',)], w=[('sel',)])
                    tt('dve', d1[0:tw, :], MC[0:tw, 0:1], lo[0:tw, :], ALU.subtract, r=[('MC',), ('lo',)], w=[('d1',)])
                    stt('dve', lo[0:tw, :], d1[0:tw, :], sel[0:tw, 0:1], lo[0:tw, :], ALU.mult, ALU.add,
                        r=[('d1',), ('sel',), ('lo',)], w=[('lo',)])
                    tt('dve', d2[0:tw, :], HC[0:tw, :], MC[0:tw, :], ALU.subtract, r=[('MC',), ('HC',)], w=[('d2',)])
                    stt('dve', HC[0:tw, :], d2[0:tw, :], sel[0:tw, 0:1], MC[0:tw, :], ALU.mult, ALU.add,
                        r=[('d2',), ('sel',), ('MC',)], w=[('HC',)])
                wk = work[0:tw, 0:n]
                ts('dve', wk, sc, HC[0:tw, 0:1], 1.0e20, ALU.is_ge, ALU.mult, r=[('score',), ('HC',)], w=[('work',)])
                tt('dve', wk, sc, wk, ALU.subtract, r=[('score',), ('work',)], w=[('work',)])
                S.op('dve', lambda e, tw=tw, n=n: e.max(out=m8[0:tw, :], in_=work[0:tw, 0:n]), r=[('work',)], w=[('m8',)])
                ts('dve', nm1[0:tw, :], HC[0:tw, 1:2], -1.0, 255.0, ALU.mult, ALU.add, r=[('HC',)], w=[('nm1',)])
                ts('dve', nm1[0:tw, :], nm1[0:tw, :], 7.0, 0.0, ALU.min, ALU.max, r=[('nm1',)], w=[('nm1',)])
                ts('dve', eq8[0:tw, :], iota8[0:tw, :], nm1[0:tw, 0:1], None, ALU.is_equal, None, r=[('nm1',), ('iota8',)],
                   w=[('eq8',)])
                tt('dve', eq8[0:tw, :], eq8[0:tw, :], m8[0:tw, :], ALU.mult, r=[('eq8',), ('m8',)], w=[('eq8',)])
                red('dve', thr[0:tw, :], eq8[0:tw, :], ALU.add, r=[('eq8',)], w=[('thr',)])
                ts('dve', mask01[0:tw, 0:n], sc, thr[0:tw, 0:1], None, ALU.is_ge, None, r=[('score',), ('thr',)],
                   w=[('mask01',)])
            else:
                ts('dve', mask01[0:tw, 0:n], sc, -1.0e29, None, ALU.is_ge, None, r=[('score',)], w=[('mask01',)])
        def X2(qt):
            t0 = qt * TT
            tw = min(TT, TP - t0)
            nkb = qt + 1
            for kb0 in range(0, nkb, 8):
                nb = min(8, nkb - kb0)
                for j in range(nb):
                    kb = kb0 + j
                    kw = kwid(kb)
                    tr(PSb[0:kw, j * 128:j * 128 + tw], mask01[0:tw, kb * 128:kb * 128 + kw], ident_bf[0:tw, 0:tw],
                       r=[('mask01',), ('ident_bf',)], w=[('psb',)])
                kwl = kwid(kb0 + nb - 1)
                nfull = nb if kwl == 128 else nb - 1
                if nfull > 0:
                    act(maskT[:, kb0:kb0 + nfull, 0:tw],
                        PSb[:, 0:nfull * 128].rearrange("p (j s) -> p j s", j=nfull)[:, :, 0:tw], AF.Identity,
                        r=[('psb',), ('negb',)], w=[('maskT', kb) for kb in range(kb0, kb0 + nfull)],
                        scale=MB, bias=negb[:, 0:1])
                if nfull < nb:
                    act(maskT[0:kwl, kb0 + nb - 1, 0:tw], PSb[0:kwl, (nb - 1) * 128:(nb - 1) * 128 + tw], AF.Identity,
                        r=[('psb',), ('negb',)], w=[('maskT', kb0 + nb - 1)], scale=MB, bias=negb[0:kwl, 0:1])

        def Y(qt):
            s = qt % 2
            t0 = qt * TT
            tw = min(TT, TP - t0)
            nkb = qt + 1
            hb = hbuf[qt % 3]
            q_ = qT[s]
            OB = [(0, 0, 7), (1, 7, 7), (2, 14, 2)]
            for (ob, h0, nh) in OB:
                S.op('pe', lambda e, ob=ob, nh=nh: e.matmul(bank(ob)[0:tw, 0:nh * 65], lhsT=zeros_bf[:, 0:tw],
                                                          rhs=zeros_bf[:, 0:nh * 65], start=True, stop=False,
                                                          skip_group_check=True),
                     r=[('zeros_bf',)], w=bk(ob))
            jobs = [(kb, g) for kb in range(nkb) for g in range(4)]

            def s_part(i):
                kb, g = jobs[i]
                kw = kwid(kb)
                pb = 3 + i % 2
                pt = PT[i % 3]
                pk = ('PT', i % 3)
                mbias = maskT[0:kw, kb, 0:tw].unsqueeze(1).to_broadcast([kw, 4, tw])
                mm(bank(pb)[0:kw, 0:4 * tw], kT[:, g, kb * 128:kb * 128 + kw], q_[:, g * 4 * tw:(g + 1) * 4 * tw],
                   True, False, r=[('kT',), ('qT', s)], w=bk(pb))
                mm(bank(pb)[0:kw, 0:4 * tw].rearrange("p (h s) -> p h s", h=4), ident_bf[0:kw, 0:kw], mbias,
                   False, True, r=[('maskT', kb), ('ident_bf',)], w=bk(pb))
                act(pt[0:kw, 0:4 * tw], bank(pb)[0:kw, 0:4 * tw], AF.Exp, r=bk(pb), w=[pk], scale=0.125)

            def v_part(i):
                kb, g = jobs[i]
                kw = kwid(kb)
                pt = PT[i % 3]
                pk = ('PT', i % 3)
                for rr in range(4):
                    h = 4 * g + rr
                    S.op('pe', lambda e, h=h, rr=rr, kw=kw, pt=pt, kb=kb, g=g, last=(kb == nkb - 1): e.matmul(
                        bank(h // 7)[0:tw, (h % 7) * 65:(h % 7 + 1) * 65], lhsT=pt[0:kw, rr * tw:(rr + 1) * tw],
                        rhs=Vaug[0:kw, kb, g, :], start=False, stop=last, skip_group_check=True),
                        r=[pk, ('Vaug',), ('Vones',)], w=bk(h // 7))

            s_part(0)
            for i in range(len(jobs)):
                if i + 1 < len(jobs):
                    s_part(i + 1)
                v_part(i)
            for (ob, h0, nh) in OB:
                o3 = bank(ob)[0:tw, 0:nh * 65].rearrange("p (h d) -> p h d", h=nh)
                S.op('dve', lambda e, o3=o3, h0=h0, nh=nh: e.reciprocal(out=rden[0:tw, h0:h0 + nh], in_=o3[:, :, 64]),
                     r=bk(ob), w=[('rden', ob)])
                tt('dve', o_tm[0:tw, h0 * 64:(h0 + nh) * 64].rearrange("p (h d) -> p h d", h=nh), o3[:, :, 0:64],
                   rden[0:tw, h0:h0 + nh].unsqueeze(2).to_broadcast([tw, nh, 64]), ALU.mult,
                   r=bk(ob) + [('rden', ob)], w=[('o_tm',)])
            for c in range(8):
                tr(PSb[:, c * 128:c * 128 + tw], o_tm[0:tw, c * 128:(c + 1) * 128], ident_bf[0:tw, 0:tw],
                   r=[('o_tm',), ('ident_bf',)], w=[('psb',)])
            cp('act', oT[:, :, 0:tw], PSb[:, :].rearrange("p (c s) -> p c s", c=8)[:, :, 0:tw],
               r=[('psb',)], w=[('oT',)])
            for co in range(8):
                pb = 3 + co % 2
                for kc in range(8):
                    mm(bank(pb)[:, 0:tw], Wo[:, kc, co * 128:(co + 1) * 128], oT[:, kc, 0:tw], kc == 0, kc == 7,
                       r=[('oT',), ('Wo',)], w=bk(pb))
                tt('dve', hb[:, co, 0:tw], bank(pb)[:, 0:tw], hb[:, co, 0:tw], ALU.add, r=bk(pb) + [('hbuf', qt % 3)],
                   w=[('hbuf', qt % 3)])
            dma('pool', hTv[:, :, t0:t0 + tw], hb[:, :, 0:tw], r=[('hbuf', qt % 3)], w=[('hT', qt)])

        XA(0)
        XR(0)
        X2(0)
        if NQT_RUN > 1:
            XA(1)
        for qt in range(NQT_RUN):
            if qt + 1 < NQT_RUN:
                S.capture()
                XR(qt + 1)
                la = S.end_capture()
                S.capture()
                Y(qt)
                lb = S.end_capture()
                S.replay_merged(la, lb, frac=MERGE_FRAC)
                if qt + 2 < NQT_RUN:
                    XA(qt + 2)
                X2(qt + 1)
            else:
                Y(qt)
        S.barrier()
        S.emit()


def ones_bf(C):
    return C.ones_bf

VEC_COLS = {}


def _vec_layout():
    cols = {}
    c = 0
    for l in range(2):
        for j in range(3):
            cols['norm_g_%d_%d' % (l, j)] = (c, 8)
            c += 8
    cols['rk_mix'] = (c, 48)
    c += 48
    return cols, c


def _vec64_layout():
    cols = {}
    c = 0
    for n in ('k_k', 'k_a', 'a0', 'r_k'):
        cols[n] = (c, 16)
        c += 16
    for n in ('q_g', 'k_g'):
        cols[n] = (c, 1)
        c += 1
    return cols, c


RK_SHAPES = {'w_r': [D, D], 'w_k': [D, D], 'w_v': [D, D], 'w_o': [D, D], 'w0': [1, D], 'w1': [D, 64], 'w2': [64, D],
             'a1': [D, 64], 'a2': [64, D], 'g1': [D, 160], 'g2': [160, D], 'lnx_g': [1, D], 'lnx_b': [1, D]}


def build_program(stages):
    nc = bass.Bass("TRN2", target_bir_lowering=False)
    C = Ctx()
    C.nc = nc
    din = lambda n, sh, dt=F32: nc.dram_tensor(n, list(sh), dt, kind="ExternalInput").ap()
    C.x = din("x", [SEQ, D])
    C.meta = din("meta", [NMETA, D])
    C.ffn_w_in = din("ffn_w_in", [2, 2, D, 2 * DFF])
    C.ffn_w_out = din("ffn_w_out", [2, 2, DFF, D])
    C.rk = {k: din("rk_" + k, sh) for k, sh in RK_SHAPES.items()}
    cols, nv = _vec_layout()
    cols64, nv64 = _vec64_layout()
    C.cols, C.cols64 = cols, cols64
    C.vec_d = din("vecs", [128, nv])
    C.vec64_d = din("vecs64", [64, nv64])
    C.ident_d = din("ident", [128, 128])
    C.masks_d = din("masks", [64, 192])
    C.tri_d = din("tri", [64, 128])
    C.at = {'w_in': din("at_w_in", [D, 2120]), 'w_o': din("at_w_o", [D, D])}
    C.rope_d = din("rope", [64, 2, TP])
    C.rot_d = din("rot", [64, 64])
    C.negmask_d = din("negmask", [128, 128])
    C.out = nc.dram_tensor("out", [SEQ, D], F32, kind="ExternalOutput").ap()
    C.hT = nc.dram_tensor("hT_scratch", [D, TP], F32, kind="Internal").ap()
    with ExitStack() as es:
        S = Sched(nc, es)
        C.S = S
        C.vec = es.enter_context(nc.sbuf_tensor("c_vec", [128, nv], F32))
        C.vec64 = es.enter_context(nc.sbuf_tensor("c_vec64", [64, nv64], F32))
        C.ident = es.enter_context(nc.sbuf_tensor("c_ident", [128, 128], F32))
        C.masks = es.enter_context(nc.sbuf_tensor("c_masks", [64, 192], F32))
        C.tri = es.enter_context(nc.sbuf_tensor("c_tri", [64, 128], F32))
        C.ones_bf = es.enter_context(nc.sbuf_tensor("c_ones_bf", [128, 128], BF16))
        C.ones_f = es.enter_context(nc.sbuf_tensor("c_ones_f", [128, 128], F32))
        C.eps_col = es.enter_context(nc.sbuf_tensor("c_eps", [128, 3], F32))
        S.op('sp', lambda e: e.dma_start(out=C.vec[:, :], in_=C.vec_d[:, :]), w=[('c_vec',)], dma=True)
        S.op('sp', lambda e: e.dma_start(out=C.vec64[:, :], in_=C.vec64_d[:, :]), w=[('c_vec64',)], dma=True)
        S.op('sp', lambda e: e.dma_start(out=C.ident[:, :], in_=C.ident_d[:, :]), w=[('c_ident',)], dma=True)
        S.op('sp', lambda e: e.dma_start(out=C.masks[:, :], in_=C.masks_d[:, :]), w=[('c_masks',)], dma=True)
        S.op('sp', lambda e: e.dma_start(out=C.tri[:, :], in_=C.tri_d[:, :]), w=[('c_tri',)], dma=True)
        S.op('pool', lambda e: e.memset(C.ones_bf[:, :], 1.0), w=[('c_ones',)])
        S.op('pool', lambda e: e.memset(C.ones_f[:, :], 1.0), w=[('c_onesf',)])
        S.op('pool', lambda e: e.memset(C.eps_col[:, 0:1], 1e-6), w=[('c_eps',)])
        S.op('pool', lambda e: e.memset(C.eps_col[:, 1:2], 64e-5), w=[('c_eps2',)])
        S.op('pool', lambda e: e.memset(C.eps_col[:, 2:3], 1e-24), w=[('c_eps3',)])
        S.barrier()
        stage_ingest(C)
        for sname in stages:
            if sname.startswith('ffn'):
                l, j = int(sname[3]), int(sname[4])
                stage_ffn(C, C.ffn_w_in[l, j], C.ffn_w_out[l, j], cols['norm_g_%d_%d' % (l, 0 if j == 0 else 2)][0],
                          "f%d%d_" % (l, j))
            elif sname == 'rwkv':
                stage_rwkv(C)
            elif sname == 'dsa':
                stage_dsa(C)
        stage_egress(C)
    return nc


def host_consts(inputs):
    cols, nv = _vec_layout()
    cols64, nv64 = _vec64_layout()
    vec = np.zeros((128, nv), np.float32)
    vec64 = np.zeros((64, nv64), np.float32)

    def put(name, v):
        c0, n = cols[name]
        vec[:, c0:c0 + n] = np.asarray(v, np.float32).reshape(n, 128).T

    def put64(name, v):
        c0, n = cols64[name]
        vec64[:, c0:c0 + n] = np.asarray(v, np.float32).reshape(n, 64).T

    ng = np.asarray(inputs['norm_g'])
    for l in range(2):
        for j in range(3):
            put('norm_g_%d_%d' % (l, j), ng[l, j])
    put('rk_mix', np.asarray(inputs['rk_mix'])[0].reshape(-1))
    put64('k_k', inputs['rk_k_k'][0])
    put64('k_a', inputs['rk_k_a'][0])
    put64('a0', inputs['rk_a0'][0])
    put64('r_k', np.asarray(inputs['rk_r_k'])[0].reshape(-1))
    vec64[:, cols64['q_g'][0]] = np.asarray(inputs['at_q_g'], np.float32)[0]
    vec64[:, cols64['k_g'][0]] = np.asarray(inputs['at_k_g'], np.float32)[0]
    inv = (np.float32(500000.0) ** (-np.arange(0, 16, 2, dtype=np.float32) / np.float32(16))).astype(np.float32)
    ang = (np.arange(TP, dtype=np.float32)[:, None] * inv[None, :]).astype(np.float32)
    rope = np.zeros((64, 2, TP), np.float32)
    rope[:, 0, :] = 1.0
    rope[0:8, 0, :] = np.cos(ang).T
    rope[8:16, 0, :] = np.cos(ang).T
    rope[0:8, 1, :] = np.sin(ang).T
    rope[8:16, 1, :] = np.sin(ang).T
    rot = np.zeros((64, 64), np.float32)
    for d in range(8):
        rot[d + 8, d] = -1.0
        rot[d, d + 8] = 1.0
    i128 = np.arange(128)
    negmask = np.where(i128[None, :] <= i128[:, None], 0.0, -1.0e30).astype(np.float32)
    ii = np.arange(64)
    su = (ii[:, None] < ii[None, :]).astype(np.float32)
    iu = (ii[:, None] <= ii[None, :]).astype(np.float32)
    sl = (ii[:, None] > ii[None, :]).astype(np.float32)
    masks = np.concatenate([su, iu, sl], axis=1)
    cdec = np.float32(-np.exp(-0.5))
    tri = np.concatenate([iu, su], axis=1) * cdec
    out = {"vecs": vec, "vecs64": vec64, "ident": np.eye(128, dtype=np.float32), "masks": masks,
           "tri": tri.astype(np.float32), "rope": rope, "rot": rot, "negmask": negmask,
           "at_w_in": np.ascontiguousarray(np.asarray(inputs['at_w_in'], np.float32)[0]),
           "at_w_o": np.ascontiguousarray(np.asarray(inputs['at_w_o'], np.float32)[0])}
    for k in RK_SHAPES:
        out["rk_" + k] = np.ascontiguousarray(np.asarray(inputs["rk_" + k], np.float32)[0].reshape(RK_SHAPES[k]))
    return out


ALL_STAGES = ['ffn00', 'rwkv', 'ffn01', 'ffn10', 'dsa', 'ffn11']
_cache = {}


def run(inputs, stages, cores=NCORES, trace=False):
    key = tuple(stages)
    if key not in _cache:
        _cache[key] = build_program(stages)
    nc = _cache[key]
    consts = host_consts(inputs)
    x = np.asarray(inputs['x'], np.float32)
    shared = {
        "meta": np.ascontiguousarray(np.asarray(inputs['meta'], np.float32)),
        "ffn_w_in": np.ascontiguousarray(np.asarray(inputs['ffn_w_in'], np.float32)),
        "ffn_w_out": np.ascontiguousarray(np.asarray(inputs['ffn_w_out'], np.float32)),
    }
    shared.update(consts)
    in_maps = []
    for b in range(cores):
        m = dict(shared)
        m["x"] = np.ascontiguousarray(x[b])
        in_maps.append(m)
    res = run_bass_kernel_spmd(nc, in_maps, core_ids=list(range(cores)), trace=trace)
    out = np.stack([np.asarray(r["out"], np.float32) for r in res.results], axis=0)
    return out, res


def kernel(**inputs):
    out, _ = run(inputs, ALL_STAGES)
    return out
```

```python
import numpy as np
from contextlib import ExitStack
import concourse.bass as bass
import concourse.mybir as mybir
from concourse.bass_utils import run_bass_kernel_spmd

F32 = mybir.dt.float32
BF16 = mybir.dt.bfloat16
F32R = mybir.dt.float32r
USE_F32R = False
IDT = BF16
TRDT = F32
ALU = mybir.AluOpType
AF = mybir.ActivationFunctionType
AX = mybir.AxisListType

D = 1024
NMETA = 16
SEQ = 4096
T = SEQ + NMETA
TP = 4160
DFF = 2816
NCORES = 8
DEBUG_NQT = 0
MERGE_FRAC = 0.03
NOVBF = False
DEBUG_RK = None


class Sched:
    ENGS = ['pe', 'act', 'dve', 'pool', 'sp']

    def __init__(self, nc, es, nds=16):
        self.nc = nc
        self.sem = {e: es.enter_context(nc.semaphore("s_" + e)) for e in self.ENGS}
        self.cnt = {e: 0 for e in self.ENGS}
        self.NDS = nds
        self.dq = {}
        self.dsem = []
        self.dcnt = []
        self.dtok = []
        for q in ('sp', 'pool', 'act'):
            base = len(self.dsem)
            for i in range(nds):
                self.dsem.append(es.enter_context(nc.semaphore("d_%s%d" % (q, i))))
                self.dcnt.append(0)
                self.dtok.append(None)
            self.dq[q] = [base, 0]
        self.pending = {e: [] for e in self.ENGS}
        self.last_w = {}
        self.readers = {}
        self.seen = {e: {} for e in self.ENGS}
        self.nops = 0

    def _semh(self, sk):
        return self.sem[sk[1]] if sk[0] == 'e' else self.dsem[sk[1]]

    def capture(self):
        self._cap = []
        return self._cap

    def end_capture(self):
        c = self._cap
        self._cap = None
        return c

    def replay_merged(self, a, b, frac=1.0):
        na, nb = len(a), len(b)
        nbe = max(1, int(nb * frac))
        ia = ib = 0
        while ia < na or ib < nb:
            if ib >= nb or (ia < na and ia * nbe <= ib * na):
                self.op(*a[ia])
                ia += 1
            else:
                self.op(*b[ib])
                ib += 1

    def op(self, eng, fn, r=(), w=(), dma=False):
        if getattr(self, '_cap', None) is not None:
            self._cap.append((eng, fn, tuple(r), tuple(w), dma))
            return None
        pr = [k for k in r if k[0] in PSKEYS and k not in w]
        if pr:
            w = list(w) + pr
        deps = []
        for k in r:
            if k in self.last_w:
                deps.append(self.last_w[k])
        for k in w:
            if k in self.last_w:
                deps.append(self.last_w[k])
            rd = self.readers.get(k)
            if rd:
                deps.extend(rd.items())
        if dma:
            qd = self.dq[eng]
            i = qd[0] + qd[1] % self.NDS
            qd[1] += 1
            if self.dtok[i] is not None:
                deps.append(self.dtok[i])
            self.dcnt[i] += 16
            tok = (('d', i), self.dcnt[i])
            self.dtok[i] = tok
        else:
            self.cnt[eng] += 1
            tok = (('e', eng), self.cnt[eng])
        waits = {}
        seen = self.seen[eng]
        for (sk, v) in deps:
            if eng == 'pe' and sk == ('e', 'pe'):
                continue
            if seen.get(sk, 0) >= v:
                continue
            if waits.get(sk, 0) < v:
                waits[sk] = v
        for sk, v in waits.items():
            seen[sk] = v
        self.pending[eng].append((fn, list(waits.items()), tok))
        self.nops += 1
        for k in w:
            self.last_w[k] = tok
            self.readers[k] = {}
        ws = set(w)
        for k in r:
            if k in ws:
                continue
            rd = self.readers.setdefault(k, {})
            if rd.get(tok[0], 0) < tok[1]:
                rd[tok[0]] = tok[1]
        return tok

    def barrier(self):
        allt = [(('e', e), self.cnt[e]) for e in self.ENGS if self.cnt[e] > 0]
        allt += [t for t in self.dtok if t is not None]
        for e in self.ENGS:
            waits = {}
            seen = self.seen[e]
            for sk, v in allt:
                if seen.get(sk, 0) >= v:
                    continue
                waits[sk] = max(waits.get(sk, 0), v)
            for sk, v in waits.items():
                seen[sk] = v
            self.pending[e].append((None, list(waits.items()), None))
        self.last_w = {}
        self.readers = {}

    def emit(self):
        nc = self.nc

        def mk(e):
            def body(engh):
                for fn, waits, tok in self.pending[e]:
                    for sk, v in waits:
                        engh.wait_ge(self._semh(sk), v)
                    if fn is None:
                        continue
                    ins = fn(engh)
                    sk, v = tok
                    if sk[0] == 'e':
                        ins.then_inc(self.sem[e], 1)
                    else:
                        ins.then_inc(self.dsem[sk[1]], 16)
            return body

        with nc.Block() as blk:
            blk.tensor(mk('pe'))
            blk.scalar(mk('act'))
            blk.vector(mk('dve'))
            blk.gpsimd(mk('pool'))
            blk.sync(mk('sp'))
        self.pending = {e: [] for e in self.ENGS}


PSKEYS = {'ps', 'psb', 'pA', 'pB', 'pO', 'psS'}


def keys(name, *idx_ranges):
    out = [(name,)]
    for r in idx_ranges:
        out = [o + (i,) for o in out for i in r]
    return out


class Ctx:
    pass


def stage_ingest(C):
    nc, S = C.nc, C.S
    with ExitStack() as st:
        xt = [st.enter_context(nc.sbuf_tensor("in_xt%d" % i, [128, 4, D], F32)) for i in range(2)]
        hx = [st.enter_context(nc.sbuf_tensor("in_hx%d" % i, [128, 8, 512], F32)) for i in range(2)]
        ps = [st.enter_context(nc.psum_tensor("in_ps%d" % i, [128, 512], F32)) for i in range(4)]
        hTv = C.hT.rearrange("(c p) t -> p c t", p=128)
        ident = C.ident
        S.op('pool', lambda e: e.memset(hx[1][:, :, 0:64], 0.0), w=keys('hx', [1], range(8)))
        S.op('pool', lambda e: e.dma_start(out=hTv[:, :, T:TP], in_=hx[1][:, :, 0:TP - T]),
             r=keys('hx', [1], range(8)), w=[('hTpad',)], dma=True)
        S.op('sp', lambda e: e.dma_start(out=xt[1][0:NMETA, 0, :], in_=C.meta[:, :]), w=[('xt', 1)], dma=True)
        for c in range(8):
            S.op('pe', lambda e, c=c: e.transpose(ps[c % 4][:, 0:NMETA], xt[1][0:NMETA, 0, c * 128:(c + 1) * 128],
                                                  ident[0:NMETA, 0:NMETA]),
                 r=[('xt', 1)], w=[('ps', c % 4)])
            S.op('dve', lambda e, c=c: e.tensor_copy(out=hx[1][:, c, 0:NMETA], in_=ps[c % 4][:, 0:NMETA]),
                 r=[('ps', c % 4)], w=[('hx', 1, c)])
        S.op('pool', lambda e: e.dma_start(out=hTv[:, :, 0:NMETA], in_=hx[1][:, :, 0:NMETA]),
             r=keys('hx', [1], range(8)), w=[('hTmeta',)], dma=True)
        xv = C.x.rearrange("(g a p) d -> g p a d", a=4, p=128)
        for g in range(SEQ // 512):
            s = g % 2
            S.op('sp', lambda e, g=g, s=s: e.dma_start(out=xt[s][:, :, :], in_=xv[g]), w=[('xt', s)], dma=True)
            for c in range(8):
                b = c % 4
                for a in range(4):
                    S.op('pe', lambda e, a=a, c=c, b=b, s=s: e.transpose(
                        ps[b][:, a * 128:(a + 1) * 128], xt[s][:, a, c * 128:(c + 1) * 128], ident[:, :]),
                        r=[('xt', s)], w=[('ps', b)])
                eng = 'dve' if c % 2 == 0 else 'act'
                if eng == 'dve':
                    S.op('dve', lambda e, c=c, b=b, s=s: e.tensor_copy(out=hx[s][:, c, :], in_=ps[b][:, :]),
                         r=[('ps', b)], w=[('hx', s, c)])
                else:
                    S.op('act', lambda e, c=c, b=b, s=s: e.copy(out=hx[s][:, c, :], in_=ps[b][:, :]),
                         r=[('ps', b)], w=[('hx', s, c)])
            t0 = NMETA + g * 512
            S.op('pool', lambda e, s=s, t0=t0: e.dma_start(out=hTv[:, :, t0:t0 + 512], in_=hx[s][:, :, :]),
                 r=keys('hx', [s], range(8)), w=[('hTin', g)], dma=True)
        S.barrier()
        S.emit()


def stage_egress(C):
    nc, S = C.nc, C.S
    with ExitStack() as st:
        xt = [st.enter_context(nc.sbuf_tensor("eg_xt%d" % i, [128, 4, D], F32)) for i in range(2)]
        hx = [st.enter_context(nc.sbuf_tensor("eg_hx%d" % i, [128, 8, 512], F32)) for i in range(2)]
        ps = [st.enter_context(nc.psum_tensor("eg_ps%d" % i, [128, 512], F32)) for i in range(4)]
        hTv = C.hT.rearrange("(c p) t -> p c t", p=128)
        ov = C.out.rearrange("(g a p) d -> g p a d", a=4, p=128)
        ident = C.ident
        for g in range(SEQ // 512):
            s = g % 2
            t0 = NMETA + g * 512
            S.op('sp', lambda e, s=s, t0=t0: e.dma_start(out=hx[s][:, :, :], in_=hTv[:, :, t0:t0 + 512]),
                 w=[('hx', s)], dma=True)
            for a in range(4):
                for hf in range(2):
                    b = (a * 2 + hf) % 4
                    for cc in range(4):
                        c = hf * 4 + cc
                        S.op('pe', lambda e, a=a, c=c, cc=cc, b=b, s=s: e.transpose(
                            ps[b][:, cc * 128:(cc + 1) * 128], hx[s][:, c, a * 128:(a + 1) * 128], ident[:, :]),
                            r=[('hx', s)], w=[('ps', b)])
                    if hf == 0:
                        S.op('dve', lambda e, a=a, b=b, s=s: e.tensor_copy(out=xt[s][:, a, 0:512], in_=ps[b][:, :]),
                             r=[('ps', b)], w=[('xt', s, a, 0)])
                    else:
                        S.op('act', lambda e, a=a, b=b, s=s: e.copy(out=xt[s][:, a, 512:1024], in_=ps[b][:, :]),
                             r=[('ps', b)], w=[('xt', s, a, 1)])
            S.op('pool', lambda e, g=g, s=s: e.dma_start(out=ov[g], in_=xt[s][:, :, :]),
                 r=keys('xt', [s], range(4), range(2)), w=[('out', g)], dma=True)
        S.barrier()
        S.emit()


def stage_ffn(C, w_in_d, w_out_d, gcol, tag):
    nc, S = C.nc, C.S
    TT = 256
    tiles = [(i * TT, TT) for i in range(TP // TT)]
    if TP % TT:
        tiles.append((TP - TP % TT, TP % TT))
    NJ = DFF // 128
    with ExitStack() as st:
        sb = lambda n, sh, dt: st.enter_context(nc.sbuf_tensor(tag + n, sh, dt))
        w_in = sb("w_in", [128, 8, 2 * DFF], BF16)
        w_out = sb("w_out", [128, NJ, D], BF16)
        x = [sb("x%d" % i, [128, 8, TT], F32) for i in range(2)]
        sq = [sb("sq%d" % i, [128, 8, TT], BF16) for i in range(2)]
        xn = [sb("xn%d" % i, [128, 8, TT], BF16) for i in range(2)]
        hm = [sb("hm%d" % i, [128, NJ, TT], BF16) for i in range(2)]
        sg = [sb("sg%d" % i, [128, TT], F32) for i in range(2)]
        rstd = [sb("rstd%d" % i, [128, TT], F32) for i in range(2)]
        psn = lambda n: st.enter_context(nc.psum_tensor(tag + n, [128, 512], F32))
        psA = [psn("pA%d" % i) for i in range(2)]
        psB = [psn("pB%d" % i) for i in range(2)]
        psO = [psn("pO%d" % i) for i in range(2)]
        psS = psn("pS")
        hTv = C.hT.rearrange("(c p) t -> p c t", p=128)
        w_in_v = w_in_d.rearrange("(k p) n -> p k n", p=128)
        w_out_v = w_out_d.rearrange("(j p) n -> p j n", p=128)
        ones = C.ones_bf
        vec = C.vec

        NB = 4
        cw = 2 * DFF // NB
        for k in range(8):
            for b in range(NB):
                S.op('pool', lambda e, k=k, b=b: e.dma_start(out=w_in[:, k, b * cw:(b + 1) * cw],
                                                             in_=w_in_v[:, k, b * cw:(b + 1) * cw]),
                     w=[('w_in', k, b)], dma=True)
        for j in range(NJ):
            S.op('pool', lambda e, j=j: e.dma_start(out=w_out[:, j, :], in_=w_out_v[:, j, :]),
                 w=[('w_out', j)], dma=True)
        win_keys = keys('w_in', range(8), range(NB))

        def load(i):
            t0, tw = tiles[i]
            s = i % 2
            S.op('sp', lambda e: e.dma_start(out=x[s][:, :, :tw], in_=hTv[:, :, t0:t0 + tw]),
                 r=[('hT', i)], w=keys('x', [s], range(8)), dma=True)

        def norm(i):
            t0, tw = tiles[i]
            s = i % 2
            S.op('act', lambda e: e.activation(out=sq[s][:, :, :tw], in_=x[s][:, :, :tw], func=AF.Square),
                 r=keys('x', [s], range(8)), w=[('sq', s)])
            for c in range(8):
                S.op('pe', lambda e, c=c: e.matmul(psS[:, :tw], lhsT=ones[:, :], rhs=sq[s][:, c, :tw],
                                                   start=(c == 0), stop=(c == 7)),
                     r=[('sq', s)], w=[('psS',)])
            S.op('act', lambda e: e.activation(out=rstd[s][:, :tw], in_=psS[:, :tw], func=AF.Sqrt,
                                               scale=1.0 / D, bias=C.eps_col[:, 0:1]),
                 r=[('psS',)], w=[('rstd', s)])
            S.op('dve', lambda e: e.reciprocal(out=rstd[s][:, :tw], in_=rstd[s][:, :tw]),
                 r=[('rstd', s)], w=[('rstd', s)])
            for c in range(8):
                S.op('dve', lambda e, c=c: e.scalar_tensor_tensor(
                    out=xn[s][:, c, :tw], in0=x[s][:, c, :tw], scalar=vec[:, gcol + c:gcol + c + 1],
                    in1=rstd[s][:, :tw], op0=ALU.mult, op1=ALU.mult),
                    r=[('x', s, c), ('rstd', s)], w=[('xn', s, c)])

        def mm_in(i):
            t0, tw = tiles[i]
            s = i % 2
            for j in range(NJ):
                q = j % 2
                for k in range(8):
                    S.op('pe', lambda e, j=j, k=k, q=q: e.matmul(
                        psA[q][:, :tw], lhsT=w_in[:, k, j * 128:(j + 1) * 128], rhs=xn[s][:, k, :tw],
                        start=(k == 0), stop=(k == 7)),
                        r=[('xn', s, k)] + (win_keys if (i == 0 and j == 0) else []), w=[('pA', q)])
                for k in range(8):
                    S.op('pe', lambda e, j=j, k=k, q=q: e.matmul(
                        psB[q][:, :tw], lhsT=w_in[:, k, DFF + j * 128:DFF + (j + 1) * 128], rhs=xn[s][:, k, :tw],
                        start=(k == 0), stop=(k == 7)),
                        r=[('xn', s, k)], w=[('pB', q)])
                S.op('act', lambda e, q=q: e.activation(out=sg[q][:, :tw], in_=psA[q][:, :tw], func=AF.Silu),
                     r=[('pA', q)], w=[('sg', q)])
                S.op('dve', lambda e, q=q, j=j: e.tensor_tensor(out=hm[s][:, j, :tw], in0=psB[q][:, :tw],
                                                                in1=sg[q][:, :tw], op=ALU.mult),
                     r=[('pB', q), ('sg', q)], w=[('hm', s, j)])

        def mm_out(i):
            t0, tw = tiles[i]
            s = i % 2
            for m in range(8):
                q = m % 2
                for j in range(NJ):
                    S.op('pe', lambda e, j=j, m=m, q=q: e.matmul(
                        psO[q][:, :tw], lhsT=w_out[:, j, m * 128:(m + 1) * 128], rhs=hm[s][:, j, :tw],
                        start=(j == 0), stop=(j == NJ - 1)),
                        r=[('hm', s, j), ('w_out', j)], w=[('pO', q)])
                S.op('dve', lambda e, m=m, q=q: e.scalar_tensor_tensor(
                    out=x[s][:, m, :tw], in0=psO[q][:, :tw], scalar=0.5, in1=x[s][:, m, :tw],
                    op0=ALU.mult, op1=ALU.add),
                    r=[('pO', q), ('x', s, m)], w=[('x', s, m)])
            S.op('pool', lambda e: e.dma_start(out=hTv[:, :, t0:t0 + tw], in_=x[s][:, :, :tw]),
                 r=keys('x', [s], range(8)), w=[('hT', i)], dma=True)

        n = len(tiles)
        load(0)
        norm(0)
        for i in range(n):
            if i + 1 < n:
                load(i + 1)
            mm_in(i)
            if i + 1 < n:
                norm(i + 1)
            mm_out(i)
        S.barrier()
        S.emit()


class H:
    def __init__(self, S):
        self.S = S

    def mm(self, out, lhsT, rhs, start, stop, r, w):
        if USE_F32R and lhsT.dtype == F32 and rhs.dtype == F32:
            lhsT = lhsT.bitcast(F32R)
            rhs = rhs.bitcast(F32R)
        self.S.op('pe', lambda e: e.matmul(out, lhsT=lhsT, rhs=rhs, start=start, stop=stop), r=r, w=w)

    def tr(self, out, in_, ident, r, w):
        self.S.op('pe', lambda e: e.transpose(out, in_, ident), r=r, w=w)

    def act(self, out, in_, func, r, w, scale=None, bias=None):
        kw = {}
        if scale is not None:
            kw['scale'] = scale
        if bias is not None:
            kw['bias'] = bias
        self.S.op('act', lambda e: e.activation(out=out, in_=in_, func=func, **kw), r=r, w=w)

    def cp(self, eng, out, in_, r, w):
        if eng == 'act':
            self.S.op('act', lambda e: e.copy(out=out, in_=in_), r=r, w=w)
        else:
            self.S.op(eng, lambda e: e.tensor_copy(out=out, in_=in_), r=r, w=w)

    def tt(self, eng, out, in0, in1, op, r, w):
        self.S.op(eng, lambda e: e.tensor_tensor(out=out, in0=in0, in1=in1, op=op), r=r, w=w)

    def ts(self, eng, out, in0, s1, s2, op0, op1, r, w):
        if s2 is None:
            self.S.op(eng, lambda e: e.tensor_scalar(out=out, in0=in0, scalar1=s1, scalar2=None, op0=op0), r=r, w=w)
        else:
            self.S.op(eng, lambda e: e.tensor_scalar(out=out, in0=in0, scalar1=s1, scalar2=s2, op0=op0, op1=op1),
                      r=r, w=w)

    def stt(self, eng, out, in0, scalar, in1, op0, op1, r, w):
        self.S.op(eng, lambda e: e.scalar_tensor_tensor(out=out, in0=in0, scalar=scalar, in1=in1, op0=op0, op1=op1),
                  r=r, w=w)

    def red(self, eng, out, in_, op, r, w):
        self.S.op(eng, lambda e: e.tensor_reduce(out=out, in_=in_, axis=AX.X, op=op), r=r, w=w)

    def dma(self, eng, out, in_, r, w):
        self.S.op(eng, lambda e: e.dma_start(out=out, in_=in_), r=r, w=w, dma=True)


def bk(*bs):
    return [('ps', b) for b in bs]


def stage_rwkv(C):
    nc, S = C.nc, C.S
    Hh = H(S)
    mm, tr, act, cp, tt, ts, stt, red, dma = Hh.mm, Hh.tr, Hh.act, Hh.cp, Hh.tt, Hh.ts, Hh.stt, Hh.red, Hh.dma
    CH = 64
    NT = TP // CH
    NH = 16
    cols = C.cols
    c64 = C.cols64
    with ExitStack() as st:
        sb = lambda n, sh, dt=F32: st.enter_context(nc.sbuf_tensor("rs_" + n, sh, dt))
        Wr = sb("Wr", [128, 8, D], BF16)
        Wk = sb("Wk", [128, 8, D], BF16)
        Wv = sb("Wv", [128, 8, D], BF16)
        Wo = sb("Wo", [128, 8, D], BF16)
        w1 = sb("w1", [128, 8, 64], BF16)
        a1 = sb("a1", [128, 8, 64], BF16)
        g1 = sb("g1", [128, 8, 160], BF16)
        a2 = sb("a2", [64, D], BF16)
        g2a = sb("g2a", [128, D], BF16)
        g2b = sb("g2b", [32, D], BF16)
        w2aug = sb("w2aug", [65, D], F32)
        lnxg = sb("lnxg", [64, D], F32)
        lnxb = sb("lnxb", [64, D], F32)
        omk = sb("omk", [64, NH], F32)
        PS = st.enter_context(nc.psum_tensor("rk_PS", [128, 3584], F32))
        PSb = st.enter_context(nc.psum_tensor("rk_PSb", [128, 1024], BF16))
        hbuf = [sb("hbuf%d" % i, [128, 8, CH]) for i in range(2)]
        sq = sb("sq", [128, 8, CH], BF16)
        rstd = sb("rstd", [128, CH])
        hn = sb("hn", [128, 8, CH + 1])
        xx = sb("xx", [128, 8, CH])
        xmf = [sb("xmf%d" % i, [128, 8, CH]) for i in range(1)]
        xm = [sb("xm%d" % i, [128, 8, CH], BF16) for i in range(6)]
        r_ = sb("r", [64, NH, CH])
        k_ = sb("k", [64, NH, CH])
        a_ = sb("a", [64, NH, CH])
        kk = sb("kk", [64, NH, CH])
        b_ = sb("b", [64, NH, CH])
        tmp1 = sb("tmp1", [64, NH, CH])
        tmp2 = sb("tmp2", [64, NH, CH])
        G = sb("G", [64, NH, CH])
        Ghat = sb("Ghat", [64, NH, CH])
        cumC = sb("cumC", [64, NH])
        AR = sb("AR", [64, NH, 2 * CH], BF16)
        Bt = sb("Bt", [64, NH, CH], BF16)
        Kt = sb("Kt", [64, NH, CH], BF16)
        Bh = sb("Bh", [64, NH, CH], TRDT)
        Kh = sb("Kh", [64, NH, CH], TRDT)
        g_tm = sb("g_tm", [64, D], BF16)
        lw_tm = sb("lw_tm", [64, D])
        twT = sb("twT", [65, CH])
        taT = sb("taT", [64, CH], BF16)
        sg0 = sb("sg0", [128, CH], BF16)
        sg1 = sb("sg1", [32, CH], BF16)
        bon = sb("bon", [64, NH])
        MX = sb("MX", [64, NH, 2 * CH], BF16)
        GG = sb("GG", [64, NH, 2 * CH])
        RKT = sb("RKT", [64, NH, CH], BF16)
        LakT = sb("LakT", [64, NH, CH], BF16)
        Hst = sb("Hst", [64, NH, CH])
        st1 = sb("st1", [64, NH])
        st2 = sb("st2", [64, NH])
        zT = sb("zT", [128, 8, CH], BF16)

        yc = sb("yc", [64, NH, CH])
        ysq = a_
        Gs = sb("Gs", [64, NH, CH], BF16)
        Us = sb("Us", [64, NH, CH], BF16)
        BhT = sb("BhT", [64, NH, CH], BF16)
        KhT = sb("KhT", [64, NH, CH], BF16)
        Lm = sb("Lm", [64, NH, CH], BF16)
        RBT = sb("RBT", [64, NH, CH], BF16)
        v_bf = sb("v_bf", [64, D], BF16)
        Hb = sb("Hb", [64, NH, CH], BF16)
        ident_bf = sb("ident_bf", [64, 64], BF16)
        Ginv = GG[:, :, 0:CH]
        Gex = GG[:, :, CH:2 * CH]

        hTv = C.hT.rearrange("(c p) t -> p c t", p=128)
        ident = C.ident
        vec = C.vec
        v64 = C.vec64

        for nm, dst, src in (("Wr", Wr, C.rk['w_r']), ("Wk", Wk, C.rk['w_k']), ("Wv", Wv, C.rk['w_v']),
                             ("Wo", Wo, C.rk['w_o'])):
            v = src.rearrange("(k p) n -> p k n", p=128)
            for k in range(8):
                dma('pool', dst[:, k, :], v[:, k, :], r=[], w=[(nm,)])
        dma('pool', w1[:, :, :], C.rk['w1'].rearrange("(k p) n -> p k n", p=128), r=[], w=[('w1',)])
        dma('pool', a1[:, :, :], C.rk['a1'].rearrange("(k p) n -> p k n", p=128), r=[], w=[('a1',)])
        dma('pool', g1[:, :, :], C.rk['g1'].rearrange("(k p) n -> p k n", p=128), r=[], w=[('g1',)])
        dma('pool', a2[:, :], C.rk['a2'][:, :], r=[], w=[('a2',)])
        dma('pool', g2a[:, :], C.rk['g2'][0:128, :], r=[], w=[('g2',)])
        dma('pool', g2b[:, :], C.rk['g2'][128:160, :], r=[], w=[('g2',)])
        dma('sp', w2aug[0:64, :], C.rk['w2'][:, :], r=[], w=[('w2aug',)])
        dma('sp', w2aug[64:65, :], C.rk['w0'][0:1, :], r=[], w=[('w2aug',)])
        dma('sp', lnxg[:, :], C.rk['lnx_g'][0:1, :].partition_broadcast(64), r=[], w=[('lnxg',)])
        dma('sp', lnxb[:, :], C.rk['lnx_b'][0:1, :].partition_broadcast(64), r=[], w=[('lnxb',)])
        kka = c64['k_a'][0]
        ts('dve', omk[:, :], v64[0:64, kka:kka + NH], -1.0, 1.0, ALU.mult, ALU.add, r=[('c_vec64',)], w=[('omk',)])
        S.op('pool', lambda e: e.memset(Hst[:, :, :], 0.0), w=[('Hst',)])
        S.op('pool', lambda e: e.memset(Hb[:, :, :], 0.0), w=[('Hb',)])
        cp('dve', ident_bf[:, :], ident[0:64, 0:64], r=[('c_ident',)], w=[('ident_bf',)])
        S.op('pool', lambda e: e.memset(hn[:, :, 0:1], 0.0), w=[('hn0',)])
        S.op('pool', lambda e: e.memset(twT[64:65, :], 1.0), w=[('twT1',)])

        def bc(ap2, n=CH):
            return ap2.unsqueeze(2).to_broadcast([64, NH, n])

        def prm(name):
            c0 = c64[name][0]
            return bc(v64[0:64, c0:c0 + NH])

        gcol = cols['norm_g_0_1'][0]
        mixc = cols['rk_mix'][0]
        psv2 = lambda b0: PS[0:64, b0 * 512:b0 * 512 + 2048].rearrange("p (h two s) -> p h two s", h=NH, two=2)
        psv1 = lambda b0: PS[0:64, b0 * 512:b0 * 512 + 1024].rearrange("p (h s) -> p h s", h=NH)
        SU = C.masks[0:64, 0:64].unsqueeze(1).to_broadcast([64, NH, CH])
        IU = C.masks[0:64, 64:128].unsqueeze(1).to_broadcast([64, NH, CH])
        SL = C.masks[0:64, 128:192].unsqueeze(1).to_broadcast([64, NH, CH])
        IDb = ident[0:64, 0:64].unsqueeze(1).to_broadcast([64, NH, CH])
        ones64 = C.ones_f[0:64, 0:64]

        def seg_A(t):
            t0 = t * CH
            hb = hbuf[t % 2]
            dma('sp', hb[:, :, :], hTv[:, :, t0:t0 + CH], r=[('hT', t)], w=[('hbuf', t % 2)])
            act(sq[:, :, :], hb[:, :, :], AF.Square, r=[('hbuf', t % 2)], w=[('sq',)])
            for c in range(8):
                mm(PS[:, 3072:3072 + CH], ones_bf(C)[:, :], sq[:, c, :], c == 0, c == 7, r=[('sq',)], w=bk(6))
            act(rstd[:, :], PS[:, 3072:3072 + CH], AF.Sqrt, r=bk(6), w=[('rstd',)], scale=1.0 / D, bias=C.eps_col[:, 0:1])
            S.op('dve', lambda e: e.reciprocal(out=rstd[:, :], in_=rstd[:, :]), r=[('rstd',)], w=[('rstd',)])
            for c in range(8):
                stt('dve', hn[:, c, 1:CH + 1], hb[:, c, :], vec[:, gcol + c:gcol + c + 1], rstd[:, :],
                    ALU.mult, ALU.mult, r=[('hbuf', t % 2), ('rstd',), ('hn0',)], w=[('hn', c)])
            hnk = keys('hn', range(8))
            tt('pool', xx[:, :, :], hn[:, :, 0:CH], hn[:, :, 1:CH + 1], ALU.subtract, r=hnk + [('hn0',)], w=[('xx',)])
            for i in range(6):
                mixb = vec[:, mixc + i * 8:mixc + i * 8 + 8].unsqueeze(2).to_broadcast([128, 8, CH])
                tt('pool', xmf[0][:, :, :], xx[:, :, :], mixb, ALU.mult, r=[('xx',), ('c_vec',)], w=[('xmf', 0)])
                tt('pool', xm[i][:, :, :], xmf[0][:, :, :], hn[:, :, 1:CH + 1], ALU.add,
                   r=[('xmf', 0)] + hnk, w=keys('xm', [i], range(8)))
            cp('pool', hn[:, :, 0:1], hn[:, :, CH:CH + 1], r=hnk + [('xx',)], w=[('hn0',)])

        def seg_P(t):
            xr, xw, xk, xv, xa, xg = xm
            for (Wt, wn, xs, xi, b0) in ((Wr, 'Wr', xr, 0, 0), (Wk, 'Wk', xk, 2, 2)):
                for h in range(NH):
                    for kc in range(8):
                        mm(PS[0:64, b0 * 512 + h * 64:b0 * 512 + (h + 1) * 64], Wt[:, kc, h * 64:(h + 1) * 64],
                           xs[:, kc, :], kc == 0, kc == 7, r=[('xm', xi, kc), (wn,)], w=bk(b0 + h // 8))

        def seg_E(t):
            t0 = t * CH
            PH = DEBUG_RK[1] if DEBUG_RK else 99
            hb = hbuf[t % 2]
            xr, xw, xk, xv, xa, xg = xm
            cp('act', r_[:, :, :], psv1(0), r=bk(0, 1), w=[('r',)])
            cp('act', k_[:, :, :], psv1(2), r=bk(2, 3), w=[('k',)])
            for n in range(2):
                for kc in range(8):
                    mm(PS[0:64, (4 + n) * 512:(5 + n) * 512], xv[:, kc, :], Wv[:, kc, n * 512:(n + 1) * 512],
                       kc == 0, kc == 7, r=[('xm', 3, kc), ('Wv',)], w=bk(4 + n))
            cp('dve', v_bf[:, :], PS[0:64, 2048:3072], r=bk(4, 5), w=[('v_bf',)])
            for kc in range(8):
                mm(PS[0:64, 3072:3072 + CH], w1[:, kc, :], xw[:, kc, :], kc == 0, kc == 7,
                   r=[('xm', 1, kc), ('w1',)], w=bk(6))
            act(twT[0:64, :], PS[0:64, 3072:3072 + CH], AF.Tanh, r=bk(6), w=[('twT',)])
            for kc in range(8):
                mm(PS[0:64, 0:CH], a1[:, kc, :], xa[:, kc, :], kc == 0, kc == 7,
                   r=[('xm', 4, kc), ('a1',)], w=bk(0))
            cp('dve', taT[:, :], PS[0:64, 0:CH], r=bk(0), w=[('taT',)])
            for kc in range(8):
                mm(PS[:, 3072:3072 + CH], g1[:, kc, 0:128], xg[:, kc, :], kc == 0, kc == 7,
                   r=[('xm', 5, kc), ('g1',)], w=bk(6))
            act(sg0[:, :], PS[:, 3072:3072 + CH], AF.Sigmoid, r=bk(6), w=[('sg0',)])
            for kc in range(8):
                mm(PS[0:32, 512:512 + CH], g1[:, kc, 128:160], xg[:, kc, :], kc == 0, kc == 7,
                   r=[('xm', 5, kc), ('g1',)], w=bk(1))
            act(sg1[:, :], PS[0:32, 512:512 + CH], AF.Sigmoid, r=bk(1), w=[('sg1',)])
            for n in range(2):
                mm(PS[0:64, n * 512:(n + 1) * 512], twT[0:65, :], w2aug[0:65, n * 512:(n + 1) * 512], True, True,
                   r=[('twT',), ('twT1',), ('w2aug',)], w=bk(n))
            act(lw_tm[:, :], PS[0:64, 0:1024], AF.Sigmoid, r=bk(0, 1), w=[('lw_tm',)])
            for h in range(NH):
                mm(PS[0:64, 1024 + h * 128:1024 + (h + 1) * 128], lw_tm[0:64, h * 64:(h + 1) * 64],
                   C.tri[0:64, 0:128], True, True, r=[('lw_tm',), ('c_tri',)], w=bk(2 + h // 4))
            pc = psv2(2)
            cb = bk(2, 3, 4, 5)
            act(G[:, :, :], pc[:, :, 0, :], AF.Exp, r=cb, w=[('G',)])
            act(Ginv[:, :, :], pc[:, :, 0, :], AF.Exp, r=cb, w=[('Ginv',)], scale=-1.0)
            act(Gex[:, :, :], pc[:, :, 1, :], AF.Exp, r=cb, w=[('Gex',)])
            cp('dve', cumC[:, :], pc[:, :, 0, CH - 1], r=cb, w=[('cumC',)])
            tt('dve', tmp1[:, :, :], bc(cumC[:, :]), pc[:, :, 0, :], ALU.subtract, r=cb + [('cumC',)], w=[('tmp1',)])
            act(Ghat[:, :, :], tmp1[:, :, :], AF.Exp, r=[('tmp1',)], w=[('Ghat',)])
            for h in range(NH):
                mm(PS[0:64, h * 64:(h + 1) * 64], a2[0:64, h * 64:(h + 1) * 64], taT[0:64, :], True, True,
                   r=[('taT',), ('a2',)], w=bk(h // 8))
            tt('dve', a_[:, :, :], psv1(0), prm('a0'), ALU.add, r=bk(0, 1) + [('c_vec64',)], w=[('a',)])
            act(a_[:, :, :], a_[:, :, :], AF.Sigmoid, r=[('a',)], w=[('a',)])
            for n in range(2):
                mm(PS[0:64, n * 512:(n + 1) * 512], sg0[:, :], g2a[:, n * 512:(n + 1) * 512], True, False,
                   r=[('sg0',), ('g2',)], w=bk(n))
                mm(PS[0:64, n * 512:(n + 1) * 512], sg1[0:32, :], g2b[0:32, n * 512:(n + 1) * 512], False, True,
                   r=[('sg1',), ('g2',)], w=bk(n))
            cp('act', g_tm[:, :], PS[0:64, 0:1024], r=bk(0, 1), w=[('g_tm',)])
            if PH < -1:
                return
            tt('dve', kk[:, :, :], k_[:, :, :], prm('k_k'), ALU.mult, r=[('k',), ('c_vec64',)], w=[('kk',)])
            act(Gs[:, :, :], kk[:, :, :], AF.Square, r=[('kk',)], w=[('Gs',)])
            t2f = Gs[:, :, :].rearrange("p h s -> p (h s)")
            for n in range(2):
                mm(PS[0:64, 1024 + n * 512:1024 + (n + 1) * 512], C.ones_bf[0:64, 0:64], t2f[:, n * 512:(n + 1) * 512],
                   True, True, r=[('Gs',), ('c_ones',)], w=bk(2 + n))
            tt('pool', tmp1[:, :, :], a_[:, :, :], prm('k_a'), ALU.mult, r=[('a',), ('c_vec64',)], w=[('tmp1',)])
            tt('pool', tmp1[:, :, :], tmp1[:, :, :], bc(omk[:, :]), ALU.add, r=[('tmp1',), ('omk',)], w=[('tmp1',)])
            tt('pool', k_[:, :, :], k_[:, :, :], tmp1[:, :, :], ALU.mult, r=[('k',), ('tmp1',)], w=[('k',)])
            tt('dve', AR[:, :, CH:2 * CH], r_[:, :, :], G[:, :, :], ALU.mult, r=[('r',), ('G',)], w=[('AR1',)])
            act(tmp2[:, :, :], psv1(2), AF.Ln, r=bk(2, 3), w=[('tmp2',)], bias=C.eps_col[0:64, 2:3])
            act(tmp2[:, :, :], tmp2[:, :, :], AF.Exp, r=[('tmp2',)], w=[('tmp2',)], scale=-0.5)
            tt('dve', Kt[:, :, :], k_[:, :, :], Ginv[:, :, :], ALU.mult, r=[('k',), ('Ginv',)], w=[('Kt',)])
            tt('dve', kk[:, :, :], kk[:, :, :], tmp2[:, :, :], ALU.mult, r=[('kk',), ('tmp2',)], w=[('kk',)])
            tt('dve', b_[:, :, :], kk[:, :, :], a_[:, :, :], ALU.mult, r=[('kk',), ('a',)], w=[('b',)])
            stt('dve', AR[:, :, 0:CH], kk[:, :, :], -1.0, Gex[:, :, :], ALU.mult, ALU.mult,
                r=[('kk',), ('Gex',)], w=[('AR0',)])
            tt('dve', Bt[:, :, :], b_[:, :, :], Ginv[:, :, :], ALU.mult, r=[('b',), ('Ginv',)], w=[('Bt',)])
            tt('dve', Bh[:, :, :], b_[:, :, :], Ghat[:, :, :], ALU.mult, r=[('b',), ('Ghat',)], w=[('Bh',)])
            tt('pool', Kh[:, :, :], k_[:, :, :], Ghat[:, :, :], ALU.mult, r=[('k',), ('Ghat',), ('Bt',)], w=[('Kh',)])
            if PH < 1:
                return
            GR = [(0, 8), (8, 8)]

            def hv(ap, g):
                return ap[:, GR[g][0]:GR[g][0] + 8, :]

            def pg2(b0):
                return PS[0:64, b0 * 512:b0 * 512 + 1024].rearrange("p (h two s) -> p h two s", h=8, two=2)

            def pg1(b0):
                return PS[0:64, b0 * 512:b0 * 512 + 512].rearrange("p (h s) -> p h s", h=8)

            SU8 = C.masks[0:64, 0:64].unsqueeze(1).to_broadcast([64, 8, CH])
            IU8 = C.masks[0:64, 64:128].unsqueeze(1).to_broadcast([64, 8, CH])
            SL8 = C.masks[0:64, 128:192].unsqueeze(1).to_broadcast([64, 8, CH])
            ID8 = ident[0:64, 0:64].unsqueeze(1).to_broadcast([64, 8, CH])
            for g in range(2):
                h0 = GR[g][0]
                bA = 0 if g == 0 else 3
                for hh in range(8):
                    h = h0 + hh
                    mm(PS[0:64, bA * 512 + hh * 128:bA * 512 + (hh + 1) * 128], Bt[:, h, :], AR[:, h, :], True, True,
                       r=[('Bt',), ('AR0',), ('AR1',)], w=bk(bA + hh // 4))
                tt('dve', hv(MX[:, :, 0:CH], g), pg2(bA)[:, :, 0, :], SU8, ALU.mult, r=bk(bA, bA + 1) + [('c_masks',)],
                   w=[('MX0', g)])
                tt('dve', hv(RBT, g), pg2(bA)[:, :, 1, :], IU8, ALU.mult, r=bk(bA, bA + 1) + [('c_masks',)],
                   w=[('RBT', g)])
                for hh in range(8):
                    h = h0 + hh
                    mm(PS[0:64, bA * 512 + hh * 128:bA * 512 + (hh + 1) * 128], Kt[:, h, :], AR[:, h, :], True, True,
                       r=[('Kt',), ('AR0',), ('AR1',)], w=bk(bA + hh // 4))
                tt('dve', hv(LakT, g), pg2(bA)[:, :, 0, :], SU8, ALU.mult, r=bk(bA, bA + 1) + [('c_masks',)],
                   w=[('LakT', g)])
                tt('dve', hv(RKT, g), pg2(bA)[:, :, 1, :], IU8, ALU.mult, r=bk(bA, bA + 1) + [('c_masks',)],
                   w=[('RKT', g)])
                for hh in range(8):
                    h = h0 + hh
                    mm(PS[0:64, (bA + 2) * 512 + hh * 64:(bA + 2) * 512 + (hh + 1) * 64], AR[:, h, 0:CH], Bt[:, h, :],
                       True, True, r=[('Bt',), ('AR0',)], w=bk(bA + 2))
                tt('dve', hv(Lm, g), pg1(bA + 2), SL8, ALU.mult, r=bk(bA + 2) + [('c_masks',)], w=[('Lm', g)])
                cp('pool', hv(MX[:, :, CH:2 * CH], g), ID8, r=[('c_ident',), ('Bt',)], w=[('MX1', g)])
            tt('pool', tmp1[:, :, :], r_[:, :, :], prm('r_k'), ALU.mult, r=[('r',), ('c_vec64',), ('Bt',)], w=[('tmp1',)])
            tt('pool', tmp1[:, :, :], tmp1[:, :, :], k_[:, :, :], ALU.mult, r=[('tmp1',), ('k',)], w=[('tmp1',)])
            if PH < 2:
                return
            for lvl in range(6):
                for g in range(2):
                    h0 = GR[g][0]
                    bA = 0 if g == 0 else 3
                    for hh in range(8):
                        h = h0 + hh
                        mm(PS[0:64, bA * 512 + hh * 128:bA * 512 + (hh + 1) * 128], Lm[:, h, :], MX[:, h, :], True, True,
                           r=[('Lm', g), ('MX0', g), ('MX1', g)], w=bk(bA + hh // 4))
                    if lvl < 5:
                        for hh in range(8):
                            h = h0 + hh
                            mm(PS[0:64, (bA + 2) * 512 + hh * 64:(bA + 2) * 512 + (hh + 1) * 64], MX[:, h, 0:CH], Lm[:, h, :],
                               True, True, r=[('Lm', g), ('MX0', g)], w=bk(bA + 2))
                for g in range(2):
                    bA = 0 if g == 0 else 3
                    tt('dve', hv(MX[:, :, CH:2 * CH], g), pg2(bA)[:, :, 1, :], hv(MX[:, :, CH:2 * CH], g), ALU.add,
                       r=bk(bA, bA + 1) + [('MX1', g)], w=[('MX1', g)])
                    if lvl < 5:
                        cp('act', hv(MX[:, :, 0:CH], g), pg2(bA)[:, :, 0, :], r=bk(bA, bA + 1), w=[('MX0', g)])
                        cp('act', hv(Lm, g), pg1(bA + 2), r=bk(bA + 2), w=[('Lm', g)])
                if lvl == 0 and t + 1 < NTR:
                    seg_A(t + 1)
            if PH < 3:
                return
            for h in range(NH):
                mm(PS[0:64, 3072 + h:3072 + h + 1], tmp1[:, h, :], C.ones_f[0:64, 0:1], True, True,
                   r=[('tmp1',), ('c_onesf',)], w=bk(6))
            cp('dve', bon[:, :], PS[0:64, 3072:3072 + NH], r=bk(6), w=[('bon',)])
            if TRDT == BF16:
                psb3 = PSb[0:64, :].rearrange("p (h s) -> p h s", h=NH)
                for h in range(NH):
                    tr(PSb[0:64, h * 64:(h + 1) * 64], Bh[:, h, :], ident_bf[:, :], r=[('Bh',), ('ident_bf',)], w=[('psb',)])
                cp('act', BhT[:, :, :], psb3, r=[('psb',)], w=[('BhT',)])
                for h in range(NH):
                    tr(PSb[0:64, h * 64:(h + 1) * 64], Kh[:, h, :], ident_bf[:, :], r=[('Kh',), ('ident_bf',)], w=[('psb',)])
                cp('dve', KhT[:, :, :], psb3, r=[('psb',)], w=[('KhT',)])
            else:
                for h in range(NH):
                    tr(PS[0:64, h * 64:(h + 1) * 64], Bh[:, h, :], ident[0:64, 0:64], r=[('Bh',), ('c_ident',)], w=bk(h // 8))
                cp('act', BhT[:, :, :], psv1(0), r=bk(0, 1), w=[('BhT',)])
                for h in range(NH):
                    tr(PS[0:64, 1024 + h * 64:1024 + (h + 1) * 64], Kh[:, h, :], ident[0:64, 0:64],
                       r=[('Kh',), ('c_ident',)], w=bk(2 + h // 8))
                cp('dve', KhT[:, :, :], psv1(2), r=bk(2, 3), w=[('KhT',)])
            if PH < 4:
                return
            for h in range(NH):
                o = PS[0:64, h * 64:(h + 1) * 64]
                mm(o, AR[:, h, 0:CH], Hb[:, h, :], True, False, r=[('AR0',), ('Hb',)], w=bk(h // 8))
                mm(o, LakT[:, h, :], v_bf[:, h * 64:(h + 1) * 64], False, True, r=[('LakT', h // 8), ('v_bf',)], w=bk(h // 8))
            cp('act', Gs[:, :, :], psv1(0), r=bk(0, 1), w=[('Gs',)])
            for h in range(NH):
                mm(PS[0:64, 1024 + h * 64:1024 + (h + 1) * 64], MX[:, h, CH:2 * CH], Gs[:, h, :], True, True,
                   r=[('MX1', h // 8), ('Gs',)], w=bk(2 + h // 8))
            cp('dve', Us[:, :, :], psv1(2), r=bk(2, 3), w=[('Us',)])
            for h in range(NH):
                o = PS[0:64, 2048 + h * 64:2048 + (h + 1) * 64]
                mm(o, AR[:, h, CH:2 * CH], Hb[:, h, :], True, False, r=[('AR1',), ('Hb',)], w=bk(4 + h // 8))
                mm(o, RBT[:, h, :], Us[:, h, :], False, False, r=[('RBT', h // 8), ('Us',)], w=bk(4 + h // 8))
                mm(o, RKT[:, h, :], v_bf[:, h * 64:(h + 1) * 64], False, True, r=[('RKT', h // 8), ('v_bf',)], w=bk(4 + h // 8))
            for h in range(NH):
                o = PS[0:64, h * 64:(h + 1) * 64]
                mm(o, BhT[:, h, :], Us[:, h, :], True, False, r=[('BhT',), ('Us',)], w=bk(h // 8))
                mm(o, KhT[:, h, :], v_bf[:, h * 64:(h + 1) * 64], False, True, r=[('KhT',), ('v_bf',)], w=bk(h // 8))
            tt('dve', Hst[:, :, :], Hst[:, :, :], G[:, :, CH - 1:CH].to_broadcast([64, NH, CH]), ALU.mult,
               r=[('Hst',), ('G',)], w=[('Hst',)])
            tt('dve', Hst[:, :, :], psv1(0), Hst[:, :, :], ALU.add, r=bk(0, 1) + [('Hst',)], w=[('Hst',)])
            cp('act', Hb[:, :, :], Hst[:, :, :], r=[('Hst',)], w=[('Hb',)])
            if PH < 5:
                return

        def seg_H(t):
            t0 = t * CH
            hb = hbuf[t % 2]
            py = psv1(4)
            yb = bk(4, 5)
            red('dve', st1[:, :], py, ALU.add, r=yb, w=[('st1',)])
            ts('dve', st1[:, :], st1[:, :], -1.0 / 64, None, ALU.mult, None, r=[('st1',)], w=[('st1',)])
            tt('dve', yc[:, :, :], py, bc(st1[:, :]), ALU.add, r=yb + [('st1',)], w=[('yc',)])
            act(ysq[:, :, :], yc[:, :, :], AF.Square, r=[('yc',)], w=[('a',)])
            red('dve', st2[:, :], ysq[:, :, :], ALU.add, r=[('a',)], w=[('st2',)])
            act(st2[:, :], st2[:, :], AF.Sqrt, r=[('st2',)], w=[('st2',)], scale=1.0 / 64, bias=C.eps_col[0:64, 1:2])
            S.op('dve', lambda e: e.reciprocal(out=st2[:, :], in_=st2[:, :]), r=[('st2',)], w=[('st2',)])
            tt('dve', yc[:, :, :], yc[:, :, :], bc(st2[:, :]), ALU.mult, r=[('yc',), ('st2',)], w=[('yc',)])
            ycf = yc[:, :, :].rearrange("p h s -> p (h s)")
            tt('dve', ycf, ycf, lnxg[:, :], ALU.mult, r=[('yc',), ('lnxg',)], w=[('yc',)])
            tt('dve', ycf, ycf, lnxb[:, :], ALU.add, r=[('yc',), ('lnxb',)], w=[('yc',)])
            vv = v_bf[:, :].rearrange("p (h s) -> p h s", h=NH)
            tt('dve', ysq[:, :, :], vv, bc(bon[:, :]), ALU.mult, r=[('v_bf',), ('bon',)], w=[('a',)])
            tt('dve', yc[:, :, :], yc[:, :, :], ysq[:, :, :], ALU.add, r=[('yc',), ('a',)], w=[('yc',)])
            tt('dve', ycf, ycf, g_tm[:, :], ALU.mult, r=[('yc',), ('g_tm',)], w=[('yc',)])
            for c in range(8):
                tr(PS[:, 3072 + c * 64:3072 + (c + 1) * 64], yc[:, 2 * c:2 * c + 2, :].rearrange("p h s -> p (h s)"),
                   ident[0:64, 0:64], r=[('yc',), ('c_ident',)], w=bk(6))
            cp('act', zT[:, :, :], PS[:, 3072:3584].rearrange("p (c s) -> p c s", c=8), r=bk(6), w=[('zT',)])
            for co in range(8):
                q = 4 + (co % 2)
                for kc in range(8):
                    mm(PS[:, q * 512:q * 512 + CH], Wo[:, kc, co * 128:(co + 1) * 128], zT[:, kc, :], kc == 0, kc == 7,
                       r=[('zT',), ('Wo',)], w=bk(q))
                tt('dve', hb[:, co, :], PS[:, q * 512:q * 512 + CH], hb[:, co, :], ALU.add,
                   r=bk(q) + [('hbuf', t % 2)], w=[('hbuf', t % 2)])
            dma('pool', hTv[:, :, t0:t0 + CH], hb[:, :, :], r=[('hbuf', t % 2)], w=[('hT', t)])

        NTR = NT if not DEBUG_RK else DEBUG_RK[0]
        if NTR > 0:
            seg_A(0)
            seg_P(0)
        for t in range(NTR):
            seg_E(t)
            if t + 1 < NTR:
                seg_P(t + 1)
            seg_H(t)
        S.barrier()
        S.emit()


def stage_dsa(C):
    nc, S = C.nc, C.S
    Hh = H(S)
    mm, tr, act, cp, tt, ts, stt, red, dma = Hh.mm, Hh.tr, Hh.act, Hh.cp, Hh.tt, Hh.ts, Hh.stt, Hh.red, Hh.dma
    TT = 128
    NQT = (TP + TT - 1) // TT
    NQT_RUN = min(NQT, DEBUG_NQT) if DEBUG_NQT else NQT
    c64 = C.cols64
    cols = C.cols
    NBIS = 15
    MB = 240000.0
    with ExitStack() as st:
        sb = lambda n, sh, dt=F32: st.enter_context(nc.sbuf_tensor("ds_" + n, sh, dt))
        Wq = sb("Wq", [128, 8, 1024], BF16)
        Wk = sb("Wk", [128, 8, 256], BF16)
        Wv = sb("Wv", [128, 8, 256], BF16)
        Wqi = sb("Wqi", [128, 8, 512], BF16)
        Wki = sb("Wki", [128, 8, 64], BF16)
        Wwi = sb("Wwi", [128, 8, 8], BF16)
        Wo = sb("Wo", [128, 8, 1024], BF16)
        kT = sb("kT", [64, 4, TP], BF16)
        Vaug = sb("Vaug", [128, NQT, 4, 65], BF16)
        kiT = sb("kiT", [64, TP], BF16)
        score = sb("score", [128, TP])
        work = sb("work", [128, TP])
        mask01 = sb("mask01", [128, TP], BF16)
        maskT = sb("maskT", [128, NQT, TT], BF16)
        hbuf = [sb("hbuf%d" % i, [128, 8, TT]) for i in range(3)]
        hnb = sb("hnb", [128, 8, TT], BF16)
        sq = sb("sq", [128, 8, TT], BF16)
        rstd = sb("rstd", [128, TT])
        qT = [sb("qT%d" % i, [64, 16 * TT], BF16) for i in range(2)]
        qiT = sb("qiT", [64, 8 * TT], BF16)
        tA = sb("tA", [64, 512])
        tB = sb("tB", [64, 512])
        tC = sb("tC", [64, 512])
        rl = [sb("rl%d" % i, [128, 512]) for i in range(2)]
        PT = [sb("PT%d" % i, [128, 512], BF16) for i in range(3)]
        o_tm = sb("o_tm", [128, 1024], BF16)
        oT = sb("oT", [128, 8, TT], BF16)
        cs = sb("cs", [64, 2, TT])
        wi = sb("wi", [128, 8])
        m8 = sb("m8", [128, 8])
        eq8 = sb("eq8", [128, 8])
        iota8 = sb("iota8", [128, 8])
        lo = sb("lo", [128, 1])
        HC = sb("HC", [128, 2])
        MC = sb("MC", [128, 2])
        sel = sb("sel", [128, 1])
        d1 = sb("d1", [128, 1])
        d2 = sb("d2", [128, 2])
        thr = sb("thr", [128, 1])
        nm1 = sb("nm1", [128, 1])
        halfc = sb("halfc", [128, 1])
        negb = sb("negb", [128, 1])
        rden = sb("rden", [128, 16])
        ident_bf = sb("ident_bf", [128, 128], BF16)
        zeros_bf = sb("zeros_bf", [128, 512], BF16)
        rot = sb("rot", [64, 64])
        negmask = sb("negmask", [128, 128])
        PS = st.enter_context(nc.psum_tensor("ds_PS", [128, 3584], F32))
        PSb = st.enter_context(nc.psum_tensor("ds_PSb", [128, 1024], BF16))
        bank = lambda b: PS[:, b * 512:(b + 1) * 512]

        hTv = C.hT.rearrange("(c p) t -> p c t", p=128)
        ident = C.ident
        vec = C.vec
        v64 = C.vec64
        win = C.at['w_in'].rearrange("(k p) n -> p k n", p=128)
        for k in range(8):
            dma('pool', Wq[:, k, :], win[:, k, 0:1024], r=[], w=[('Wq',)])
        dma('pool', Wk[:, :, :], win[:, :, 1024:1280], r=[], w=[('Wk',)])
        dma('pool', Wv[:, :, :], win[:, :, 1280:1536], r=[], w=[('Wv',)])
        for k in range(8):
            dma('pool', Wqi[:, k, :], win[:, k, 1536:2048], r=[], w=[('Wqi',)])
        dma('pool', Wki[:, :, :], win[:, :, 2048:2112], r=[], w=[('Wki',)])
        dma('pool', Wwi[:, :, :], win[:, :, 2112:2120], r=[], w=[('Wwi',)])
        wov = C.at['w_o'].rearrange("(k p) n -> p k n", p=128)
        for k in range(8):
            dma('pool', Wo[:, k, :], wov[:, k, :], r=[], w=[('Wo',)])
        dma('sp', rot[:, :], C.rot_d[:, :], r=[], w=[('rot',)])
        dma('sp', negmask[:, :], C.negmask_d[:, :], r=[], w=[('negmask',)])
        cp('dve', ident_bf[:, :], ident[:, :], r=[('c_ident',)], w=[('ident_bf',)])
        S.op('pool', lambda e: e.memset(zeros_bf[:, :], 0.0), w=[('zeros_bf',)])
        S.op('pool', lambda e: e.memset(Vaug[:, :, :, 64:65], 1.0), w=[('Vones',)])
        S.op('pool', lambda e: e.memset(halfc[:, :], 0.5), w=[('halfc',)])
        S.op('pool', lambda e: e.memset(negb[:, :], -MB), w=[('negb',)])
        for j in range(8):
            S.op('pool', lambda e, j=j: e.memset(iota8[:, j:j + 1], float(j)), w=[('iota8',)])
        gcol = cols['norm_g_1_1'][0]
        qg = v64[0:64, c64['q_g'][0]:c64['q_g'][0] + 1]
        kg = v64[0:64, c64['k_g'][0]:c64['k_g'][0] + 1]
        kwid = lambda kb: min(128, TP - kb * 128)

        def norm_rope(pb, nh, tw, gcolap, out3, okeys_w):
            n = nh * tw
            pin = bank(pb)[0:64, 0:n]
            v3 = lambda ap: ap.rearrange("p (h s) -> p h s", h=nh)
            if gcolap is not None:
                act(tC[:, 0:n], pin, AF.Copy, r=bk(pb) + [('c_vec64',)], w=[('tC',)], scale=gcolap)
            else:
                cp('act', tC[:, 0:n], pin, r=bk(pb), w=[('tC',)])
            mm(bank(6)[0:64, 0:n], rot[:, :], tC[:, 0:n], True, True, r=[('tC',), ('rot',)], w=bk(6))
            cosb = cs[:, 0, 0:tw].unsqueeze(1).to_broadcast([64, nh, tw])
            sinb = cs[:, 1, 0:tw].unsqueeze(1).to_broadcast([64, nh, tw])
            if gcolap is not None:
                act(tA[:, 0:n], pin, AF.Square, r=bk(pb), w=[('tA',)])
            tt('pool', v3(tC[:, 0:n]), v3(tC[:, 0:n]), cosb, ALU.mult, r=[('tC',), ('cs',)], w=[('tC',)])
            tt('dve', v3(tB[:, 0:n]), v3(bank(6)[0:64, 0:n]), sinb, ALU.mult, r=bk(6) + [('cs',)], w=[('tB',)])
            if gcolap is None:
                tt('pool', out3, v3(tC[:, 0:n]), v3(tB[:, 0:n]), ALU.add, r=[('tC',), ('tB',)], w=okeys_w)
                return
            mm(bank(6)[0:64, 0:n], C.ones_f[0:64, 0:64], tA[:, 0:n], True, True, r=[('tA',), ('c_onesf',)], w=bk(6))
            act(tA[:, 0:n], bank(6)[0:64, 0:n], AF.Ln, r=bk(6), w=[('tA',)], scale=1.0 / 64, bias=C.eps_col[0:64, 0:1])
            act(tA[:, 0:n], tA[:, 0:n], AF.Exp, r=[('tA',)], w=[('tA',)], scale=-0.5)
            tt('pool', tC[:, 0:n], tC[:, 0:n], tB[:, 0:n], ALU.add, r=[('tC',), ('tB',)], w=[('tC',)])
            tt('dve', out3, v3(tC[:, 0:n]), v3(tA[:, 0:n]), ALU.mult, r=[('tC',), ('tA',)], w=okeys_w)

        def XA(qt):
            s = qt % 2
            s3 = qt % 3
            t0 = qt * TT
            tw = min(TT, TP - t0)
            n = t0 + tw
            nkb = qt + 1
            hb = hbuf[s3]
            dma('sp', hb[:, :, 0:tw], hTv[:, :, t0:t0 + tw], r=[('hT', qt)], w=[('hbuf', s3)])
            dma('sp', cs[:, :, 0:tw], C.rope_d[:, :, t0:t0 + tw], r=[], w=[('cs',)])
            act(sq[:, :, 0:tw], hb[:, :, 0:tw], AF.Square, r=[('hbuf', s3)], w=[('sq',)])
            for c in range(8):
                mm(bank(6)[:, 0:tw], C.ones_bf[:, :], sq[:, c, 0:tw], c == 0, c == 7, r=[('sq',)], w=bk(6))
            act(rstd[:, 0:tw], bank(6)[:, 0:tw], AF.Sqrt, r=bk(6), w=[('rstd',)], scale=1.0 / D, bias=C.eps_col[:, 0:1])
            S.op('dve', lambda e: e.reciprocal(out=rstd[:, 0:tw], in_=rstd[:, 0:tw]), r=[('rstd',)], w=[('rstd',)])
            for c in range(8):
                stt('dve', hnb[:, c, 0:tw], hb[:, c, 0:tw], vec[:, gcol + c:gcol + c + 1], rstd[:, 0:tw],
                    ALU.mult, ALU.mult, r=[('hbuf', s3), ('rstd',)], w=[('hnb',)])

        def XR(qt):
            s = qt % 2
            t0 = qt * TT
            tw = min(TT, TP - t0)
            n = t0 + tw
            nkb = qt + 1
            for g in range(4):
                for kc in range(8):
                    mm(bank(5)[0:64, g * tw:(g + 1) * tw], Wk[:, kc, g * 64:(g + 1) * 64], hnb[:, kc, 0:tw], kc == 0, kc == 7,
                       r=[('hnb',), ('Wk',)], w=bk(5))
            norm_rope(5, 4, tw, kg, kT[:, :, t0:t0 + tw], [('kT',)])
            for kc in range(8):
                mm(bank(5)[0:tw, 0:256], hnb[:, kc, 0:tw], Wv[:, kc, :], kc == 0, kc == 7, r=[('hnb',), ('Wv',)], w=bk(5))
            cp('act', Vaug[0:tw, qt, :, 0:64], bank(5)[0:tw, 0:256].rearrange("p (g d) -> p g d", g=4), r=bk(5),
               w=[('Vaug',)])
            for kc in range(8):
                mm(bank(5)[0:64, 0:tw], Wki[:, kc, :], hnb[:, kc, 0:tw], kc == 0, kc == 7, r=[('hnb',), ('Wki',)], w=bk(5))
            norm_rope(5, 1, tw, None, kiT[:, t0:t0 + tw].unsqueeze(1), [('kiT',)])
            for grp in range(4):
                pb = 5
                for hh in range(4):
                    h = grp * 4 + hh
                    for kc in range(8):
                        mm(bank(pb)[0:64, hh * tw:(hh + 1) * tw], Wq[:, kc, h * 64:(h + 1) * 64], hnb[:, kc, 0:tw],
                           kc == 0, kc == 7, r=[('hnb',), ('Wq',)], w=bk(pb))
                norm_rope(pb, 4, tw, qg, qT[s][:, grp * 4 * tw:(grp + 1) * 4 * tw].rearrange("p (h s) -> p h s", h=4),
                          [('qT', s)])
            for grp in range(2):
                pb = 5
                for hh in range(4):
                    h = grp * 4 + hh
                    for kc in range(8):
                        mm(bank(pb)[0:64, hh * tw:(hh + 1) * tw], Wqi[:, kc, h * 64:(h + 1) * 64], hnb[:, kc, 0:tw],
                           kc == 0, kc == 7, r=[('hnb',), ('Wqi',)], w=bk(pb))
                norm_rope(pb, 4, tw, None, qiT[:, grp * 4 * tw:(grp + 1) * 4 * tw].rearrange("p (h s) -> p h s", h=4),
                          [('qiT',)])
            for kc in range(8):
                mm(bank(6)[0:tw, 0:8], hnb[:, kc, 0:tw], Wwi[:, kc, :], kc == 0, kc == 7, r=[('hnb',), ('Wwi',)], w=bk(6))
            ts('dve', wi[0:tw, :], bank(6)[0:tw, 0:8], float(512.0 ** -0.5), None, ALU.mult, None, r=bk(6), w=[('wi',)])
            idx = 0
            for k0 in range(0, n, 512):
                nk = min(512, n - k0)
                for h in range(8):
                    pb = 5 + idx % 2
                    rb = rl[idx % 2]
                    rk = ('rl', idx % 2)
                    idx += 1
                    mm(bank(pb)[0:tw, 0:nk], qiT[:, h * tw:(h + 1) * tw], kiT[:, k0:k0 + nk], True, True,
                       r=[('qiT',), ('kiT',)], w=bk(pb))
                    act(rb[0:tw, 0:nk], bank(pb)[0:tw, 0:nk], AF.Relu, r=bk(pb), w=[rk])
                    if h == 0:
                        ts('dve', score[0:tw, k0:k0 + nk], rb[0:tw, 0:nk], wi[0:tw, 0:1], None, ALU.mult, None,
                           r=[rk, ('wi',)], w=[('score',)])
                    else:
                        stt('dve', score[0:tw, k0:k0 + nk], rb[0:tw, 0:nk], wi[0:tw, h:h + 1], score[0:tw, k0:k0 + nk],
                            ALU.mult, ALU.add, r=[rk, ('wi',), ('score',)], w=[('score',)])
            sc = score[0:tw, 0:n]
            if n > 256:
                red('dve', HC[0:tw, 0:1], sc, ALU.max, r=[('score',)], w=[('HC',)])
                red('dve', lo[0:tw, :], sc, ALU.min, r=[('score',)], w=[('lo',)])
            tt('dve', score[0:tw, t0:t0 + tw], score[0:tw, t0:t0 + tw], negmask[0:tw, 0:tw], ALU.add,
               r=[('score',), ('negmask',)], w=[('score',)])
            if n > 256:
                tt('dve', d1[0:tw, :], HC[0:tw, 0:1], lo[0:tw, :], ALU.subtract, r=[('HC',), ('lo',)], w=[('d1',)])
                stt('dve', HC[0:tw, 0:1], d1[0:tw, :], 1.0e-6, HC[0:tw, 0:1], ALU.mult, ALU.add, r=[('d1',), ('HC',)],
                    w=[('HC',)])
                ts('dve', HC[0:tw, 1:2], d1[0:tw, :], 0.0, None, ALU.mult, None, r=[('d1',), ('HC',)], w=[('HC',)])
                for it in range(NBIS):
                    stt('dve', MC[0:tw, 0:1], lo[0:tw, :], HC[0:tw, 0:1], halfc[0:tw, :], ALU.add, ALU.mult,
                        r=[('lo',), ('HC',), ('halfc',)], w=[('MC',)])
                    S.op('dve', lambda e, tw=tw, n=n: e.tensor_scalar(
                        out=mask01[0:tw, 0:n], in0=score[0:tw, 0:n], scalar1=MC[0:tw, 0:1], scalar2=0.0,
                        op0=ALU.is_ge, op1=ALU.add, accum_out=MC[0:tw, 1:2]),
                        r=[('score',), ('MC',)], w=[('mask01',), ('MC',)])
                    ts('dve', sel[0:tw, :], MC[0:tw, 1:2], 256.0, None, ALU.is_ge, None, r=[('MC',)], w=[('sel',)])
                    tt('dve', d1[0:tw, :], MC[0:tw, 0:1], lo[0:tw, :], ALU.subtract, r=[('MC',), ('lo',)], w=[('d1',)])
                    stt('dve', lo[0:tw, :], d1[0:tw, :], sel[0:tw, 0:1], lo[0:tw, :], ALU.mult, ALU.add,
                        r=[('d1',), ('sel',), ('lo',)], w=[('lo',)])
                    tt('dve', d2[0:tw, :], HC[0:tw, :], MC[0:tw, :], ALU.subtract, r=[('MC',), ('HC',)], w=[('d2',)])
                    stt('dve', HC[0:tw, :], d2[0:tw, :], sel[0:tw, 0:1], MC[0:tw, :], ALU.mult, ALU.add,
                        r=[('d2',), ('sel',), ('MC',)], w=[('HC',)])
                wk = work[0:tw, 0:n]
                ts('dve', wk, sc, HC[0:tw, 0:1], 1.0e20, ALU.is_ge, ALU.mult, r=[('score',), ('HC',)], w=[('work',)])
                tt('dve', wk, sc, wk, ALU.subtract, r=[('score',), ('work',)], w=[('work',)])
                S.op('dve', lambda e, tw=tw, n=n: e.max(out=m8[0:tw, :], in_=work[0:tw, 0:n]), r=[('work',)], w=[('m8',)])
                ts('dve', nm1[0:tw, :], HC[0:tw, 1:2], -1.0, 255.0, ALU.mult, ALU.add, r=[('HC',)], w=[('nm1',)])
                ts('dve', nm1[0:tw, :], nm1[0:tw, :], 7.0, 0.0, ALU.min, ALU.max, r=[('nm1',)], w=[('nm1',)])
                ts('dve', eq8[0:tw, :], iota8[0:tw, :], nm1[0:tw, 0:1], None, ALU.is_equal, None, r=[('nm1',), ('iota8',)],
                   w=[('eq8',)])
                tt('dve', eq8[0:tw, :], eq8[0:tw, :], m8[0:tw, :], ALU.mult, r=[('eq8',), ('m8',)], w=[('eq8',)])
                red('dve', thr[0:tw, :], eq8[0:tw, :], ALU.add, r=[('eq8',)], w=[('thr',)])
                ts('dve', mask01[0:tw, 0:n], sc, thr[0:tw, 0:1], None, ALU.is_ge, None, r=[('score',), ('thr',)],
                   w=[('mask01',)])
            else:
                ts('dve', mask01[0:tw, 0:n], sc, -1.0e29, None, ALU.is_ge, None, r=[('score',)], w=[('mask01',)])
        def X2(qt):
            t0 = qt * TT
            tw = min(TT, TP - t0)
            nkb = qt + 1
            for kb0 in range(0, nkb, 8):
                nb = min(8, nkb - kb0)
                for j in range(nb):
                    kb = kb0 + j
                    kw = kwid(kb)
                    tr(PSb[0:kw, j * 128:j * 128 + tw], mask01[0:tw, kb * 128:kb * 128 + kw], ident_bf[0:tw, 0:tw],
                       r=[('mask01',), ('ident_bf',)], w=[('psb',)])
                kwl = kwid(kb0 + nb - 1)
                nfull = nb if kwl == 128 else nb - 1
                if nfull > 0:
                    act(maskT[:, kb0:kb0 + nfull, 0:tw],
                        PSb[:, 0:nfull * 128].rearrange("p (j s) -> p j s", j=nfull)[:, :, 0:tw], AF.Identity,
                        r=[('psb',), ('negb',)], w=[('maskT', kb) for kb in range(kb0, kb0 + nfull)],
                        scale=MB, bias=negb[:, 0:1])
                if nfull < nb:
                    act(maskT[0:kwl, kb0 + nb - 1, 0:tw], PSb[0:kwl, (nb - 1) * 128:(nb - 1) * 128 + tw], AF.Identity,
                        r=[('psb',), ('negb',)], w=[('maskT', kb0 + nb - 1)], scale=MB, bias=negb[0:kwl, 0:1])

        def Y(qt):
            s = qt % 2
            t0 = qt * TT
            tw = min(TT, TP - t0)
            nkb = qt + 1
            hb = hbuf[qt % 3]
            q_ = qT[s]
            OB = [(0, 0, 7), (1, 7, 7), (2, 14, 2)]
            for (ob, h0, nh) in OB:
                S.op('pe', lambda e, ob=ob, nh=nh: e.matmul(bank(ob)[0:tw, 0:nh * 65], lhsT=zeros_bf[:, 0:tw],
                                                          rhs=zeros_bf[:, 0:nh * 65], start=True, stop=False,
                                                          skip_group_check=True),
                     r=[('zeros_bf',)], w=bk(ob))
            jobs = [(kb, g) for kb in range(nkb) for g in range(4)]

            def s_part(i):
                kb, g = jobs[i]
                kw = kwid(kb)
                pb = 3 + i % 2
                pt = PT[i % 3]
                pk = ('PT', i % 3)
                mbias = maskT[0:kw, kb, 0:tw].unsqueeze(1).to_broadcast([kw, 4, tw])
                mm(bank(pb)[0:kw, 0:4 * tw], kT[:, g, kb * 128:kb * 128 + kw], q_[:, g * 4 * tw:(g + 1) * 4 * tw],
                   True, False, r=[('kT',), ('qT', s)], w=bk(pb))
                mm(bank(pb)[0:kw, 0:4 * tw].rearrange("p (h s) -> p h s", h=4), ident_bf[0:kw, 0:kw], mbias,
                   False, True, r=[('maskT', kb), ('ident_bf',)], w=bk(pb))
                act(pt[0:kw, 0:4 * tw], bank(pb)[0:kw, 0:4 * tw], AF.Exp, r=bk(pb), w=[pk], scale=0.125)

            def v_part(i):
                kb, g = jobs[i]
                kw = kwid(kb)
                pt = PT[i % 3]
                pk = ('PT', i % 3)
                for rr in range(4):
                    h = 4 * g + rr
                    S.op('pe', lambda e, h=h, rr=rr, kw=kw, pt=pt, kb=kb, g=g, last=(kb == nkb - 1): e.matmul(
                        bank(h // 7)[0:tw, (h % 7) * 65:(h % 7 + 1) * 65], lhsT=pt[0:kw, rr * tw:(rr + 1) * tw],
                        rhs=Vaug[0:kw, kb, g, :], start=False, stop=last, skip_group_check=True),
                        r=[pk, ('Vaug',), ('Vones',)], w=bk(h // 7))

            s_part(0)
            for i in range(len(jobs)):
                if i + 1 < len(jobs):
                    s_part(i + 1)
                v_part(i)
            for (ob, h0, nh) in OB:
                o3 = bank(ob)[0:tw, 0:nh * 65].rearrange("p (h d) -> p h d", h=nh)
                S.op('dve', lambda e, o3=o3, h0=h0, nh=nh: e.reciprocal(out=rden[0:tw, h0:h0 + nh], in_=o3[:, :, 64]),
                     r=bk(ob), w=[('rden', ob)])
                tt('dve', o_tm[0:tw, h0 * 64:(h0 + nh) * 64].rearrange("p (h d) -> p h d", h=nh), o3[:, :, 0:64],
                   rden[0:tw, h0:h0 + nh].unsqueeze(2).to_broadcast([tw, nh, 64]), ALU.mult,
                   r=bk(ob) + [('rden', ob)], w=[('o_tm',)])
            for c in range(8):
                tr(PSb[:, c * 128:c * 128 + tw], o_tm[0:tw, c * 128:(c + 1) * 128], ident_bf[0:tw, 0:tw],
                   r=[('o_tm',), ('ident_bf',)], w=[('psb',)])
            cp('act', oT[:, :, 0:tw], PSb[:, :].rearrange("p (c s) -> p c s", c=8)[:, :, 0:tw],
               r=[('psb',)], w=[('oT',)])
            for co in range(8):
                pb = 3 + co % 2
                for kc in range(8):
                    mm(bank(pb)[:, 0:tw], Wo[:, kc, co * 128:(co + 1) * 128], oT[:, kc, 0:tw], kc == 0, kc == 7,
                       r=[('oT',), ('Wo',)], w=bk(pb))
                tt('dve', hb[:, co, 0:tw], bank(pb)[:, 0:tw], hb[:, co, 0:tw], ALU.add, r=bk(pb) + [('hbuf', qt % 3)],
                   w=[('hbuf', qt % 3)])
            dma('pool', hTv[:, :, t0:t0 + tw], hb[:, :, 0:tw], r=[('hbuf', qt % 3)], w=[('hT', qt)])

        XA(0)
        XR(0)
        X2(0)
        if NQT_RUN > 1:
            XA(1)
        for qt in range(NQT_RUN):
            if qt + 1 < NQT_RUN:
                S.capture()
                XR(qt + 1)
                la = S.end_capture()
                S.capture()
                Y(qt)
                lb = S.end_capture()
                S.replay_merged(la, lb, frac=MERGE_FRAC)
                if qt + 2 < NQT_RUN:
                    XA(qt + 2)
                X2(qt + 1)
            else:
                Y(qt)
        S.barrier()
        S.emit()


def ones_bf(C):
    return C.ones_bf

VEC_COLS = {}


def _vec_layout():
    cols = {}
    c = 0
    for l in range(2):
        for j in range(3):
            cols['norm_g_%d_%d' % (l, j)] = (c, 8)
            c += 8
    cols['rk_mix'] = (c, 48)
    c += 48
    return cols, c


def _vec64_layout():
    cols = {}
    c = 0
    for n in ('k_k', 'k_a', 'a0', 'r_k'):
        cols[n] = (c, 16)
        c += 16
    for n in ('q_g', 'k_g'):
        cols[n] = (c, 1)
        c += 1
    return cols, c


RK_SHAPES = {'w_r': [D, D], 'w_k': [D, D], 'w_v': [D, D], 'w_o': [D, D], 'w0': [1, D], 'w1': [D, 64], 'w2': [64, D],
             'a1': [D, 64], 'a2': [64, D], 'g1': [D, 160], 'g2': [160, D], 'lnx_g': [1, D], 'lnx_b': [1, D]}


def build_program(stages):
    nc = bass.Bass("TRN2", target_bir_lowering=False)
    C = Ctx()
    C.nc = nc
    din = lambda n, sh, dt=F32: nc.dram_tensor(n, list(sh), dt, kind="ExternalInput").ap()
    C.x = din("x", [SEQ, D])
    C.meta = din("meta", [NMETA, D])
    C.ffn_w_in = din("ffn_w_in", [2, 2, D, 2 * DFF])
    C.ffn_w_out = din("ffn_w_out", [2, 2, DFF, D])
    C.rk = {k: din("rk_" + k, sh) for k, sh in RK_SHAPES.items()}
    cols, nv = _vec_layout()
    cols64, nv64 = _vec64_layout()
    C.cols, C.cols64 = cols, cols64
    C.vec_d = din("vecs", [128, nv])
    C.vec64_d = din("vecs64", [64, nv64])
    C.ident_d = din("ident", [128, 128])
    C.masks_d = din("masks", [64, 192])
    C.tri_d = din("tri", [64, 128])
    C.at = {'w_in': din("at_w_in", [D, 2120]), 'w_o': din("at_w_o", [D, D])}
    C.rope_d = din("rope", [64, 2, TP])
    C.rot_d = din("rot", [64, 64])
    C.negmask_d = din("negmask", [128, 128])
    C.out = nc.dram_tensor("out", [SEQ, D], F32, kind="ExternalOutput").ap()
    C.hT = nc.dram_tensor("hT_scratch", [D, TP], F32, kind="Internal").ap()
    with ExitStack() as es:
        S = Sched(nc, es)
        C.S = S
        C.vec = es.enter_context(nc.sbuf_tensor("c_vec", [128, nv], F32))
        C.vec64 = es.enter_context(nc.sbuf_tensor("c_vec64", [64, nv64], F32))
        C.ident = es.enter_context(nc.sbuf_tensor("c_ident", [128, 128], F32))
        C.masks = es.enter_context(nc.sbuf_tensor("c_masks", [64, 192], F32))
        C.tri = es.enter_context(nc.sbuf_tensor("c_tri", [64, 128], F32))
        C.ones_bf = es.enter_context(nc.sbuf_tensor("c_ones_bf", [128, 128], BF16))
        C.ones_f = es.enter_context(nc.sbuf_tensor("c_ones_f", [128, 128], F32))
        C.eps_col = es.enter_context(nc.sbuf_tensor("c_eps", [128, 3], F32))
        S.op('sp', lambda e: e.dma_start(out=C.vec[:, :], in_=C.vec_d[:, :]), w=[('c_vec',)], dma=True)
        S.op('sp', lambda e: e.dma_start(out=C.vec64[:, :], in_=C.vec64_d[:, :]), w=[('c_vec64',)], dma=True)
        S.op('sp', lambda e: e.dma_start(out=C.ident[:, :], in_=C.ident_d[:, :]), w=[('c_ident',)], dma=True)
        S.op('sp', lambda e: e.dma_start(out=C.masks[:, :], in_=C.masks_d[:, :]), w=[('c_masks',)], dma=True)
        S.op('sp', lambda e: e.dma_start(out=C.tri[:, :], in_=C.tri_d[:, :]), w=[('c_tri',)], dma=True)
        S.op('pool', lambda e: e.memset(C.ones_bf[:, :], 1.0), w=[('c_ones',)])
        S.op('pool', lambda e: e.memset(C.ones_f[:, :], 1.0), w=[('c_onesf',)])
        S.op('pool', lambda e: e.memset(C.eps_col[:, 0:1], 1e-6), w=[('c_eps',)])
        S.op('pool', lambda e: e.memset(C.eps_col[:, 1:2], 64e-5), w=[('c_eps2',)])
        S.op('pool', lambda e: e.memset(C.eps_col[:, 2:3], 1e-24), w=[('c_eps3',)])
        S.barrier()
        stage_ingest(C)
        for sname in stages:
            if sname.startswith('ffn'):
                l, j = int(sname[3]), int(sname[4])
                stage_ffn(C, C.ffn_w_in[l, j], C.ffn_w_out[l, j], cols['norm_g_%d_%d' % (l, 0 if j == 0 else 2)][0],
                          "f%d%d_" % (l, j))
            elif sname == 'rwkv':
                stage_rwkv(C)
            elif sname == 'dsa':
                stage_dsa(C)
        stage_egress(C)
    return nc


def host_consts(inputs):
    cols, nv = _vec_layout()
    cols64, nv64 = _vec64_layout()
    vec = np.zeros((128, nv), np.float32)
    vec64 = np.zeros((64, nv64), np.float32)

    def put(name, v):
        c0, n = cols[name]
        vec[:, c0:c0 + n] = np.asarray(v, np.float32).reshape(n, 128).T

    def put64(name, v):
        c0, n = cols64[name]
        vec64[:, c0:c0 + n] = np.asarray(v, np.float32).reshape(n, 64).T

    ng = np.asarray(inputs['norm_g'])
    for l in range(2):
        for j in range(3):
            put('norm_g_%d_%d' % (l, j), ng[l, j])
    put('rk_mix', np.asarray(inputs['rk_mix'])[0].reshape(-1))
    put64('k_k', inputs['rk_k_k'][0])
    put64('k_a', inputs['rk_k_a'][0])
    put64('a0', inputs['rk_a0'][0])
    put64('r_k', np.asarray(inputs['rk_r_k'])[0].reshape(-1))
    vec64[:, cols64['q_g'][0]] = np.asarray(inputs['at_q_g'], np.float32)[0]
    vec64[:, cols64['k_g'][0]] = np.asarray(inputs['at_k_g'], np.float32)[0]
    inv = (np.float32(500000.0) ** (-np.arange(0, 16, 2, dtype=np.float32) / np.float32(16))).astype(np.float32)
    ang = (np.arange(TP, dtype=np.float32)[:, None] * inv[None, :]).astype(np.float32)
    rope = np.zeros((64, 2, TP), np.float32)
    rope[:, 0, :] = 1.0
    rope[0:8, 0, :] = np.cos(ang).T
    rope[8:16, 0, :] = np.cos(ang).T
    rope[0:8, 1, :] = np.sin(ang).T
    rope[8:16, 1, :] = np.sin(ang).T
    rot = np.zeros((64, 64), np.float32)
    for d in range(8):
        rot[d + 8, d] = -1.0
        rot[d, d + 8] = 1.0
    i128 = np.arange(128)
    negmask = np.where(i128[None, :] <= i128[:, None], 0.0, -1.0e30).astype(np.float32)
    ii = np.arange(64)
    su = (ii[:, None] < ii[None, :]).astype(np.float32)
    iu = (ii[:, None] <= ii[None, :]).astype(np.float32)
    sl = (ii[:, None] > ii[None, :]).astype(np.float32)
    masks = np.concatenate([su, iu, sl], axis=1)
    cdec = np.float32(-np.exp(-0.5))
    tri = np.concatenate([iu, su], axis=1) * cdec
    out = {"vecs": vec, "vecs64": vec64, "ident": np.eye(128, dtype=np.float32), "masks": masks,
           "tri": tri.astype(np.float32), "rope": rope, "rot": rot, "negmask": negmask,
           "at_w_in": np.ascontiguousarray(np.asarray(inputs['at_w_in'], np.float32)[0]),
           "at_w_o": np.ascontiguousarray(np.asarray(inputs['at_w_o'], np.float32)[0])}
    for k in RK_SHAPES:
        out["rk_" + k] = np.ascontiguousarray(np.asarray(inputs["rk_" + k], np.float32)[0].reshape(RK_SHAPES[k]))
    return out


ALL_STAGES = ['ffn00', 'rwkv', 'ffn01', 'ffn10', 'dsa', 'ffn11']
_cache = {}


def run(inputs, stages, cores=NCORES, trace=False):
    key = tuple(stages)
    if key not in _cache:
        _cache[key] = build_program(stages)
    nc = _cache[key]
    consts = host_consts(inputs)
    x = np.asarray(inputs['x'], np.float32)
    shared = {
        "meta": np.ascontiguousarray(np.asarray(inputs['meta'], np.float32)),
        "ffn_w_in": np.ascontiguousarray(np.asarray(inputs['ffn_w_in'], np.float32)),
        "ffn_w_out": np.ascontiguousarray(np.asarray(inputs['ffn_w_out'], np.float32)),
    }
    shared.update(consts)
    in_maps = []
    for b in range(cores):
        m = dict(shared)
        m["x"] = np.ascontiguousarray(x[b])
        in_maps.append(m)
    res = run_bass_kernel_spmd(nc, in_maps, core_ids=list(range(cores)), trace=trace)
    out = np.stack([np.asarray(r["out"], np.float32) for r in res.results], axis=0)
    return out, res


def kernel(**inputs):
    out, _ = run(inputs, ALL_STAGES)
    return out
```

```python
import numpy as np
from contextlib import ExitStack
import concourse.bass as bass
import concourse.mybir as mybir
from concourse.bass_utils import run_bass_kernel_spmd

F32 = mybir.dt.float32
BF16 = mybir.dt.bfloat16
F32R = mybir.dt.float32r
USE_F32R = False
IDT = BF16
TRDT = F32
ALU = mybir.AluOpType
AF = mybir.ActivationFunctionType
AX = mybir.AxisListType

D = 1024
NMETA = 16
SEQ = 4096
T = SEQ + NMETA
TP = 4160
DFF = 2816
NCORES = 8
DEBUG_NQT = 0
MERGE_FRAC = 0.0
NOVBF = False
DEBUG_RK = None


class Sched:
    ENGS = ['pe', 'act', 'dve', 'pool', 'sp']

    def __init__(self, nc, es, nds=16):
        self.nc = nc
        self.sem = {e: es.enter_context(nc.semaphore("s_" + e)) for e in self.ENGS}
        self.cnt = {e: 0 for e in self.ENGS}
        self.NDS = nds
        self.dq = {}
        self.dsem = []
        self.dcnt = []
        self.dtok = []
        for q in ('sp', 'pool', 'act'):
            base = len(self.dsem)
            for i in range(nds):
                self.dsem.append(es.enter_context(nc.semaphore("d_%s%d" % (q, i))))
                self.dcnt.append(0)
                self.dtok.append(None)
            self.dq[q] = [base, 0]
        self.pending = {e: [] for e in self.ENGS}
        self.last_w = {}
        self.readers = {}
        self.seen = {e: {} for e in self.ENGS}
        self.nops = 0

    def _semh(self, sk):
        return self.sem[sk[1]] if sk[0] == 'e' else self.dsem[sk[1]]

    def capture(self):
        self._cap = []
        return self._cap

    def end_capture(self):
        c = self._cap
        self._cap = None
        return c

    def replay_merged(self, a, b, frac=1.0):
        na, nb = len(a), len(b)
        nbe = max(1, int(nb * frac))
        ia = ib = 0
        while ia < na or ib < nb:
            if ib >= nb or (ia < na and ia * nbe <= ib * na):
                self.op(*a[ia])
                ia += 1
            else:
                self.op(*b[ib])
                ib += 1

    def op(self, eng, fn, r=(), w=(), dma=False):
        if getattr(self, '_cap', None) is not None:
            self._cap.append((eng, fn, tuple(r), tuple(w), dma))
            return None
        pr = [k for k in r if k[0] in PSKEYS and k not in w]
        if pr:
            w = list(w) + pr
        deps = []
        for k in r:
            if k in self.last_w:
                deps.append(self.last_w[k])
        for k in w:
            if k in self.last_w:
                deps.append(self.last_w[k])
            rd = self.readers.get(k)
            if rd:
                deps.extend(rd.items())
        if dma:
            qd = self.dq[eng]
            i = qd[0] + qd[1] % self.NDS
            qd[1] += 1
            if self.dtok[i] is not None:
                deps.append(self.dtok[i])
            self.dcnt[i] += 16
            tok = (('d', i), self.dcnt[i])
            self.dtok[i] = tok
        else:
            self.cnt[eng] += 1
            tok = (('e', eng), self.cnt[eng])
        waits = {}
        seen = self.seen[eng]
        for (sk, v) in deps:
            if eng == 'pe' and sk == ('e', 'pe'):
                continue
            if seen.get(sk, 0) >= v:
                continue
            if waits.get(sk, 0) < v:
                waits[sk] = v
        for sk, v in waits.items():
            seen[sk] = v
        self.pending[eng].append((fn, list(waits.items()), tok))
        self.nops += 1
        for k in w:
            self.last_w[k] = tok
            self.readers[k] = {}
        ws = set(w)
        for k in r:
            if k in ws:
                continue
            rd = self.readers.setdefault(k, {})
            if rd.get(tok[0], 0) < tok[1]:
                rd[tok[0]] = tok[1]
        return tok

    def barrier(self):
        allt = [(('e', e), self.cnt[e]) for e in self.ENGS if self.cnt[e] > 0]
        allt += [t for t in self.dtok if t is not None]
        for e in self.ENGS:
            waits = {}
            seen = self.seen[e]
            for sk, v in allt:
                if seen.get(sk, 0) >= v:
                    continue
                waits[sk] = max(waits.get(sk, 0), v)
            for sk, v in waits.items():
                seen[sk] = v
            self.pending[e].append((None, list(waits.items()), None))
        self.last_w = {}
        self.readers = {}

    def emit(self):
        nc = self.nc

        def mk(e):
            def body(engh):
                for fn, waits, tok in self.pending[e]:
                    for sk, v in waits:
                        engh.wait_ge(self._semh(sk), v)
                    if fn is None:
                        continue
                    ins = fn(engh)
                    sk, v = tok
                    if sk[0] == 'e':
                        ins.then_inc(self.sem[e], 1)
                    else:
                        ins.then_inc(self.dsem[sk[1]], 16)
            return body

        with nc.Block() as blk:
            blk.tensor(mk('pe'))
            blk.scalar(mk('act'))
            blk.vector(mk('dve'))
            blk.gpsimd(mk('pool'))
            blk.sync(mk('sp'))
        self.pending = {e: [] for e in self.ENGS}


PSKEYS = {'ps', 'psb', 'pA', 'pB', 'pO', 'psS'}


def keys(name, *idx_ranges):
    out = [(name,)]
    for r in idx_ranges:
        out = [o + (i,) for o in out for i in r]
    return out


class Ctx:
    pass


def stage_ingest(C):
    nc, S = C.nc, C.S
    with ExitStack() as st:
        xt = [st.enter_context(nc.sbuf_tensor("in_xt%d" % i, [128, 4, D], F32)) for i in range(2)]
        hx = [st.enter_context(nc.sbuf_tensor("in_hx%d" % i, [128, 8, 512], F32)) for i in range(2)]
        ps = [st.enter_context(nc.psum_tensor("in_ps%d" % i, [128, 512], F32)) for i in range(4)]
        hTv = C.hT.rearrange("(c p) t -> p c t", p=128)
        ident = C.ident
        S.op('pool', lambda e: e.memset(hx[1][:, :, 0:64], 0.0), w=keys('hx', [1], range(8)))
        S.op('pool', lambda e: e.dma_start(out=hTv[:, :, T:TP], in_=hx[1][:, :, 0:TP - T]),
             r=keys('hx', [1], range(8)), w=[('hTpad',)], dma=True)
        S.op('sp', lambda e: e.dma_start(out=xt[1][0:NMETA, 0, :], in_=C.meta[:, :]), w=[('xt', 1)], dma=True)
        for c in range(8):
            S.op('pe', lambda e, c=c: e.transpose(ps[c % 4][:, 0:NMETA], xt[1][0:NMETA, 0, c * 128:(c + 1) * 128],
                                                  ident[0:NMETA, 0:NMETA]),
                 r=[('xt', 1)], w=[('ps', c % 4)])
            S.op('dve', lambda e, c=c: e.tensor_copy(out=hx[1][:, c, 0:NMETA], in_=ps[c % 4][:, 0:NMETA]),
                 r=[('ps', c % 4)], w=[('hx', 1, c)])
        S.op('pool', lambda e: e.dma_start(out=hTv[:, :, 0:NMETA], in_=hx[1][:, :, 0:NMETA]),
             r=keys('hx', [1], range(8)), w=[('hTmeta',)], dma=True)
        xv = C.x.rearrange("(g a p) d -> g p a d", a=4, p=128)
        for g in range(SEQ // 512):
            s = g % 2
            S.op('sp', lambda e, g=g, s=s: e.dma_start(out=xt[s][:, :, :], in_=xv[g]), w=[('xt', s)], dma=True)
            for c in range(8):
                b = c % 4
                for a in range(4):
                    S.op('pe', lambda e, a=a, c=c, b=b, s=s: e.transpose(
                        ps[b][:, a * 128:(a + 1) * 128], xt[s][:, a, c * 128:(c + 1) * 128], ident[:, :]),
                        r=[('xt', s)], w=[('ps', b)])
                eng = 'dve' if c % 2 == 0 else 'act'
                if eng == 'dve':
                    S.op('dve', lambda e, c=c, b=b, s=s: e.tensor_copy(out=hx[s][:, c, :], in_=ps[b][:, :]),
                         r=[('ps', b)], w=[('hx', s, c)])
                else:
                    S.op('act', lambda e, c=c, b=b, s=s: e.copy(out=hx[s][:, c, :], in_=ps[b][:, :]),
                         r=[('ps', b)], w=[('hx', s, c)])
            t0 = NMETA + g * 512
            S.op('pool', lambda e, s=s, t0=t0: e.dma_start(out=hTv[:, :, t0:t0 + 512], in_=hx[s][:, :, :]),
                 r=keys('hx', [s], range(8)), w=[('hTin', g)], dma=True)
        S.barrier()
        S.emit()


def stage_egress(C):
    nc, S = C.nc, C.S
    with ExitStack() as st:
        xt = [st.enter_context(nc.sbuf_tensor("eg_xt%d" % i, [128, 4, D], F32)) for i in range(2)]
        hx = [st.enter_context(nc.sbuf_tensor("eg_hx%d" % i, [128, 8, 512], F32)) for i in range(2)]
        ps = [st.enter_context(nc.psum_tensor("eg_ps%d" % i, [128, 512], F32)) for i in range(4)]
        hTv = C.hT.rearrange("(c p) t -> p c t", p=128)
        ov = C.out.rearrange("(g a p) d -> g p a d", a=4, p=128)
        ident = C.ident
        for g in range(SEQ // 512):
            s = g % 2
            t0 = NMETA + g * 512
            S.op('sp', lambda e, s=s, t0=t0: e.dma_start(out=hx[s][:, :, :], in_=hTv[:, :, t0:t0 + 512]),
                 w=[('hx', s)], dma=True)
            for a in range(4):
                for hf in range(2):
                    b = (a * 2 + hf) % 4
                    for cc in range(4):
                        c = hf * 4 + cc
                        S.op('pe', lambda e, a=a, c=c, cc=cc, b=b, s=s: e.transpose(
                            ps[b][:, cc * 128:(cc + 1) * 128], hx[s][:, c, a * 128:(a + 1) * 128], ident[:, :]),
                            r=[('hx', s)], w=[('ps', b)])
                    if hf == 0:
                        S.op('dve', lambda e, a=a, b=b, s=s: e.tensor_copy(out=xt[s][:, a, 0:512], in_=ps[b][:, :]),
                             r=[('ps', b)], w=[('xt', s, a, 0)])
                    else:
                        S.op('act', lambda e, a=a, b=b, s=s: e.copy(out=xt[s][:, a, 512:1024], in_=ps[b][:, :]),
                             r=[('ps', b)], w=[('xt', s, a, 1)])
            S.op('pool', lambda e, g=g, s=s: e.dma_start(out=ov[g], in_=xt[s][:, :, :]),
                 r=keys('xt', [s], range(4), range(2)), w=[('out', g)], dma=True)
        S.barrier()
        S.emit()


def stage_ffn(C, w_in_d, w_out_d, gcol, tag):
    nc, S = C.nc, C.S
    TT = 256
    tiles = [(i * TT, TT) for i in range(TP // TT)]
    if TP % TT:
        tiles.append((TP - TP % TT, TP % TT))
    NJ = DFF // 128
    with ExitStack() as st:
        sb = lambda n, sh, dt: st.enter_context(nc.sbuf_tensor(tag + n, sh, dt))
        w_in = sb("w_in", [128, 8, 2 * DFF], BF16)
        w_out = sb("w_out", [128, NJ, D], BF16)
        x = [sb("x%d" % i, [128, 8, TT], F32) for i in range(2)]
        sq = [sb("sq%d" % i, [128, 8, TT], BF16) for i in range(2)]
        xn = [sb("xn%d" % i, [128, 8, TT], BF16) for i in range(2)]
        hm = [sb("hm%d" % i, [128, NJ, TT], BF16) for i in range(2)]
        sg = [sb("sg%d" % i, [128, TT], F32) for i in range(2)]
        rstd = [sb("rstd%d" % i, [128, TT], F32) for i in range(2)]
        psn = lambda n: st.enter_context(nc.psum_tensor(tag + n, [128, 512], F32))
        psA = [psn("pA%d" % i) for i in range(2)]
        psB = [psn("pB%d" % i) for i in range(2)]
        psO = [psn("pO%d" % i) for i in range(2)]
        psS = psn("pS")
        hTv = C.hT.rearrange("(c p) t -> p c t", p=128)
        w_in_v = w_in_d.rearrange("(k p) n -> p k n", p=128)
        w_out_v = w_out_d.rearrange("(j p) n -> p j n", p=128)
        ones = C.ones_bf
        vec = C.vec

        NB = 4
        cw = 2 * DFF // NB
        for k in range(8):
            for b in range(NB):
                S.op('pool', lambda e, k=k, b=b: e.dma_start(out=w_in[:, k, b * cw:(b + 1) * cw],
                                                             in_=w_in_v[:, k, b * cw:(b + 1) * cw]),
                     w=[('w_in', k, b)], dma=True)
        for j in range(NJ):
            S.op('pool', lambda e, j=j: e.dma_start(out=w_out[:, j, :], in_=w_out_v[:, j, :]),
                 w=[('w_out', j)], dma=True)
        win_keys = keys('w_in', range(8), range(NB))

        def load(i):
            t0, tw = tiles[i]
            s = i % 2
            S.op('sp', lambda e: e.dma_start(out=x[s][:, :, :tw], in_=hTv[:, :, t0:t0 + tw]),
                 r=[('hT', i)], w=keys('x', [s], range(8)), dma=True)

        def norm(i):
            t0, tw = tiles[i]
            s = i % 2
            S.op('act', lambda e: e.activation(out=sq[s][:, :, :tw], in_=x[s][:, :, :tw], func=AF.Square),
                 r=keys('x', [s], range(8)), w=[('sq', s)])
            for c in range(8):
                S.op('pe', lambda e, c=c: e.matmul(psS[:, :tw], lhsT=ones[:, :], rhs=sq[s][:, c, :tw],
                                                   start=(c == 0), stop=(c == 7)),
                     r=[('sq', s)], w=[('psS',)])
            S.op('act', lambda e: e.activation(out=rstd[s][:, :tw], in_=psS[:, :tw], func=AF.Sqrt,
                                               scale=1.0 / D, bias=C.eps_col[:, 0:1]),
                 r=[('psS',)], w=[('rstd', s)])
            S.op('dve', lambda e: e.reciprocal(out=rstd[s][:, :tw], in_=rstd[s][:, :tw]),
                 r=[('rstd', s)], w=[('rstd', s)])
            for c in range(8):
                S.op('dve', lambda e, c=c: e.scalar_tensor_tensor(
                    out=xn[s][:, c, :tw], in0=x[s][:, c, :tw], scalar=vec[:, gcol + c:gcol + c + 1],
                    in1=rstd[s][:, :tw], op0=ALU.mult, op1=ALU.mult),
                    r=[('x', s, c), ('rstd', s)], w=[('xn', s, c)])

        def mm_in(i):
            t0, tw = tiles[i]
            s = i % 2
            for j in range(NJ):
                q = j % 2
                for k in range(8):
                    S.op('pe', lambda e, j=j, k=k, q=q: e.matmul(
                        psA[q][:, :tw], lhsT=w_in[:, k, j * 128:(j + 1) * 128], rhs=xn[s][:, k, :tw],
                        start=(k == 0), stop=(k == 7)),
                        r=[('xn', s, k)] + (win_keys if (i == 0 and j == 0) else []), w=[('pA', q)])
                for k in range(8):
                    S.op('pe', lambda e, j=j, k=k, q=q: e.matmul(
                        psB[q][:, :tw], lhsT=w_in[:, k, DFF + j * 128:DFF + (j + 1) * 128], rhs=xn[s][:, k, :tw],
                        start=(k == 0), stop=(k == 7)),
                        r=[('xn', s, k)], w=[('pB', q)])
                S.op('act', lambda e, q=q: e.activation(out=sg[q][:, :tw], in_=psA[q][:, :tw], func=AF.Silu),
                     r=[('pA', q)], w=[('sg', q)])
                S.op('dve', lambda e, q=q, j=j: e.tensor_tensor(out=hm[s][:, j, :tw], in0=psB[q][:, :tw],
                                                                in1=sg[q][:, :tw], op=ALU.mult),
                     r=[('pB', q), ('sg', q)], w=[('hm', s, j)])

        def mm_out(i):
            t0, tw = tiles[i]
            s = i % 2
            for m in range(8):
                q = m % 2
                for j in range(NJ):
                    S.op('pe', lambda e, j=j, m=m, q=q: e.matmul(
                        psO[q][:, :tw], lhsT=w_out[:, j, m * 128:(m + 1) * 128], rhs=hm[s][:, j, :tw],
                        start=(j == 0), stop=(j == NJ - 1)),
                        r=[('hm', s, j), ('w_out', j)], w=[('pO', q)])
                S.op('dve', lambda e, m=m, q=q: e.scalar_tensor_tensor(
                    out=x[s][:, m, :tw], in0=psO[q][:, :tw], scalar=0.5, in1=x[s][:, m, :tw],
                    op0=ALU.mult, op1=ALU.add),
                    r=[('pO', q), ('x', s, m)], w=[('x', s, m)])
            S.op('pool', lambda e: e.dma_start(out=hTv[:, :, t0:t0 + tw], in_=x[s][:, :, :tw]),
                 r=keys('x', [s], range(8)), w=[('hT', i)], dma=True)

        n = len(tiles)
        load(0)
        norm(0)
        for i in range(n):
            if i + 1 < n:
                load(i + 1)
            mm_in(i)
            if i + 1 < n:
                norm(i + 1)
            mm_out(i)
        S.barrier()
        S.emit()


class H:
    def __init__(self, S):
        self.S = S

    def mm(self, out, lhsT, rhs, start, stop, r, w):
        if USE_F32R and lhsT.dtype == F32 and rhs.dtype == F32:
            lhsT = lhsT.bitcast(F32R)
            rhs = rhs.bitcast(F32R)
        self.S.op('pe', lambda e: e.matmul(out, lhsT=lhsT, rhs=rhs, start=start, stop=stop), r=r, w=w)

    def tr(self, out, in_, ident, r, w):
        self.S.op('pe', lambda e: e.transpose(out, in_, ident), r=r, w=w)

    def act(self, out, in_, func, r, w, scale=None, bias=None):
        kw = {}
        if scale is not None:
            kw['scale'] = scale
        if bias is not None:
            kw['bias'] = bias
        self.S.op('act', lambda e: e.activation(out=out, in_=in_, func=func, **kw), r=r, w=w)

    def cp(self, eng, out, in_, r, w):
        if eng == 'act':
            self.S.op('act', lambda e: e.copy(out=out, in_=in_), r=r, w=w)
        else:
            self.S.op(eng, lambda e: e.tensor_copy(out=out, in_=in_), r=r, w=w)

    def tt(self, eng, out, in0, in1, op, r, w):
        self.S.op(eng, lambda e: e.tensor_tensor(out=out, in0=in0, in1=in1, op=op), r=r, w=w)

    def ts(self, eng, out, in0, s1, s2, op0, op1, r, w):
        if s2 is None:
            self.S.op(eng, lambda e: e.tensor_scalar(out=out, in0=in0, scalar1=s1, scalar2=None, op0=op0), r=r, w=w)
        else:
            self.S.op(eng, lambda e: e.tensor_scalar(out=out, in0=in0, scalar1=s1, scalar2=s2, op0=op0, op1=op1),
                      r=r, w=w)

    def stt(self, eng, out, in0, scalar, in1, op0, op1, r, w):
        self.S.op(eng, lambda e: e.scalar_tensor_tensor(out=out, in0=in0, scalar=scalar, in1=in1, op0=op0, op1=op1),
                  r=r, w=w)

    def red(self, eng, out, in_, op, r, w):
        self.S.op(eng, lambda e: e.tensor_reduce(out=out, in_=in_, axis=AX.X, op=op), r=r, w=w)

    def dma(self, eng, out, in_, r, w):
        self.S.op(eng, lambda e: e.dma_start(out=out, in_=in_), r=r, w=w, dma=True)


def bk(*bs):
    return [('ps', b) for b in bs]


def stage_rwkv(C):
    nc, S = C.nc, C.S
    Hh = H(S)
    mm, tr, act, cp, tt, ts, stt, red, dma = Hh.mm, Hh.tr, Hh.act, Hh.cp, Hh.tt, Hh.ts, Hh.stt, Hh.red, Hh.dma
    CH = 64
    NT = TP // CH
    NH = 16
    cols = C.cols
    c64 = C.cols64
    with ExitStack() as st:
        sb = lambda n, sh, dt=F32: st.enter_context(nc.sbuf_tensor("rs_" + n, sh, dt))
        Wr = sb("Wr", [128, 8, D], BF16)
        Wk = sb("Wk", [128, 8, D], BF16)
        Wv = sb("Wv", [128, 8, D], BF16)
        Wo = sb("Wo", [128, 8, D], BF16)
        w1 = sb("w1", [128, 8, 64], BF16)
        a1 = sb("a1", [128, 8, 64], BF16)
        g1 = sb("g1", [128, 8, 160], BF16)
        a2 = sb("a2", [64, D], BF16)
        g2a = sb("g2a", [128, D], BF16)
        g2b = sb("g2b", [32, D], BF16)
        w2aug = sb("w2aug", [65, D], F32)
        lnxg = sb("lnxg", [64, D], F32)
        lnxb = sb("lnxb", [64, D], F32)
        omk = sb("omk", [64, NH], F32)
        PS = st.enter_context(nc.psum_tensor("rk_PS", [128, 3584], F32))
        PSb = st.enter_context(nc.psum_tensor("rk_PSb", [128, 1024], BF16))
        hbuf = [sb("hbuf%d" % i, [128, 8, CH]) for i in range(2)]
        sq = sb("sq", [128, 8, CH], BF16)
        rstd = sb("rstd", [128, CH])
        hn = sb("hn", [128, 8, CH + 1])
        xx = sb("xx", [128, 8, CH])
        xmf = [sb("xmf%d" % i, [128, 8, CH]) for i in range(1)]
        xm = [sb("xm%d" % i, [128, 8, CH], BF16) for i in range(6)]
        r_ = sb("r", [64, NH, CH])
        k_ = sb("k", [64, NH, CH])
        a_ = sb("a", [64, NH, CH])
        kk = sb("kk", [64, NH, CH])
        b_ = sb("b", [64, NH, CH])
        tmp1 = sb("tmp1", [64, NH, CH])
        tmp2 = sb("tmp2", [64, NH, CH])
        G = sb("G", [64, NH, CH])
        Ghat = sb("Ghat", [64, NH, CH])
        cumC = sb("cumC", [64, NH])
        AR = sb("AR", [64, NH, 2 * CH], BF16)
        Bt = sb("Bt", [64, NH, CH], BF16)
        Kt = sb("Kt", [64, NH, CH], BF16)
        Bh = sb("Bh", [64, NH, CH], TRDT)
        Kh = sb("Kh", [64, NH, CH], TRDT)
        g_tm = sb("g_tm", [64, D], BF16)
        lw_tm = sb("lw_tm", [64, D])
        twT = sb("twT", [65, CH])
        taT = sb("taT", [64, CH], BF16)
        sg0 = sb("sg0", [128, CH], BF16)
        sg1 = sb("sg1", [32, CH], BF16)
        bon = sb("bon", [64, NH])
        MX = sb("MX", [64, NH, 2 * CH], BF16)
        GG = sb("GG", [64, NH, 2 * CH])
        RKT = sb("RKT", [64, NH, CH], BF16)
        LakT = sb("LakT", [64, NH, CH], BF16)
        Hst = sb("Hst", [64, NH, CH])
        st1 = sb("st1", [64, NH])
        st2 = sb("st2", [64, NH])
        zT = sb("zT", [128, 8, CH], BF16)

        yc = sb("yc", [64, NH, CH])
        ysq = a_
        Gs = sb("Gs", [64, NH, CH], BF16)
        Us = sb("Us", [64, NH, CH], BF16)
        BhT = sb("BhT", [64, NH, CH], BF16)
        KhT = sb("KhT", [64, NH, CH], BF16)
        Lm = sb("Lm", [64, NH, CH], BF16)
        RBT = sb("RBT", [64, NH, CH], BF16)
        v_bf = sb("v_bf", [64, D], BF16)
        Hb = sb("Hb", [64, NH, CH], BF16)
        ident_bf = sb("ident_bf", [64, 64], BF16)
        Ginv = GG[:, :, 0:CH]
        Gex = GG[:, :, CH:2 * CH]

        hTv = C.hT.rearrange("(c p) t -> p c t", p=128)
        ident = C.ident
        vec = C.vec
        v64 = C.vec64

        for nm, dst, src in (("Wr", Wr, C.rk['w_r']), ("Wk", Wk, C.rk['w_k']), ("Wv", Wv, C.rk['w_v']),
                             ("Wo", Wo, C.rk['w_o'])):
            v = src.rearrange("(k p) n -> p k n", p=128)
            for k in range(8):
                dma('pool', dst[:, k, :], v[:, k, :], r=[], w=[(nm,)])
        dma('pool', w1[:, :, :], C.rk['w1'].rearrange("(k p) n -> p k n", p=128), r=[], w=[('w1',)])
        dma('pool', a1[:, :, :], C.rk['a1'].rearrange("(k p) n -> p k n", p=128), r=[], w=[('a1',)])
        dma('pool', g1[:, :, :], C.rk['g1'].rearrange("(k p) n -> p k n", p=128), r=[], w=[('g1',)])
        dma('pool', a2[:, :], C.rk['a2'][:, :], r=[], w=[('a2',)])
        dma('pool', g2a[:, :], C.rk['g2'][0:128, :], r=[], w=[('g2',)])
        dma('pool', g2b[:, :], C.rk['g2'][128:160, :], r=[], w=[('g2',)])
        dma('sp', w2aug[0:64, :], C.rk['w2'][:, :], r=[], w=[('w2aug',)])
        dma('sp', w2aug[64:65, :], C.rk['w0'][0:1, :], r=[], w=[('w2aug',)])
        dma('sp', lnxg[:, :], C.rk['lnx_g'][0:1, :].partition_broadcast(64), r=[], w=[('lnxg',)])
        dma('sp', lnxb[:, :], C.rk['lnx_b'][0:1, :].partition_broadcast(64), r=[], w=[('lnxb',)])
        kka = c64['k_a'][0]
        ts('dve', omk[:, :], v64[0:64, kka:kka + NH], -1.0, 1.0, ALU.mult, ALU.add, r=[('c_vec64',)], w=[('omk',)])
        S.op('pool', lambda e: e.memset(Hst[:, :, :], 0.0), w=[('Hst',)])
        S.op('pool', lambda e: e.memset(Hb[:, :, :], 0.0), w=[('Hb',)])
        cp('dve', ident_bf[:, :], ident[0:64, 0:64], r=[('c_ident',)], w=[('ident_bf',)])
        S.op('pool', lambda e: e.memset(hn[:, :, 0:1], 0.0), w=[('hn0',)])
        S.op('pool', lambda e: e.memset(twT[64:65, :], 1.0), w=[('twT1',)])

        def bc(ap2, n=CH):
            return ap2.unsqueeze(2).to_broadcast([64, NH, n])

        def prm(name):
            c0 = c64[name][0]
            return bc(v64[0:64, c0:c0 + NH])

        gcol = cols['norm_g_0_1'][0]
        mixc = cols['rk_mix'][0]
        psv2 = lambda b0: PS[0:64, b0 * 512:b0 * 512 + 2048].rearrange("p (h two s) -> p h two s", h=NH, two=2)
        psv1 = lambda b0: PS[0:64, b0 * 512:b0 * 512 + 1024].rearrange("p (h s) -> p h s", h=NH)
        SU = C.masks[0:64, 0:64].unsqueeze(1).to_broadcast([64, NH, CH])
        IU = C.masks[0:64, 64:128].unsqueeze(1).to_broadcast([64, NH, CH])
        SL = C.masks[0:64, 128:192].unsqueeze(1).to_broadcast([64, NH, CH])
        IDb = ident[0:64, 0:64].unsqueeze(1).to_broadcast([64, NH, CH])
        ones64 = C.ones_f[0:64, 0:64]

        def seg_A(t):
            t0 = t * CH
            hb = hbuf[t % 2]
            dma('sp', hb[:, :, :], hTv[:, :, t0:t0 + CH], r=[('hT', t)], w=[('hbuf', t % 2)])
            act(sq[:, :, :], hb[:, :, :], AF.Square, r=[('hbuf', t % 2)], w=[('sq',)])
            for c in range(8):
                mm(PS[:, 3072:3072 + CH], ones_bf(C)[:, :], sq[:, c, :], c == 0, c == 7, r=[('sq',)], w=bk(6))
            act(rstd[:, :], PS[:, 3072:3072 + CH], AF.Sqrt, r=bk(6), w=[('rstd',)], scale=1.0 / D, bias=C.eps_col[:, 0:1])
            S.op('dve', lambda e: e.reciprocal(out=rstd[:, :], in_=rstd[:, :]), r=[('rstd',)], w=[('rstd',)])
            for c in range(8):
                stt('dve', hn[:, c, 1:CH + 1], hb[:, c, :], vec[:, gcol + c:gcol + c + 1], rstd[:, :],
                    ALU.mult, ALU.mult, r=[('hbuf', t % 2), ('rstd',), ('hn0',)], w=[('hn', c)])
            hnk = keys('hn', range(8))
            tt('pool', xx[:, :, :], hn[:, :, 0:CH], hn[:, :, 1:CH + 1], ALU.subtract, r=hnk + [('hn0',)], w=[('xx',)])
            for i in range(6):
                mixb = vec[:, mixc + i * 8:mixc + i * 8 + 8].unsqueeze(2).to_broadcast([128, 8, CH])
                tt('pool', xmf[0][:, :, :], xx[:, :, :], mixb, ALU.mult, r=[('xx',), ('c_vec',)], w=[('xmf', 0)])
                tt('pool', xm[i][:, :, :], xmf[0][:, :, :], hn[:, :, 1:CH + 1], ALU.add,
                   r=[('xmf', 0)] + hnk, w=keys('xm', [i], range(8)))
            cp('pool', hn[:, :, 0:1], hn[:, :, CH:CH + 1], r=hnk + [('xx',)], w=[('hn0',)])

        def seg_P(t):
            xr, xw, xk, xv, xa, xg = xm
            for (Wt, wn, xs, xi, b0) in ((Wr, 'Wr', xr, 0, 0), (Wk, 'Wk', xk, 2, 2)):
                for h in range(NH):
                    for kc in range(8):
                        mm(PS[0:64, b0 * 512 + h * 64:b0 * 512 + (h + 1) * 64], Wt[:, kc, h * 64:(h + 1) * 64],
                           xs[:, kc, :], kc == 0, kc == 7, r=[('xm', xi, kc), (wn,)], w=bk(b0 + h // 8))

        def seg_E(t):
            t0 = t * CH
            PH = DEBUG_RK[1] if DEBUG_RK else 99
            hb = hbuf[t % 2]
            xr, xw, xk, xv, xa, xg = xm
            cp('act', r_[:, :, :], psv1(0), r=bk(0, 1), w=[('r',)])
            cp('act', k_[:, :, :], psv1(2), r=bk(2, 3), w=[('k',)])
            for n in range(2):
                for kc in range(8):
                    mm(PS[0:64, (4 + n) * 512:(5 + n) * 512], xv[:, kc, :], Wv[:, kc, n * 512:(n + 1) * 512],
                       kc == 0, kc == 7, r=[('xm', 3, kc), ('Wv',)], w=bk(4 + n))
            cp('dve', v_bf[:, :], PS[0:64, 2048:3072], r=bk(4, 5), w=[('v_bf',)])
            for kc in range(8):
                mm(PS[0:64, 3072:3072 + CH], w1[:, kc, :], xw[:, kc, :], kc == 0, kc == 7,
                   r=[('xm', 1, kc), ('w1',)], w=bk(6))
            act(twT[0:64, :], PS[0:64, 3072:3072 + CH], AF.Tanh, r=bk(6), w=[('twT',)])
            for kc in range(8):
                mm(PS[0:64, 0:CH], a1[:, kc, :], xa[:, kc, :], kc == 0, kc == 7,
                   r=[('xm', 4, kc), ('a1',)], w=bk(0))
            cp('dve', taT[:, :], PS[0:64, 0:CH], r=bk(0), w=[('taT',)])
            for kc in range(8):
                mm(PS[:, 3072:3072 + CH], g1[:, kc, 0:128], xg[:, kc, :], kc == 0, kc == 7,
                   r=[('xm', 5, kc), ('g1',)], w=bk(6))
            act(sg0[:, :], PS[:, 3072:3072 + CH], AF.Sigmoid, r=bk(6), w=[('sg0',)])
            for kc in range(8):
                mm(PS[0:32, 512:512 + CH], g1[:, kc, 128:160], xg[:, kc, :], kc == 0, kc == 7,
                   r=[('xm', 5, kc), ('g1',)], w=bk(1))
            act(sg1[:, :], PS[0:32, 512:512 + CH], AF.Sigmoid, r=bk(1), w=[('sg1',)])
            for n in range(2):
                mm(PS[0:64, n * 512:(n + 1) * 512], twT[0:65, :], w2aug[0:65, n * 512:(n + 1) * 512], True, True,
                   r=[('twT',), ('twT1',), ('w2aug',)], w=bk(n))
            act(lw_tm[:, :], PS[0:64, 0:1024], AF.Sigmoid, r=bk(0, 1), w=[('lw_tm',)])
            for h in range(NH):
                mm(PS[0:64, 1024 + h * 128:1024 + (h + 1) * 128], lw_tm[0:64, h * 64:(h + 1) * 64],
                   C.tri[0:64, 0:128], True, True, r=[('lw_tm',), ('c_tri',)], w=bk(2 + h // 4))
            pc = psv2(2)
            cb = bk(2, 3, 4, 5)
            act(G[:, :, :], pc[:, :, 0, :], AF.Exp, r=cb, w=[('G',)])
            act(Ginv[:, :, :], pc[:, :, 0, :], AF.Exp, r=cb, w=[('Ginv',)], scale=-1.0)
            act(Gex[:, :, :], pc[:, :, 1, :], AF.Exp, r=cb, w=[('Gex',)])
            cp('dve', cumC[:, :], pc[:, :, 0, CH - 1], r=cb, w=[('cumC',)])
            tt('dve', tmp1[:, :, :], bc(cumC[:, :]), pc[:, :, 0, :], ALU.subtract, r=cb + [('cumC',)], w=[('tmp1',)])
            act(Ghat[:, :, :], tmp1[:, :, :], AF.Exp, r=[('tmp1',)], w=[('Ghat',)])
            for h in range(NH):
                mm(PS[0:64, h * 64:(h + 1) * 64], a2[0:64, h * 64:(h + 1) * 64], taT[0:64, :], True, True,
                   r=[('taT',), ('a2',)], w=bk(h // 8))
            tt('dve', a_[:, :, :], psv1(0), prm('a0'), ALU.add, r=bk(0, 1) + [('c_vec64',)], w=[('a',)])
            act(a_[:, :, :], a_[:, :, :], AF.Sigmoid, r=[('a',)], w=[('a',)])
            for n in range(2):
                mm(PS[0:64, n * 512:(n + 1) * 512], sg0[:, :], g2a[:, n * 512:(n + 1) * 512], True, False,
                   r=[('sg0',), ('g2',)], w=bk(n))
                mm(PS[0:64, n * 512:(n + 1) * 512], sg1[0:32, :], g2b[0:32, n * 512:(n + 1) * 512], False, True,
                   r=[('sg1',), ('g2',)], w=bk(n))
            cp('act', g_tm[:, :], PS[0:64, 0:1024], r=bk(0, 1), w=[('g_tm',)])
            if PH < -1:
                return
            tt('dve', kk[:, :, :], k_[:, :, :], prm('k_k'), ALU.mult, r=[('k',), ('c_vec64',)], w=[('kk',)])
            act(Gs[:, :, :], kk[:, :, :], AF.Square, r=[('kk',)], w=[('Gs',)])
            t2f = Gs[:, :, :].rearrange("p h s -> p (h s)")
            for n in range(2):
                mm(PS[0:64, 1024 + n * 512:1024 + (n + 1) * 512], C.ones_bf[0:64, 0:64], t2f[:, n * 512:(n + 1) * 512],
                   True, True, r=[('Gs',), ('c_ones',)], w=bk(2 + n))
            tt('pool', tmp1[:, :, :], a_[:, :, :], prm('k_a'), ALU.mult, r=[('a',), ('c_vec64',)], w=[('tmp1',)])
            tt('pool', tmp1[:, :, :], tmp1[:, :, :], bc(omk[:, :]), ALU.add, r=[('tmp1',), ('omk',)], w=[('tmp1',)])
            tt('pool', k_[:, :, :], k_[:, :, :], tmp1[:, :, :], ALU.mult, r=[('k',), ('tmp1',)], w=[('k',)])
            tt('dve', AR[:, :, CH:2 * CH], r_[:, :, :], G[:, :, :], ALU.mult, r=[('r',), ('G',)], w=[('AR1',)])
            act(tmp2[:, :, :], psv1(2), AF.Ln, r=bk(2, 3), w=[('tmp2',)], bias=C.eps_col[0:64, 2:3])
            act(tmp2[:, :, :], tmp2[:, :, :], AF.Exp, r=[('tmp2',)], w=[('tmp2',)], scale=-0.5)
            tt('dve', Kt[:, :, :], k_[:, :, :], Ginv[:, :, :], ALU.mult, r=[('k',), ('Ginv',)], w=[('Kt',)])
            tt('dve', kk[:, :, :], kk[:, :, :], tmp2[:, :, :], ALU.mult, r=[('kk',), ('tmp2',)], w=[('kk',)])
            tt('dve', b_[:, :, :], kk[:, :, :], a_[:, :, :], ALU.mult, r=[('kk',), ('a',)], w=[('b',)])
            stt('dve', AR[:, :, 0:CH], kk[:, :, :], -1.0, Gex[:, :, :], ALU.mult, ALU.mult,
                r=[('kk',), ('Gex',)], w=[('AR0',)])
            tt('dve', Bt[:, :, :], b_[:, :, :], Ginv[:, :, :], ALU.mult, r=[('b',), ('Ginv',)], w=[('Bt',)])
            tt('dve', Bh[:, :, :], b_[:, :, :], Ghat[:, :, :], ALU.mult, r=[('b',), ('Ghat',)], w=[('Bh',)])
            tt('pool', Kh[:, :, :], k_[:, :, :], Ghat[:, :, :], ALU.mult, r=[('k',), ('Ghat',), ('Bt',)], w=[('Kh',)])
            if PH < 1:
                return
            GR = [(0, 8), (8, 8)]

            def hv(ap, g):
                return ap[:, GR[g][0]:GR[g][0] + 8, :]

            def pg2(b0):
                return PS[0:64, b0 * 512:b0 * 512 + 1024].rearrange("p (h two s) -> p h two s", h=8, two=2)

            def pg1(b0):
                return PS[0:64, b0 * 512:b0 * 512 + 512].rearrange("p (h s) -> p h s", h=8)

            SU8 = C.masks[0:64, 0:64].unsqueeze(1).to_broadcast([64, 8, CH])
            IU8 = C.masks[0:64, 64:128].unsqueeze(1).to_broadcast([64, 8, CH])
            SL8 = C.masks[0:64, 128:192].unsqueeze(1).to_broadcast([64, 8, CH])
            ID8 = ident[0:64, 0:64].unsqueeze(1).to_broadcast([64, 8, CH])
            for g in range(2):
                h0 = GR[g][0]
                bA = 0 if g == 0 else 3
                for hh in range(8):
                    h = h0 + hh
                    mm(PS[0:64, bA * 512 + hh * 128:bA * 512 + (hh + 1) * 128], Bt[:, h, :], AR[:, h, :], True, True,
                       r=[('Bt',), ('AR0',), ('AR1',)], w=bk(bA + hh // 4))
                tt('dve', hv(MX[:, :, 0:CH], g), pg2(bA)[:, :, 0, :], SU8, ALU.mult, r=bk(bA, bA + 1) + [('c_masks',)],
                   w=[('MX0', g)])
                tt('dve', hv(RBT, g), pg2(bA)[:, :, 1, :], IU8, ALU.mult, r=bk(bA, bA + 1) + [('c_masks',)],
                   w=[('RBT', g)])
                for hh in range(8):
                    h = h0 + hh
                    mm(PS[0:64, bA * 512 + hh * 128:bA * 512 + (hh + 1) * 128], Kt[:, h, :], AR[:, h, :], True, True,
                       r=[('Kt',), ('AR0',), ('AR1',)], w=bk(bA + hh // 4))
                tt('dve', hv(LakT, g), pg2(bA)[:, :, 0, :], SU8, ALU.mult, r=bk(bA, bA + 1) + [('c_masks',)],
                   w=[('LakT', g)])
                tt('dve', hv(RKT, g), pg2(bA)[:, :, 1, :], IU8, ALU.mult, r=bk(bA, bA + 1) + [('c_masks',)],
                   w=[('RKT', g)])
                for hh in range(8):
                    h = h0 + hh
                    mm(PS[0:64, (bA + 2) * 512 + hh * 64:(bA + 2) * 512 + (hh + 1) * 64], AR[:, h, 0:CH], Bt[:, h, :],
                       True, True, r=[('Bt',), ('AR0',)], w=bk(bA + 2))
                tt('dve', hv(Lm, g), pg1(bA + 2), SL8, ALU.mult, r=bk(bA + 2) + [('c_masks',)], w=[('Lm', g)])
                cp('pool', hv(MX[:, :, CH:2 * CH], g), ID8, r=[('c_ident',), ('Bt',)], w=[('MX1', g)])
            tt('pool', tmp1[:, :, :], r_[:, :, :], prm('r_k'), ALU.mult, r=[('r',), ('c_vec64',), ('Bt',)], w=[('tmp1',)])
            tt('pool', tmp1[:, :, :], tmp1[:, :, :], k_[:, :, :], ALU.mult, r=[('tmp1',), ('k',)], w=[('tmp1',)])
            if PH < 2:
                return
            for lvl in range(6):
                for g in range(2):
                    h0 = GR[g][0]
                    bA = 0 if g == 0 else 3
                    for hh in range(8):
                        h = h0 + hh
                        mm(PS[0:64, bA * 512 + hh * 128:bA * 512 + (hh + 1) * 128], Lm[:, h, :], MX[:, h, :], True, True,
                           r=[('Lm', g), ('MX0', g), ('MX1', g)], w=bk(bA + hh // 4))
                    if lvl < 5:
                        for hh in range(8):
                            h = h0 + hh
                            mm(PS[0:64, (bA + 2) * 512 + hh * 64:(bA + 2) * 512 + (hh + 1) * 64], MX[:, h, 0:CH], Lm[:, h, :],
                               True, True, r=[('Lm', g), ('MX0', g)], w=bk(bA + 2))
                for g in range(2):
                    bA = 0 if g == 0 else 3
                    tt('dve', hv(MX[:, :, CH:2 * CH], g), pg2(bA)[:, :, 1, :], hv(MX[:, :, CH:2 * CH], g), ALU.add,
                       r=bk(bA, bA + 1) + [('MX1', g)], w=[('MX1', g)])
                    if lvl < 5:
                        cp('act', hv(MX[:, :, 0:CH], g), pg2(bA)[:, :, 0, :], r=bk(bA, bA + 1), w=[('MX0', g)])
                        cp('act', hv(Lm, g), pg1(bA + 2), r=bk(bA + 2), w=[('Lm', g)])
                if lvl == 0 and t + 1 < NTR:
                    seg_A(t + 1)
            if PH < 3:
                return
            for h in range(NH):
                mm(PS[0:64, 3072 + h:3072 + h + 1], tmp1[:, h, :], C.ones_f[0:64, 0:1], True, True,
                   r=[('tmp1',), ('c_onesf',)], w=bk(6))
            cp('dve', bon[:, :], PS[0:64, 3072:3072 + NH], r=bk(6), w=[('bon',)])
            if TRDT == BF16:
                psb3 = PSb[0:64, :].rearrange("p (h s) -> p h s", h=NH)
                for h in range(NH):
                    tr(PSb[0:64, h * 64:(h + 1) * 64], Bh[:, h, :], ident_bf[:, :], r=[('Bh',), ('ident_bf',)], w=[('psb',)])
                cp('act', BhT[:, :, :], psb3, r=[('psb',)], w=[('BhT',)])
                for h in range(NH):
                    tr(PSb[0:64, h * 64:(h + 1) * 64], Kh[:, h, :], ident_bf[:, :], r=[('Kh',), ('ident_bf',)], w=[('psb',)])
                cp('dve', KhT[:, :, :], psb3, r=[('psb',)], w=[('KhT',)])
            else:
                for h in range(NH):
                    tr(PS[0:64, h * 64:(h + 1) * 64], Bh[:, h, :], ident[0:64, 0:64], r=[('Bh',), ('c_ident',)], w=bk(h // 8))
                cp('act', BhT[:, :, :], psv1(0), r=bk(0, 1), w=[('BhT',)])
                for h in range(NH):
                    tr(PS[0:64, 1024 + h * 64:1024 + (h + 1) * 64], Kh[:, h, :], ident[0:64, 0:64],
                       r=[('Kh',), ('c_ident',)], w=bk(2 + h // 8))
                cp('dve', KhT[:, :, :], psv1(2), r=bk(2, 3), w=[('KhT',)])
            if PH < 4:
                return
            for h in range(NH):
                o = PS[0:64, h * 64:(h + 1) * 64]
                mm(o, AR[:, h, 0:CH], Hb[:, h, :], True, False, r=[('AR0',), ('Hb',)], w=bk(h // 8))
                mm(o, LakT[:, h, :], v_bf[:, h * 64:(h + 1) * 64], False, True, r=[('LakT', h // 8), ('v_bf',)], w=bk(h // 8))
            cp('act', Gs[:, :, :], psv1(0), r=bk(0, 1), w=[('Gs',)])
            for h in range(NH):
                mm(PS[0:64, 1024 + h * 64:1024 + (h + 1) * 64], MX[:, h, CH:2 * CH], Gs[:, h, :], True, True,
                   r=[('MX1', h // 8), ('Gs',)], w=bk(2 + h // 8))
            cp('dve', Us[:, :, :], psv1(2), r=bk(2, 3), w=[('Us',)])
            for h in range(NH):
                o = PS[0:64, 2048 + h * 64:2048 + (h + 1) * 64]
                mm(o, AR[:, h, CH:2 * CH], Hb[:, h, :], True, False, r=[('AR1',), ('Hb',)], w=bk(4 + h // 8))
                mm(o, RBT[:, h, :], Us[:, h, :], False, False, r=[('RBT', h // 8), ('Us',)], w=bk(4 + h // 8))
                mm(o, RKT[:, h, :], v_bf[:, h * 64:(h + 1) * 64], False, True, r=[('RKT', h // 8), ('v_bf',)], w=bk(4 + h // 8))
            for h in range(NH):
                o = PS[0:64, h * 64:(h + 1) * 64]
                mm(o, BhT[:, h, :], Us[:, h, :], True, False, r=[('BhT',), ('Us',)], w=bk(h // 8))
                mm(o, KhT[:, h, :], v_bf[:, h * 64:(h + 1) * 64], False, True, r=[('KhT',), ('v_bf',)], w=bk(h // 8))
            tt('dve', Hst[:, :, :], Hst[:, :, :], G[:, :, CH - 1:CH].to_broadcast([64, NH, CH]), ALU.mult,
               r=[('Hst',), ('G',)], w=[('Hst',)])
            tt('dve', Hst[:, :, :], psv1(0), Hst[:, :, :], ALU.add, r=bk(0, 1) + [('Hst',)], w=[('Hst',)])
            cp('act', Hb[:, :, :], Hst[:, :, :], r=[('Hst',)], w=[('Hb',)])
            if PH < 5:
                return

        def seg_H(t):
            t0 = t * CH
            hb = hbuf[t % 2]
            py = psv1(4)
            yb = bk(4, 5)
            red('dve', st1[:, :], py, ALU.add, r=yb, w=[('st1',)])
            ts('dve', st1[:, :], st1[:, :], -1.0 / 64, None, ALU.mult, None, r=[('st1',)], w=[('st1',)])
            tt('dve', yc[:, :, :], py, bc(st1[:, :]), ALU.add, r=yb + [('st1',)], w=[('yc',)])
            act(ysq[:, :, :], yc[:, :, :], AF.Square, r=[('yc',)], w=[('a',)])
            red('dve', st2[:, :], ysq[:, :, :], ALU.add, r=[('a',)], w=[('st2',)])
            act(st2[:, :], st2[:, :], AF.Sqrt, r=[('st2',)], w=[('st2',)], scale=1.0 / 64, bias=C.eps_col[0:64, 1:2])
            S.op('dve', lambda e: e.reciprocal(out=st2[:, :], in_=st2[:, :]), r=[('st2',)], w=[('st2',)])
            tt('dve', yc[:, :, :], yc[:, :, :], bc(st2[:, :]), ALU.mult, r=[('yc',), ('st2',)], w=[('yc',)])
            ycf = yc[:, :, :].rearrange("p h s -> p (h s)")
            tt('dve', ycf, ycf, lnxg[:, :], ALU.mult, r=[('yc',), ('lnxg',)], w=[('yc',)])
            tt('dve', ycf, ycf, lnxb[:, :], ALU.add, r=[('yc',), ('lnxb',)], w=[('yc',)])
            vv = v_bf[:, :].rearrange("p (h s) -> p h s", h=NH)
            tt('dve', ysq[:, :, :], vv, bc(bon[:, :]), ALU.mult, r=[('v_bf',), ('bon',)], w=[('a',)])
            tt('dve', yc[:, :, :], yc[:, :, :], ysq[:, :, :], ALU.add, r=[('yc',), ('a',)], w=[('yc',)])
            tt('dve', ycf, ycf, g_tm[:, :], ALU.mult, r=[('yc',), ('g_tm',)], w=[('yc',)])
            for c in range(8):
                tr(PS[:, 3072 + c * 64:3072 + (c + 1) * 64], yc[:, 2 * c:2 * c + 2, :].rearrange("p h s -> p (h s)"),
                   ident[0:64, 0:64], r=[('yc',), ('c_ident',)], w=bk(6))
            cp('act', zT[:, :, :], PS[:, 3072:3584].rearrange("p (c s) -> p c s", c=8), r=bk(6), w=[('zT',)])
            for co in range(8):
                q = 4 + (co % 2)
                for kc in range(8):
                    mm(PS[:, q * 512:q * 512 + CH], Wo[:, kc, co * 128:(co + 1) * 128], zT[:, kc, :], kc == 0, kc == 7,
                       r=[('zT',), ('Wo',)], w=bk(q))
                tt('dve', hb[:, co, :], PS[:, q * 512:q * 512 + CH], hb[:, co, :], ALU.add,
                   r=bk(q) + [('hbuf', t % 2)], w=[('hbuf', t % 2)])
            dma('pool', hTv[:, :, t0:t0 + CH], hb[:, :, :], r=[('hbuf', t % 2)], w=[('hT', t)])

        NTR = NT if not DEBUG_RK else DEBUG_RK[0]
        if NTR > 0:
            seg_A(0)
            seg_P(0)
        for t in range(NTR):
            seg_E(t)
            if t + 1 < NTR:
                seg_P(t + 1)
            seg_H(t)
        S.barrier()
        S.emit()


def stage_dsa(C):
    nc, S = C.nc, C.S
    Hh = H(S)
    mm, tr, act, cp, tt, ts, stt, red, dma = Hh.mm, Hh.tr, Hh.act, Hh.cp, Hh.tt, Hh.ts, Hh.stt, Hh.red, Hh.dma
    TT = 128
    NQT = (TP + TT - 1) // TT
    NQT_RUN = min(NQT, DEBUG_NQT) if DEBUG_NQT else NQT
    c64 = C.cols64
    cols = C.cols
    NBIS = 13
    MB = 240000.0
    with ExitStack() as st:
        sb = lambda n, sh, dt=F32: st.enter_context(nc.sbuf_tensor("ds_" + n, sh, dt))
        Wq = sb("Wq", [128, 8, 1024], BF16)
        Wk = sb("Wk", [128, 8, 256], BF16)
        Wv = sb("Wv", [128, 8, 256], BF16)
        Wqi = sb("Wqi", [128, 8, 512], BF16)
        Wki = sb("Wki", [128, 8, 64], BF16)
        Wwi = sb("Wwi", [128, 8, 8], BF16)
        Wo = sb("Wo", [128, 8, 1024], BF16)
        kT = sb("kT", [64, 4, TP], BF16)
        Vaug = sb("Vaug", [128, NQT, 4, 65], BF16)
        kiT = sb("kiT", [64, TP], BF16)
        score = sb("score", [128, TP])
        work = sb("work", [128, TP])
        mask01 = sb("mask01", [128, TP], BF16)
        maskT = sb("maskT", [128, NQT, TT], BF16)
        hbuf = [sb("hbuf%d" % i, [128, 8, TT]) for i in range(3)]
        hnb = sb("hnb", [128, 8, TT], BF16)
        sq = sb("sq", [128, 8, TT], BF16)
        rstd = sb("rstd", [128, TT])
        qT = [sb("qT%d" % i, [64, 16 * TT], BF16) for i in range(2)]
        qiT = sb("qiT", [64, 8 * TT], BF16)
        tA = sb("tA", [64, 512])
        tB = sb("tB", [64, 512])
        tC = sb("tC", [64, 512])
        rl = [sb("rl%d" % i, [128, 512]) for i in range(2)]
        PT = [sb("PT%d" % i, [128, 512], BF16) for i in range(3)]
        o_tm = sb("o_tm", [128, 1024], BF16)
        oT = sb("oT", [128, 8, TT], BF16)
        cs = sb("cs", [64, 2, TT])
        wi = sb("wi", [128, 8])
        m8 = sb("m8", [128, 8])
        eq8 = sb("eq8", [128, 8])
        iota8 = sb("iota8", [128, 8])
        lo = sb("lo", [128, 1])
        HC = sb("HC", [128, 2])
        MC = sb("MC", [128, 2])
        sel = sb("sel", [128, 1])
        d1 = sb("d1", [128, 1])
        d2 = sb("d2", [128, 2])
        thr = sb("thr", [128, 1])
        nm1 = sb("nm1", [128, 1])
        halfc = sb("halfc", [128, 1])
        negb = sb("negb", [128, 1])
        rden = sb("rden", [128, 16])
        ident_bf = sb("ident_bf", [128, 128], BF16)
        zeros_bf = sb("zeros_bf", [128, 512], BF16)
        rot = sb("rot", [64, 64])
        negmask = sb("negmask", [128, 128])
        PS = st.enter_context(nc.psum_tensor("ds_PS", [128, 3584], F32))
        PSb = st.enter_context(nc.psum_tensor("ds_PSb", [128, 1024], BF16))
        bank = lambda b: PS[:, b * 512:(b + 1) * 512]

        hTv = C.hT.rearrange("(c p) t -> p c t", p=128)
        ident = C.ident
        vec = C.vec
        v64 = C.vec64
        win = C.at['w_in'].rearrange("(k p) n -> p k n", p=128)
        for k in range(8):
            dma('pool', Wq[:, k, :], win[:, k, 0:1024], r=[], w=[('Wq',)])
        dma('pool', Wk[:, :, :], win[:, :, 1024:1280], r=[], w=[('Wk',)])
        dma('pool', Wv[:, :, :], win[:, :, 1280:1536], r=[], w=[('Wv',)])
        for k in range(8):
            dma('pool', Wqi[:, k, :], win[:, k, 1536:2048], r=[], w=[('Wqi',)])
        dma('pool', Wki[:, :, :], win[:, :, 2048:2112], r=[], w=[('Wki',)])
        dma('pool', Wwi[:, :, :], win[:, :, 2112:2120], r=[], w=[('Wwi',)])
        wov = C.at['w_o'].rearrange("(k p) n -> p k n", p=128)
        for k in range(8):
            dma('pool', Wo[:, k, :], wov[:, k, :], r=[], w=[('Wo',)])
        dma('sp', rot[:, :], C.rot_d[:, :], r=[], w=[('rot',)])
        dma('sp', negmask[:, :], C.negmask_d[:, :], r=[], w=[('negmask',)])
        cp('dve', ident_bf[:, :], ident[:, :], r=[('c_ident',)], w=[('ident_bf',)])
        S.op('pool', lambda e: e.memset(zeros_bf[:, :], 0.0), w=[('zeros_bf',)])
        S.op('pool', lambda e: e.memset(Vaug[:, :, :, 64:65], 1.0), w=[('Vones',)])
        S.op('pool', lambda e: e.memset(halfc[:, :], 0.5), w=[('halfc',)])
        S.op('pool', lambda e: e.memset(negb[:, :], -MB), w=[('negb',)])
        for j in range(8):
            S.op('pool', lambda e, j=j: e.memset(iota8[:, j:j + 1], float(j)), w=[('iota8',)])
        gcol = cols['norm_g_1_1'][0]
        qg = v64[0:64, c64['q_g'][0]:c64['q_g'][0] + 1]
        kg = v64[0:64, c64['k_g'][0]:c64['k_g'][0] + 1]
        kwid = lambda kb: min(128, TP - kb * 128)

        def norm_rope(pb, nh, tw, gcolap, out3, okeys_w):
            n = nh * tw
            pin = bank(pb)[0:64, 0:n]
            v3 = lambda ap: ap.rearrange("p (h s) -> p h s", h=nh)
            if gcolap is not None:
                act(tC[:, 0:n], pin, AF.Copy, r=bk(pb) + [('c_vec64',)], w=[('tC',)], scale=gcolap)
            else:
                cp('act', tC[:, 0:n], pin, r=bk(pb), w=[('tC',)])
            mm(bank(6)[0:64, 0:n], rot[:, :], tC[:, 0:n], True, True, r=[('tC',), ('rot',)], w=bk(6))
            cosb = cs[:, 0, 0:tw].unsqueeze(1).to_broadcast([64, nh, tw])
            sinb = cs[:, 1, 0:tw].unsqueeze(1).to_broadcast([64, nh, tw])
            if gcolap is not None:
                act(tA[:, 0:n], pin, AF.Square, r=bk(pb), w=[('tA',)])
            tt('pool', v3(tC[:, 0:n]), v3(tC[:, 0:n]), cosb, ALU.mult, r=[('tC',), ('cs',)], w=[('tC',)])
            tt('dve', v3(tB[:, 0:n]), v3(bank(6)[0:64, 0:n]), sinb, ALU.mult, r=bk(6) + [('cs',)], w=[('tB',)])
            if gcolap is None:
                tt('pool', out3, v3(tC[:, 0:n]), v3(tB[:, 0:n]), ALU.add, r=[('tC',), ('tB',)], w=okeys_w)
                return
            mm(bank(6)[0:64, 0:n], C.ones_f[0:64, 0:64], tA[:, 0:n], True, True, r=[('tA',), ('c_onesf',)], w=bk(6))
            act(tA[:, 0:n], bank(6)[0:64, 0:n], AF.Ln, r=bk(6), w=[('tA',)], scale=1.0 / 64, bias=C.eps_col[0:64, 0:1])
            act(tA[:, 0:n], tA[:, 0:n], AF.Exp, r=[('tA',)], w=[('tA',)], scale=-0.5)
            tt('pool', tC[:, 0:n], tC[:, 0:n], tB[:, 0:n], ALU.add, r=[('tC',), ('tB',)], w=[('tC',)])
            tt('dve', out3, v3(tC[:, 0:n]), v3(tA[:, 0:n]), ALU.mult, r=[('tC',), ('tA',)], w=okeys_w)

        def XA(qt):
            s = qt % 2
            s3 = qt % 3
            t0 = qt * TT
            tw = min(TT, TP - t0)
            n = t0 + tw
            nkb = qt + 1
            hb = hbuf[s3]
            dma('sp', hb[:, :, 0:tw], hTv[:, :, t0:t0 + tw], r=[('hT', qt)], w=[('hbuf', s3)])
            dma('sp', cs[:, :, 0:tw], C.rope_d[:, :, t0:t0 + tw], r=[], w=[('cs',)])
            act(sq[:, :, 0:tw], hb[:, :, 0:tw], AF.Square, r=[('hbuf', s3)], w=[('sq',)])
            for c in range(8):
                mm(bank(6)[:, 0:tw], C.ones_bf[:, :], sq[:, c, 0:tw], c == 0, c == 7, r=[('sq',)], w=bk(6))
            act(rstd[:, 0:tw], bank(6)[:, 0:tw], AF.Sqrt, r=bk(6), w=[('rstd',)], scale=1.0 / D, bias=C.eps_col[:, 0:1])
            S.op('dve', lambda e: e.reciprocal(out=rstd[:, 0:tw], in_=rstd[:, 0:tw]), r=[('rstd',)], w=[('rstd',)])
            for c in range(8):
                stt('dve', hnb[:, c, 0:tw], hb[:, c, 0:tw], vec[:, gcol + c:gcol + c + 1], rstd[:, 0:tw],
                    ALU.mult, ALU.mult, r=[('hbuf', s3), ('rstd',)], w=[('hnb',)])

        def XR(qt):
            s = qt % 2
            t0 = qt * TT
            tw = min(TT, TP - t0)
            n = t0 + tw
            nkb = qt + 1
            for g in range(4):
                for kc in range(8):
                    mm(bank(5)[0:64, g * tw:(g + 1) * tw], Wk[:, kc, g * 64:(g + 1) * 64], hnb[:, kc, 0:tw], kc == 0, kc == 7,
                       r=[('hnb',), ('Wk',)], w=bk(5))
            norm_rope(5, 4, tw, kg, kT[:, :, t0:t0 + tw], [('kT',)])
            for kc in range(8):
                mm(bank(5)[0:tw, 0:256], hnb[:, kc, 0:tw], Wv[:, kc, :], kc == 0, kc == 7, r=[('hnb',), ('Wv',)], w=bk(5))
            cp('act', Vaug[0:tw, qt, :, 0:64], bank(5)[0:tw, 0:256].rearrange("p (g d) -> p g d", g=4), r=bk(5),
               w=[('Vaug',)])
            for kc in range(8):
                mm(bank(5)[0:64, 0:tw], Wki[:, kc, :], hnb[:, kc, 0:tw], kc == 0, kc == 7, r=[('hnb',), ('Wki',)], w=bk(5))
            norm_rope(5, 1, tw, None, kiT[:, t0:t0 + tw].unsqueeze(1), [('kiT',)])
            for grp in range(4):
                pb = 5
                for hh in range(4):
                    h = grp * 4 + hh
                    for kc in range(8):
                        mm(bank(pb)[0:64, hh * tw:(hh + 1) * tw], Wq[:, kc, h * 64:(h + 1) * 64], hnb[:, kc, 0:tw],
                           kc == 0, kc == 7, r=[('hnb',), ('Wq',)], w=bk(pb))
                norm_rope(pb, 4, tw, qg, qT[s][:, grp * 4 * tw:(grp + 1) * 4 * tw].rearrange("p (h s) -> p h s", h=4),
                          [('qT', s)])
            for grp in range(2):
                pb = 5
                for hh in range(4):
                    h = grp * 4 + hh
                    for kc in range(8):
                        mm(bank(pb)[0:64, hh * tw:(hh + 1) * tw], Wqi[:, kc, h * 64:(h + 1) * 64], hnb[:, kc, 0:tw],
                           kc == 0, kc == 7, r=[('hnb',), ('Wqi',)], w=bk(pb))
                norm_rope(pb, 4, tw, None, qiT[:, grp * 4 * tw:(grp + 1) * 4 * tw].rearrange("p (h s) -> p h s", h=4),
                          [('qiT',)])
            for kc in range(8):
                mm(bank(6)[0:tw, 0:8], hnb[:, kc, 0:tw], Wwi[:, kc, :], kc == 0, kc == 7, r=[('hnb',), ('Wwi',)], w=bk(6))
            ts('dve', wi[0:tw, :], bank(6)[0:tw, 0:8], float(512.0 ** -0.5), None, ALU.mult, None, r=bk(6), w=[('wi',)])
            idx = 0
            for k0 in range(0, n, 512):
                nk = min(512, n - k0)
                for h in range(8):
                    pb = 5 + idx % 2
                    rb = rl[idx % 2]
                    rk = ('rl', idx % 2)
                    idx += 1
                    mm(bank(pb)[0:tw, 0:nk], qiT[:, h * tw:(h + 1) * tw], kiT[:, k0:k0 + nk], True, True,
                       r=[('qiT',), ('kiT',)], w=bk(pb))
                    act(rb[0:tw, 0:nk], bank(pb)[0:tw, 0:nk], AF.Relu, r=bk(pb), w=[rk])
                    if h == 0:
                        ts('dve', score[0:tw, k0:k0 + nk], rb[0:tw, 0:nk], wi[0:tw, 0:1], None, ALU.mult, None,
                           r=[rk, ('wi',)], w=[('score',)])
                    else:
                        stt('dve', score[0:tw, k0:k0 + nk], rb[0:tw, 0:nk], wi[0:tw, h:h + 1], score[0:tw, k0:k0 + nk],
                            ALU.mult, ALU.add, r=[rk, ('wi',), ('score',)], w=[('score',)])
            sc = score[0:tw, 0:n]
            if n > 256:
                red('dve', HC[0:tw, 0:1], sc, ALU.max, r=[('score',)], w=[('HC',)])
                red('dve', lo[0:tw, :], sc, ALU.min, r=[('score',)], w=[('lo',)])
            tt('dve', score[0:tw, t0:t0 + tw], score[0:tw, t0:t0 + tw], negmask[0:tw, 0:tw], ALU.add,
               r=[('score',), ('negmask',)], w=[('score',)])
            if n > 256:
                tt('dve', d1[0:tw, :], HC[0:tw, 0:1], lo[0:tw, :], ALU.subtract, r=[('HC',), ('lo',)], w=[('d1',)])
                stt('dve', HC[0:tw, 0:1], d1[0:tw, :], 1.0e-6, HC[0:tw, 0:1], ALU.mult, ALU.add, r=[('d1',), ('HC',)],
                    w=[('HC',)])
                ts('dve', HC[0:tw, 1:2], d1[0:tw, :], 0.0, None, ALU.mult, None, r=[('d1',), ('HC',)], w=[('HC',)])
                for it in range(NBIS):
                    stt('dve', MC[0:tw, 0:1], lo[0:tw, :], HC[0:tw, 0:1], halfc[0:tw, :], ALU.add, ALU.mult,
                        r=[('lo',), ('HC',), ('halfc',)], w=[('MC',)])
                    S.op('dve', lambda e, tw=tw, n=n: e.tensor_scalar(
                        out=mask01[0:tw, 0:n], in0=score[0:tw, 0:n], scalar1=MC[0:tw, 0:1], scalar2=0.0,
                        op0=ALU.is_ge, op1=ALU.add, accum_out=MC[0:tw, 1:2]),
                        r=[('score',), ('MC',)], w=[('mask01',), ('MC',)])
                    ts('dve', sel[0:tw, :], MC[0:tw, 1:2], 256.0, None, ALU.is_ge, None, r=[('MC',)], w=[('sel',)])
                    tt('dve', d1[0:tw, :], MC[0:tw, 0:1], lo[0:tw, :], ALU.subtract, r=[('MC',), ('lo',)], w=[('d1',)])
                    stt('dve', lo[0:tw, :], d1[0:tw, :], sel[0:tw, 0:1], lo[0:tw, :], ALU.mult, ALU.add,
                        r=[('d1',), ('sel',), ('lo',)], w=[('lo',)])
                    tt('dve', d2[0:tw, :], HC[0:tw, :], MC[0:tw, :], ALU.subtract, r=[('MC',), ('HC',)], w=[('d2',)])
                    stt('dve', HC[0:tw, :], d2[0:tw, :], sel[0:tw, 0:1], MC[0:tw, :], ALU.mult, ALU.add,
                        r=[('d2',), ('sel',), ('MC',)], w=[('HC',)])
                wk = work[0:tw, 0:n]
                ts('dve', wk, sc, HC[0:tw, 0:1], 1.0e20, ALU.is_ge, ALU.mult, r=[('score',), ('HC',)], w=[('work',)])
                tt('dve', wk, sc, wk, ALU.subtract, r=[('score',), ('work',)], w=[('work',)])
                S.op('dve', lambda e, tw=tw, n=n: e.max(out=m8[0:tw, :], in_=work[0:tw, 0:n]), r=[('work',)], w=[('m8',)])
                ts('dve', nm1[0:tw, :], HC[0:tw, 1:2], -1.0, 255.0, ALU.mult, ALU.add, r=[('HC',)], w=[('nm1',)])
                ts('dve', nm1[0:tw, :], nm1[0:tw, :], 7.0, 0.0, ALU.min, ALU.max, r=[('nm1',)], w=[('nm1',)])
                ts('dve', eq8[0:tw, :], iota8[0:tw, :], nm1[0:tw, 0:1], None, ALU.is_equal, None, r=[('nm1',), ('iota8',)],
                   w=[('eq8',)])
                tt('dve', eq8[0:tw, :], eq8[0:tw, :], m8[0:tw, :], ALU.mult, r=[('eq8',), ('m8',)], w=[('eq8',)])
                red('dve', thr[0:tw, :], eq8[0:tw, :], ALU.add, r=[('eq8',)], w=[('thr',)])
                ts('dve', mask01[0:tw, 0:n], sc, thr[0:tw, 0:1], None, ALU.is_ge, None, r=[('score',), ('thr',)],
                   w=[('mask01',)])
            else:
                ts('dve', mask01[0:tw, 0:n], sc, -1.0e29, None, ALU.is_ge, None, r=[('score',)], w=[('mask01',)])
        def X2(qt):
            t0 = qt * TT
            tw = min(TT, TP - t0)
            nkb = qt + 1
            for kb0 in range(0, nkb, 8):
                nb = min(8, nkb - kb0)
                for j in range(nb):
                    kb = kb0 + j
                    kw = kwid(kb)
                    tr(PSb[0:kw, j * 128:j * 128 + tw], mask01[0:tw, kb * 128:kb * 128 + kw], ident_bf[0:tw, 0:tw],
                       r=[('mask01',), ('ident_bf',)], w=[('psb',)])
                kwl = kwid(kb0 + nb - 1)
                nfull = nb if kwl == 128 else nb - 1
                if nfull > 0:
                    act(maskT[:, kb0:kb0 + nfull, 0:tw],
                        PSb[:, 0:nfull * 128].rearrange("p (j s) -> p j s", j=nfull)[:, :, 0:tw], AF.Identity,
                        r=[('psb',), ('negb',)], w=[('maskT', kb) for kb in range(kb0, kb0 + nfull)],
                        scale=MB, bias=negb[:, 0:1])
                if nfull < nb:
                    act(maskT[0:kwl, kb0 + nb - 1, 0:tw], PSb[0:kwl, (nb - 1) * 128:(nb - 1) * 128 + tw], AF.Identity,
                        r=[('psb',), ('negb',)], w=[('maskT', kb0 + nb - 1)], scale=MB, bias=negb[0:kwl, 0:1])

        def Y(qt):
            s = qt % 2
            t0 = qt * TT
            tw = min(TT, TP - t0)
            nkb = qt + 1
            hb = hbuf[qt % 3]
            q_ = qT[s]
            OB = [(0, 0, 7), (1, 7, 7), (2, 14, 2)]
            for (ob, h0, nh) in OB:
                S.op('pe', lambda e, ob=ob, nh=nh: e.matmul(bank(ob)[0:tw, 0:nh * 65], lhsT=zeros_bf[:, 0:tw],
                                                          rhs=zeros_bf[:, 0:nh * 65], start=True, stop=False,
                                                          skip_group_check=True),
                     r=[('zeros_bf',)], w=bk(ob))
            jobs = [(kb, g) for kb in range(nkb) for g in range(4)]

            def s_part(i):
                kb, g = jobs[i]
                kw = kwid(kb)
                pb = 3 + i % 2
                pt = PT[i % 3]
                pk = ('PT', i % 3)
                mbias = maskT[0:kw, kb, 0:tw].unsqueeze(1).to_broadcast([kw, 4, tw])
                mm(bank(pb)[0:kw, 0:4 * tw], kT[:, g, kb * 128:kb * 128 + kw], q_[:, g * 4 * tw:(g + 1) * 4 * tw],
                   True, False, r=[('kT',), ('qT', s)], w=bk(pb))
                mm(bank(pb)[0:kw, 0:4 * tw].rearrange("p (h s) -> p h s", h=4), ident_bf[0:kw, 0:kw], mbias,
                   False, True, r=[('maskT', kb), ('ident_bf',)], w=bk(pb))
                act(pt[0:kw, 0:4 * tw], bank(pb)[0:kw, 0:4 * tw], AF.Exp, r=bk(pb), w=[pk], scale=0.125)

            def v_part(i):
                kb, g = jobs[i]
                kw = kwid(kb)
                pt = PT[i % 3]
                pk = ('PT', i % 3)
                for rr in range(4):
                    h = 4 * g + rr
                    S.op('pe', lambda e, h=h, rr=rr, kw=kw, pt=pt, kb=kb, g=g, last=(kb == nkb - 1): e.matmul(
                        bank(h // 7)[0:tw, (h % 7) * 65:(h % 7 + 1) * 65], lhsT=pt[0:kw, rr * tw:(rr + 1) * tw],
                        rhs=Vaug[0:kw, kb, g, :], start=False, stop=last, skip_group_check=True),
                        r=[pk, ('Vaug',), ('Vones',)], w=bk(h // 7))

            s_part(0)
            for i in range(len(jobs)):
                if i + 1 < len(jobs):
                    s_part(i + 1)
                v_part(i)
            for (ob, h0, nh) in OB:
                o3 = bank(ob)[0:tw, 0:nh * 65].rearrange("p (h d) -> p h d", h=nh)
                S.op('dve', lambda e, o3=o3, h0=h0, nh=nh: e.reciprocal(out=rden[0:tw, h0:h0 + nh], in_=o3[:, :, 64]),
                     r=bk(ob), w=[('rden', ob)])
                tt('dve', o_tm[0:tw, h0 * 64:(h0 + nh) * 64].rearrange("p (h d) -> p h d", h=nh), o3[:, :, 0:64],
                   rden[0:tw, h0:h0 + nh].unsqueeze(2).to_broadcast([tw, nh, 64]), ALU.mult,
                   r=bk(ob) + [('rden', ob)], w=[('o_tm',)])
            for c in range(8):
                tr(PSb[:, c * 128:c * 128 + tw], o_tm[0:tw, c * 128:(c + 1) * 128], ident_bf[0:tw, 0:tw],
                   r=[('o_tm',), ('ident_bf',)], w=[('psb',)])
            cp('act', oT[:, :, 0:tw], PSb[:, :].rearrange("p (c s) -> p c s", c=8)[:, :, 0:tw],
               r=[('psb',)], w=[('oT',)])
            for co in range(8):
                pb = 3 + co % 2
                for kc in range(8):
                    mm(bank(pb)[:, 0:tw], Wo[:, kc, co * 128:(co + 1) * 128], oT[:, kc, 0:tw], kc == 0, kc == 7,
                       r=[('oT',), ('Wo',)], w=bk(pb))
                tt('dve', hb[:, co, 0:tw], bank(pb)[:, 0:tw], hb[:, co, 0:tw], ALU.add, r=bk(pb) + [('hbuf', qt % 3)],
                   w=[('hbuf', qt % 3)])
            dma('pool', hTv[:, :, t0:t0 + tw], hb[:, :, 0:tw], r=[('hbuf', qt % 3)], w=[('hT', qt)])

        XA(0)
        XR(0)
        X2(0)
        if NQT_RUN > 1:
            XA(1)
        for qt in range(NQT_RUN):
            if qt + 1 < NQT_RUN:
                S.capture()
                XR(qt + 1)
                la = S.end_capture()
                S.capture()
                Y(qt)
                lb = S.end_capture()
                S.replay_merged(la, lb, frac=MERGE_FRAC)
                if qt + 2 < NQT_RUN:
                    XA(qt + 2)
                X2(qt + 1)
            else:
                Y(qt)
        S.barrier()
        S.emit()


def ones_bf(C):
    return C.ones_bf

VEC_COLS = {}


def _vec_layout():
    cols = {}
    c = 0
    for l in range(2):
        for j in range(3):
            cols['norm_g_%d_%d' % (l, j)] = (c, 8)
            c += 8
    cols['rk_mix'] = (c, 48)
    c += 48
    return cols, c


def _vec64_layout():
    cols = {}
    c = 0
    for n in ('k_k', 'k_a', 'a0', 'r_k'):
        cols[n] = (c, 16)
        c += 16
    for n in ('q_g', 'k_g'):
        cols[n] = (c, 1)
        c += 1
    return cols, c


RK_SHAPES = {'w_r': [D, D], 'w_k': [D, D], 'w_v': [D, D], 'w_o': [D, D], 'w0': [1, D], 'w1': [D, 64], 'w2': [64, D],
             'a1': [D, 64], 'a2': [64, D], 'g1': [D, 160], 'g2': [160, D], 'lnx_g': [1, D], 'lnx_b': [1, D]}


def build_program(stages):
    nc = bass.Bass("TRN2", target_bir_lowering=False)
    C = Ctx()
    C.nc = nc
    din = lambda n, sh, dt=F32: nc.dram_tensor(n, list(sh), dt, kind="ExternalInput").ap()
    C.x = din("x", [SEQ, D])
    C.meta = din("meta", [NMETA, D])
    C.ffn_w_in = din("ffn_w_in", [2, 2, D, 2 * DFF])
    C.ffn_w_out = din("ffn_w_out", [2, 2, DFF, D])
    C.rk = {k: din("rk_" + k, sh) for k, sh in RK_SHAPES.items()}
    cols, nv = _vec_layout()
    cols64, nv64 = _vec64_layout()
    C.cols, C.cols64 = cols, cols64
    C.vec_d = din("vecs", [128, nv])
    C.vec64_d = din("vecs64", [64, nv64])
    C.ident_d = din("ident", [128, 128])
    C.masks_d = din("masks", [64, 192])
    C.tri_d = din("tri", [64, 128])
    C.at = {'w_in': din("at_w_in", [D, 2120]), 'w_o': din("at_w_o", [D, D])}
    C.rope_d = din("rope", [64, 2, TP])
    C.rot_d = din("rot", [64, 64])
    C.negmask_d = din("negmask", [128, 128])
    C.out = nc.dram_tensor("out", [SEQ, D], F32, kind="ExternalOutput").ap()
    C.hT = nc.dram_tensor("hT_scratch", [D, TP], F32, kind="Internal").ap()
    with ExitStack() as es:
        S = Sched(nc, es)
        C.S = S
        C.vec = es.enter_context(nc.sbuf_tensor("c_vec", [128, nv], F32))
        C.vec64 = es.enter_context(nc.sbuf_tensor("c_vec64", [64, nv64], F32))
        C.ident = es.enter_context(nc.sbuf_tensor("c_ident", [128, 128], F32))
        C.masks = es.enter_context(nc.sbuf_tensor("c_masks", [64, 192], F32))
        C.tri = es.enter_context(nc.sbuf_tensor("c_tri", [64, 128], F32))
        C.ones_bf = es.enter_context(nc.sbuf_tensor("c_ones_bf", [128, 128], BF16))
        C.ones_f = es.enter_context(nc.sbuf_tensor("c_ones_f", [128, 128], F32))
        C.eps_col = es.enter_context(nc.sbuf_tensor("c_eps", [128, 3], F32))
        S.op('sp', lambda e: e.dma_start(out=C.vec[:, :], in_=C.vec_d[:, :]), w=[('c_vec',)], dma=True)
        S.op('sp', lambda e: e.dma_start(out=C.vec64[:, :], in_=C.vec64_d[:, :]), w=[('c_vec64',)], dma=True)
        S.op('sp', lambda e: e.dma_start(out=C.ident[:, :], in_=C.ident_d[:, :]), w=[('c_ident',)], dma=True)
        S.op('sp', lambda e: e.dma_start(out=C.masks[:, :], in_=C.masks_d[:, :]), w=[('c_masks',)], dma=True)
        S.op('sp', lambda e: e.dma_start(out=C.tri[:, :], in_=C.tri_d[:, :]), w=[('c_tri',)], dma=True)
        S.op('pool', lambda e: e.memset(C.ones_bf[:, :], 1.0), w=[('c_ones',)])
        S.op('pool', lambda e: e.memset(C.ones_f[:, :], 1.0), w=[('c_onesf',)])
        S.op('pool', lambda e: e.memset(C.eps_col[:, 0:1], 1e-6), w=[('c_eps',)])
        S.op('pool', lambda e: e.memset(C.eps_col[:, 1:2], 64e-5), w=[('c_eps2',)])
        S.op('pool', lambda e: e.memset(C.eps_col[:, 2:3], 1e-24), w=[('c_eps3',)])
        S.barrier()
        stage_ingest(C)
        for sname in stages:
            if sname.startswith('ffn'):
                l, j = int(sname[3]), int(sname[4])
                stage_ffn(C, C.ffn_w_in[l, j], C.ffn_w_out[l, j], cols['norm_g_%d_%d' % (l, 0 if j == 0 else 2)][0],
                          "f%d%d_" % (l, j))
            elif sname == 'rwkv':
                stage_rwkv(C)
            elif sname == 'dsa':
                stage_dsa(C)
        stage_egress(C)
    return nc


def host_consts(inputs):
    cols, nv = _vec_layout()
    cols64, nv64 = _vec64_layout()
    vec = np.zeros((128, nv), np.float32)
    vec64 = np.zeros((64, nv64), np.float32)

    def put(name, v):
        c0, n = cols[name]
        vec[:, c0:c0 + n] = np.asarray(v, np.float32).reshape(n, 128).T

    def put64(name, v):
        c0, n = cols64[name]
        vec64[:, c0:c0 + n] = np.asarray(v, np.float32).reshape(n, 64).T

    ng = np.asarray(inputs['norm_g'])
    for l in range(2):
        for j in range(3):
            put('norm_g_%d_%d' % (l, j), ng[l, j])
    put('rk_mix', np.asarray(inputs['rk_mix'])[0].reshape(-1))
    put64('k_k', inputs['rk_k_k'][0])
    put64('k_a', inputs['rk_k_a'][0])
    put64('a0', inputs['rk_a0'][0])
    put64('r_k', np.asarray(inputs['rk_r_k'])[0].reshape(-1))
    vec64[:, cols64['q_g'][0]] = np.asarray(inputs['at_q_g'], np.float32)[0]
    vec64[:, cols64['k_g'][0]] = np.asarray(inputs['at_k_g'], np.float32)[0]
    inv = (np.float32(500000.0) ** (-np.arange(0, 16, 2, dtype=np.float32) / np.float32(16))).astype(np.float32)
    ang = (np.arange(TP, dtype=np.float32)[:, None] * inv[None, :]).astype(np.float32)
    rope = np.zeros((64, 2, TP), np.float32)
    rope[:, 0, :] = 1.0
    rope[0:8, 0, :] = np.cos(ang).T
    rope[8:16, 0, :] = np.cos(ang).T
    rope[0:8, 1, :] = np.sin(ang).T
    rope[8:16, 1, :] = np.sin(ang).T
    rot = np.zeros((64, 64), np.float32)
    for d in range(8):
        rot[d + 8, d] = -1.0
        rot[d, d + 8] = 1.0
    i128 = np.arange(128)
    negmask = np.where(i128[None, :] <= i128[:, None], 0.0, -1.0e30).astype(np.float32)
    ii = np.arange(64)
    su = (ii[:, None] < ii[None, :]).astype(np.float32)
    iu = (ii[:, None] <= ii[None, :]).astype(np.float32)
    sl = (ii[:, None] > ii[None, :]).astype(np.float32)
    masks = np.concatenate([su, iu, sl], axis=1)
    cdec = np.float32(-np.exp(-0.5))
    tri = np.concatenate([iu, su], axis=1) * cdec
    out = {"vecs": vec, "vecs64": vec64, "ident": np.eye(128, dtype=np.float32), "masks": masks,
           "tri": tri.astype(np.float32), "rope": rope, "rot": rot, "negmask": negmask,
           "at_w_in": np.ascontiguousarray(np.asarray(inputs['at_w_in'], np.float32)[0]),
           "at_w_o": np.ascontiguousarray(np.asarray(inputs['at_w_o'], np.float32)[0])}
    for k in RK_SHAPES:
        out["rk_" + k] = np.ascontiguousarray(np.asarray(inputs["rk_" + k], np.float32)[0].reshape(RK_SHAPES[k]))
    return out


ALL_STAGES = ['ffn00', 'rwkv', 'ffn01', 'ffn10', 'dsa', 'ffn11']
_cache = {}


def run(inputs, stages, cores=NCORES, trace=False):
    key = tuple(stages)
    if key not in _cache:
        _cache[key] = build_program(stages)
    nc = _cache[key]
    consts = host_consts(inputs)
    x = np.asarray(inputs['x'], np.float32)
    shared = {
        "meta": np.ascontiguousarray(np.asarray(inputs['meta'], np.float32)),
        "ffn_w_in": np.ascontiguousarray(np.asarray(inputs['ffn_w_in'], np.float32)),
        "ffn_w_out": np.ascontiguousarray(np.asarray(inputs['ffn_w_out'], np.float32)),
    }
    shared.update(consts)
    in_maps = []
    for b in range(cores):
        m = dict(shared)
        m["x"] = np.ascontiguousarray(x[b])
        in_maps.append(m)
    res = run_bass_kernel_spmd(nc, in_maps, core_ids=list(range(cores)), trace=trace)
    out = np.stack([np.asarray(r["out"], np.float32) for r in res.results], axis=0)
    return out, res


def kernel(**inputs):
    out, _ = run(inputs, ALL_STAGES)
    return out
```

```python
import numpy as np
from contextlib import ExitStack
import concourse.bass as bass
import concourse.mybir as mybir
from concourse.bass_utils import run_bass_kernel_spmd

F32 = mybir.dt.float32
BF16 = mybir.dt.bfloat16
F32R = mybir.dt.float32r
USE_F32R = False
IDT = BF16
TRDT = F32
ALU = mybir.AluOpType
AF = mybir.ActivationFunctionType
AX = mybir.AxisListType

D = 1024
NMETA = 16
SEQ = 4096
T = SEQ + NMETA
TP = 4160
DFF = 2816
NCORES = 8
DEBUG_NQT = 0
MERGE_FRAC = 0.0
NOVBF = False
DEBUG_RK = None


class Sched:
    ENGS = ['pe', 'act', 'dve', 'pool', 'sp']

    def __init__(self, nc, es, nds=16):
        self.nc = nc
        self.sem = {e: es.enter_context(nc.semaphore("s_" + e)) for e in self.ENGS}
        self.cnt = {e: 0 for e in self.ENGS}
        self.NDS = nds
        self.dq = {}
        self.dsem = []
        self.dcnt = []
        self.dtok = []
        for q in ('sp', 'pool', 'act'):
            base = len(self.dsem)
            for i in range(nds):
                self.dsem.append(es.enter_context(nc.semaphore("d_%s%d" % (q, i))))
                self.dcnt.append(0)
                self.dtok.append(None)
            self.dq[q] = [base, 0]
        self.pending = {e: [] for e in self.ENGS}
        self.last_w = {}
        self.readers = {}
        self.seen = {e: {} for e in self.ENGS}
        self.nops = 0

    def _semh(self, sk):
        return self.sem[sk[1]] if sk[0] == 'e' else self.dsem[sk[1]]

    def capture(self):
        self._cap = []
        return self._cap

    def end_capture(self):
        c = self._cap
        self._cap = None
        return c

    def replay_merged(self, a, b, frac=1.0):
        na, nb = len(a), len(b)
        nbe = max(1, int(nb * frac))
        ia = ib = 0
        while ia < na or ib < nb:
            if ib >= nb or (ia < na and ia * nbe <= ib * na):
                self.op(*a[ia])
                ia += 1
            else:
                self.op(*b[ib])
                ib += 1

    def op(self, eng, fn, r=(), w=(), dma=False):
        if getattr(self, '_cap', None) is not None:
            self._cap.append((eng, fn, tuple(r), tuple(w), dma))
            return None
        pr = [k for k in r if k[0] in PSKEYS and k not in w]
        if pr:
            w = list(w) + pr
        deps = []
        for k in r:
            if k in self.last_w:
                deps.append(self.last_w[k])
        for k in w:
            if k in self.last_w:
                deps.append(self.last_w[k])
            rd = self.readers.get(k)
            if rd:
                deps.extend(rd.items())
        if dma:
            qd = self.dq[eng]
            i = qd[0] + qd[1] % self.NDS
            qd[1] += 1
            if self.dtok[i] is not None:
                deps.append(self.dtok[i])
            self.dcnt[i] += 16
            tok = (('d', i), self.dcnt[i])
            self.dtok[i] = tok
        else:
            self.cnt[eng] += 1
            tok = (('e', eng), self.cnt[eng])
        waits = {}
        seen = self.seen[eng]
        for (sk, v) in deps:
            if eng == 'pe' and sk == ('e', 'pe'):
                continue
            if seen.get(sk, 0) >= v:
                continue
            if waits.get(sk, 0) < v:
                waits[sk] = v
        for sk, v in waits.items():
            seen[sk] = v
        self.pending[eng].append((fn, list(waits.items()), tok))
        self.nops += 1
        for k in w:
            self.last_w[k] = tok
            self.readers[k] = {}
        ws = set(w)
        for k in r:
            if k in ws:
                continue
            rd = self.readers.setdefault(k, {})
            if rd.get(tok[0], 0) < tok[1]:
                rd[tok[0]] = tok[1]
        return tok

    def barrier(self):
        allt = [(('e', e), self.cnt[e]) for e in self.ENGS if self.cnt[e] > 0]
        allt += [t for t in self.dtok if t is not None]
        for e in self.ENGS:
            waits = {}
            seen = self.seen[e]
            for sk, v in allt:
                if seen.get(sk, 0) >= v:
                    continue
                waits[sk] = max(waits.get(sk, 0), v)
            for sk, v in waits.items():
                seen[sk] = v
            self.pending[e].append((None, list(waits.items()), None))
        self.last_w = {}
        self.readers = {}

    def emit(self):
        nc = self.nc

        def mk(e):
            def body(engh):
                for fn, waits, tok in self.pending[e]:
                    for sk, v in waits:
                        engh.wait_ge(self._semh(sk), v)
                    if fn is None:
                        continue
                    ins = fn(engh)
                    sk, v = tok
                    if sk[0] == 'e':
                        ins.then_inc(self.sem[e], 1)
                    else:
                        ins.then_inc(self.dsem[sk[1]], 16)
            return body

        with nc.Block() as blk:
            blk.tensor(mk('pe'))
            blk.scalar(mk('act'))
            blk.vector(mk('dve'))
            blk.gpsimd(mk('pool'))
            blk.sync(mk('sp'))
        self.pending = {e: [] for e in self.ENGS}


PSKEYS = {'ps', 'psb', 'pA', 'pB', 'pO', 'psS'}


def keys(name, *idx_ranges):
    out = [(name,)]
    for r in idx_ranges:
        out = [o + (i,) for o in out for i in r]
    return out


class Ctx:
    pass


def stage_ingest(C):
    nc, S = C.nc, C.S
    with ExitStack() as st:
        xt = [st.enter_context(nc.sbuf_tensor("in_xt%d" % i, [128, 4, D], F32)) for i in range(2)]
        hx = [st.enter_context(nc.sbuf_tensor("in_hx%d" % i, [128, 8, 512], F32)) for i in range(2)]
        ps = [st.enter_context(nc.psum_tensor("in_ps%d" % i, [128, 512], F32)) for i in range(4)]
        hTv = C.hT.rearrange("(c p) t -> p c t", p=128)
        ident = C.ident
        S.op('pool', lambda e: e.memset(hx[1][:, :, 0:64], 0.0), w=keys('hx', [1], range(8)))
        S.op('pool', lambda e: e.dma_start(out=hTv[:, :, T:TP], in_=hx[1][:, :, 0:TP - T]),
             r=keys('hx', [1], range(8)), w=[('hTpad',)], dma=True)
        S.op('sp', lambda e: e.dma_start(out=xt[1][0:NMETA, 0, :], in_=C.meta[:, :]), w=[('xt', 1)], dma=True)
        for c in range(8):
            S.op('pe', lambda e, c=c: e.transpose(ps[c % 4][:, 0:NMETA], xt[1][0:NMETA, 0, c * 128:(c + 1) * 128],
                                                  ident[0:NMETA, 0:NMETA]),
                 r=[('xt', 1)], w=[('ps', c % 4)])
            S.op('dve', lambda e, c=c: e.tensor_copy(out=hx[1][:, c, 0:NMETA], in_=ps[c % 4][:, 0:NMETA]),
                 r=[('ps', c % 4)], w=[('hx', 1, c)])
        S.op('pool', lambda e: e.dma_start(out=hTv[:, :, 0:NMETA], in_=hx[1][:, :, 0:NMETA]),
             r=keys('hx', [1], range(8)), w=[('hTmeta',)], dma=True)
        xv = C.x.rearrange("(g a p) d -> g p a d", a=4, p=128)
        for g in range(SEQ // 512):
            s = g % 2
            S.op('sp', lambda e, g=g, s=s: e.dma_start(out=xt[s][:, :, :], in_=xv[g]), w=[('xt', s)], dma=True)
            for c in range(8):
                b = c % 4
                for a in range(4):
                    S.op('pe', lambda e, a=a, c=c, b=b, s=s: e.transpose(
                        ps[b][:, a * 128:(a + 1) * 128], xt[s][:, a, c * 128:(c + 1) * 128], ident[:, :]),
                        r=[('xt', s)], w=[('ps', b)])
                eng = 'dve' if c % 2 == 0 else 'act'
                if eng == 'dve':
                    S.op('dve', lambda e, c=c, b=b, s=s: e.tensor_copy(out=hx[s][:, c, :], in_=ps[b][:, :]),
                         r=[('ps', b)], w=[('hx', s, c)])
                else:
                    S.op('act', lambda e, c=c, b=b, s=s: e.copy(out=hx[s][:, c, :], in_=ps[b][:, :]),
                         r=[('ps', b)], w=[('hx', s, c)])
            t0 = NMETA + g * 512
            S.op('pool', lambda e, s=s, t0=t0: e.dma_start(out=hTv[:, :, t0:t0 + 512], in_=hx[s][:, :, :]),
                 r=keys('hx', [s], range(8)), w=[('hTin', g)], dma=True)
        S.barrier()
        S.emit()


def stage_egress(C):
    nc, S = C.nc, C.S
    with ExitStack() as st:
        xt = [st.enter_context(nc.sbuf_tensor("eg_xt%d" % i, [128, 4, D], F32)) for i in range(2)]
        hx = [st.enter_context(nc.sbuf_tensor("eg_hx%d" % i, [128, 8, 512], F32)) for i in range(2)]
        ps = [st.enter_context(nc.psum_tensor("eg_ps%d" % i, [128, 512], F32)) for i in range(4)]
        hTv = C.hT.rearrange("(c p) t -> p c t", p=128)
        ov = C.out.rearrange("(g a p) d -> g p a d", a=4, p=128)
        ident = C.ident
        for g in range(SEQ // 512):
            s = g % 2
            t0 = NMETA + g * 512
            S.op('sp', lambda e, s=s, t0=t0: e.dma_start(out=hx[s][:, :, :], in_=hTv[:, :, t0:t0 + 512]),
                 w=[('hx', s)], dma=True)
            for a in range(4):
                for hf in range(2):
                    b = (a * 2 + hf) % 4
                    for cc in range(4):
                        c = hf * 4 + cc
                        S.op('pe', lambda e, a=a, c=c, cc=cc, b=b, s=s: e.transpose(
                            ps[b][:, cc * 128:(cc + 1) * 128], hx[s][:, c, a * 128:(a + 1) * 128], ident[:, :]),
                            r=[('hx', s)], w=[('ps', b)])
                    if hf == 0:
                        S.op('dve', lambda e, a=a, b=b, s=s: e.tensor_copy(out=xt[s][:, a, 0:512], in_=ps[b][:, :]),
                             r=[('ps', b)], w=[('xt', s, a, 0)])
                    else:
                        S.op('act', lambda e, a=a, b=b, s=s: e.copy(out=xt[s][:, a, 512:1024], in_=ps[b][:, :]),
                             r=[('ps', b)], w=[('xt', s, a, 1)])
            S.op('pool', lambda e, g=g, s=s: e.dma_start(out=ov[g], in_=xt[s][:, :, :]),
                 r=keys('xt', [s], range(4), range(2)), w=[('out', g)], dma=True)
        S.barrier()
        S.emit()


def stage_ffn(C, w_in_d, w_out_d, gcol, tag):
    nc, S = C.nc, C.S
    TT = 256
    tiles = [(i * TT, TT) for i in range(TP // TT)]
    if TP % TT:
        tiles.append((TP - TP % TT, TP % TT))
    NJ = DFF // 128
    with ExitStack() as st:
        sb = lambda n, sh, dt: st.enter_context(nc.sbuf_tensor(tag + n, sh, dt))
        w_in = sb("w_in", [128, 8, 2 * DFF], BF16)
        w_out = sb("w_out", [128, NJ, D], BF16)
        x = [sb("x%d" % i, [128, 8, TT], F32) for i in range(2)]
        sq = [sb("sq%d" % i, [128, 8, TT], BF16) for i in range(2)]
        xn = [sb("xn%d" % i, [128, 8, TT], BF16) for i in range(2)]
        hm = [sb("hm%d" % i, [128, NJ, TT], BF16) for i in range(2)]
        sg = [sb("sg%d" % i, [128, TT], F32) for i in range(2)]
        rstd = [sb("rstd%d" % i, [128, TT], F32) for i in range(2)]
        psn = lambda n: st.enter_context(nc.psum_tensor(tag + n, [128, 512], F32))
        psA = [psn("pA%d" % i) for i in range(2)]
        psB = [psn("pB%d" % i) for i in range(2)]
        psO = [psn("pO%d" % i) for i in range(2)]
        psS = psn("pS")
        hTv = C.hT.rearrange("(c p) t -> p c t", p=128)
        w_in_v = w_in_d.rearrange("(k p) n -> p k n", p=128)
        w_out_v = w_out_d.rearrange("(j p) n -> p j n", p=128)
        ones = C.ones_bf
        vec = C.vec

        NB = 4
        cw = 2 * DFF // NB
        for b in (0, 2, 1, 3):
            for k in range(8):
                S.op('pool', lambda e, k=k, b=b: e.dma_start(out=w_in[:, k, b * cw:(b + 1) * cw],
                                                             in_=w_in_v[:, k, b * cw:(b + 1) * cw]),
                     w=[('w_in', k, b)], dma=True)
        for j in range(NJ):
            S.op('pool', lambda e, j=j: e.dma_start(out=w_out[:, j, :], in_=w_out_v[:, j, :]),
                 w=[('w_out', j)], dma=True)
        win_keys = keys('w_in', range(8), range(NB))

        def load(i):
            t0, tw = tiles[i]
            s = i % 2
            S.op('sp', lambda e: e.dma_start(out=x[s][:, :, :tw], in_=hTv[:, :, t0:t0 + tw]),
                 r=[('hT', i)], w=keys('x', [s], range(8)), dma=True)

        def norm(i):
            t0, tw = tiles[i]
            s = i % 2
            S.op('act', lambda e: e.activation(out=sq[s][:, :, :tw], in_=x[s][:, :, :tw], func=AF.Square),
                 r=keys('x', [s], range(8)), w=[('sq', s)])
            for c in range(8):
                S.op('pe', lambda e, c=c: e.matmul(psS[:, :tw], lhsT=ones[:, :], rhs=sq[s][:, c, :tw],
                                                   start=(c == 0), stop=(c == 7)),
                     r=[('sq', s)], w=[('psS',)])
            S.op('act', lambda e: e.activation(out=rstd[s][:, :tw], in_=psS[:, :tw], func=AF.Sqrt,
                                               scale=1.0 / D, bias=C.eps_col[:, 0:1]),
                 r=[('psS',)], w=[('rstd', s)])
            S.op('dve', lambda e: e.reciprocal(out=rstd[s][:, :tw], in_=rstd[s][:, :tw]),
                 r=[('rstd', s)], w=[('rstd', s)])
            for c in range(8):
                S.op('dve', lambda e, c=c: e.scalar_tensor_tensor(
                    out=xn[s][:, c, :tw], in0=x[s][:, c, :tw], scalar=vec[:, gcol + c:gcol + c + 1],
                    in1=rstd[s][:, :tw], op0=ALU.mult, op1=ALU.mult),
                    r=[('x', s, c), ('rstd', s)], w=[('xn', s, c)])

        def mm_in(i):
            t0, tw = tiles[i]
            s = i % 2
            for j in range(NJ):
                q = j % 2
                for k in range(8):
                    S.op('pe', lambda e, j=j, k=k, q=q: e.matmul(
                        psA[q][:, :tw], lhsT=w_in[:, k, j * 128:(j + 1) * 128], rhs=xn[s][:, k, :tw],
                        start=(k == 0), stop=(k == 7)),
                        r=[('xn', s, k), ('w_in', k, j // 11)], w=[('pA', q)])
                for k in range(8):
                    S.op('pe', lambda e, j=j, k=k, q=q: e.matmul(
                        psB[q][:, :tw], lhsT=w_in[:, k, DFF + j * 128:DFF + (j + 1) * 128], rhs=xn[s][:, k, :tw],
                        start=(k == 0), stop=(k == 7)),
                        r=[('xn', s, k), ('w_in', k, 2 + j // 11)], w=[('pB', q)])
                S.op('act', lambda e, q=q: e.activation(out=sg[q][:, :tw], in_=psA[q][:, :tw], func=AF.Silu),
                     r=[('pA', q)], w=[('sg', q)])
                S.op('dve', lambda e, q=q, j=j: e.tensor_tensor(out=hm[s][:, j, :tw], in0=psB[q][:, :tw],
                                                                in1=sg[q][:, :tw], op=ALU.mult),
                     r=[('pB', q), ('sg', q)], w=[('hm', s, j)])

        def mm_out(i):
            t0, tw = tiles[i]
            s = i % 2
            for m in range(8):
                q = m % 2
                for j in range(NJ):
                    S.op('pe', lambda e, j=j, m=m, q=q: e.matmul(
                        psO[q][:, :tw], lhsT=w_out[:, j, m * 128:(m + 1) * 128], rhs=hm[s][:, j, :tw],
                        start=(j == 0), stop=(j == NJ - 1)),
                        r=[('hm', s, j), ('w_out', j)], w=[('pO', q)])
                S.op('dve', lambda e, m=m, q=q: e.scalar_tensor_tensor(
                    out=x[s][:, m, :tw], in0=psO[q][:, :tw], scalar=0.5, in1=x[s][:, m, :tw],
                    op0=ALU.mult, op1=ALU.add),
                    r=[('pO', q), ('x', s, m)], w=[('x', s, m)])
            S.op('pool', lambda e: e.dma_start(out=hTv[:, :, t0:t0 + tw], in_=x[s][:, :, :tw]),
                 r=keys('x', [s], range(8)), w=[('hT', i)], dma=True)

        n = len(tiles)
        load(0)
        norm(0)
        for i in range(n):
            if i + 1 < n:
                load(i + 1)
            mm_in(i)
            if i + 1 < n:
                norm(i + 1)
            mm_out(i)
        S.barrier()
        S.emit()


class H:
    def __init__(self, S):
        self.S = S

    def mm(self, out, lhsT, rhs, start, stop, r, w):
        if USE_F32R and lhsT.dtype == F32 and rhs.dtype == F32:
            lhsT = lhsT.bitcast(F32R)
            rhs = rhs.bitcast(F32R)
        self.S.op('pe', lambda e: e.matmul(out, lhsT=lhsT, rhs=rhs, start=start, stop=stop), r=r, w=w)

    def tr(self, out, in_, ident, r, w):
        self.S.op('pe', lambda e: e.transpose(out, in_, ident), r=r, w=w)

    def act(self, out, in_, func, r, w, scale=None, bias=None):
        kw = {}
        if scale is not None:
            kw['scale'] = scale
        if bias is not None:
            kw['bias'] = bias
        self.S.op('act', lambda e: e.activation(out=out, in_=in_, func=func, **kw), r=r, w=w)

    def cp(self, eng, out, in_, r, w):
        if eng == 'act':
            self.S.op('act', lambda e: e.copy(out=out, in_=in_), r=r, w=w)
        else:
            self.S.op(eng, lambda e: e.tensor_copy(out=out, in_=in_), r=r, w=w)

    def tt(self, eng, out, in0, in1, op, r, w):
        self.S.op(eng, lambda e: e.tensor_tensor(out=out, in0=in0, in1=in1, op=op), r=r, w=w)

    def ts(self, eng, out, in0, s1, s2, op0, op1, r, w):
        if s2 is None:
            self.S.op(eng, lambda e: e.tensor_scalar(out=out, in0=in0, scalar1=s1, scalar2=None, op0=op0), r=r, w=w)
        else:
            self.S.op(eng, lambda e: e.tensor_scalar(out=out, in0=in0, scalar1=s1, scalar2=s2, op0=op0, op1=op1),
                      r=r, w=w)

    def stt(self, eng, out, in0, scalar, in1, op0, op1, r, w):
        self.S.op(eng, lambda e: e.scalar_tensor_tensor(out=out, in0=in0, scalar=scalar, in1=in1, op0=op0, op1=op1),
                  r=r, w=w)

    def red(self, eng, out, in_, op, r, w):
        self.S.op(eng, lambda e: e.tensor_reduce(out=out, in_=in_, axis=AX.X, op=op), r=r, w=w)

    def dma(self, eng, out, in_, r, w):
        self.S.op(eng, lambda e: e.dma_start(out=out, in_=in_), r=r, w=w, dma=True)


def bk(*bs):
    return [('ps', b) for b in bs]


def stage_rwkv(C):
    nc, S = C.nc, C.S
    Hh = H(S)
    mm, tr, act, cp, tt, ts, stt, red, dma = Hh.mm, Hh.tr, Hh.act, Hh.cp, Hh.tt, Hh.ts, Hh.stt, Hh.red, Hh.dma
    CH = 64
    NT = TP // CH
    NH = 16
    cols = C.cols
    c64 = C.cols64
    with ExitStack() as st:
        sb = lambda n, sh, dt=F32: st.enter_context(nc.sbuf_tensor("rs_" + n, sh, dt))
        Wr = sb("Wr", [128, 8, D], BF16)
        Wk = sb("Wk", [128, 8, D], BF16)
        Wv = sb("Wv", [128, 8, D], BF16)
        Wo = sb("Wo", [128, 8, D], BF16)
        w1 = sb("w1", [128, 8, 64], BF16)
        a1 = sb("a1", [128, 8, 64], BF16)
        g1 = sb("g1", [128, 8, 160], BF16)
        a2 = sb("a2", [64, D], BF16)
        g2a = sb("g2a", [128, D], BF16)
        g2b = sb("g2b", [32, D], BF16)
        w2aug = sb("w2aug", [65, D], F32)
        lnxg = sb("lnxg", [64, D], F32)
        lnxb = sb("lnxb", [64, D], F32)
        omk = sb("omk", [64, NH], F32)
        PS = st.enter_context(nc.psum_tensor("rk_PS", [128, 3584], F32))
        PSb = st.enter_context(nc.psum_tensor("rk_PSb", [128, 1024], BF16))
        hbuf = [sb("hbuf%d" % i, [128, 8, CH]) for i in range(2)]
        sq = sb("sq", [128, 8, CH], BF16)
        rstd = sb("rstd", [128, CH])
        hn = sb("hn", [128, 8, CH + 1])
        xx = sb("xx", [128, 8, CH])
        xmf = [sb("xmf%d" % i, [128, 8, CH]) for i in range(1)]
        xm = [sb("xm%d" % i, [128, 8, CH], BF16) for i in range(6)]
        r_ = sb("r", [64, NH, CH])
        k_ = sb("k", [64, NH, CH])
        a_ = sb("a", [64, NH, CH])
        kk = sb("kk", [64, NH, CH])
        b_ = sb("b", [64, NH, CH])
        tmp1 = sb("tmp1", [64, NH, CH])
        tmp2 = sb("tmp2", [64, NH, CH])
        G = sb("G", [64, NH, CH])
        Ghat = sb("Ghat", [64, NH, CH])
        cumC = sb("cumC", [64, NH])
        AR = sb("AR", [64, NH, 2 * CH], BF16)
        Bt = sb("Bt", [64, NH, CH], BF16)
        Kt = sb("Kt", [64, NH, CH], BF16)
        Bh = sb("Bh", [64, NH, CH], TRDT)
        Kh = sb("Kh", [64, NH, CH], TRDT)
        g_tm = sb("g_tm", [64, D], BF16)
        lw_tm = sb("lw_tm", [64, D])
        twT = sb("twT", [65, CH])
        taT = sb("taT", [64, CH], BF16)
        sg0 = sb("sg0", [128, CH], BF16)
        sg1 = sb("sg1", [32, CH], BF16)
        bon = sb("bon", [64, NH])
        MX = sb("MX", [64, NH, 2 * CH], BF16)
        GG = sb("GG", [64, NH, 2 * CH])
        RKT = sb("RKT", [64, NH, CH], BF16)
        LakT = sb("LakT", [64, NH, CH], BF16)
        Hst = sb("Hst", [64, NH, CH])
        st1 = sb("st1", [64, NH])
        st2 = sb("st2", [64, NH])
        zT = sb("zT", [128, 8, CH], BF16)

        yc = sb("yc", [64, NH, CH])
        ysq = a_
        Gs = sb("Gs", [64, NH, CH], BF16)
        Us = sb("Us", [64, NH, CH], BF16)
        BhT = sb("BhT", [64, NH, CH], BF16)
        KhT = sb("KhT", [64, NH, CH], BF16)
        Lm = sb("Lm", [64, NH, CH], BF16)
        RBT = sb("RBT", [64, NH, CH], BF16)
        v_bf = sb("v_bf", [64, D], BF16)
        Hb = sb("Hb", [64, NH, CH], BF16)
        ident_bf = sb("ident_bf", [64, 64], BF16)
        Ginv = GG[:, :, 0:CH]
        Gex = GG[:, :, CH:2 * CH]

        hTv = C.hT.rearrange("(c p) t -> p c t", p=128)
        ident = C.ident
        vec = C.vec
        v64 = C.vec64

        for nm, dst, src in (("Wr", Wr, C.rk['w_r']), ("Wk", Wk, C.rk['w_k']), ("Wv", Wv, C.rk['w_v']),
                             ("Wo", Wo, C.rk['w_o'])):
            v = src.rearrange("(k p) n -> p k n", p=128)
            for k in range(8):
                dma('pool', dst[:, k, :], v[:, k, :], r=[], w=[(nm,)])
        dma('pool', w1[:, :, :], C.rk['w1'].rearrange("(k p) n -> p k n", p=128), r=[], w=[('w1',)])
        dma('pool', a1[:, :, :], C.rk['a1'].rearrange("(k p) n -> p k n", p=128), r=[], w=[('a1',)])
        dma('pool', g1[:, :, :], C.rk['g1'].rearrange("(k p) n -> p k n", p=128), r=[], w=[('g1',)])
        dma('pool', a2[:, :], C.rk['a2'][:, :], r=[], w=[('a2',)])
        dma('pool', g2a[:, :], C.rk['g2'][0:128, :], r=[], w=[('g2',)])
        dma('pool', g2b[:, :], C.rk['g2'][128:160, :], r=[], w=[('g2',)])
        dma('sp', w2aug[0:64, :], C.rk['w2'][:, :], r=[], w=[('w2aug',)])
        dma('sp', w2aug[64:65, :], C.rk['w0'][0:1, :], r=[], w=[('w2aug',)])
        dma('sp', lnxg[:, :], C.rk['lnx_g'][0:1, :].partition_broadcast(64), r=[], w=[('lnxg',)])
        dma('sp', lnxb[:, :], C.rk['lnx_b'][0:1, :].partition_broadcast(64), r=[], w=[('lnxb',)])
        kka = c64['k_a'][0]
        ts('dve', omk[:, :], v64[0:64, kka:kka + NH], -1.0, 1.0, ALU.mult, ALU.add, r=[('c_vec64',)], w=[('omk',)])
        S.op('pool', lambda e: e.memset(Hst[:, :, :], 0.0), w=[('Hst',)])
        S.op('pool', lambda e: e.memset(Hb[:, :, :], 0.0), w=[('Hb',)])
        cp('dve', ident_bf[:, :], ident[0:64, 0:64], r=[('c_ident',)], w=[('ident_bf',)])
        S.op('pool', lambda e: e.memset(hn[:, :, 0:1], 0.0), w=[('hn0',)])
        S.op('pool', lambda e: e.memset(twT[64:65, :], 1.0), w=[('twT1',)])

        def bc(ap2, n=CH):
            return ap2.unsqueeze(2).to_broadcast([64, NH, n])

        def prm(name):
            c0 = c64[name][0]
            return bc(v64[0:64, c0:c0 + NH])

        gcol = cols['norm_g_0_1'][0]
        mixc = cols['rk_mix'][0]
        psv2 = lambda b0: PS[0:64, b0 * 512:b0 * 512 + 2048].rearrange("p (h two s) -> p h two s", h=NH, two=2)
        psv1 = lambda b0: PS[0:64, b0 * 512:b0 * 512 + 1024].rearrange("p (h s) -> p h s", h=NH)
        SU = C.masks[0:64, 0:64].unsqueeze(1).to_broadcast([64, NH, CH])
        IU = C.masks[0:64, 64:128].unsqueeze(1).to_broadcast([64, NH, CH])
        SL = C.masks[0:64, 128:192].unsqueeze(1).to_broadcast([64, NH, CH])
        IDb = ident[0:64, 0:64].unsqueeze(1).to_broadcast([64, NH, CH])
        ones64 = C.ones_f[0:64, 0:64]

        def seg_A(t):
            t0 = t * CH
            hb = hbuf[t % 2]
            dma('sp', hb[:, :, :], hTv[:, :, t0:t0 + CH], r=[('hT', t)], w=[('hbuf', t % 2)])
            act(sq[:, :, :], hb[:, :, :], AF.Square, r=[('hbuf', t % 2)], w=[('sq',)])
            for c in range(8):
                mm(PS[:, 3072:3072 + CH], ones_bf(C)[:, :], sq[:, c, :], c == 0, c == 7, r=[('sq',)], w=bk(6))
            act(rstd[:, :], PS[:, 3072:3072 + CH], AF.Sqrt, r=bk(6), w=[('rstd',)], scale=1.0 / D, bias=C.eps_col[:, 0:1])
            S.op('dve', lambda e: e.reciprocal(out=rstd[:, :], in_=rstd[:, :]), r=[('rstd',)], w=[('rstd',)])
            for c in range(8):
                stt('dve', hn[:, c, 1:CH + 1], hb[:, c, :], vec[:, gcol + c:gcol + c + 1], rstd[:, :],
                    ALU.mult, ALU.mult, r=[('hbuf', t % 2), ('rstd',), ('hn0',)], w=[('hn', c)])
            hnk = keys('hn', range(8))
            tt('pool', xx[:, :, :], hn[:, :, 0:CH], hn[:, :, 1:CH + 1], ALU.subtract, r=hnk + [('hn0',)], w=[('xx',)])
            for i in range(6):
                mixb = vec[:, mixc + i * 8:mixc + i * 8 + 8].unsqueeze(2).to_broadcast([128, 8, CH])
                tt('pool', xmf[0][:, :, :], xx[:, :, :], mixb, ALU.mult, r=[('xx',), ('c_vec',)], w=[('xmf', 0)])
                tt('pool', xm[i][:, :, :], xmf[0][:, :, :], hn[:, :, 1:CH + 1], ALU.add,
                   r=[('xmf', 0)] + hnk, w=keys('xm', [i], range(8)))
            cp('pool', hn[:, :, 0:1], hn[:, :, CH:CH + 1], r=hnk + [('xx',)], w=[('hn0',)])

        def seg_P(t):
            xr, xw, xk, xv, xa, xg = xm
            for (Wt, wn, xs, xi, b0) in ((Wr, 'Wr', xr, 0, 0), (Wk, 'Wk', xk, 2, 2)):
                for h in range(NH):
                    for kc in range(8):
                        mm(PS[0:64, b0 * 512 + h * 64:b0 * 512 + (h + 1) * 64], Wt[:, kc, h * 64:(h + 1) * 64],
                           xs[:, kc, :], kc == 0, kc == 7, r=[('xm', xi, kc), (wn,)], w=bk(b0 + h // 8))

        def seg_E(t):
            t0 = t * CH
            PH = DEBUG_RK[1] if DEBUG_RK else 99
            hb = hbuf[t % 2]
            xr, xw, xk, xv, xa, xg = xm
            cp('act', r_[:, :, :], psv1(0), r=bk(0, 1), w=[('r',)])
            cp('act', k_[:, :, :], psv1(2), r=bk(2, 3), w=[('k',)])
            for n in range(2):
                for kc in range(8):
                    mm(PS[0:64, (4 + n) * 512:(5 + n) * 512], xv[:, kc, :], Wv[:, kc, n * 512:(n + 1) * 512],
                       kc == 0, kc == 7, r=[('xm', 3, kc), ('Wv',)], w=bk(4 + n))
            cp('dve', v_bf[:, :], PS[0:64, 2048:3072], r=bk(4, 5), w=[('v_bf',)])
            for kc in range(8):
                mm(PS[0:64, 3072:3072 + CH], w1[:, kc, :], xw[:, kc, :], kc == 0, kc == 7,
                   r=[('xm', 1, kc), ('w1',)], w=bk(6))
            act(twT[0:64, :], PS[0:64, 3072:3072 + CH], AF.Tanh, r=bk(6), w=[('twT',)])
            for kc in range(8):
                mm(PS[0:64, 0:CH], a1[:, kc, :], xa[:, kc, :], kc == 0, kc == 7,
                   r=[('xm', 4, kc), ('a1',)], w=bk(0))
            cp('dve', taT[:, :], PS[0:64, 0:CH], r=bk(0), w=[('taT',)])
            for kc in range(8):
                mm(PS[:, 3072:3072 + CH], g1[:, kc, 0:128], xg[:, kc, :], kc == 0, kc == 7,
                   r=[('xm', 5, kc), ('g1',)], w=bk(6))
            act(sg0[:, :], PS[:, 3072:3072 + CH], AF.Sigmoid, r=bk(6), w=[('sg0',)])
            for kc in range(8):
                mm(PS[0:32, 512:512 + CH], g1[:, kc, 128:160], xg[:, kc, :], kc == 0, kc == 7,
                   r=[('xm', 5, kc), ('g1',)], w=bk(1))
            act(sg1[:, :], PS[0:32, 512:512 + CH], AF.Sigmoid, r=bk(1), w=[('sg1',)])
            for n in range(2):
                mm(PS[0:64, n * 512:(n + 1) * 512], twT[0:65, :], w2aug[0:65, n * 512:(n + 1) * 512], True, True,
                   r=[('twT',), ('twT1',), ('w2aug',)], w=bk(n))
            act(lw_tm[:, :], PS[0:64, 0:1024], AF.Sigmoid, r=bk(0, 1), w=[('lw_tm',)])
            for h in range(NH):
                mm(PS[0:64, 1024 + h * 128:1024 + (h + 1) * 128], lw_tm[0:64, h * 64:(h + 1) * 64],
                   C.tri[0:64, 0:128], True, True, r=[('lw_tm',), ('c_tri',)], w=bk(2 + h // 4))
            pc = psv2(2)
            cb = bk(2, 3, 4, 5)
            act(G[:, :, :], pc[:, :, 0, :], AF.Exp, r=cb, w=[('G',)])
            act(Ginv[:, :, :], pc[:, :, 0, :], AF.Exp, r=cb, w=[('Ginv',)], scale=-1.0)
            act(Gex[:, :, :], pc[:, :, 1, :], AF.Exp, r=cb, w=[('Gex',)])
            cp('dve', cumC[:, :], pc[:, :, 0, CH - 1], r=cb, w=[('cumC',)])
            tt('dve', tmp1[:, :, :], bc(cumC[:, :]), pc[:, :, 0, :], ALU.subtract, r=cb + [('cumC',)], w=[('tmp1',)])
            act(Ghat[:, :, :], tmp1[:, :, :], AF.Exp, r=[('tmp1',)], w=[('Ghat',)])
            for h in range(NH):
                mm(PS[0:64, h * 64:(h + 1) * 64], a2[0:64, h * 64:(h + 1) * 64], taT[0:64, :], True, True,
                   r=[('taT',), ('a2',)], w=bk(h // 8))
            tt('dve', a_[:, :, :], psv1(0), prm('a0'), ALU.add, r=bk(0, 1) + [('c_vec64',)], w=[('a',)])
            act(a_[:, :, :], a_[:, :, :], AF.Sigmoid, r=[('a',)], w=[('a',)])
            for n in range(2):
                mm(PS[0:64, n * 512:(n + 1) * 512], sg0[:, :], g2a[:, n * 512:(n + 1) * 512], True, False,
                   r=[('sg0',), ('g2',)], w=bk(n))
                mm(PS[0:64, n * 512:(n + 1) * 512], sg1[0:32, :], g2b[0:32, n * 512:(n + 1) * 512], False, True,
                   r=[('sg1',), ('g2',)], w=bk(n))
            cp('act', g_tm[:, :], PS[0:64, 0:1024], r=bk(0, 1), w=[('g_tm',)])
            if PH < -1:
                return
            tt('dve', kk[:, :, :], k_[:, :, :], prm('k_k'), ALU.mult, r=[('k',), ('c_vec64',)], w=[('kk',)])
            act(Gs[:, :, :], kk[:, :, :], AF.Square, r=[('kk',)], w=[('Gs',)])
            t2f = Gs[:, :, :].rearrange("p h s -> p (h s)")
            for n in range(2):
                mm(PS[0:64, 1024 + n * 512:1024 + (n + 1) * 512], C.ones_bf[0:64, 0:64], t2f[:, n * 512:(n + 1) * 512],
                   True, True, r=[('Gs',), ('c_ones',)], w=bk(2 + n))
            tt('pool', tmp1[:, :, :], a_[:, :, :], prm('k_a'), ALU.mult, r=[('a',), ('c_vec64',)], w=[('tmp1',)])
            tt('pool', tmp1[:, :, :], tmp1[:, :, :], bc(omk[:, :]), ALU.add, r=[('tmp1',), ('omk',)], w=[('tmp1',)])
            tt('pool', k_[:, :, :], k_[:, :, :], tmp1[:, :, :], ALU.mult, r=[('k',), ('tmp1',)], w=[('k',)])
            tt('dve', AR[:, :, CH:2 * CH], r_[:, :, :], G[:, :, :], ALU.mult, r=[('r',), ('G',)], w=[('AR1',)])
            act(tmp2[:, :, :], psv1(2), AF.Ln, r=bk(2, 3), w=[('tmp2',)], bias=C.eps_col[0:64, 2:3])
            act(tmp2[:, :, :], tmp2[:, :, :], AF.Exp, r=[('tmp2',)], w=[('tmp2',)], scale=-0.5)
            tt('dve', Kt[:, :, :], k_[:, :, :], Ginv[:, :, :], ALU.mult, r=[('k',), ('Ginv',)], w=[('Kt',)])
            tt('dve', kk[:, :, :], kk[:, :, :], tmp2[:, :, :], ALU.mult, r=[('kk',), ('tmp2',)], w=[('kk',)])
            tt('dve', b_[:, :, :], kk[:, :, :], a_[:, :, :], ALU.mult, r=[('kk',), ('a',)], w=[('b',)])
            stt('dve', AR[:, :, 0:CH], kk[:, :, :], -1.0, Gex[:, :, :], ALU.mult, ALU.mult,
                r=[('kk',), ('Gex',)], w=[('AR0',)])
            tt('dve', Bt[:, :, :], b_[:, :, :], Ginv[:, :, :], ALU.mult, r=[('b',), ('Ginv',)], w=[('Bt',)])
            tt('dve', Bh[:, :, :], b_[:, :, :], Ghat[:, :, :], ALU.mult, r=[('b',), ('Ghat',)], w=[('Bh',)])
            tt('pool', Kh[:, :, :], k_[:, :, :], Ghat[:, :, :], ALU.mult, r=[('k',), ('Ghat',), ('Bt',)], w=[('Kh',)])
            if PH < 1:
                return
            GR = [(0, 8), (8, 8)]

            def hv(ap, g):
                return ap[:, GR[g][0]:GR[g][0] + 8, :]

            def pg2(b0):
                return PS[0:64, b0 * 512:b0 * 512 + 1024].rearrange("p (h two s) -> p h two s", h=8, two=2)

            def pg1(b0):
                return PS[0:64, b0 * 512:b0 * 512 + 512].rearrange("p (h s) -> p h s", h=8)

            SU8 = C.masks[0:64, 0:64].unsqueeze(1).to_broadcast([64, 8, CH])
            IU8 = C.masks[0:64, 64:128].unsqueeze(1).to_broadcast([64, 8, CH])
            SL8 = C.masks[0:64, 128:192].unsqueeze(1).to_broadcast([64, 8, CH])
            ID8 = ident[0:64, 0:64].unsqueeze(1).to_broadcast([64, 8, CH])
            for g in range(2):
                h0 = GR[g][0]
                bA = 0 if g == 0 else 3
                for hh in range(8):
                    h = h0 + hh
                    mm(PS[0:64, bA * 512 + hh * 128:bA * 512 + (hh + 1) * 128], Bt[:, h, :], AR[:, h, :], True, True,
                       r=[('Bt',), ('AR0',), ('AR1',)], w=bk(bA + hh // 4))
                tt('dve', hv(MX[:, :, 0:CH], g), pg2(bA)[:, :, 0, :], SU8, ALU.mult, r=bk(bA, bA + 1) + [('c_masks',)],
                   w=[('MX0', g)])
                tt('dve', hv(RBT, g), pg2(bA)[:, :, 1, :], IU8, ALU.mult, r=bk(bA, bA + 1) + [('c_masks',)],
                   w=[('RBT', g)])
                for hh in range(8):
                    h = h0 + hh
                    mm(PS[0:64, bA * 512 + hh * 128:bA * 512 + (hh + 1) * 128], Kt[:, h, :], AR[:, h, :], True, True,
                       r=[('Kt',), ('AR0',), ('AR1',)], w=bk(bA + hh // 4))
                tt('dve', hv(LakT, g), pg2(bA)[:, :, 0, :], SU8, ALU.mult, r=bk(bA, bA + 1) + [('c_masks',)],
                   w=[('LakT', g)])
                tt('dve', hv(RKT, g), pg2(bA)[:, :, 1, :], IU8, ALU.mult, r=bk(bA, bA + 1) + [('c_masks',)],
                   w=[('RKT', g)])
                for hh in range(8):
                    h = h0 + hh
                    mm(PS[0:64, (bA + 2) * 512 + hh * 64:(bA + 2) * 512 + (hh + 1) * 64], AR[:, h, 0:CH], Bt[:, h, :],
                       True, True, r=[('Bt',), ('AR0',)], w=bk(bA + 2))
                tt('dve', hv(Lm, g), pg1(bA + 2), SL8, ALU.mult, r=bk(bA + 2) + [('c_masks',)], w=[('Lm', g)])
                cp('pool', hv(MX[:, :, CH:2 * CH], g), ID8, r=[('c_ident',), ('Bt',)], w=[('MX1', g)])
            tt('pool', tmp1[:, :, :], r_[:, :, :], prm('r_k'), ALU.mult, r=[('r',), ('c_vec64',), ('Bt',)], w=[('tmp1',)])
            tt('pool', tmp1[:, :, :], tmp1[:, :, :], k_[:, :, :], ALU.mult, r=[('tmp1',), ('k',)], w=[('tmp1',)])
            if PH < 2:
                return
            for lvl in range(6):
                for g in range(2):
                    h0 = GR[g][0]
                    bA = 0 if g == 0 else 3
                    for hh in range(8):
                        h = h0 + hh
                        mm(PS[0:64, bA * 512 + hh * 128:bA * 512 + (hh + 1) * 128], Lm[:, h, :], MX[:, h, :], True, True,
                           r=[('Lm', g), ('MX0', g), ('MX1', g)], w=bk(bA + hh // 4))
                    if lvl < 5:
                        for hh in range(8):
                            h = h0 + hh
                            mm(PS[0:64, (bA + 2) * 512 + hh * 64:(bA + 2) * 512 + (hh + 1) * 64], MX[:, h, 0:CH], Lm[:, h, :],
                               True, True, r=[('Lm', g), ('MX0', g)], w=bk(bA + 2))
                for g in range(2):
                    bA = 0 if g == 0 else 3
                    tt('dve', hv(MX[:, :, CH:2 * CH], g), pg2(bA)[:, :, 1, :], hv(MX[:, :, CH:2 * CH], g), ALU.add,
                       r=bk(bA, bA + 1) + [('MX1', g)], w=[('MX1', g)])
                    if lvl < 5:
                        cp('act', hv(MX[:, :, 0:CH], g), pg2(bA)[:, :, 0, :], r=bk(bA, bA + 1), w=[('MX0', g)])
                        cp('act', hv(Lm, g), pg1(bA + 2), r=bk(bA + 2), w=[('Lm', g)])
                if lvl == 0 and t + 1 < NTR:
                    seg_A(t + 1)
            if PH < 3:
                return
            for h in range(NH):
                mm(PS[0:64, 3072 + h:3072 + h + 1], tmp1[:, h, :], C.ones_f[0:64, 0:1], True, True,
                   r=[('tmp1',), ('c_onesf',)], w=bk(6))
            cp('dve', bon[:, :], PS[0:64, 3072:3072 + NH], r=bk(6), w=[('bon',)])
            if TRDT == BF16:
                psb3 = PSb[0:64, :].rearrange("p (h s) -> p h s", h=NH)
                for h in range(NH):
                    tr(PSb[0:64, h * 64:(h + 1) * 64], Bh[:, h, :], ident_bf[:, :], r=[('Bh',), ('ident_bf',)], w=[('psb',)])
                cp('act', BhT[:, :, :], psb3, r=[('psb',)], w=[('BhT',)])
                for h in range(NH):
                    tr(PSb[0:64, h * 64:(h + 1) * 64], Kh[:, h, :], ident_bf[:, :], r=[('Kh',), ('ident_bf',)], w=[('psb',)])
                cp('dve', KhT[:, :, :], psb3, r=[('psb',)], w=[('KhT',)])
            else:
                for h in range(NH):
                    tr(PS[0:64, h * 64:(h + 1) * 64], Bh[:, h, :], ident[0:64, 0:64], r=[('Bh',), ('c_ident',)], w=bk(h // 8))
                cp('act', BhT[:, :, :], psv1(0), r=bk(0, 1), w=[('BhT',)])
                for h in range(NH):
                    tr(PS[0:64, 1024 + h * 64:1024 + (h + 1) * 64], Kh[:, h, :], ident[0:64, 0:64],
                       r=[('Kh',), ('c_ident',)], w=bk(2 + h // 8))
                cp('dve', KhT[:, :, :], psv1(2), r=bk(2, 3), w=[('KhT',)])
            if PH < 4:
                return
            for h in range(NH):
                o = PS[0:64, h * 64:(h + 1) * 64]
                mm(o, AR[:, h, 0:CH], Hb[:, h, :], True, False, r=[('AR0',), ('Hb',)], w=bk(h // 8))
                mm(o, LakT[:, h, :], v_bf[:, h * 64:(h + 1) * 64], False, True, r=[('LakT', h // 8), ('v_bf',)], w=bk(h // 8))
            cp('act', Gs[:, :, :], psv1(0), r=bk(0, 1), w=[('Gs',)])
            for h in range(NH):
                mm(PS[0:64, 1024 + h * 64:1024 + (h + 1) * 64], MX[:, h, CH:2 * CH], Gs[:, h, :], True, True,
                   r=[('MX1', h // 8), ('Gs',)], w=bk(2 + h // 8))
            cp('dve', Us[:, :, :], psv1(2), r=bk(2, 3), w=[('Us',)])
            for h in range(NH):
                o = PS[0:64, 2048 + h * 64:2048 + (h + 1) * 64]
                mm(o, AR[:, h, CH:2 * CH], Hb[:, h, :], True, False, r=[('AR1',), ('Hb',)], w=bk(4 + h // 8))
                mm(o, RBT[:, h, :], Us[:, h, :], False, False, r=[('RBT', h // 8), ('Us',)], w=bk(4 + h // 8))
                mm(o, RKT[:, h, :], v_bf[:, h * 64:(h + 1) * 64], False, True, r=[('RKT', h // 8), ('v_bf',)], w=bk(4 + h // 8))
            for h in range(NH):
                o = PS[0:64, h * 64:(h + 1) * 64]
                mm(o, BhT[:, h, :], Us[:, h, :], True, False, r=[('BhT',), ('Us',)], w=bk(h // 8))
                mm(o, KhT[:, h, :], v_bf[:, h * 64:(h + 1) * 64], False, True, r=[('KhT',), ('v_bf',)], w=bk(h // 8))
            tt('dve', Hst[:, :, :], Hst[:, :, :], G[:, :, CH - 1:CH].to_broadcast([64, NH, CH]), ALU.mult,
               r=[('Hst',), ('G',)], w=[('Hst',)])
            tt('dve', Hst[:, :, :], psv1(0), Hst[:, :, :], ALU.add, r=bk(0, 1) + [('Hst',)], w=[('Hst',)])
            cp('act', Hb[:, :, :], Hst[:, :, :], r=[('Hst',)], w=[('Hb',)])
            if PH < 5:
                return

        def seg_H(t):
            t0 = t * CH
            hb = hbuf[t % 2]
            py = psv1(4)
            yb = bk(4, 5)
            red('dve', st1[:, :], py, ALU.add, r=yb, w=[('st1',)])
            ts('dve', st1[:, :], st1[:, :], -1.0 / 64, None, ALU.mult, None, r=[('st1',)], w=[('st1',)])
            tt('dve', yc[:, :, :], py, bc(st1[:, :]), ALU.add, r=yb + [('st1',)], w=[('yc',)])
            act(ysq[:, :, :], yc[:, :, :], AF.Square, r=[('yc',)], w=[('a',)])
            red('dve', st2[:, :], ysq[:, :, :], ALU.add, r=[('a',)], w=[('st2',)])
            act(st2[:, :], st2[:, :], AF.Sqrt, r=[('st2',)], w=[('st2',)], scale=1.0 / 64, bias=C.eps_col[0:64, 1:2])
            S.op('dve', lambda e: e.reciprocal(out=st2[:, :], in_=st2[:, :]), r=[('st2',)], w=[('st2',)])
            tt('dve', yc[:, :, :], yc[:, :, :], bc(st2[:, :]), ALU.mult, r=[('yc',), ('st2',)], w=[('yc',)])
            ycf = yc[:, :, :].rearrange("p h s -> p (h s)")
            tt('dve', ycf, ycf, lnxg[:, :], ALU.mult, r=[('yc',), ('lnxg',)], w=[('yc',)])
            tt('dve', ycf, ycf, lnxb[:, :], ALU.add, r=[('yc',), ('lnxb',)], w=[('yc',)])
            vv = v_bf[:, :].rearrange("p (h s) -> p h s", h=NH)
            tt('dve', ysq[:, :, :], vv, bc(bon[:, :]), ALU.mult, r=[('v_bf',), ('bon',)], w=[('a',)])
            tt('dve', yc[:, :, :], yc[:, :, :], ysq[:, :, :], ALU.add, r=[('yc',), ('a',)], w=[('yc',)])
            tt('dve', ycf, ycf, g_tm[:, :], ALU.mult, r=[('yc',), ('g_tm',)], w=[('yc',)])
            for c in range(8):
                tr(PS[:, 3072 + c * 64:3072 + (c + 1) * 64], yc[:, 2 * c:2 * c + 2, :].rearrange("p h s -> p (h s)"),
                   ident[0:64, 0:64], r=[('yc',), ('c_ident',)], w=bk(6))
            cp('act', zT[:, :, :], PS[:, 3072:3584].rearrange("p (c s) -> p c s", c=8), r=bk(6), w=[('zT',)])
            for co in range(8):
                q = 4 + (co % 2)
                for kc in range(8):
                    mm(PS[:, q * 512:q * 512 + CH], Wo[:, kc, co * 128:(co + 1) * 128], zT[:, kc, :], kc == 0, kc == 7,
                       r=[('zT',), ('Wo',)], w=bk(q))
                tt('dve', hb[:, co, :], PS[:, q * 512:q * 512 + CH], hb[:, co, :], ALU.add,
                   r=bk(q) + [('hbuf', t % 2)], w=[('hbuf', t % 2)])
            dma('pool', hTv[:, :, t0:t0 + CH], hb[:, :, :], r=[('hbuf', t % 2)], w=[('hT', t)])

        NTR = NT if not DEBUG_RK else DEBUG_RK[0]
        if NTR > 0:
            seg_A(0)
            seg_P(0)
        for t in range(NTR):
            seg_E(t)
            if t + 1 < NTR:
                seg_P(t + 1)
            seg_H(t)
        S.barrier()
        S.emit()


def stage_dsa(C):
    nc, S = C.nc, C.S
    Hh = H(S)
    mm, tr, act, cp, tt, ts, stt, red, dma = Hh.mm, Hh.tr, Hh.act, Hh.cp, Hh.tt, Hh.ts, Hh.stt, Hh.red, Hh.dma
    TT = 128
    NQT = (TP + TT - 1) // TT
    NQT_RUN = min(NQT, DEBUG_NQT) if DEBUG_NQT else NQT
    c64 = C.cols64
    cols = C.cols
    NBIS = 13
    MB = 240000.0
    with ExitStack() as st:
        sb = lambda n, sh, dt=F32: st.enter_context(nc.sbuf_tensor("ds_" + n, sh, dt))
        Wq = sb("Wq", [128, 8, 1024], BF16)
        Wk = sb("Wk", [128, 8, 256], BF16)
        Wv = sb("Wv", [128, 8, 256], BF16)
        Wqi = sb("Wqi", [128, 8, 512], BF16)
        Wki = sb("Wki", [128, 8, 64], BF16)
        Wwi = sb("Wwi", [128, 8, 8], BF16)
        Wo = sb("Wo", [128, 8, 1024], BF16)
        kT = sb("kT", [64, 4, TP], BF16)
        Vaug = sb("Vaug", [128, NQT, 4, 65], BF16)
        kiT = sb("kiT", [64, TP], BF16)
        score = sb("score", [128, TP])
        work = sb("work", [128, TP])
        mask01 = sb("mask01", [128, TP], BF16)
        maskT = sb("maskT", [128, NQT, TT], BF16)
        hbuf = [sb("hbuf%d" % i, [128, 8, TT]) for i in range(3)]
        hnb = sb("hnb", [128, 8, TT], BF16)
        sq = sb("sq", [128, 8, TT], BF16)
        rstd = sb("rstd", [128, TT])
        qT = [sb("qT%d" % i, [64, 16 * TT], BF16) for i in range(2)]
        qiT = sb("qiT", [64, 8 * TT], BF16)
        tA = sb("tA", [64, 512])
        tB = sb("tB", [64, 512])
        tC = sb("tC", [64, 512])
        rl = [sb("rl%d" % i, [128, 512]) for i in range(2)]
        PT = [sb("PT%d" % i, [128, 512], BF16) for i in range(3)]
        o_tm = sb("o_tm", [128, 1024], BF16)
        oT = sb("oT", [128, 8, TT], BF16)
        cs = sb("cs", [64, 2, TT])
        wi = sb("wi", [128, 8])
        m8 = sb("m8", [128, 8])
        eq8 = sb("eq8", [128, 8])
        iota8 = sb("iota8", [128, 8])
        lo = sb("lo", [128, 1])
        HC = sb("HC", [128, 2])
        MC = sb("MC", [128, 2])
        sel = sb("sel", [128, 1])
        d1 = sb("d1", [128, 1])
        d2 = sb("d2", [128, 2])
        thr = sb("thr", [128, 1])
        nm1 = sb("nm1", [128, 1])
        halfc = sb("halfc", [128, 1])
        negb = sb("negb", [128, 1])
        rden = sb("rden", [128, 16])
        ident_bf = sb("ident_bf", [128, 128], BF16)
        zeros_bf = sb("zeros_bf", [128, 512], BF16)
        rot = sb("rot", [64, 64])
        negmask = sb("negmask", [128, 128])
        PS = st.enter_context(nc.psum_tensor("ds_PS", [128, 3584], F32))
        PSb = st.enter_context(nc.psum_tensor("ds_PSb", [128, 1024], BF16))
        bank = lambda b: PS[:, b * 512:(b + 1) * 512]

        hTv = C.hT.rearrange("(c p) t -> p c t", p=128)
        ident = C.ident
        vec = C.vec
        v64 = C.vec64
        win = C.at['w_in'].rearrange("(k p) n -> p k n", p=128)
        for k in range(8):
            dma('pool', Wq[:, k, :], win[:, k, 0:1024], r=[], w=[('Wq',)])
        dma('pool', Wk[:, :, :], win[:, :, 1024:1280], r=[], w=[('Wk',)])
        dma('pool', Wv[:, :, :], win[:, :, 1280:1536], r=[], w=[('Wv',)])
        for k in range(8):
            dma('pool', Wqi[:, k, :], win[:, k, 1536:2048], r=[], w=[('Wqi',)])
        dma('pool', Wki[:, :, :], win[:, :, 2048:2112], r=[], w=[('Wki',)])
        dma('pool', Wwi[:, :, :], win[:, :, 2112:2120], r=[], w=[('Wwi',)])
        wov = C.at['w_o'].rearrange("(k p) n -> p k n", p=128)
        for k in range(8):
            dma('pool', Wo[:, k, :], wov[:, k, :], r=[], w=[('Wo',)])
        dma('sp', rot[:, :], C.rot_d[:, :], r=[], w=[('rot',)])
        dma('sp', negmask[:, :], C.negmask_d[:, :], r=[], w=[('negmask',)])
        cp('dve', ident_bf[:, :], ident[:, :], r=[('c_ident',)], w=[('ident_bf',)])
        S.op('pool', lambda e: e.memset(zeros_bf[:, :], 0.0), w=[('zeros_bf',)])
        S.op('pool', lambda e: e.memset(Vaug[:, :, :, 64:65], 1.0), w=[('Vones',)])
        S.op('pool', lambda e: e.memset(halfc[:, :], 0.5), w=[('halfc',)])
        S.op('pool', lambda e: e.memset(negb[:, :], -MB), w=[('negb',)])
        for j in range(8):
            S.op('pool', lambda e, j=j: e.memset(iota8[:, j:j + 1], float(j)), w=[('iota8',)])
        gcol = cols['norm_g_1_1'][0]
        qg = v64[0:64, c64['q_g'][0]:c64['q_g'][0] + 1]
        kg = v64[0:64, c64['k_g'][0]:c64['k_g'][0] + 1]
        kwid = lambda kb: min(128, TP - kb * 128)

        def norm_rope(pb, nh, tw, gcolap, out3, okeys_w):
            n = nh * tw
            pin = bank(pb)[0:64, 0:n]
            v3 = lambda ap: ap.rearrange("p (h s) -> p h s", h=nh)
            if gcolap is not None:
                act(tC[:, 0:n], pin, AF.Copy, r=bk(pb) + [('c_vec64',)], w=[('tC',)], scale=gcolap)
            else:
                cp('act', tC[:, 0:n], pin, r=bk(pb), w=[('tC',)])
            mm(bank(6)[0:64, 0:n], rot[:, :], tC[:, 0:n], True, True, r=[('tC',), ('rot',)], w=bk(6))
            cosb = cs[:, 0, 0:tw].unsqueeze(1).to_broadcast([64, nh, tw])
            sinb = cs[:, 1, 0:tw].unsqueeze(1).to_broadcast([64, nh, tw])
            if gcolap is not None:
                act(tA[:, 0:n], pin, AF.Square, r=bk(pb), w=[('tA',)])
            tt('pool', v3(tC[:, 0:n]), v3(tC[:, 0:n]), cosb, ALU.mult, r=[('tC',), ('cs',)], w=[('tC',)])
            tt('dve', v3(tB[:, 0:n]), v3(bank(6)[0:64, 0:n]), sinb, ALU.mult, r=bk(6) + [('cs',)], w=[('tB',)])
            if gcolap is None:
                tt('pool', out3, v3(tC[:, 0:n]), v3(tB[:, 0:n]), ALU.add, r=[('tC',), ('tB',)], w=okeys_w)
                return
            mm(bank(6)[0:64, 0:n], C.ones_f[0:64, 0:64], tA[:, 0:n], True, True, r=[('tA',), ('c_onesf',)], w=bk(6))
            act(tA[:, 0:n], bank(6)[0:64, 0:n], AF.Ln, r=bk(6), w=[('tA',)], scale=1.0 / 64, bias=C.eps_col[0:64, 0:1])
            act(tA[:, 0:n], tA[:, 0:n], AF.Exp, r=[('tA',)], w=[('tA',)], scale=-0.5)
            tt('pool', tC[:, 0:n], tC[:, 0:n], tB[:, 0:n], ALU.add, r=[('tC',), ('tB',)], w=[('tC',)])
            tt('dve', out3, v3(tC[:, 0:n]), v3(tA[:, 0:n]), ALU.mult, r=[('tC',), ('tA',)], w=okeys_w)

        def XA(qt):
            s = qt % 2
            s3 = qt % 3
            t0 = qt * TT
            tw = min(TT, TP - t0)
            n = t0 + tw
            nkb = qt + 1
            hb = hbuf[s3]
            dma('sp', hb[:, :, 0:tw], hTv[:, :, t0:t0 + tw], r=[('hT', qt)], w=[('hbuf', s3)])
            dma('sp', cs[:, :, 0:tw], C.rope_d[:, :, t0:t0 + tw], r=[], w=[('cs',)])
            act(sq[:, :, 0:tw], hb[:, :, 0:tw], AF.Square, r=[('hbuf', s3)], w=[('sq',)])
            for c in range(8):
                mm(bank(6)[:, 0:tw], C.ones_bf[:, :], sq[:, c, 0:tw], c == 0, c == 7, r=[('sq',)], w=bk(6))
            act(rstd[:, 0:tw], bank(6)[:, 0:tw], AF.Sqrt, r=bk(6), w=[('rstd',)], scale=1.0 / D, bias=C.eps_col[:, 0:1])
            S.op('dve', lambda e: e.reciprocal(out=rstd[:, 0:tw], in_=rstd[:, 0:tw]), r=[('rstd',)], w=[('rstd',)])
            for c in range(8):
                stt('dve', hnb[:, c, 0:tw], hb[:, c, 0:tw], vec[:, gcol + c:gcol + c + 1], rstd[:, 0:tw],
                    ALU.mult, ALU.mult, r=[('hbuf', s3), ('rstd',)], w=[('hnb',)])

        def XR(qt):
            s = qt % 2
            t0 = qt * TT
            tw = min(TT, TP - t0)
            n = t0 + tw
            nkb = qt + 1
            for g in range(4):
                for kc in range(8):
                    mm(bank(5)[0:64, g * tw:(g + 1) * tw], Wk[:, kc, g * 64:(g + 1) * 64], hnb[:, kc, 0:tw], kc == 0, kc == 7,
                       r=[('hnb',), ('Wk',)], w=bk(5))
            norm_rope(5, 4, tw, kg, kT[:, :, t0:t0 + tw], [('kT',)])
            for kc in range(8):
                mm(bank(5)[0:tw, 0:256], hnb[:, kc, 0:tw], Wv[:, kc, :], kc == 0, kc == 7, r=[('hnb',), ('Wv',)], w=bk(5))
            cp('act', Vaug[0:tw, qt, :, 0:64], bank(5)[0:tw, 0:256].rearrange("p (g d) -> p g d", g=4), r=bk(5),
               w=[('Vaug',)])
            for kc in range(8):
                mm(bank(5)[0:64, 0:tw], Wki[:, kc, :], hnb[:, kc, 0:tw], kc == 0, kc == 7, r=[('hnb',), ('Wki',)], w=bk(5))
            norm_rope(5, 1, tw, None, kiT[:, t0:t0 + tw].unsqueeze(1), [('kiT',)])
            for grp in range(4):
                pb = 5
                for hh in range(4):
                    h = grp * 4 + hh
                    for kc in range(8):
                        mm(bank(pb)[0:64, hh * tw:(hh + 1) * tw], Wq[:, kc, h * 64:(h + 1) * 64], hnb[:, kc, 0:tw],
                           kc == 0, kc == 7, r=[('hnb',), ('Wq',)], w=bk(pb))
                norm_rope(pb, 4, tw, qg, qT[s][:, grp * 4 * tw:(grp + 1) * 4 * tw].rearrange("p (h s) -> p h s", h=4),
                          [('qT', s)])
            for grp in range(2):
                pb = 5
                for hh in range(4):
                    h = grp * 4 + hh
                    for kc in range(8):
                        mm(bank(pb)[0:64, hh * tw:(hh + 1) * tw], Wqi[:, kc, h * 64:(h + 1) * 64], hnb[:, kc, 0:tw],
                           kc == 0, kc == 7, r=[('hnb',), ('Wqi',)], w=bk(pb))
                norm_rope(pb, 4, tw, None, qiT[:, grp * 4 * tw:(grp + 1) * 4 * tw].rearrange("p (h s) -> p h s", h=4),
                          [('qiT',)])
            for kc in range(8):
                mm(bank(6)[0:tw, 0:8], hnb[:, kc, 0:tw], Wwi[:, kc, :], kc == 0, kc == 7, r=[('hnb',), ('Wwi',)], w=bk(6))
            ts('dve', wi[0:tw, :], bank(6)[0:tw, 0:8], float(512.0 ** -0.5), None, ALU.mult, None, r=bk(6), w=[('wi',)])
            idx = 0
            for k0 in range(0, n, 512):
                nk = min(512, n - k0)
                for h in range(8):
                    pb = 5 + idx % 2
                    rb = rl[idx % 2]
                    rk = ('rl', idx % 2)
                    idx += 1
                    mm(bank(pb)[0:tw, 0:nk], qiT[:, h * tw:(h + 1) * tw], kiT[:, k0:k0 + nk], True, True,
                       r=[('qiT',), ('kiT',)], w=bk(pb))
                    act(rb[0:tw, 0:nk], bank(pb)[0:tw, 0:nk], AF.Relu, r=bk(pb), w=[rk])
                    if h == 0:
                        ts('dve', score[0:tw, k0:k0 + nk], rb[0:tw, 0:nk], wi[0:tw, 0:1], None, ALU.mult, None,
                           r=[rk, ('wi',)], w=[('score',)])
                    else:
                        stt('dve', score[0:tw, k0:k0 + nk], rb[0:tw, 0:nk], wi[0:tw, h:h + 1], score[0:tw, k0:k0 + nk],
                            ALU.mult, ALU.add, r=[rk, ('wi',), ('score',)], w=[('score',)])
            sc = score[0:tw, 0:n]
            if n > 256:
                red('dve', HC[0:tw, 0:1], sc, ALU.max, r=[('score',)], w=[('HC',)])
                red('dve', lo[0:tw, :], sc, ALU.min, r=[('score',)], w=[('lo',)])
            tt('dve', score[0:tw, t0:t0 + tw], score[0:tw, t0:t0 + tw], negmask[0:tw, 0:tw], ALU.add,
               r=[('score',), ('negmask',)], w=[('score',)])
            if n > 256:
                tt('dve', d1[0:tw, :], HC[0:tw, 0:1], lo[0:tw, :], ALU.subtract, r=[('HC',), ('lo',)], w=[('d1',)])
                stt('dve', HC[0:tw, 0:1], d1[0:tw, :], 1.0e-6, HC[0:tw, 0:1], ALU.mult, ALU.add, r=[('d1',), ('HC',)],
                    w=[('HC',)])
                ts('dve', HC[0:tw, 1:2], d1[0:tw, :], 0.0, None, ALU.mult, None, r=[('d1',), ('HC',)], w=[('HC',)])
                for it in range(NBIS):
                    stt('dve', MC[0:tw, 0:1], lo[0:tw, :], HC[0:tw, 0:1], halfc[0:tw, :], ALU.add, ALU.mult,
                        r=[('lo',), ('HC',), ('halfc',)], w=[('MC',)])
                    S.op('dve', lambda e, tw=tw, n=n: e.tensor_scalar(
                        out=mask01[0:tw, 0:n], in0=score[0:tw, 0:n], scalar1=MC[0:tw, 0:1], scalar2=0.0,
                        op0=ALU.is_ge, op1=ALU.add, accum_out=MC[0:tw, 1:2]),
                        r=[('score',), ('MC',)], w=[('mask01',), ('MC',)])
                    ts('dve', sel[0:tw, :], MC[0:tw, 1:2], 256.0, None, ALU.is_ge, None, r=[('MC',)], w=[('sel',)])
                    tt('dve', d1[0:tw, :], MC[0:tw, 0:1], lo[0:tw, :], ALU.subtract, r=[('MC',), ('lo',)], w=[('d1',)])
                    stt('dve', lo[0:tw, :], d1[0:tw, :], sel[0:tw, 0:1], lo[0:tw, :], ALU.mult, ALU.add,
                        r=[('d1',), ('sel',), ('lo',)], w=[('lo',)])
                    tt('dve', d2[0:tw, :], HC[0:tw, :], MC[0:tw, :], ALU.subtract, r=[('MC',), ('HC',)], w=[('d2',)])
                    stt('dve', HC[0:tw, :], d2[0:tw, :], sel[0:tw, 0:1], MC[0:tw, :], ALU.mult, ALU.add,
                        r=[('d2',), ('sel',), ('MC',)], w=[('HC',)])
                wk = work[0:tw, 0:n]
                ts('dve', wk, sc, HC[0:tw, 0:1], 1.0e20, ALU.is_ge, ALU.mult, r=[('score',), ('HC',)], w=[('work',)])
                tt('dve', wk, sc, wk, ALU.subtract, r=[('score',), ('work',)], w=[('work',)])
                S.op('dve', lambda e, tw=tw, n=n: e.max(out=m8[0:tw, :], in_=work[0:tw, 0:n]), r=[('work',)], w=[('m8',)])
                ts('dve', nm1[0:tw, :], HC[0:tw, 1:2], -1.0, 255.0, ALU.mult, ALU.add, r=[('HC',)], w=[('nm1',)])
                ts('dve', nm1[0:tw, :], nm1[0:tw, :], 7.0, 0.0, ALU.min, ALU.max, r=[('nm1',)], w=[('nm1',)])
                ts('dve', eq8[0:tw, :], iota8[0:tw, :], nm1[0:tw, 0:1], None, ALU.is_equal, None, r=[('nm1',), ('iota8',)],
                   w=[('eq8',)])
                tt('dve', eq8[0:tw, :], eq8[0:tw, :], m8[0:tw, :], ALU.mult, r=[('eq8',), ('m8',)], w=[('eq8',)])
                red('dve', thr[0:tw, :], eq8[0:tw, :], ALU.add, r=[('eq8',)], w=[('thr',)])
                ts('dve', mask01[0:tw, 0:n], sc, thr[0:tw, 0:1], None, ALU.is_ge, None, r=[('score',), ('thr',)],
                   w=[('mask01',)])
            else:
                ts('dve', mask01[0:tw, 0:n], sc, -1.0e29, None, ALU.is_ge, None, r=[('score',)], w=[('mask01',)])
        def X2(qt):
            t0 = qt * TT
            tw = min(TT, TP - t0)
            nkb = qt + 1
            for kb0 in range(0, nkb, 8):
                nb = min(8, nkb - kb0)
                for j in range(nb):
                    kb = kb0 + j
                    kw = kwid(kb)
                    tr(PSb[0:kw, j * 128:j * 128 + tw], mask01[0:tw, kb * 128:kb * 128 + kw], ident_bf[0:tw, 0:tw],
                       r=[('mask01',), ('ident_bf',)], w=[('psb',)])
                kwl = kwid(kb0 + nb - 1)
                nfull = nb if kwl == 128 else nb - 1
                if nfull > 0:
                    act(maskT[:, kb0:kb0 + nfull, 0:tw],
                        PSb[:, 0:nfull * 128].rearrange("p (j s) -> p j s", j=nfull)[:, :, 0:tw], AF.Identity,
                        r=[('psb',), ('negb',)], w=[('maskT', kb) for kb in range(kb0, kb0 + nfull)],
                        scale=MB, bias=negb[:, 0:1])
                if nfull < nb:
                    act(maskT[0:kwl, kb0 + nb - 1, 0:tw], PSb[0:kwl, (nb - 1) * 128:(nb - 1) * 128 + tw], AF.Identity,
                        r=[('psb',), ('negb',)], w=[('maskT', kb0 + nb - 1)], scale=MB, bias=negb[0:kwl, 0:1])

        def Y(qt):
            s = qt % 2
            t0 = qt * TT
            tw = min(TT, TP - t0)
            nkb = qt + 1
            hb = hbuf[qt % 3]
            q_ = qT[s]
            OB = [(0, 0, 7), (1, 7, 7), (2, 14, 2)]
            for (ob, h0, nh) in OB:
                S.op('pe', lambda e, ob=ob, nh=nh: e.matmul(bank(ob)[0:tw, 0:nh * 65], lhsT=zeros_bf[:, 0:tw],
                                                          rhs=zeros_bf[:, 0:nh * 65], start=True, stop=False,
                                                          skip_group_check=True),
                     r=[('zeros_bf',)], w=bk(ob))
            jobs = [(kb, g) for kb in range(nkb) for g in range(4)]

            def s_part(i):
                kb, g = jobs[i]
                kw = kwid(kb)
                pb = 3 + i % 2
                pt = PT[i % 3]
                pk = ('PT', i % 3)
                mbias = maskT[0:kw, kb, 0:tw].unsqueeze(1).to_broadcast([kw, 4, tw])
                mm(bank(pb)[0:kw, 0:4 * tw], kT[:, g, kb * 128:kb * 128 + kw], q_[:, g * 4 * tw:(g + 1) * 4 * tw],
                   True, False, r=[('kT',), ('qT', s)], w=bk(pb))
                mm(bank(pb)[0:kw, 0:4 * tw].rearrange("p (h s) -> p h s", h=4), ident_bf[0:kw, 0:kw], mbias,
                   False, True, r=[('maskT', kb), ('ident_bf',)], w=bk(pb))
                act(pt[0:kw, 0:4 * tw], bank(pb)[0:kw, 0:4 * tw], AF.Exp, r=bk(pb), w=[pk], scale=0.125)

            def v_part(i):
                kb, g = jobs[i]
                kw = kwid(kb)
                pt = PT[i % 3]
                pk = ('PT', i % 3)
                for rr in range(4):
                    h = 4 * g + rr
                    S.op('pe', lambda e, h=h, rr=rr, kw=kw, pt=pt, kb=kb, g=g, last=(kb == nkb - 1): e.matmul(
                        bank(h // 7)[0:tw, (h % 7) * 65:(h % 7 + 1) * 65], lhsT=pt[0:kw, rr * tw:(rr + 1) * tw],
                        rhs=Vaug[0:kw, kb, g, :], start=False, stop=last, skip_group_check=True),
                        r=[pk, ('Vaug',), ('Vones',)], w=bk(h // 7))

            s_part(0)
            for i in range(len(jobs)):
                if i + 1 < len(jobs):
                    s_part(i + 1)
                v_part(i)
            for (ob, h0, nh) in OB:
                o3 = bank(ob)[0:tw, 0:nh * 65].rearrange("p (h d) -> p h d", h=nh)
                S.op('dve', lambda e, o3=o3, h0=h0, nh=nh: e.reciprocal(out=rden[0:tw, h0:h0 + nh], in_=o3[:, :, 64]),
                     r=bk(ob), w=[('rden', ob)])
                tt('dve', o_tm[0:tw, h0 * 64:(h0 + nh) * 64].rearrange("p (h d) -> p h d", h=nh), o3[:, :, 0:64],
                   rden[0:tw, h0:h0 + nh].unsqueeze(2).to_broadcast([tw, nh, 64]), ALU.mult,
                   r=bk(ob) + [('rden', ob)], w=[('o_tm',)])
            for c in range(8):
                tr(PSb[:, c * 128:c * 128 + tw], o_tm[0:tw, c * 128:(c + 1) * 128], ident_bf[0:tw, 0:tw],
                   r=[('o_tm',), ('ident_bf',)], w=[('psb',)])
            cp('act', oT[:, :, 0:tw], PSb[:, :].rearrange("p (c s) -> p c s", c=8)[:, :, 0:tw],
               r=[('psb',)], w=[('oT',)])
            for co in range(8):
                pb = 3 + co % 2
                for kc in range(8):
                    mm(bank(pb)[:, 0:tw], Wo[:, kc, co * 128:(co + 1) * 128], oT[:, kc, 0:tw], kc == 0, kc == 7,
                       r=[('oT',), ('Wo',)], w=bk(pb))
                tt('dve', hb[:, co, 0:tw], bank(pb)[:, 0:tw], hb[:, co, 0:tw], ALU.add, r=bk(pb) + [('hbuf', qt % 3)],
                   w=[('hbuf', qt % 3)])
            dma('pool', hTv[:, :, t0:t0 + tw], hb[:, :, 0:tw], r=[('hbuf', qt % 3)], w=[('hT', qt)])

        XA(0)
        XR(0)
        X2(0)
        if NQT_RUN > 1:
            XA(1)
        for qt in range(NQT_RUN):
            if qt + 1 < NQT_RUN:
                S.capture()
                XR(qt + 1)
                la = S.end_capture()
                S.capture()
                Y(qt)
                lb = S.end_capture()
                S.replay_merged(la, lb, frac=MERGE_FRAC)
                if qt + 2 < NQT_RUN:
                    XA(qt + 2)
                X2(qt + 1)
            else:
                Y(qt)
        S.barrier()
        S.emit()


def ones_bf(C):
    return C.ones_bf

VEC_COLS = {}


def _vec_layout():
    cols = {}
    c = 0
    for l in range(2):
        for j in range(3):
            cols['norm_g_%d_%d' % (l, j)] = (c, 8)
            c += 8
    cols['rk_mix'] = (c, 48)
    c += 48
    return cols, c


def _vec64_layout():
    cols = {}
    c = 0
    for n in ('k_k', 'k_a', 'a0', 'r_k'):
        cols[n] = (c, 16)
        c += 16
    for n in ('q_g', 'k_g'):
        cols[n] = (c, 1)
        c += 1
    return cols, c


RK_SHAPES = {'w_r': [D, D], 'w_k': [D, D], 'w_v': [D, D], 'w_o': [D, D], 'w0': [1, D], 'w1': [D, 64], 'w2': [64, D],
             'a1': [D, 64], 'a2': [64, D], 'g1': [D, 160], 'g2': [160, D], 'lnx_g': [1, D], 'lnx_b': [1, D]}


def build_program(stages):
    nc = bass.Bass("TRN2", target_bir_lowering=False)
    C = Ctx()
    C.nc = nc
    din = lambda n, sh, dt=F32: nc.dram_tensor(n, list(sh), dt, kind="ExternalInput").ap()
    C.x = din("x", [SEQ, D])
    C.meta = din("meta", [NMETA, D])
    C.ffn_w_in = din("ffn_w_in", [2, 2, D, 2 * DFF])
    C.ffn_w_out = din("ffn_w_out", [2, 2, DFF, D])
    C.rk = {k: din("rk_" + k, sh) for k, sh in RK_SHAPES.items()}
    cols, nv = _vec_layout()
    cols64, nv64 = _vec64_layout()
    C.cols, C.cols64 = cols, cols64
    C.vec_d = din("vecs", [128, nv])
    C.vec64_d = din("vecs64", [64, nv64])
    C.ident_d = din("ident", [128, 128])
    C.masks_d = din("masks", [64, 192])
    C.tri_d = din("tri", [64, 128])
    C.at = {'w_in': din("at_w_in", [D, 2120]), 'w_o': din("at_w_o", [D, D])}
    C.rope_d = din("rope", [64, 2, TP])
    C.rot_d = din("rot", [64, 64])
    C.negmask_d = din("negmask", [128, 128])
    C.out = nc.dram_tensor("out", [SEQ, D], F32, kind="ExternalOutput").ap()
    C.hT = nc.dram_tensor("hT_scratch", [D, TP], F32, kind="Internal").ap()
    with ExitStack() as es:
        S = Sched(nc, es)
        C.S = S
        C.vec = es.enter_context(nc.sbuf_tensor("c_vec", [128, nv], F32))
        C.vec64 = es.enter_context(nc.sbuf_tensor("c_vec64", [64, nv64], F32))
        C.ident = es.enter_context(nc.sbuf_tensor("c_ident", [128, 128], F32))
        C.masks = es.enter_context(nc.sbuf_tensor("c_masks", [64, 192], F32))
        C.tri = es.enter_context(nc.sbuf_tensor("c_tri", [64, 128], F32))
        C.ones_bf = es.enter_context(nc.sbuf_tensor("c_ones_bf", [128, 128], BF16))
        C.ones_f = es.enter_context(nc.sbuf_tensor("c_ones_f", [128, 128], F32))
        C.eps_col = es.enter_context(nc.sbuf_tensor("c_eps", [128, 3], F32))
        S.op('sp', lambda e: e.dma_start(out=C.vec[:, :], in_=C.vec_d[:, :]), w=[('c_vec',)], dma=True)
        S.op('sp', lambda e: e.dma_start(out=C.vec64[:, :], in_=C.vec64_d[:, :]), w=[('c_vec64',)], dma=True)
        S.op('sp', lambda e: e.dma_start(out=C.ident[:, :], in_=C.ident_d[:, :]), w=[('c_ident',)], dma=True)
        S.op('sp', lambda e: e.dma_start(out=C.masks[:, :], in_=C.masks_d[:, :]), w=[('c_masks',)], dma=True)
        S.op('sp', lambda e: e.dma_start(out=C.tri[:, :], in_=C.tri_d[:, :]), w=[('c_tri',)], dma=True)
        S.op('pool', lambda e: e.memset(C.ones_bf[:, :], 1.0), w=[('c_ones',)])
        S.op('pool', lambda e: e.memset(C.ones_f[:, :], 1.0), w=[('c_onesf',)])
        S.op('pool', lambda e: e.memset(C.eps_col[:, 0:1], 1e-6), w=[('c_eps',)])
        S.op('pool', lambda e: e.memset(C.eps_col[:, 1:2], 64e-5), w=[('c_eps2',)])
        S.op('pool', lambda e: e.memset(C.eps_col[:, 2:3], 1e-24), w=[('c_eps3',)])
        S.barrier()
        stage_ingest(C)
        for sname in stages:
            if sname.startswith('ffn'):
                l, j = int(sname[3]), int(sname[4])
                stage_ffn(C, C.ffn_w_in[l, j], C.ffn_w_out[l, j], cols['norm_g_%d_%d' % (l, 0 if j == 0 else 2)][0],
                          "f%d%d_" % (l, j))
            elif sname == 'rwkv':
                stage_rwkv(C)
            elif sname == 'dsa':
                stage_dsa(C)
        stage_egress(C)
    return nc


def host_consts(inputs):
    cols, nv = _vec_layout()
    cols64, nv64 = _vec64_layout()
    vec = np.zeros((128, nv), np.float32)
    vec64 = np.zeros((64, nv64), np.float32)

    def put(name, v):
        c0, n = cols[name]
        vec[:, c0:c0 + n] = np.asarray(v, np.float32).reshape(n, 128).T

    def put64(name, v):
        c0, n = cols64[name]
        vec64[:, c0:c0 + n] = np.asarray(v, np.float32).reshape(n, 64).T

    ng = np.asarray(inputs['norm_g'])
    for l in range(2):
        for j in range(3):
            put('norm_g_%d_%d' % (l, j), ng[l, j])
    put('rk_mix', np.asarray(inputs['rk_mix'])[0].reshape(-1))
    put64('k_k', inputs['rk_k_k'][0])
    put64('k_a', inputs['rk_k_a'][0])
    put64('a0', inputs['rk_a0'][0])
    put64('r_k', np.asarray(inputs['rk_r_k'])[0].reshape(-1))
    vec64[:, cols64['q_g'][0]] = np.asarray(inputs['at_q_g'], np.float32)[0]
    vec64[:, cols64['k_g'][0]] = np.asarray(inputs['at_k_g'], np.float32)[0]
    inv = (np.float32(500000.0) ** (-np.arange(0, 16, 2, dtype=np.float32) / np.float32(16))).astype(np.float32)
    ang = (np.arange(TP, dtype=np.float32)[:, None] * inv[None, :]).astype(np.float32)
    rope = np.zeros((64, 2, TP), np.float32)
    rope[:, 0, :] = 1.0
    rope[0:8, 0, :] = np.cos(ang).T
    rope[8:16, 0, :] = np.cos(ang).T
    rope[0:8, 1, :] = np.sin(ang).T
    rope[8:16, 1, :] = np.sin(ang).T
    rot = np.zeros((64, 64), np.float32)
    for d in range(8):
        rot[d + 8, d] = -1.0
        rot[d, d + 8] = 1.0
    i128 = np.arange(128)
    negmask = np.where(i128[None, :] <= i128[:, None], 0.0, -1.0e30).astype(np.float32)
    ii = np.arange(64)
    su = (ii[:, None] < ii[None, :]).astype(np.float32)
    iu = (ii[:, None] <= ii[None, :]).astype(np.float32)
    sl = (ii[:, None] > ii[None, :]).astype(np.float32)
    masks = np.concatenate([su, iu, sl], axis=1)
    cdec = np.float32(-np.exp(-0.5))
    tri = np.concatenate([iu, su], axis=1) * cdec
    out = {"vecs": vec, "vecs64": vec64, "ident": np.eye(128, dtype=np.float32), "masks": masks,
           "tri": tri.astype(np.float32), "rope": rope, "rot": rot, "negmask": negmask,
           "at_w_in": np.ascontiguousarray(np.asarray(inputs['at_w_in'], np.float32)[0]),
           "at_w_o": np.ascontiguousarray(np.asarray(inputs['at_w_o'], np.float32)[0])}
    for k in RK_SHAPES:
        out["rk_" + k] = np.ascontiguousarray(np.asarray(inputs["rk_" + k], np.float32)[0].reshape(RK_SHAPES[k]))
    return out


ALL_STAGES = ['ffn00', 'rwkv', 'ffn01', 'ffn10', 'dsa', 'ffn11']
_cache = {}


def run(inputs, stages, cores=NCORES, trace=False):
    key = tuple(stages)
    if key not in _cache:
        _cache[key] = build_program(stages)
    nc = _cache[key]
    consts = host_consts(inputs)
    x = np.asarray(inputs['x'], np.float32)
    shared = {
        "meta": np.ascontiguousarray(np.asarray(inputs['meta'], np.float32)),
        "ffn_w_in": np.ascontiguousarray(np.asarray(inputs['ffn_w_in'], np.float32)),
        "ffn_w_out": np.ascontiguousarray(np.asarray(inputs['ffn_w_out'], np.float32)),
    }
    shared.update(consts)
    in_maps = []
    for b in range(cores):
        m = dict(shared)
        m["x"] = np.ascontiguousarray(x[b])
        in_maps.append(m)
    res = run_bass_kernel_spmd(nc, in_maps, core_ids=list(range(cores)), trace=trace)
    out = np.stack([np.asarray(r["out"], np.float32) for r in res.results], axis=0)
    return out, res


def kernel(**inputs):
    out, _ = run(inputs, ALL_STAGES)
    return out
```
